# Optimizing a Trainium2 kernel written in Bass

```python
import math
import jax, jax.numpy as jnp
from jax import lax
import numpy as np

D_MODEL = 2048
BATCH = 32
SEQ = 256
DEPTH = 2
DEC_BATCH = 2
DEC_SEQ = 4096
PAST_LEN = 256

GRID_W = 64
D_SSM = 512
SSM_GROUP = 16
N_SSM_GROUPS = D_SSM // SSM_GROUP
SSM_STATE = 64
HEAD_DIM = 128
N_HEADS = 8
N_KV_HEADS = 2
D_ATTN = N_HEADS * HEAD_DIM
D_KV = N_KV_HEADS * HEAD_DIM
WINDOW = 128
BLOCK = 128
ROPE_BASE = 10000.0
D_SGU = 512
SGU_CHUNK = 128
SGU_GROUP_CH = 128
N_SGU_GROUPS = D_SGU // SGU_GROUP_CH
N_BRANCH = 3
IN_SIZES = (D_SSM, D_SSM, D_ATTN, D_KV, D_KV, D_ATTN, D_SGU, D_SGU, D_SGU, N_BRANCH * D_MODEL)
IN_SPLITS = tuple(sum(IN_SIZES[:i + 1]) for i in range(len(IN_SIZES) - 1))
D_IN = sum(IN_SIZES)
DEEPNORM_ALPHA = (2 * DEPTH) ** 0.25
DEEPNORM_BETA = (8 * DEPTH) ** -0.25
LN_EPS = 1e-5

kernel_name = 'hybrid_s5_swa_sgu_diffusion_step'


def _layernorm(x):
    xf = x.astype(jnp.float32)
    mu = jnp.mean(xf, axis=-1, keepdims=True)
    var = jnp.mean(jnp.square(xf - mu), axis=-1, keepdims=True)
    return ((xf - mu) * lax.rsqrt(var + LN_EPS)).astype(x.dtype)


def _axial_rope(x, rows):
    row = jnp.repeat(jnp.arange(rows), GRID_W)
    col = jnp.tile(jnp.arange(GRID_W), rows)
    half = HEAD_DIM // 2
    inv = ROPE_BASE ** (-jnp.arange(0, half, 2, dtype=jnp.float32) / half)

    def rot(xh, pos):
        ang = pos.astype(jnp.float32)[:, None] * inv[None, :]
        cos = jnp.cos(ang)[None, :, None, :]
        sin = jnp.sin(ang)[None, :, None, :]
        x1, x2 = jnp.split(xh.astype(jnp.float32), 2, axis=-1)
        return jnp.concatenate([x1 * cos - x2 * sin, x1 * sin + x2 * cos], axis=-1)

    xr, xc = jnp.split(x, 2, axis=-1)
    return jnp.concatenate([rot(xr, row), rot(xc, col)], axis=-1).astype(x.dtype)


def _softmax_with_sink(s, sink_kg):
    sk = jnp.broadcast_to(sink_kg.astype(jnp.float32)[None, :, :, None, None], s.shape[:-1] + (1,))
    p = jax.nn.softmax(jnp.concatenate([s, sk], axis=-1), axis=-1)
    return p[..., :-1]


def _context_attention(q, k, v, sink):
    b, lc = q.shape[:2]
    nb = lc // BLOCK
    g = N_HEADS // N_KV_HEADS
    scale = HEAD_DIM ** -0.5
    sink_kg = sink.reshape(N_KV_HEADS, g)
    qb = q.reshape(b, nb, BLOCK, N_KV_HEADS, g, HEAD_DIM).transpose(1, 0, 2, 3, 4, 5)

    def one(qblk):
        s = jnp.einsum('bqkgd,bskd->bkgqs', qblk, k).astype(jnp.float32) * scale
        p = _softmax_with_sink(s, sink_kg).astype(v.dtype)
        return jnp.einsum('bkgqs,bskd->bqkgd', p, v)

    o = lax.map(one, qb)
    return o.transpose(1, 0, 2, 3, 4, 5).reshape(b, lc, D_ATTN)


def _latent_attention(q, k, v, kc, vc, sink):
    b, l = q.shape[:2]
    nb = l // BLOCK
    g = N_HEADS // N_KV_HEADS
    scale = HEAD_DIM ** -0.5
    sink_kg = sink.reshape(N_KV_HEADS, g)
    pad = ((0, 0), (BLOCK, BLOCK), (0, 0), (0, 0))
    kp = jnp.pad(k, pad).reshape(b, nb + 2, BLOCK, N_KV_HEADS, HEAD_DIM)
    vp = jnp.pad(v, pad).reshape(b, nb + 2, BLOCK, N_KV_HEADS, HEAD_DIM)
    kband = jnp.concatenate([kp[:, :-2], kp[:, 1:-1], kp[:, 2:]], axis=2)
    vband = jnp.concatenate([vp[:, :-2], vp[:, 1:-1], vp[:, 2:]], axis=2)
    qpos = jnp.arange(nb)[:, None] * BLOCK + jnp.arange(BLOCK)[None, :]
    kpos = jnp.arange(nb)[:, None] * BLOCK - BLOCK + jnp.arange(3 * BLOCK)[None, :]
    kq = kpos[:, None, :]
    mask = (jnp.abs(kq - qpos[:, :, None]) <= WINDOW) & (kq >= 0) & (kq < l)
    qb = q.reshape(b, nb, BLOCK, N_KV_HEADS, g, HEAD_DIM).transpose(1, 0, 2, 3, 4, 5)
    xs = (qb, kband.transpose(1, 0, 2, 3, 4), vband.transpose(1, 0, 2, 3, 4), mask)
    n_loc = 3 * BLOCK

    def one(args):
        qblk, kblk, vblk, m = args
        s_loc = jnp.einsum('bqkgd,bskd->bkgqs', qblk, kblk).astype(jnp.float32) * scale
        s_loc = jnp.where(m[None, None, None], s_loc, -jnp.inf)
        s_ctx = jnp.einsum('bqkgd,bskd->bkgqs', qblk, kc).astype(jnp.float32) * scale
        p = _softmax_with_sink(jnp.concatenate([s_loc, s_ctx], axis=-1), sink_kg).astype(v.dtype)
        return (jnp.einsum('bkgqs,bskd->bqkgd', p[..., :n_loc], vblk)
                + jnp.einsum('bkgqs,bskd->bqkgd', p[..., n_loc:], vc))

    o = lax.map(one, xs)
    return o.transpose(1, 0, 2, 3, 4, 5).reshape(b, l, D_ATTN)


def _zoh(lam_re, lam_im, log_step, b_re, b_im):
    lr = lam_re.astype(jnp.float32)
    li = lam_im.astype(jnp.float32)
    dt = jnp.exp(log_step.astype(jnp.float32))[:, None]
    mag = jnp.exp(lr * dt)
    ab_re = mag * jnp.cos(li * dt)
    ab_im = mag * jnp.sin(li * dt)
    den = lr * lr + li * li
    nr = ab_re - 1.0
    f_re = (nr * lr + ab_im * li) / den
    f_im = (ab_im * lr - nr * li) / den
    br = b_re.astype(jnp.float32)
    bi = b_im.astype(jnp.float32)
    bb_re = f_re[..., None] * br - f_im[..., None] * bi
    bb_im = f_re[..., None] * bi + f_im[..., None] * br
    return ab_re, ab_im, bb_re, bb_im


def _complex_scan(a_re, a_im, u_re, u_im, s0_re, s0_im):
    u_re = u_re.at[:, 0].add(a_re * s0_re - a_im * s0_im)
    u_im = u_im.at[:, 0].add(a_re * s0_im + a_im * s0_re)
    A_re = jnp.broadcast_to(a_re, u_re.shape)
    A_im = jnp.broadcast_to(a_im, u_im.shape)

    def comb(e1, e2):
        a1r, a1i, b1r, b1i = e1
        a2r, a2i, b2r, b2i = e2
        return (a2r * a1r - a2i * a1i, a2r * a1i + a2i * a1r,
                a2r * b1r - a2i * b1i + b2r, a2r * b1i + a2i * b1r + b2i)

    _, _, s_re, s_im = lax.associative_scan(comb, (A_re, A_im, u_re, u_im), axis=1)
    return s_re, s_im


def _ssm_branch(xa, s0, lam_re, lam_im, log_step, b_re, b_im, c_re, c_im, d_skip, w_glu, b_glu):
    bsz, L = xa.shape[:2]
    u = xa.astype(jnp.float32).reshape(bsz, L, N_SSM_GROUPS, SSM_GROUP)
    y = u * d_skip.astype(jnp.float32).reshape(N_SSM_GROUPS, SSM_GROUP)
    s0 = s0.astype(jnp.float32)
    finals = []
    for d in range(2):
        ab_re, ab_im, bb_re, bb_im = _zoh(lam_re[d], lam_im[d], log_step[d], b_re[d], b_im[d])
        ud = u if d == 0 else jnp.flip(u, axis=1)
        bu_re = jnp.einsum('gpc,blgc->blgp', bb_re, ud)
        bu_im = jnp.einsum('gpc,blgc->blgp', bb_im, ud)
        s_re, s_im = _complex_scan(ab_re, ab_im, bu_re, bu_im, s0[:, d, 0], s0[:, d, 1])
        yd = (jnp.einsum('gcp,blgp->blgc', c_re[d].astype(jnp.float32), s_re)
              - jnp.einsum('gcp,blgp->blgc', c_im[d].astype(jnp.float32), s_im))
        if d == 1:
            yd = jnp.flip(yd, axis=1)
        y = y + yd
        finals.append(jnp.stack([s_re[:, -1], s_im[:, -1]], axis=1))
    y = jax.nn.gelu(y.reshape(bsz, L, D_SSM))
    y = y * jax.nn.sigmoid(y @ w_glu.astype(jnp.float32) + b_glu.astype(jnp.float32))
    return y.astype(xa.dtype), jnp.stack(finals, axis=1)


def _sgu_branch(u, v, ln_g, ln_b, w_s, b_s):
    u = jax.nn.gelu(u)
    v = _layernorm(jax.nn.gelu(v)) * ln_g + ln_b
    b, L = v.shape[:2]
    nc = L // SGU_CHUNK
    vc = v.reshape(b, nc, SGU_CHUNK, N_SGU_GROUPS, SGU_GROUP_CH)
    vm = jnp.einsum('gpq,bnqgc->bnpgc', w_s, vc) + b_s.T[None, None, :, :, None]
    return u * vm.reshape(b, L, D_SGU)


def _trunk_layer(x, cond, lp, ctx_k, ctx_v, ssm_s0):
    is_context = ctx_k is None
    b, L = x.shape[:2]
    mod = jax.nn.silu(cond) @ lp['w_ada'] + lp['b_ada']
    shift, scale, gate = jnp.split(mod[:, None, :], 3, axis=-1)
    h = _layernorm(x) * (1.0 + scale) + shift
    proj = h @ lp['w_in']
    xa, za, q, k, v, zb, u, vs, zc, mg = jnp.split(proj, IN_SPLITS, axis=-1)
    if is_context:
        ssm_s0 = jnp.zeros((b, 2, 2, N_SSM_GROUPS, SSM_STATE), jnp.float32)
    ya, ssm_final = _ssm_branch(xa, ssm_s0, lp['lam_re'], lp['lam_im'], lp['log_step'],
                                lp['b_re'], lp['b_im'], lp['c_re'], lp['c_im'],
                                lp['d_skip'], lp['w_glu'], lp['b_glu'])
    ya = ya * jax.nn.silu(za)
    q = q.reshape(b, L, N_HEADS, HEAD_DIM)
    k = k.reshape(b, L, N_KV_HEADS, HEAD_DIM)
    v = v.reshape(b, L, N_KV_HEADS, HEAD_DIM)
    if is_context:
        yb = _context_attention(q, k, v, lp['sink'])
    else:
        rows = L // GRID_W
        yb = _latent_attention(_axial_rope(q, rows), _axial_rope(k, rows), v, ctx_k, ctx_v, lp['sink'])
    yb = yb * jax.nn.silu(zb)
    yc = _sgu_branch(u, vs, lp['sgu_g'], lp['sgu_b'], lp['w_s'], lp['b_s']) * jax.nn.silu(zc)
    g_a, g_b, g_c = jnp.split(jax.nn.sigmoid(mg), 3, axis=-1)
    merged = g_a * (ya @ lp['w_pa']) + g_b * (yb @ lp['w_pb']) + g_c * (yc @ lp['w_pc'])
    out = merged @ lp['w_out']
    y = _layernorm(DEEPNORM_ALPHA * x + gate * out) * lp['ln_g'] + lp['ln_b']
    return y, k, v, ssm_final


def setup_inputs(seed: int = 0) -> dict:
    key = jax.random.key(seed)
    ks = jax.random.split(key, 40)
    f32 = jnp.float32

    def nrm(k, shape, scale):
        return jax.random.normal(k, shape, f32) * scale

    G, P, C = N_SSM_GROUPS, SSM_STATE, SSM_GROUP
    n = jnp.arange(SSM_STATE, dtype=f32)
    return {
        'x_prompt': nrm(ks[0], (BATCH, SEQ, D_MODEL), 1.0),
        'x_sample': nrm(ks[1], (DEC_BATCH, DEC_SEQ, D_MODEL), 1.0),
        'cache_k': nrm(ks[2], (DEC_BATCH, DEPTH, PAST_LEN, N_KV_HEADS, HEAD_DIM), 1.0),
        'cache_v': nrm(ks[3], (DEC_BATCH, DEPTH, PAST_LEN, N_KV_HEADS, HEAD_DIM), 1.0),
        'state_ssm': nrm(ks[4], (DEC_BATCH, DEPTH, 2, 2, G, P), 0.1),
        'c': nrm(ks[5], (DEC_BATCH, D_MODEL), 1.0),
        'c_ctx': nrm(ks[6], (D_MODEL,), 1.0),
        'w_ada': nrm(ks[7], (DEPTH, D_MODEL, 3 * D_MODEL), 0.5 * D_MODEL ** -0.5),
        'b_ada': nrm(ks[8], (DEPTH, 3 * D_MODEL), 0.01),
        'w_in': nrm(ks[9], (DEPTH, D_MODEL, D_IN), D_MODEL ** -0.5),
        'ssm_lam_re': -0.5 + nrm(ks[10], (DEPTH, 2, G, P), 0.01),
        'ssm_lam_im': math.pi * n + nrm(ks[11], (DEPTH, 2, G, P), 0.01),
        'ssm_log_step': jax.random.uniform(ks[12], (DEPTH, 2, G), f32,
                                           minval=math.log(1e-3), maxval=math.log(1e-1)),
        'ssm_b_re': nrm(ks[13], (DEPTH, 2, G, P, C), (2.0 * C) ** -0.5),
        'ssm_b_im': nrm(ks[14], (DEPTH, 2, G, P, C), (2.0 * C) ** -0.5),
        'ssm_c_re': nrm(ks[15], (DEPTH, 2, G, C, P), (2.0 * P) ** -0.5),
        'ssm_c_im': nrm(ks[16], (DEPTH, 2, G, C, P), (2.0 * P) ** -0.5),
        'ssm_d': nrm(ks[17], (DEPTH, D_SSM), 1.0),
        'w_glu': nrm(ks[18], (DEPTH, D_SSM, D_SSM), D_SSM ** -0.5),
        'b_glu': nrm(ks[19], (DEPTH, D_SSM), 0.01),
        'attn_sink': nrm(ks[20], (DEPTH, N_HEADS), 0.5),
        'sgu_ln_g': 1.0 + nrm(ks[21], (DEPTH, D_SGU), 0.01),
        'sgu_ln_b': nrm(ks[22], (DEPTH, D_SGU), 0.01),
        'w_spatial': nrm(ks[23], (DEPTH, N_SGU_GROUPS, SGU_CHUNK, SGU_CHUNK), 0.5 * SGU_CHUNK ** -0.5),
        'b_spatial': 1.0 + nrm(ks[24], (DEPTH, N_SGU_GROUPS, SGU_CHUNK), 0.01),
        'w_proj_a': nrm(ks[25], (DEPTH, D_SSM, D_MODEL), DEEPNORM_BETA * D_SSM ** -0.5),
        'w_proj_b': nrm(ks[26], (DEPTH, D_ATTN, D_MODEL), DEEPNORM_BETA * D_ATTN ** -0.5),
        'w_proj_c': nrm(ks[27], (DEPTH, D_SGU, D_MODEL), DEEPNORM_BETA * D_SGU ** -0.5),
        'w_out': nrm(ks[28], (DEPTH, D_MODEL, D_MODEL), DEEPNORM_BETA * D_MODEL ** -0.5),
        'ln_g': 1.0 + nrm(ks[29], (DEPTH, D_MODEL), 0.01),
        'ln_b': nrm(ks[30], (DEPTH, D_MODEL), 0.01),
    }


def reference(x_prompt, x_sample, cache_k, cache_v, state_ssm, c, c_ctx,
              w_ada, b_ada, w_in, ssm_lam_re, ssm_lam_im, ssm_log_step,
              ssm_b_re, ssm_b_im, ssm_c_re, ssm_c_im, ssm_d, w_glu, b_glu,
              attn_sink, sgu_ln_g, sgu_ln_b, w_spatial, b_spatial,
              w_proj_a, w_proj_b, w_proj_c, w_out, ln_g, ln_b):
    def layer_params(l):
        return {
            'w_ada': w_ada[l], 'b_ada': b_ada[l], 'w_in': w_in[l],
            'lam_re': ssm_lam_re[l], 'lam_im': ssm_lam_im[l], 'log_step': ssm_log_step[l],
            'b_re': ssm_b_re[l], 'b_im': ssm_b_im[l], 'c_re': ssm_c_re[l], 'c_im': ssm_c_im[l],
            'd_skip': ssm_d[l], 'w_glu': w_glu[l], 'b_glu': b_glu[l],
            'sink': attn_sink[l], 'sgu_g': sgu_ln_g[l], 'sgu_b': sgu_ln_b[l],
            'w_s': w_spatial[l], 'b_s': b_spatial[l],
            'w_pa': w_proj_a[l], 'w_pb': w_proj_b[l], 'w_pc': w_proj_c[l],
            'w_out': w_out[l], 'ln_g': ln_g[l], 'ln_b': ln_b[l],
        }

    h = x_prompt
    ks_list, vs_list, ss_list = [], [], []
    for l in range(DEPTH):
        h, k_l, v_l, s_l = _trunk_layer(h, c_ctx[None, :], layer_params(l), None, None, None)
        ks_list.append(k_l)
        vs_list.append(v_l)
        ss_list.append(s_l)
    y_prompt = h
    new_cache_k = jnp.stack(ks_list, axis=1)
    new_cache_v = jnp.stack(vs_list, axis=1)
    new_state_ssm = jnp.stack(ss_list, axis=1)

    z = x_sample
    for l in range(DEPTH):
        z, _, _, _ = _trunk_layer(z, c, layer_params(l), cache_k[:, l], cache_v[:, l], state_ssm[:, l])
    y_sample = z
    return (y_prompt, y_sample, new_cache_k, new_cache_v, new_state_ssm)
```

```python
import contextlib
import math
import numpy as np
import concourse.bass as bass
import concourse.mybir as mybir
from concourse.bass_utils import run_bass_kernel_spmd

F32 = mybir.dt.float32
BF = mybir.dt.bfloat16
AF = mybir.ActivationFunctionType
ALU = mybir.AluOpType

D = 2048
KT = 16
NT = 256
DIN = 11264
LP = 256
LS = 4096
NPS = 4
DEPTH = 2
ALPHA = (2 * DEPTH) ** 0.25
EPS = 1e-5
GC1 = 1.5957691216057308
GC2 = 0.044715
SAME_ENG_SYNC = True
SAME_ENG_DIST = 8


class Sched:
    NS = 8

    def __init__(self, nc):
        self.nc = nc
        self.ops = []

    def op(self, eng, fn, r=(), w=()):
        w = tuple(w) + tuple(k for k in r if k.startswith('ps') and k not in w)
        self.ops.append((eng, fn, tuple(r), tuple(w), False))

    def dma(self, q, out, in_, r=(), w=(), slow=False, ind=None):
        self.ops.append((q, (out, in_, slow, ind), tuple(r), tuple(w), True))

    def emit(self, es):
        nc = self.nc
        ops = self.ops
        n = len(ops)
        last_w = {}
        readers = {}
        deps = [None] * n
        for i, (eng, fn, r, w, isd) in enumerate(ops):
            d = set()
            for k in r:
                if k in last_w:
                    d.add(last_w[k])
            for k in w:
                if k in last_w:
                    d.add(last_w[k])
                for j in readers.get(k, ()):
                    d.add(j)
            d.discard(i)
            deps[i] = d
            for k in w:
                last_w[k] = i
                readers[k] = []
            for k in r:
                if k not in w:
                    readers.setdefault(k, []).append(i)
        need = [False] * n
        lidx = [0] * n
        lc = {}
        for i in range(n):
            lidx[i] = lc.get(ops[i][0], 0)
            lc[ops[i][0]] = lidx[i] + 1

        def same_eng_skip(i, j):
            if ops[i][0] == 'pe' or not SAME_ENG_SYNC:
                return True
            return (lidx[i] - lidx[j]) > SAME_ENG_DIST

        for i in range(n):
            ei = ops[i][0]
            for j in deps[i]:
                ej, _, _, _, dj = ops[j]
                if dj or ej != ei or not same_eng_skip(i, j):
                    need[j] = True
        engs = ['pe', 'act', 'dve', 'pool', 'sp']
        esem = {e: es.enter_context(nc.semaphore('es_' + e)) for e in engs}
        dsem = {q: [es.enter_context(nc.semaphore('ds_%s%d' % (q, k))) for k in range(self.NS)]
                for q in ('sp', 'pool', 'act')}
        cnt = {e: 0 for e in engs}
        dcnt = {q: 0 for q in dsem}
        sig = [None] * n
        streams = {e: [] for e in engs}
        waited = {e: {} for e in engs}

        def addwait(e, lst, sem, val):
            key = id(sem)
            if waited[e].get(key, 0) >= val:
                return
            waited[e][key] = val
            lst.append(('w', sem, val))

        for i, (eng, fn, r, w, isd) in enumerate(ops):
            lst = streams[eng]
            wmax = {}
            for j in deps[i]:
                if sig[j] is None:
                    continue
                ej, dj = ops[j][0], ops[j][4]
                if (not dj) and ej == eng and same_eng_skip(i, j):
                    continue
                key = id(sig[j][0])
                if key not in wmax or wmax[key][1] < sig[j][1]:
                    wmax[key] = sig[j]
            for key in sorted(wmax, key=lambda k_: wmax[k_][1]):
                addwait(eng, lst, wmax[key][0], wmax[key][1])
            if isd:
                k = dcnt[eng]
                dcnt[eng] += 1
                sem = dsem[eng][k % self.NS]
                rnd = k // self.NS
                if rnd > 0:
                    addwait(eng, lst, sem, 16 * rnd)
                sig[i] = (sem, 16 * (rnd + 1))
                lst.append(('d', fn, sem))
            else:
                if need[i]:
                    cnt[eng] += 1
                    sig[i] = (esem[eng], cnt[eng])
                    lst.append(('o', fn, esem[eng]))
                else:
                    lst.append(('o', fn, None))
        for q in dsem:
            for k in range(self.NS):
                tot = (dcnt[q] - k + self.NS - 1) // self.NS if dcnt[q] > k else 0
                if tot > 0:
                    streams[q].append(('w', dsem[q][k], 16 * tot))

        def run(engine, lst):
            for it in lst:
                if it[0] == 'w':
                    engine.wait_ge(it[1], it[2])
                elif it[0] == 'd':
                    out, in_, slow, ind = it[1]
                    if ind is not None:
                        engine.indirect_dma_start(out=out, out_offset=None, in_=in_,
                                                  in_offset=bass.IndirectOffsetOnAxis(ap=ind, axis=0)).then_inc(it[2], 16)
                    elif slow:
                        engine.dma_start(out=out, in_=in_, allow_slow_non_contiguous=True).then_inc(it[2], 16)
                    else:
                        engine.dma_start(out=out, in_=in_).then_inc(it[2], 16)
                else:
                    ins = it[1](engine)
                    if it[2] is not None:
                        ins.then_inc(it[2], 1)

        block = es.enter_context(nc.Block())

        @block.tensor
        def _(e):
            run(e, streams['pe'])

        @block.scalar
        def _(e):
            run(e, streams['act'])

        @block.vector
        def _(e):
            run(e, streams['dve'])

        @block.gpsimd
        def _(e):
            run(e, streams['pool'])

        @block.sync
        def _(e):
            run(e, streams['sp'])
        return {e: (len(streams[e]), cnt[e]) for e in engs}


def _core_consts(core, consts):
    j = core % 4
    m = {}
    p = np.arange(128)[:, None]
    cidx = np.arange(10)[None, :]
    m['ridx'] = np.clip(1024 * j + 128 * (cidx - 1) + p, 0, LS - 1).astype(np.int32)
    q = np.clip(1024 * j - 128 + np.arange(1280), 0, LS - 1)
    m['ropec_o'] = np.ascontiguousarray(consts['ropec'][:, q])
    m['ropes_o'] = np.ascontiguousarray(consts['ropes'][:, q])
    oh = np.zeros((128, 4), np.float32); oh[:, j] = 1.0
    m['oh4'] = oh
    vm = np.ones((128, 2), np.float32)
    if j == 0:
        vm[:, 0] = 0.0
    if j == 3:
        vm[:, 1] = 0.0
    m['vmask'] = vm
    return m


def _host_consts():
    c = {}
    c['ident'] = np.eye(128, dtype=np.float32)
    R = np.zeros((128, 128), np.float32)
    for d in range(128):
        if d % 64 < 32:
            R[d, d + 32] = -1.0
        else:
            R[d, d - 32] = 1.0
    c['rotT'] = np.ascontiguousarray(R.T)
    pos = np.arange(LS)
    row = pos // 64
    col = pos % 64
    inv = 10000.0 ** (-np.arange(0, 64, 2, dtype=np.float32) / 64.0)
    ang = np.zeros((128, LS), np.float32)
    for d in range(128):
        p = row if d < 64 else col
        ang[d] = p.astype(np.float32) * inv[d % 32]
    c['ropec'] = np.cos(ang).astype(np.float32)
    c['ropes'] = np.sin(ang).astype(np.float32)
    kk = np.arange(128)[:, None]
    qq = np.arange(128)[None, :]
    c['mlo'] = (kk >= qq).astype(np.float32)
    c['mhi'] = (kk <= qq).astype(np.float32)
    tp = (np.arange(128) // 16)[:, None]
    tt = (np.arange(128) // 16)[None, :]
    c['cmf'] = (tt >= tp).astype(np.float32)
    c['cmb'] = (tp >= tt).astype(np.float32)
    return c


def build(stop=None, dumps=()):
    nc = bass.Bass("TRN2", target_bir_lowering=False)
    es = contextlib.ExitStack()
    sc = Sched(nc)
    PI = math.pi

    def din(name, shape, dt=F32):
        return nc.dram_tensor(name, list(shape), dt, kind="ExternalInput").ap()

    def dout(name, shape):
        return nc.dram_tensor(name, list(shape), F32, kind="ExternalOutput").ap()

    def dscr(name, shape, dt=F32):
        return nc.dram_tensor(name, list(shape), dt, kind="Internal").ap()

    xp = din("xp", [NPS * LP, D]); xs = din("xs", [LS, D])
    ck = din("ck", [2, 256, 256]); cv = din("cv", [2, 256, 256])
    st0 = din("st0", [2, 2, 2, 32, 64]); cvec = din("cvec", [2, D])
    w_ada = din("w_ada", [2, D, 3 * D]); b_ada = din("b_ada", [2, 3 * D]); w_in = din("w_in", [2, D, DIN])
    lam_re = din("lam_re", [2, 2, 32, 64]); lam_im = din("lam_im", [2, 2, 32, 64]); log_step = din("log_step", [2, 2, 32])
    b_re = din("b_re", [2, 2, 32, 64, 16]); b_im = din("b_im", [2, 2, 32, 64, 16])
    c_re = din("c_re", [2, 2, 32, 16, 64]); c_im = din("c_im", [2, 2, 32, 16, 64])
    ssm_d = din("ssm_d", [2, 512]); w_glu = din("w_glu", [2, 512, 512]); b_glu = din("b_glu", [2, 512])
    sink = din("sink", [2, 8]); sgu_g = din("sgu_g", [2, 512]); sgu_b = din("sgu_b", [2, 512])
    w_s = din("w_s", [2, 4, 128, 128]); b_s = din("b_s", [2, 512])
    w_pa = din("w_pa", [2, 512, D]); w_pb = din("w_pb", [2, 1024, D]); w_pc = din("w_pc", [2, 512, D])
    w_out = din("w_out", [2, D, D]); ln_g = din("ln_g", [2, D]); ln_b = din("ln_b", [2, D])
    c_ident = din("ident", [128, 128]); c_rotT = din("rotT", [128, 128])
    c_ropec = din("ropec", [128, LS]); c_ropes = din("ropes", [128, LS])
    c_mlo = din("mlo", [128, 128]); c_mhi = din("mhi", [128, 128])
    c_cmf = din("cmf", [128, 128]); c_cmb = din("cmb", [128, 128])
    ridx_d = din("ridx", [128, 10], mybir.dt.int32); c_ropec_o = din("ropec_o", [128, 1280]); c_ropes_o = din("ropes_o", [128, 1280])
    oh4_d = din("oh4", [128, 4]); vmask_d = din("vmask", [128, 2])
    yp = dout("yp", [NPS * LP, D]); ys = dout("ys_own", [1024, D])
    nk = dout("nk", [NPS, 2, LP, 256]); nv = dout("nv", [NPS, 2, LP, 256]); nst = dout("nst", [NPS, 2, 2, 2, 32, 64])
    zp = dscr("zp", [NPS * LP, D]); zs = dscr("zs", [LS, D]); gsc = dscr("gsc", [2, D])
    s5w = dscr("s5w", [64, 3, 128, 128], BF)
    wsc = dscr("wsc", [2, 76, 128, 16, 256], BF)

    def sb(name, shape, dt=F32):
        return es.enter_context(nc.sbuf_tensor("s_" + name, list(shape), dt))

    ps = [es.enter_context(nc.psum_tensor("ps%d" % i, [128, 512], F32)) for i in range(8)]
    psn = [0]

    def bank():
        i = psn[0] % 8
        psn[0] += 1
        return ps[i], 'ps%d' % i

    xin = [sb("xin0", [128, D]), sb("xin1", [128, D])]
    rr = sb("rr", [128, D])
    stats = sb("stats", [128, 4, 6]); mv = sb("mv", [128, 2]); rstd = sb("rstd", [128, 1])
    hT = sb("hT", [128, KT, 512], BF)
    wb = [sb("wb%d" % i, [128, KT, 256], BF) for i in range(3)]
    zaT = sb("zaT", [128, 4, NT], BF); qT = sb("qT", [128, 8, NT], BF); yaT = zaT; ybT = qT; kT = sb("kT", [128, 2, 512], BF)
    zbT = sb("zbT", [128, 8, NT], BF); uT = sb("uT", [128, 4, NT], BF); zcT = sb("zcT", [128, 4, NT], BF); ycT = zcT
    Xp = sb("Xp", [32, 32, 8, 16], BF); Xpp = sb("Xpp", [128, 32, 32], BF)
    vtok = sb("vtok", [128, 4, 256], BF); vsg = rr[:, 0:1024].rearrange("p (s c) -> p s c", c=512)
    vsln = sb("vsln", [128, 2, 512], BF)
    V = sb("V", [128, 64, 32]); Vs = sb("Vs", [128, 64, 32]); Sb = sb("Sb", [128, 64, 32], BF)
    tA = sb("tA", [128, 64]); tB = sb("tB", [128, 64])
    yg = sb("yg", [32, 8, 32, 16], BF); ygf = sb("ygf", [32, 4, 8, 16]); ygf2 = sb("ygf2", [32, 4, 8, 16])
    ygT = sb("ygT", [128, 4, NT], BF)
    s5wb = [sb("s5wb%d" % i, [128, 8, 3, 128], BF) for i in range(2)]
    merged = sb("merged", [128, KT, NT], BF)
    tfall = sb("tfall", [128, 4, 512])
    tf = [tfall[:, i, :] for i in range(4)]
    pTs = [sb("pTs%d" % i, [128, 512], BF) for i in range(5)]
    Fg = V[:, 0:32, :].rearrange("p (g a) n -> p g (a n)", a=4)
    Gg = V[:, 32:64, :].rearrange("p (g a) n -> p g (a n)", a=4)
    Eg = Vs[:, 0:32, :].rearrange("p (g a) n -> p g (a n)", a=4)
    w3 = s5wb[0]
    wsn = tf[0][:, :].rearrange("p (g q) -> p g q", q=128)
    ckn = tf[1][:, :].rearrange("p (b c) -> p b c", c=256)
    rc = sb("rc", [128, 512]); rs = sb("rs", [128, 512]); qraw = pTs[4]
    ident = sb("ident", [128, 128]); identb = sb("identb", [128, 128], BF); rotT = sb("rotT", [128, 128], BF)
    onesb = sb("onesb", [128, 128], BF); mlo = sb("mlo", [128, 128], BF); mhi = sb("mhi", [128, 128], BF)
    cmf = sb("cmf", [128, 128]); cmb = sb("cmb", [128, 128])
    scT = sb("scT", [128, 2, KT], BF); cvT = sb("cvT", [128, 2, KT])
    modT = sb("modT", [128, 48, 2]); badaT = sb("badaT", [128, 48]); sc1 = sb("sc1", [128, KT, 2])
    sgb = sb("sgb", [128, 512]); sbb = sb("sbb", [128, 512]); bsb = sb("bsb", [128, 512])
    wsT = sb("wsT", [128, 4, 128], BF)
    esink = sb("esink", [128, 8]); ckT = sb("ckT", [128, 2, 256], BF)
    cvb = sb("cvb", [128, 2, 256], BF)
    wglu = sb("wglu", [128, 4, 512], BF); bglu = sb("bglu", [128, 4]); dsk = sb("dsk", [32, 512])
    lr = sb("lr", [128, 32]); li = sb("li", [128, 32]); dtt = sb("dtt", [128, 32])
    p1 = sb("p1", [128, 32]); p2 = sb("p2", [128, 32]); p3 = sb("p3", [128, 32]); p4 = sb("p4", [128, 32])
    cosv = sb("cosv", [128, 32]); sinv = sb("sinv", [128, 32])
    fre = sb("fre", [128, 32]); fim = sb("fim", [128, 32])
    PWr = xin[1][:, 0:544].rearrange("p (k g) -> p k g", g=32); PWi = xin[1][:, 544:1088].rearrange("p (k g) -> p k g", g=32)
    nPWi = xin[1][:, 1088:1632].rearrange("p (k g) -> p k g", g=32)
    Bre = sb("Bre", [128, 8, 16]); Bim = sb("Bim", [128, 8, 16]); Bbr = sb("Bbr", [128, 8, 16]); Bbi = sb("Bbi", [128, 8, 16])
    bt1 = sb("bt1", [128, 8, 16]); bt2 = sb("bt2", [128, 8, 16])
    cnat = sb("cnat", [128, 2, 64]); Cre = sb("Cre", [128, 8, 16]); nCim = sb("nCim", [128, 8, 16])
    Ar = sb("Ar", [128, 64]); Aix = sb("Aix", [128, 64]); nAix = sb("nAix", [128, 64]); sgn = sb("sgn", [128, 1])
    A2r = sb("A2r", [128, 64]); A2i = sb("A2i", [128, 64]); A2ix = sb("A2ix", [128, 64]); nA2ix = sb("nA2ix", [128, 64])
    sinit = sb("sinit", [128, 2, 32]); sinitw = sb("sinitw", [128, 2, 32])
    Pst = sb("Pst", [128, 17, 2, 64])
    Lst = sb("Lst", [128, 16, 2, 32])
    stg = sb("stg", [32, 128])
    Ssum = sb("Ssum", [128, 4, 64])
    ridx = sb("ridx", [128, 10], mybir.dt.int32); oh4 = sb("oh4", [128, 4]); vmask = sb("vmask", [128, 2])
    Psel = sb("Psel", [128, 2, 2, 32])

    def tt(eng, out, a, b, op, r, w):
        sc.op(eng, lambda e: e.tensor_tensor(out=out, in0=a, in1=b, op=op), r, w)

    def ts(eng, out, a, s1, op0, r, w, s2=None, op1=None):
        if op1 is None:
            sc.op(eng, lambda e: e.tensor_scalar(out=out, in0=a, scalar1=s1, scalar2=None, op0=op0), r, w)
        else:
            sc.op(eng, lambda e: e.tensor_scalar(out=out, in0=a, scalar1=s1, scalar2=s2, op0=op0, op1=op1), r, w)

    def act(out, in_, func, r, w, bias=None, scale=None):
        kw = {}
        if bias is not None:
            kw['bias'] = bias
        if scale is not None:
            kw['scale'] = scale
        sc.op('act', lambda e: e.activation(out=out, in_=in_, func=func, **kw), r, w)

    def mm(out, lhsT, rhs, start, stop, r, w):
        sc.op('pe', lambda e: e.matmul(out, lhsT, rhs, start=start, stop=stop), r, w)

    def tr(out, in_, idn, r, w):
        sc.op('pe', lambda e: e.transpose(out, in_, idn), r, w)

    def cp(eng, out, in_, r, w):
        if eng == 'act':
            sc.op(eng, lambda e: e.copy(out=out, in_=in_), r, w)
        else:
            sc.op(eng, lambda e: e.tensor_copy(out=out, in_=in_), r, w)

    def recip(out, in_, r, w):
        sc.op('dve', lambda e: e.reciprocal(out=out, in_=in_), r, w)

    def mset(eng, ap, val, w):
        sc.op(eng, lambda e: e.memset(ap, val), (), w)

    def bc(ap, shape):
        return ap.broadcast_to(list(shape))

    sc.dma('sp', ident[:], c_ident, w=['ident'])
    cp('dve', identb[:], ident[:], ['ident'], ['identb'])
    sc.dma('sp', tf[0][:, 0:128], c_rotT, w=['tf0'])
    cp('dve', rotT[:], tf[0][:, 0:128], ['tf0'], ['rotT'])
    sc.dma('sp', tf[1][:, 0:128], c_mlo, w=['tf1'])
    cp('dve', mlo[:], tf[1][:, 0:128], ['tf1'], ['mlo'])
    sc.dma('sp', tf[2][:, 0:128], c_mhi, w=['tf2'])
    cp('dve', mhi[:], tf[2][:, 0:128], ['tf2'], ['mhi'])
    sc.dma('sp', cmf[:], c_cmf, w=['cmf'])
    sc.dma('sp', cmb[:], c_cmb, w=['cmb'])
    sc.dma('sp', ridx[:], ridx_d, w=['ridx'])
    sc.dma('sp', oh4[:], oh4_d, w=['oh4'])
    sc.dma('sp', vmask[:], vmask_d, w=['vmask'])
    mset('dve', onesb[:], 1.0, ['onesb'])
    mset('dve', sgn[0:64, :], -1.0, ['sgn'])
    mset('dve', sgn[64:128, :], 1.0, ['sgn'])
    sc.dma('sp', cvT[:], cvec.rearrange("c (k p) -> p c k", p=128), w=['cvT'], slow=True)
    act(scT[:], cvT[:], AF.Silu, ['cvT'], ['scT'])

    wslot = [0]

    WNAMES = {'in': (w_in, 16, 0), 'pp': (None, 16, 44), 'out': (w_out, 16, 68)}
    WPACK = (('pa', w_pa, 4, 0), ('pb', w_pb, 8, 4), ('pc', w_pc, 4, 12))

    def wload_cast(src):
        i = wslot[0] % 3
        wslot[0] += 1
        K = src.shape[0] // 128
        sc.dma('pool', wb[i][:, 0:K, :], src.rearrange("(k p) c -> p k c", p=128), w=['wb%d' % i])
        return wb[i], 'wb%d' % i

    def wconvert(l):
        for nm, (wt_, K, g0) in WNAMES.items():
            if wt_ is None:
                continue
            ng = wt_.shape[2] // 256
            for gi in range(ng):
                t_, k_ = wload_cast(wt_[l][:, gi * 256:(gi + 1) * 256])
                sc.dma('sp', wsc[l, g0 + gi, :, 0:K, :], t_[:, 0:K, :], r=[k_], w=['wsc%d_%d' % (l, g0 + gi)])
        for fg in range(8):
            for nm, wt_, K, koff in WPACK:
                t_, k_ = wload_cast(wt_[l][:, fg * 256:(fg + 1) * 256])
                sc.dma('sp', wsc[l, 44 + fg, :, koff:koff + K, :], t_[:, 0:K, :], r=[k_], w=['wsc%d_%d' % (l, 44 + fg)])

    def wload(nm, l, gi, slot=None):
        wt_, K, g0 = WNAMES[nm]
        if slot is None:
            i = wslot[0] % 3
            wslot[0] += 1
        else:
            i = slot
        sc.dma('sp', wb[i][:, 0:K, :], wsc[l, g0 + gi, :, 0:K, :], r=['wsc%d_%d' % (l, g0 + gi)], w=['wb%d' % i])
        return wb[i], 'wb%d' % i

    def gelu_evac(out, pin, pk, shape, tix, rextra=(), wk=()):
        n = 1
        for s_ in shape[1:]:
            n *= s_
        t1 = tf[tix][0:shape[0], 0:n]
        t2 = tf[tix + 1][0:shape[0], 0:n]
        k1, k2 = 'tf%d' % tix, 'tf%d' % (tix + 1)
        pin2 = pin
        act(t1, pin2, AF.Square, [pk], [k1])
        ts('dve', t1, t1, GC2, ALU.mult, [k1], [k1], s2=1.0, op1=ALU.add)
        tt('dve', t1, t1, pin2, ALU.mult, [k1, pk], [k1])
        act(t2, t1, AF.Sigmoid, [k1], [k2], scale=GC1)
        tt('dve', out, t2, pin2, ALU.mult, [k2, pk] + list(rextra), list(wk))

    def wrap(out, x, kx, ko):
        mset('dve', p4[:], 0.0, ['p4'])
        for m in range(1, 9):
            ts('dve', p3[:], x, (2 * m - 1) * PI, ALU.is_gt, [kx], ['p3'])
            tt('dve', p4[:], p4[:], p3[:], ALU.add, ['p3', 'p4'], ['p4'])
        ts('dve', p4[:], p4[:], -2.0 * PI, ALU.mult, ['p4'], ['p4'])
        tt('dve', out, x, p4[:], ALU.add, [kx, 'p4'], [ko])

    def cmul(ore, oim, are, aim, bre, bim, keys_r, ko):
        tt('dve', p1[:], are, bre, ALU.mult, keys_r, ['p1'])
        tt('dve', p2[:], aim, bim, ALU.mult, keys_r, ['p2'])
        tt('dve', p3[:], are, bim, ALU.mult, keys_r, ['p3'])
        tt('dve', p4[:], aim, bre, ALU.mult, keys_r, ['p4'])
        tt('dve', ore, p1[:], p2[:], ALU.subtract, ['p1', 'p2'], ko)
        tt('dve', oim, p3[:], p4[:], ALU.add, ['p3', 'p4'], ko)

    def mix(out, g0, ng, Wre_, Wim_, Xre, Xim, kr, ko):
        for h in (0, 1):
            P = slice(h * 64, h * 64 + 64)
            wr = bc(Wre_[P, g0:g0 + ng].unsqueeze(2), [64, ng, 16])
            wi = bc(Wim_[P, g0:g0 + ng].unsqueeze(2), [64, ng, 16])
            xa_ = Xre[P, 0:ng, :] if h == 0 else Xim[P, 0:ng, :]
            xb_ = Xim[P, 0:ng, :] if h == 0 else Xre[P, 0:ng, :]
            tt('dve', bt1[P, 0:ng, :], xa_, wr, ALU.mult, kr, ['bt1'])
            tt('dve', bt2[P, 0:ng, :], xb_, wi, ALU.mult, kr, ['bt2'])
            tt('dve', out[P], bt1[P, 0:ng, :], bt2[P, 0:ng, :], ALU.subtract if h == 0 else ALU.add,
               ['bt1', 'bt2'], ko)

    EF = [[t + 1 for t in range(8)], [8 - t for t in range(8)]]

    def layer_prep(l):
        sc.dma('act', badaT[:], b_ada[l].rearrange("(c p) -> p c", p=128), w=['badaT'], slow=True)
        pm, pmk = bank()
        for gi in range(24):
            wt, wk = wload_cast(w_ada[l][:, gi * 256:(gi + 1) * 256])
            for j in range(2):
                ch = gi * 2 + j
                for k in range(KT):
                    mm(pm[:, ch * 2:ch * 2 + 2], wt[:, k, j * 128:(j + 1) * 128], scT[:, :, k],
                       k == 0, k == KT - 1, [wk, 'scT'], [pmk])
        tt('dve', modT[:], pm[:, 0:96].rearrange("p (c t) -> p c t", t=2), bc(badaT[:].unsqueeze(2), [128, 48, 2]),
           ALU.add, [pmk, 'badaT'], ['modT'])
        ts('dve', sc1[:], modT[:, 16:32, :], 1.0, ALU.add, ['modT'], ['sc1'])
        for c_ in range(2):
            sc.dma('act', gsc[c_].rearrange("(k p) -> p k", p=128), modT[:, 32:48, c_], r=['modT'], w=['gsc'], slow=True)
        sc.dma('act', sgb[:], sgu_g[l].partition_broadcast(128), w=['sgb'])
        sc.dma('act', sbb[:], sgu_b[l].partition_broadcast(128), w=['sbb'])
        sc.dma('act', bsb[:], b_s[l].partition_broadcast(128), w=['bsb'])
        sc.dma('act', wsn[:], w_s[l].rearrange("g p q -> p g q"), w=['tf0'])
        pw, pwk = bank()
        for g in range(4):
            tr(pw[:, g * 128:(g + 1) * 128], wsn[:, g, :], ident[:], ['tf0', 'ident'], [pwk])
        cp('dve', wsT[:], pw[:, 0:512].rearrange("p (g q) -> p g q", q=128), [pwk], ['wsT'])
        sc.dma('act', esink[:], sink[l].partition_broadcast(128), w=['esink'])
        act(esink[:], esink[:], AF.Exp, ['esink'], ['esink'])
        sc.dma('act', ckn[:], ck[l].rearrange("(b p) c -> p b c", p=128), w=['tf1'])
        pc_, pck = bank()
        for kvh in range(2):
            for b_ in range(2):
                tr(pc_[:, (kvh * 2 + b_) * 128:(kvh * 2 + b_ + 1) * 128], ckn[:, b_, kvh * 128:(kvh + 1) * 128],
                   ident[:], ['tf1', 'ident'], [pck])
        cp('dve', ckT[:], pc_[:, 0:512].rearrange("p (h t) -> p h t", t=256), [pck], ['ckT'])
        sc.dma('pool', cvb[:], cv[l].rearrange("(b p) c -> p b c", p=128), w=['cvb'])
        sc.dma('pool', wglu[:], w_glu[l].rearrange("(j p) c -> p j c", p=128), w=['wglu'])
        sc.dma('act', bglu[:], b_glu[l].rearrange("(j p) -> p j", p=128), w=['bglu'], slow=True)
        sc.dma('act', dsk[:], ssm_d[l].partition_broadcast(32), w=['dsk'])
        for d in range(2):
            for h in range(2):
                P = slice(h * 64, h * 64 + 64)
                sc.dma('act', lr[P, :], lam_re[l, d].rearrange("g p -> p g"), w=['lr'], slow=True)
                sc.dma('act', li[P, :], lam_im[l, d].rearrange("g p -> p g"), w=['li'], slow=True)
                for ri in range(2):
                    sc.dma('act', sinit[ri * 64:(ri + 1) * 64, d, :] if h == 0 else sinitw[(1 - ri) * 64:(2 - ri) * 64, d, :],
                           st0[l, d, ri].rearrange("g p -> p g"), w=['sinit' if h == 0 else 'sinitw'], slow=True)
            sc.dma('act', dtt[:], log_step[l, d].partition_broadcast(128), w=['dtt'])
            act(dtt[:], dtt[:], AF.Exp, ['dtt'], ['dtt'])
            tt('dve', p1[:], li[:], dtt[:], ALU.mult, ['li', 'dtt'], ['p1'])
            wrap(p2[:], p1[:], 'p1', 'p2')
            act(sinv[:], p2[:], AF.Sin, ['p2'], ['sinv'])
            if stop == 'wrap':
                return
            ts('dve', p1[:], p1[:], PI / 2, ALU.add, ['p1'], ['p1'])
            wrap(p2[:], p1[:], 'p1', 'p2')
            act(cosv[:], p2[:], AF.Sin, ['p2'], ['cosv'])
            tt('dve', p1[:], lr[:], dtt[:], ALU.mult, ['lr', 'dtt'], ['p1'])
            act(p2[:], p1[:], AF.Exp, ['p1'], ['p2'])
            act(p3[:], p1[:], AF.Exp, ['p1'], ['p3'], scale=-1.0)
            mset('dve', PWr[:, 8, :], 1.0, ['PW', 'nPW', 'xin1'])
            mset('dve', PWi[:, 8, :], 0.0, ['PW'])
            tt('dve', PWr[:, 9, :], p2[:], cosv[:], ALU.mult, ['p2', 'cosv'], ['PW'])
            tt('dve', PWi[:, 9, :], p2[:], sinv[:], ALU.mult, ['p2', 'sinv'], ['PW'])
            tt('dve', PWr[:, 7, :], p3[:], cosv[:], ALU.mult, ['p3', 'cosv'], ['PW'])
            tt('dve', PWi[:, 7, :], p3[:], sinv[:], ALU.mult, ['p3', 'sinv'], ['PW'])
            ts('dve', PWi[:, 7, :], PWi[:, 7, :], -1.0, ALU.mult, ['PW'], ['PW'])
            for k in range(2, 9):
                cmul(PWr[:, 8 + k, :], PWi[:, 8 + k, :], PWr[:, 7 + k, :], PWi[:, 7 + k, :], PWr[:, 9, :], PWi[:, 9, :], ['PW'], ['PW'])
                cmul(PWr[:, 8 - k, :], PWi[:, 8 - k, :], PWr[:, 9 - k, :], PWi[:, 9 - k, :], PWr[:, 7, :], PWi[:, 7, :], ['PW'], ['PW'])
            ts('dve', nPWi[:], PWi[:], -1.0, ALU.mult, ['PW'], ['nPW'])
            tt('dve', p1[:], lr[:], lr[:], ALU.mult, ['lr'], ['p1'])
            tt('dve', p2[:], li[:], li[:], ALU.mult, ['li'], ['p2'])
            tt('dve', p1[:], p1[:], p2[:], ALU.add, ['p1', 'p2'], ['p1'])
            recip(p1[:], p1[:], ['p1'], ['p1'])
            ts('dve', p2[:], PWr[:, 9, :], -1.0, ALU.add, ['PW'], ['p2'])
            tt('dve', p3[:], p2[:], lr[:], ALU.mult, ['p2', 'lr'], ['p3'])
            tt('dve', p4[:], PWi[:, 9, :], li[:], ALU.mult, ['PW', 'li'], ['p4'])
            tt('dve', p3[:], p3[:], p4[:], ALU.add, ['p3', 'p4'], ['p3'])
            tt('dve', fre[:], p3[:], p1[:], ALU.mult, ['p3', 'p1'], ['fre'])
            tt('dve', p3[:], PWi[:, 9, :], lr[:], ALU.mult, ['PW', 'lr'], ['p3'])
            tt('dve', p4[:], p2[:], li[:], ALU.mult, ['p2', 'li'], ['p4'])
            tt('dve', p3[:], p3[:], p4[:], ALU.subtract, ['p3', 'p4'], ['p3'])
            tt('dve', fim[:], p3[:], p1[:], ALU.mult, ['p3', 'p1'], ['fim'])
            cp('dve', Ar[:, d * 32:(d + 1) * 32], PWr[:, 16, :], ['PW'], ['Ar'])
            ts('dve', Aix[:, d * 32:(d + 1) * 32], PWi[:, 16, :], sgn[:, 0:1], ALU.mult, ['PW', 'sgn'], ['Aix'])
            ts('dve', nAix[:, d * 32:(d + 1) * 32], Aix[:, d * 32:(d + 1) * 32], -1.0, ALU.mult, ['Aix'], ['nAix'])
            cp('dve', A2r[:, d * 32:(d + 1) * 32], PWr[:, 16, :], ['PW'], ['A2'])
            cp('dve', A2i[:, d * 32:(d + 1) * 32], PWi[:, 16, :], ['PW'], ['A2'])
            cm = cmf if d == 0 else cmb
            cmk = 'cmf' if d == 0 else 'cmb'
            for gb in range(4):
                g0 = gb * 8
                for h in range(2):
                    P = slice(h * 64, h * 64 + 64)
                    sc.dma('act', Bre[P], b_re[l, d, g0:g0 + 8].rearrange("g p c -> p g c"), w=['Bre'])
                    sc.dma('act', Bim[P], b_im[l, d, g0:g0 + 8].rearrange("g p c -> p g c"), w=['Bim'])
                for (src_c, dst_c, neg) in ((c_re, Cre, False), (c_im, nCim, True)):
                    for h in range(2):
                        sc.dma('act', cnat[:, h, :], src_c[l, d, g0:g0 + 8].rearrange("g c p -> (g c) p"), w=['cnat'])
                    pb_, pbk = bank()
                    tr(pb_[:, 0:128], cnat[:].rearrange("q h p -> q (h p)"), ident[:], ['cnat', 'ident'], [pbk])
                    if neg:
                        ts('dve', dst_c[:], pb_[:, 0:128].rearrange("p (g c) -> p g c", c=16), -1.0, ALU.mult, [pbk], ['C'])
                    else:
                        cp('dve', dst_c[:], pb_[:, 0:128].rearrange("p (g c) -> p g c", c=16), [pbk], ['C'])
                fr_b = bc(fre[:, g0:g0 + 8].unsqueeze(2), [128, 8, 16]); fi_b = bc(fim[:, g0:g0 + 8].unsqueeze(2), [128, 8, 16])
                tt('dve', bt1[:], Bre[:], fr_b, ALU.mult, ['Bre', 'fre'], ['bt1'])
                tt('dve', bt2[:], Bim[:], fi_b, ALU.mult, ['Bim', 'fim'], ['bt2'])
                tt('dve', Bbr[:], bt1[:], bt2[:], ALU.subtract, ['bt1', 'bt2'], ['Bb'])
                tt('dve', bt1[:], Bim[:], fr_b, ALU.mult, ['Bim', 'fre'], ['bt1'])
                tt('dve', bt2[:], Bre[:], fi_b, ALU.mult, ['Bre', 'fim'], ['bt2'])
                tt('dve', Bbi[:], bt1[:], bt2[:], ALU.add, ['bt1', 'bt2'], ['Bb'])
                for t in range(8):
                    e = EF[d][t]
                    mix(Fg[:, :, t * 16:(t + 1) * 16], g0, 8, PWr[:, 8 + e, :], nPWi[:, 8 + e, :], Cre, nCim, ['PW', 'nPW', 'C'], ['V'])
                    mix(Gg[:, :, t * 16:(t + 1) * 16], g0, 8, PWr[:, 8 - e, :], PWi[:, 8 - e, :], Bbr, Bbi, ['PW', 'Bb'], ['V'])
                    mix(Eg[:, :, t * 16:(t + 1) * 16], g0, 8, PWr[:, 16 - e, :], PWi[:, 16 - e, :], Bbr, Bbi, ['PW', 'Bb'], ['Vs'])
                cp('dve', w3[:, :, 0, :], Fg[:], ['V'], ['s5wb0'])
                for gl in range(8):
                    pb_, pbk = bank()
                    tr(pb_[:, 0:128], Eg[:, gl, :], ident[:], ['Vs', 'ident'], [pbk])
                    mm(pb_[:, 128:256], Gg[:, gl, :], Fg[:, gl, :], True, True, ['V', 'V'], [pbk])
                    cp('dve', w3[:, gl, 1, :], pb_[:, 0:128], [pbk], ['s5wb0'])
                    tt('dve', w3[:, gl, 2, :], pb_[:, 128:256], cm[:], ALU.mult, [pbk, cmk], ['s5wb0'])
                sc.dma('act', s5w[d * 32 + g0:d * 32 + g0 + 8].rearrange("g t k m -> k g t m"), w3[:], r=['s5wb0'], w=['s5w'])
        TRv = rr[:].rearrange("p (g n) -> p g n", n=32)
        TIv = tfall[:].rearrange("p a c -> p (a c)").rearrange("p (g n) -> p g n", n=32)
        tkeys = ['rr', 'tf0', 'tf1', 'tf2', 'tf3']
        mset('dve', TRv[:, 0:32, 31:32], 1.0, tkeys)
        mset('dve', TIv[:, 0:32, 31:32], 0.0, tkeys)
        mset('dve', TRv[:, 32:64, 0:1], 1.0, tkeys)
        mset('dve', TIv[:, 32:64, 0:1], 0.0, tkeys)
        for it_ in range(5):
            m = 1 << it_
            for d in range(2):
                G = slice(d * 32, d * 32 + 32)
                if d == 0:
                    srcs, dsts = slice(32 - m, 32), slice(32 - 2 * m, 32 - m)
                else:
                    srcs, dsts = slice(0, m), slice(m, 2 * m)
                amr = bc(A2r[:, G].unsqueeze(2), [128, 32, m])
                ami = bc(A2i[:, G].unsqueeze(2), [128, 32, m])
                t1_ = V[:, 0:32, 0:m]
                t2_ = V[:, 32:64, 0:m]
                tt('dve', t1_, TRv[:, G, srcs], amr, ALU.mult, tkeys + ['A2'], ['V'])
                tt('dve', t2_, TIv[:, G, srcs], ami, ALU.mult, tkeys + ['A2'], ['V'])
                tt('dve', TRv[:, G, dsts], t1_, t2_, ALU.subtract, ['V'], tkeys)
                tt('dve', t1_, TRv[:, G, srcs], ami, ALU.mult, tkeys + ['A2'], ['V'])
                tt('dve', t2_, TIv[:, G, srcs], amr, ALU.mult, tkeys + ['A2'], ['V'])
                tt('dve', TIv[:, G, dsts], t1_, t2_, ALU.add, ['V'], tkeys)
            tt('dve', tA[:], A2r[:], A2r[:], ALU.mult, ['A2'], ['tA'])
            tt('dve', tB[:], A2i[:], A2i[:], ALU.mult, ['A2'], ['tB'])
            tt('dve', tA[:], tA[:], tB[:], ALU.subtract, ['tA', 'tB'], ['tA'])
            tt('dve', tB[:], A2r[:], A2i[:], ALU.mult, ['A2'], ['tB'])
            cp('dve', A2r[:], tA[:], ['tA'], ['A2'])
            ts('dve', A2i[:], tB[:], 2.0, ALU.mult, ['tB'], ['A2'])
        ts('dve', TIv[:], TIv[:], sgn[:, 0:1], ALU.mult, tkeys + ['sgn'], tkeys)
        ts('dve', A2ix[:], A2i[:], sgn[:, 0:1], ALU.mult, ['A2', 'sgn'], ['A2x'])
        ts('dve', nA2ix[:], A2ix[:], -1.0, ALU.mult, ['A2x'], ['A2x'])

    xslot = [0]

    def ln_ht(src, rows, col0, cond, rkeys=()):
        for si, r0 in enumerate(rows):
            b = xslot[0] % 2
            xslot[0] += 1
            xk = 'xin%d' % b
            xt = xin[b]
            xw = [xk] if b == 0 else [xk, 'PW', 'nPW']
            if isinstance(r0, tuple):
                sc.dma('pool', xt[:], src, r=['ridx'] + list(rkeys), w=xw, ind=ridx[:, r0[1]:r0[1] + 1])
            else:
                sc.dma('sp', xt[:], src[r0:r0 + 128, :], r=list(rkeys), w=xw)
            for q in range(4):
                sc.op('dve', lambda e, q=q, xt=xt: e.bn_stats(out=stats[:, q, :], in_=xt[:, q * 512:(q + 1) * 512]), [xk], ['stats'])
            sc.op('dve', lambda e: e.bn_aggr(out=mv[:], in_=stats[:].rearrange("p a b -> p (a b)")), ['stats'], ['mv'])
            act(rstd[:], mv[:, 1:2], AF.Sqrt, ['mv'], ['rstd'], bias=EPS)
            recip(rstd[:], rstd[:], ['rstd'], ['rstd'])
            ts('dve', mv[:, 0:1], mv[:, 0:1], rstd[:, 0:1], ALU.mult, ['mv', 'rstd'], ['mv'], s2=-1.0, op1=ALU.mult)
            act(xt[:], xt[:], AF.Identity, [xk, 'mv', 'rstd'], [xk], bias=mv[:, 0:1], scale=rstd[:, 0:1])
            c0 = col0 + si * 128
            for kq in range(4):
                pb_, pbk = bank()
                for kk_ in range(4):
                    k = kq * 4 + kk_
                    tr(pb_[:, kk_ * 128:(kk_ + 1) * 128], xt[:, k * 128:(k + 1) * 128], ident[:], [xk, 'ident'], [pbk])
                for kk_ in range(4):
                    k = kq * 4 + kk_
                    if k % 2 == 0:
                        act(hT[:, k, c0:c0 + 128], pb_[:, kk_ * 128:(kk_ + 1) * 128], AF.Identity, [pbk, 'sc1', 'modT'], ['hT'],
                            bias=modT[:, k, cond:cond + 1], scale=sc1[:, k, cond:cond + 1])
                    else:
                        ts('dve', hT[:, k, c0:c0 + 128], pb_[:, kk_ * 128:(kk_ + 1) * 128], sc1[:, k, cond:cond + 1], ALU.mult,
                           [pbk, 'sc1', 'modT'], ['hT'], s2=modT[:, k, cond:cond + 1], op1=ALU.add)

    def proj_xa(l, cc):
        for gi in range(2):
            wt, wk = wload('in', l, gi)
            for t in range(8):
                pb_, pbk = bank()
                for k in range(KT):
                    mm(pb_[0:32, 0:256], hT[:, k, cc + t:cc + 256:8], wt[:, k, :], k == 0, k == KT - 1, [wk, 'hT'], [pbk])
                cp('dve' if t % 2 == 0 else 'act', Xp[:, gi * 16:(gi + 1) * 16, t, :],
                   pb_[0:32, 0:256].rearrange("p (g c) -> p g c", c=16), [pbk], ['Xp'])

    def s5_states(t_init, zero_init, reng='dve', sel=False, rec=True):
        for gq in range(4):
            pb_, pbk = bank()
            for gl in range(8):
                g = gq * 8 + gl
                mm(pb_[:, gl * 32:(gl + 1) * 32], Xp[:, g, :, :].rearrange("p t c -> p (t c)"), identb[0:32, 0:32], True, True,
                   ['Xp', 'identb'], [pbk])
            cp('act', Xpp[:, gq * 8:(gq + 1) * 8, :], pb_[:, 0:256].rearrange("p (g n) -> p g n", n=32), [pbk], ['Xpp'])
        for d in range(2):
            for gq in range(4):
                i = (d * 4 + gq) % 2
                wk = 's5wb%d' % i
                sc.dma('sp', s5wb[i][:, :, 1, :], s5w[d * 32 + gq * 8:d * 32 + gq * 8 + 8, 1].rearrange("g k m -> k g m"),
                       r=['s5w'], w=[wk])
                pv, pvk = bank()
                pw_, pwk = bank()
                for gl in range(8):
                    g = gq * 8 + gl
                    mm(pv[:, gl * 32:(gl + 1) * 32], s5wb[i][:, gl, 1, :], Xpp[:, g, :], True, True, [wk, 'Xpp'], [pvk])
                    mm(pw_[0:64, gl * 32:(gl + 1) * 32], s5wb[i][:, gl, 1, 64:128], Xpp[:, g, :], True, True, [wk, 'Xpp'], [pwk])
                    mm(pw_[64:128, gl * 32:(gl + 1) * 32], s5wb[i][:, gl, 1, 0:64], Xpp[:, g, :], True, True, [wk, 'Xpp'], [pwk])
                gd0 = d * 32 + gq * 8
                cp('dve', V[:, gd0:gd0 + 8, :], pv[:, 0:256].rearrange("p (g n) -> p g n", n=32), [pvk], ['V'])
                cp('act', Vs[:, gd0:gd0 + 8, :], pw_[:, 0:256].rearrange("p (g n) -> p g n", n=32), [pwk], ['Vs'])
        for d in (range(2) if rec else ()):
            G = slice(d * 32, d * 32 + 32)
            order = list(range(32)) if d == 0 else list(range(31, -1, -1))
            for step, n_ in enumerate(order):
                if step == 0:
                    if zero_init:
                        continue
                    if sel:
                        pS = Psel[:, d, 0, :]
                        pW = Psel[:, d, 1, :]
                        kp = ['Psel']
                    else:
                        pS = Pst[:, t_init + (0 if d == 0 else 1), 0, G]
                        pW = Pst[:, t_init + (0 if d == 0 else 1), 1, G]
                        kp = ['Pst']
                else:
                    pn = order[step - 1]
                    pS = V[:, G, pn]
                    pW = Vs[:, G, pn]
                    kp = ['V', 'Vs']
                tt(reng, tA[:, 0:32], pS, Ar[:, G], ALU.mult, kp + ['Ar'], ['tA'])
                tt(reng, tB[:, 0:32], pW, Aix[:, G], ALU.mult, kp + ['Aix'], ['tB'])
                tt(reng, tA[:, 0:32], tA[:, 0:32], tB[:, 0:32], ALU.add, ['tA', 'tB'], ['tA'])
                tt(reng, tA[:, 32:64], pW, Ar[:, G], ALU.mult, kp + ['Ar'], ['tA2'])
                tt(reng, tB[:, 32:64], pS, nAix[:, G], ALU.mult, kp + ['nAix'], ['tB2'])
                tt(reng, tA[:, 32:64], tA[:, 32:64], tB[:, 32:64], ALU.add, ['tA2', 'tB2'], ['tA2'])
                tt(reng, V[:, G, n_], V[:, G, n_], tA[:, 0:32], ALU.add, ['V', 'tA'], ['V'])
                tt(reng, Vs[:, G, n_], Vs[:, G, n_], tA[:, 32:64], ALU.add, ['Vs', 'tA2'], ['Vs'])

    def s5_main(l, t_init, zero_init, seq, sel=False):
        if zero_init:
            mset('dve', Sb[:, 0:32, 0:1], 0.0, ['Sb'])
            mset('dve', Sb[:, 32:64, 31:32], 0.0, ['Sb'])
        elif sel:
            cp('dve', Sb[:, 0:32, 0:1], Psel[:, 0, 0, :].unsqueeze(2), ['Psel'], ['Sb'])
            cp('dve', Sb[:, 32:64, 31:32], Psel[:, 1, 0, :].unsqueeze(2), ['Psel'], ['Sb'])
        else:
            cp('dve', Sb[:, 0:32, 0:1], Pst[:, t_init, 0, 0:32].unsqueeze(2), ['Pst'], ['Sb'])
            cp('dve', Sb[:, 32:64, 31:32], Pst[:, t_init + 1, 0, 32:64].unsqueeze(2), ['Pst'], ['Sb'])
        cp('dve', Sb[:, 0:32, 1:32], V[:, 0:32, 0:31], ['V'], ['Sb'])
        cp('act', Sb[:, 32:64, 0:31], V[:, 32:64, 1:32], ['V'], ['Sb'])
        if seq is not None:
            for d in range(2):
                pb_, pbk = bank()
                src_ = V[:, 0:32, 31] if d == 0 else V[:, 32:64, 0]
                cp('dve', tf[0][:, 0:32], src_, ['V'], ['tf0'])
                tr(pb_[0:32, 0:128], tf[0][:, 0:32], ident[:], ['tf0', 'ident'], [pbk])
                cp('dve', stg[:], pb_[0:32, 0:128], [pbk], ['stg'])
                sc.dma('pool', nst[seq, l, d].rearrange("r g p -> g r p"), stg[:].rearrange("g (r p) -> g r p", r=2),
                       r=['stg'], w=['nst'])
        for gq in range(4):
            p0, p0k = bank()
            p1_, p1k = bank()
            for d in range(2):
                for t_ in (0, 2):
                    sc.dma('sp', s5wb[d][:, :, t_, :], s5w[d * 32 + gq * 8:d * 32 + gq * 8 + 8, t_].rearrange("g k m -> k g m"),
                           r=['s5w'], w=['s5wb%d' % d])
            for gl in range(8):
                g = gq * 8 + gl
                pp, ppk = (p0, p0k) if gl < 4 else (p1_, p1k)
                o = pp[0:32, (gl % 4) * 128:(gl % 4 + 1) * 128]
                for d in range(2):
                    wk = 's5wb%d' % d
                    mm(o, Xpp[:, g, :], s5wb[d][:, gl, 2, :], d == 0, False, [wk, 'Xpp'], [ppk])
                    mm(o, Sb[:, d * 32 + g, :], s5wb[d][:, gl, 0, :], False, d == 1, [wk, 'Sb'], [ppk])
            for hf, (pp, ppk) in enumerate(((p0, p0k), (p1_, p1k))):
                g0 = gq * 8 + hf * 4
                yv = ygf[:]
                dsl = bc(dsk[:, g0 * 16:(g0 + 4) * 16].rearrange("p (g c) -> p g c", c=16).unsqueeze(2), [32, 4, 8, 16])
                tt('dve', yv, Xp[:, g0:g0 + 4, :, :], dsl, ALU.mult, ['Xp', 'dsk'], ['ygf'])
                tt('dve', yv, yv, pp[0:32, 0:512].rearrange("p (g t c) -> p g t c", t=8, c=16), ALU.add, ['ygf', ppk], ['ygf'])
                yf = ygf[:].rearrange("p g t c -> p (g t c)")
                y2 = ygf2[:].rearrange("p g t c -> p (g t c)")
                act(y2, yf, AF.Square, ['ygf'], ['ygf2'])
                ts('dve', y2, y2, GC2, ALU.mult, ['ygf2'], ['ygf2'], s2=1.0, op1=ALU.add)
                tt('dve', y2, y2, yf, ALU.mult, ['ygf2', 'ygf'], ['ygf2'])
                act(y2, y2, AF.Sigmoid, ['ygf2'], ['ygf2'], scale=GC1)
                tt('dve', yg[:, :, g0:g0 + 4, :].rearrange("p t g c -> p g t c"), ygf2[:], ygf[:], ALU.mult, ['ygf2', 'ygf'], ['yg'])
        for j in range(4):
            pb_, pbk = bank()
            for t in range(8):
                mm(pb_[:, t * 32:(t + 1) * 32], yg[:, t, j * 8:(j + 1) * 8, :].rearrange("p g c -> p (g c)"), identb[0:32, 0:32], True, True, ['yg', 'identb'], [pbk])
            cp('dve' if j % 2 == 0 else 'act', ygT[:, j, :].rearrange("p (n t) -> p t n", t=8),
               pb_[:, 0:256].rearrange("p (t n) -> p t n", n=32), [pbk], ['ygT'])
        for jo in range(4):
            pb_, pbk = bank()
            for j in range(4):
                mm(pb_[:, 0:NT], wglu[:, j, jo * 128:(jo + 1) * 128], ygT[:, j, :], j == 0, j == 3, ['wglu', 'ygT'], [pbk])
            act(tf[0][:, 0:NT], pb_[:, 0:NT], AF.Sigmoid, [pbk, 'bglu'], ['tf0'], bias=bglu[:, jo:jo + 1])
            tt('dve', tf[0][:, 0:NT], tf[0][:, 0:NT], ygT[:, jo, :], ALU.mult, ['tf0', 'ygT'], ['tf0'])
            tt('dve', yaT[:, jo, :], tf[0][:, 0:NT], zaT[:, jo, :], ALU.mult, ['tf0', 'zaT'], ['zaT'])

    def proj_fm(l, gi, nchunk, cols, evac):
        wt, wk = wload('in', l, gi)
        for j in range(nchunk):
            pb_, pbk = bank()
            n = cols.stop - cols.start
            for k in range(KT):
                mm(pb_[:, 0:n], wt[:, k, j * 128:(j + 1) * 128], hT[:, k, cols], k == 0, k == KT - 1, [wk, 'hT'], [pbk])
            evac(gi, j, pb_[:, 0:n], pbk)

    def rope(out, pin, pk, n, okey):
        cp('act', qraw[:, 0:n], pin, [pk], ['pTs4'])
        pr, prk = bank()
        mm(pr[:, 0:n], rotT[:], qraw[:, 0:n], True, True, ['rotT', 'pTs4'], [prk])
        tt('dve', tf[2][:, 0:n], pin, rc[:, 0:n], ALU.mult, [pk, 'rc'], ['tf2'])
        tt('dve', tf[3][:, 0:n], pr[:, 0:n], rs[:, 0:n], ALU.mult, [prk, 'rs'], ['tf3'])
        tt('dve', out, tf[2][:, 0:n], tf[3][:, 0:n], ALU.add, ['tf2', 'tf3'], [okey])

    def attn_finish(pvb, pvk, pdb, pdk, h0, nh, hw, c0):
        n_ = nh * hw
        cp('act', tf[1][:, 0:n_], pvb[:, 0:n_], [pvk], ['tf1'])
        for hi in range(nh):
            cs = slice(hi * hw, (hi + 1) * hw)
            ts('dve', tf[0][:, cs], pdb[:, cs], esink[:, h0 + hi:h0 + hi + 1], ALU.add, [pdk, 'esink'], ['tf0'])
        recip(tf[0][:, 0:n_], tf[0][:, 0:n_], ['tf0'], ['tf0'])
        tt('dve', tf[0][:, 0:n_], tf[0][:, 0:n_], tf[1][:, 0:n_], ALU.mult, ['tf0', 'tf1'], ['tf0'])
        tt('dve', ybT[:, h0:h0 + nh, c0:c0 + hw], tf[0][:, 0:n_].rearrange("p (h q) -> p h q", q=hw), zbT[:, h0:h0 + nh, c0:c0 + hw],
           ALU.mult, ['tf0', 'zbT'], ['qT'])

    import os as _os

    def tile_main(l, kind, idx):
        cond = 0 if kind == 'p' else 1
        own = kind == 'o'
        src = (xp if l == 0 else zp) if kind == 'p' else (xs if l == 0 else zs)
        dst = (zp if l == 0 else yp) if kind == 'p' else (zs if l == 0 else ys)
        rkeys = [] if l == 0 else ['dst_p_0' if kind == 'p' else 'dst_s_0']
        dkey = 'dst_%s_%d' % ('p' if kind == 'p' else 's', l)
        t0 = idx * NT
        if kind == 'p':
            rows = [t0, t0 + 128]
            cc = 0
            E = 256
        elif own:
            rows = [('ind', 2 * idx + b_) for b_ in range(4)]
            cc = 128
            E = 512
            sc.dma('pool', rc[:, 0:E], c_ropec_o[:, 256 * idx:256 * idx + 512], w=['rc'])
            sc.dma('pool', rs[:, 0:E], c_ropes_o[:, 256 * idx:256 * idx + 512], w=['rs'])
            for d in range(2):
                for w_ in range(2):
                    G = slice(d * 32, d * 32 + 32)
                    ts('dve', Psel[:, d, w_, :], Pst[:, idx + d, w_, G], oh4[:, 0:1], ALU.mult, ['Pst', 'oh4'], ['Psel'])
                    for j_ in range(1, 4):
                        sc.op('dve', lambda e, d=d, w_=w_, j_=j_, G=G: e.scalar_tensor_tensor(
                            out=Psel[:, d, w_, :], in0=Pst[:, 4 * j_ + idx + d, w_, G], scalar=oh4[:, j_:j_ + 1], in1=Psel[:, d, w_, :],
                            op0=ALU.mult, op1=ALU.add), ['Pst', 'oh4', 'Psel'], ['Psel'])
        else:
            lo = t0 - 128 if idx > 0 else t0
            hi = t0 + NT + 128 if idx < 15 else t0 + NT
            rows = list(range(lo, hi, 128))
            cc = t0 - lo
            E = hi - lo
            sc.dma('pool', rc[:, 0:E], c_ropec[:, lo:hi], w=['rc'])
            sc.dma('pool', rs[:, 0:E], c_ropes[:, lo:hi], w=['rs'])
        ln_ht(src, rows, 0, cond, rkeys)
        ctr = slice(cc, cc + NT)
        proj_xa(l, cc)
        scale = 128 ** -0.5

        def ev(gi, j, pin, pk):
            if gi in (2, 3):
                act(zaT[:, (gi - 2) * 2 + j, :], pin, AF.Silu, [pk], ['zaT'])
            elif 4 <= gi <= 7:
                hh = (gi - 4) * 2 + j
                if kind == 'p' or 'rope' in _os.environ.get('SKIP', ''):
                    cp('act', qT[:, hh, :], pin, [pk], ['qT'])
                else:
                    rope_q(hh, pin, pk)
            elif gi == 8:
                if kind == 'p' or 'rope' in _os.environ.get('SKIP', ''):
                    cp('act', kT[:, j, 0:E], pin, [pk], ['kT'])
                else:
                    rope(kT[:, j, 0:E], pin, pk, E, 'kT')
            elif 10 <= gi <= 13:
                act(zbT[:, (gi - 10) * 2 + j, :], pin, AF.Silu, [pk], ['zbT'])
            elif gi in (14, 15):
                gelu_evac(uT[:, (gi - 14) * 2 + j, :], pin, pk, [128, NT], 0, wk=['uT'])
            elif gi in (18, 19):
                act(zcT[:, (gi - 18) * 2 + j, :], pin, AF.Silu, [pk], ['zcT'])

        def rope_q(hh, pin, pk):
            cp('act', qraw[:, 0:NT], pin, [pk], ['pTs4'])
            pr, prk = bank()
            mm(pr[:, 0:NT], rotT[:], qraw[:, 0:NT], True, True, ['rotT', 'pTs4'], [prk])
            tt('dve', tf[2][:, 0:NT], pin, rc[:, ctr], ALU.mult, [pk, 'rc'], ['tf2'])
            tt('dve', tf[3][:, 0:NT], pr[:, 0:NT], rs[:, ctr], ALU.mult, [prk, 'rs'], ['tf3'])
            tt('dve', qT[:, hh, :], tf[2][:, 0:NT], tf[3][:, 0:NT], ALU.add, ['tf2', 'tf3'], ['qT'])

        for gi in (2, 3):
            proj_fm(l, gi, 2, ctr, ev)
        s5_states(idx if kind == 's' else 0, kind == 'p', reng=_os.environ.get('RENG', 'pool'), sel=own)
        for gi in (4, 5, 6, 7):
            proj_fm(l, gi, 2, ctr, ev)
        proj_fm(l, 8, 2, slice(0, E), ev)
        for gi in (8, 9):
            if gi == 8 and kind != 'p':
                continue
            wt, wk = wload('in', l, gi)
            for s_ in range(E // 128):
                pb_, pbk = bank()
                for k in range(KT):
                    mm(pb_[:, 0:256], hT[:, k, s_ * 128:(s_ + 1) * 128], wt[:, k, :], k == 0, k == KT - 1, [wk, 'hT'], [pbk])
                if gi == 9:
                    cp('act', vtok[:, s_, :], pb_[:, 0:256], [pbk], ['vtok'])
                if kind == 'p':
                    o_ = nk if gi == 8 else nv
                    cp('dve', tf[2 + s_][:, 0:256], pb_[:, 0:256], [pbk], ['tf%d' % (2 + s_)])
                    sc.dma('pool', o_[idx, l, s_ * 128:(s_ + 1) * 128, :], tf[2 + s_][:, 0:256], r=['tf%d' % (2 + s_)], w=['nkv'])
        for gi in (10, 11, 12, 13, 14, 15):
            proj_fm(l, gi, 2, ctr, ev)
        for gi in (16, 17):
            wt, wk = wload('in', l, gi)
            for s_ in range(2):
                pb_, pbk = bank()
                for k in range(KT):
                    mm(pb_[:, 0:256], hT[:, k, cc + s_ * 128:cc + (s_ + 1) * 128], wt[:, k, :], k == 0, k == KT - 1, [wk, 'hT'], [pbk])
                gelu_evac(vsg[:, s_, (gi - 16) * 256:(gi - 15) * 256], pb_[:, 0:256], pbk, [128, 256], 0, wk=['rr'])
        for s_ in range(2):
            sc.op('dve', lambda e, s_=s_: e.bn_stats(out=stats[:, 0, :], in_=vsg[:, s_, :]), ['rr'], ['stats'])
            sc.op('dve', lambda e: e.bn_aggr(out=mv[:], in_=stats[:, 0, :]), ['stats'], ['mv'])
            act(rstd[:], mv[:, 1:2], AF.Sqrt, ['mv'], ['rstd'], bias=EPS)
            recip(rstd[:], rstd[:], ['rstd'], ['rstd'])
            ts('dve', vsg[:, s_, :], vsg[:, s_, :], mv[:, 0:1], ALU.subtract, ['rr', 'mv', 'rstd'], ['rr'], s2=rstd[:, 0:1], op1=ALU.mult)
            tt('dve', vsg[:, s_, :], vsg[:, s_, :], sgb[:], ALU.mult, ['rr', 'sgb'], ['rr'])
            tt('dve', vsln[:, s_, :], vsg[:, s_, :], sbb[:], ALU.add, ['rr', 'sbb'], ['vsln'])
        for gi in (18, 19):
            proj_fm(l, gi, 2, ctr, ev)
        for g in range(4):
            for s_ in range(2):
                pb_, pbk = bank()
                mm(pb_[:, 0:128], vsln[:, s_, g * 128:(g + 1) * 128], wsT[:, g, :], True, True, ['vsln', 'wsT'], [pbk])
                tt('dve', tf[1][:, 0:128], pb_[:, 0:128], bsb[:, g * 128:(g + 1) * 128], ALU.add, [pbk, 'bsb'], ['tf1'])
                tt('dve', tf[1][:, 0:128], tf[1][:, 0:128], uT[:, g, s_ * 128:(s_ + 1) * 128], ALU.mult, ['tf1', 'uT'], ['tf1'])
                tt('dve', ycT[:, g, s_ * 128:(s_ + 1) * 128], tf[1][:, 0:128], zcT[:, g, s_ * 128:(s_ + 1) * 128], ALU.mult,
                   ['tf1', 'zcT'], ['zcT'])
        if kind == 'p':
            for kvh in range(2):
                for kb in range(2):
                    pa, pak = bank()
                    pb2, pb2k = bank()
                    for h in range(4):
                        pp, ppk = (pa, pak) if h < 2 else (pb2, pb2k)
                        mm(pp[:, (h % 2) * 256:(h % 2 + 1) * 256], kT[:, kvh, kb * 128:(kb + 1) * 128], qT[:, kvh * 4 + h, :], True, True,
                           ['kT', 'qT'], [ppk])
                    act(pTs[kb * 2][:], pa[:, 0:512], AF.Exp, [pak], ['pTs%d' % (kb * 2)], scale=scale)
                    act(pTs[kb * 2 + 1][:], pb2[:, 0:512], AF.Exp, [pb2k], ['pTs%d' % (kb * 2 + 1)], scale=scale)
                for half in range(2):
                    pv, pvk = bank()
                    pd, pdk = bank()
                    for kb in range(2):
                        mm(pv[:, 0:512], vtok[:, kb, kvh * 128:(kvh + 1) * 128], pTs[kb * 2 + half][:], kb == 0, kb == 1,
                           ['vtok', 'pTs%d' % (kb * 2 + half)], [pvk])
                        mm(pd[:, 0:512], onesb[:], pTs[kb * 2 + half][:], kb == 0, kb == 1, ['onesb', 'pTs%d' % (kb * 2 + half)], [pdk])
                    attn_finish(pv, pvk, pd, pdk, kvh * 4 + half * 2, 2, 256, 0)
        elif 'attn' not in _os.environ.get('SKIP', ''):
            for kvh in range(2):
                for qb in range(2):
                    ia = 2 * idx + qb
                    blocks = []
                    if own:
                        eq = cc + qb * 128
                        blocks.append(('l', eq - 128, mlo, 'mlo', 0 if (idx == 0 and qb == 0) else None))
                        blocks.append(('l', eq, None, None, None))
                        blocks.append(('l', eq + 128, mhi, 'mhi', 1 if (idx == 3 and qb == 1) else None))
                    else:
                        if ia - 1 >= 0:
                            blocks.append(('l', (ia - 1) * 128 - (t0 - cc), mlo, 'mlo', None))
                        blocks.append(('l', ia * 128 - (t0 - cc), None, None, None))
                        if ia + 1 <= 31:
                            blocks.append(('l', (ia + 1) * 128 - (t0 - cc), mhi, 'mhi', None))
                    blocks.append(('c', 0, None, None, None))
                    blocks.append(('c', 1, None, None, None))
                    qv = qT[:, kvh * 4:(kvh + 1) * 4, qb * 128:(qb + 1) * 128]
                    for bi, (bt_, a_, msk, mk, vc) in enumerate(blocks):
                        pa, pak = bank()
                        if bt_ == 'l':
                            e0 = a_
                            kk_ = kT[:, kvh, e0:e0 + 128]
                            kkey = 'kT'
                        else:
                            kk_ = ckT[:, kvh, a_ * 128:(a_ + 1) * 128]
                            kkey = 'ckT'
                        for h_ in range(4):
                            mm(pa[:, h_ * 128:(h_ + 1) * 128], kk_, qT[:, kvh * 4 + h_, qb * 128:(qb + 1) * 128], True, True,
                               [kkey, 'qT'], [pak])
                        act(pTs[bi][:], pa[:, 0:512], AF.Exp, [pak], ['pTs%d' % bi], scale=scale)
                        if msk is not None and vc is not None:
                            sc.op('dve', lambda e, bi=bi, msk=msk, vc=vc: e.scalar_tensor_tensor(
                                out=pTs[bi][:].rearrange("p (h q) -> p h q", q=128), in0=pTs[bi][:].rearrange("p (h q) -> p h q", q=128),
                                scalar=vmask[:, vc:vc + 1], in1=bc(msk[:].unsqueeze(1), [128, 4, 128]), op0=ALU.mult, op1=ALU.mult),
                                ['pTs%d' % bi, mk, 'vmask'], ['pTs%d' % bi])
                        elif msk is not None:
                            tt('dve', pTs[bi][:].rearrange("p (h q) -> p h q", q=128), pTs[bi][:].rearrange("p (h q) -> p h q", q=128),
                               bc(msk[:].unsqueeze(1), [128, 4, 128]), ALU.mult, ['pTs%d' % bi, mk], ['pTs%d' % bi])
                    pv, pvk = bank()
                    pd, pdk = bank()
                    nb = len(blocks)
                    for bi, (bt_, a_, msk, mk, vc) in enumerate(blocks):
                        if bt_ == 'l':
                            e0 = a_
                            vv = vtok[:, e0 // 128, kvh * 128:(kvh + 1) * 128]
                            vkey = 'vtok'
                        else:
                            vv = cvb[:, a_, kvh * 128:(kvh + 1) * 128]
                            vkey = 'cvb'
                        mm(pv[:, 0:512], vv, pTs[bi][:], bi == 0, bi == nb - 1, [vkey, 'pTs%d' % bi], [pvk])
                        mm(pd[:, 0:512], onesb[:], pTs[bi][:], bi == 0, bi == nb - 1, ['onesb', 'pTs%d' % bi], [pdk])
                    attn_finish(pv, pvk, pd, pdk, kvh * 4, 4, 128, qb * 128)
        s5_main(l, idx if kind == 's' else 0, kind == 'p', idx if kind == 'p' else None, sel=own)
        for fg in range(8):
            sl = (0, 1, 2, 1) if fg % 2 == 0 else (2, 0, 1, 0)
            wq, wqk = wload('pp', l, fg, slot=sl[0])
            for br, (yT_, yk, Kp, koff) in enumerate(((yaT, 'zaT', 4, 0), (ybT, 'qT', 8, 4), (ycT, 'zcT', 4, 12))):
                c0 = 5120 + br * 2048 + fg * 256
                wa, wak = wload('in', l, c0 // 256, slot=sl[1 + br])
                for j in range(2):
                    pg, pgk = bank()
                    pq, pqk = bank()
                    for k in range(KT):
                        mm(pg[:, 0:NT], wa[:, k, j * 128:(j + 1) * 128], hT[:, k, ctr], k == 0, k == KT - 1, [wak, 'hT'], [pgk])
                    for k in range(Kp):
                        mm(pq[:, 0:NT], wq[:, koff + k, j * 128:(j + 1) * 128], yT_[:, k, :], k == 0, k == Kp - 1, [wqk, yk], [pqk])
                    act(tf[2][:, 0:NT], pg[:, 0:NT], AF.Sigmoid, [pgk], ['tf2'])
                    f = fg * 2 + j
                    if br == 0:
                        tt('dve', merged[:, f, :], tf[2][:, 0:NT], pq[:, 0:NT], ALU.mult, ['tf2', pqk], ['merged'])
                    else:
                        tt('dve', tf[2][:, 0:NT], tf[2][:, 0:NT], pq[:, 0:NT], ALU.mult, ['tf2', pqk], ['tf2'])
                        tt('dve', merged[:, f, :], merged[:, f, :], tf[2][:, 0:NT], ALU.add, ['merged', 'tf2'], ['merged'])
        gate_b = V[:].rearrange("p g n -> p (g n)")
        lng_b = Vs[:].rearrange("p g n -> p (g n)")
        lnb_b = tfall[:].rearrange("p a c -> p (a c)")
        tfk = ['tf0', 'tf1', 'tf2', 'tf3']
        sc.dma('pool', gate_b, gsc[cond].partition_broadcast(128), r=['gsc'], w=['V'])
        sc.dma('pool', lng_b, ln_g[l].partition_broadcast(128), w=['Vs'])
        sc.dma('pool', lnb_b, ln_b[l].partition_broadcast(128), w=tfk)
        for s_ in range(2):
            for fb in range(8):
                wt, wk = wload('out', l, fb)
                pb_, pbk = bank()
                for k in range(KT):
                    mm(pb_[:, 0:256], merged[:, k, s_ * 128:(s_ + 1) * 128], wt[:, k, :], k == 0, k == KT - 1, [wk, 'merged'], [pbk])
                tt('dve', rr[:, fb * 256:(fb + 1) * 256], pb_[:, 0:256], gate_b[:, fb * 256:(fb + 1) * 256], ALU.mult, [pbk, 'V'], ['rr'])
            xk = 'xin0'
            rk = 'rr'
            if own:
                sc.dma('pool', xin[0][:], src, r=['ridx'] + rkeys, w=[xk], ind=ridx[:, 2 * idx + 1 + s_:2 * idx + 2 + s_])
            else:
                sc.dma('sp', xin[0][:], src[t0 + s_ * 128:t0 + (s_ + 1) * 128, :], r=rkeys, w=[xk])
            sc.op('dve', lambda e: e.scalar_tensor_tensor(out=rr[:], in0=xin[0][:], scalar=ALPHA, in1=rr[:],
                                                          op0=ALU.mult, op1=ALU.add), [xk, rk], [rk])
            for q in range(4):
                sc.op('dve', lambda e, q=q: e.bn_stats(out=stats[:, q, :], in_=rr[:, q * 512:(q + 1) * 512]), [rk], ['stats'])
            sc.op('dve', lambda e: e.bn_aggr(out=mv[:], in_=stats[:].rearrange("p a b -> p (a b)")), ['stats'], ['mv'])
            act(rstd[:], mv[:, 1:2], AF.Sqrt, ['mv'], ['rstd'], bias=EPS)
            recip(rstd[:], rstd[:], ['rstd'], ['rstd'])
            ts('dve', rr[:], rr[:], mv[:, 0:1], ALU.subtract, [rk, 'mv', 'rstd'], [rk], s2=rstd[:, 0:1], op1=ALU.mult)
            tt('dve', rr[:], rr[:], lng_b, ALU.mult, [rk, 'Vs'], [rk])
            tt('dve', rr[:], rr[:], lnb_b, ALU.add, [rk] + tfk, [rk])
            sc.dma('pool', dst[t0 + s_ * 128:t0 + (s_ + 1) * 128, :], rr[:], r=[rk], w=[dkey])

    def chain(pin_, pout, G, Ls, Lw, keys):
        for w_ in range(2):
            cx = A2ix if w_ == 0 else nA2ix
            tt('dve', tA[:, 0:32], Pst[:, pin_, w_, G], A2r[:, G], ALU.mult, ['Pst', 'A2'], ['tA'])
            tt('dve', tB[:, 0:32], Pst[:, pin_, 1 - w_, G], cx[:, G], ALU.mult, ['Pst', 'A2x'], ['tB'])
            tt('dve', tA[:, 0:32], tA[:, 0:32], tB[:, 0:32], ALU.add, ['tA', 'tB'], ['tA'])
            tt('dve', Pst[:, pout, w_, G], tA[:, 0:32], Ls if w_ == 0 else Lw, ALU.add, ['tA'] + keys, ['Pst'])

    def sample_prepass(l):
        src = xs if l == 0 else zs
        cp('dve', Pst[:, 0, 0, 0:32], sinit[:, 0, :], ['sinit'], ['Pst'])
        cp('dve', Pst[:, 0, 1, 0:32], sinitw[:, 0, :], ['sinitw'], ['Pst'])
        cp('dve', Pst[:, 16, 0, 32:64], sinit[:, 1, :], ['sinit'], ['Pst'])
        cp('dve', Pst[:, 16, 1, 32:64], sinitw[:, 1, :], ['sinitw'], ['Pst'])
        import os as _os
        for t in range(int(_os.environ.get('NPRE', '16'))):
            ln_ht(src, [t * NT, t * NT + 128], 0, 1, [] if l == 0 else ['dst_s_0'])
            proj_xa(l, 0)
            s5_states(0, True, rec=False)
            TRv = rr[:].rearrange("p (g n) -> p g n", n=32)
            TIv = tfall[:].rearrange("p a c -> p (a c)").rearrange("p (g n) -> p g n", n=32)
            tkeys = ['rr', 'tf0', 'tf1', 'tf2', 'tf3']
            tmp = xin[0][:].rearrange("p (g n) -> p g n", n=32)
            AX = mybir.AxisListType.X
            for q_, (ta_, va_, vk_) in enumerate(((TRv, V, 'V'), (TIv, Vs, 'Vs'), (TRv, Vs, 'Vs'), (TIv, V, 'V'))):
                tt('dve', tmp, ta_, va_[:], ALU.mult, tkeys + [vk_], ['xin0'])
                sc.op('dve', lambda e, q_=q_: e.tensor_reduce(out=Ssum[:, q_, :], in_=tmp, axis=AX, op=ALU.add), ['xin0'], ['Ssum'])
            tt('dve', Ssum[:, 0, :], Ssum[:, 0, :], Ssum[:, 1, :], ALU.add, ['Ssum'], ['Ssum'])
            tt('dve', Ssum[:, 2, :], Ssum[:, 2, :], Ssum[:, 3, :], ALU.subtract, ['Ssum'], ['Ssum'])
            chain(t, t + 1, slice(0, 32), Ssum[:, 0, 0:32], Ssum[:, 2, 0:32], ['Ssum'])
            cp('dve', Lst[:, t, 0, :], Ssum[:, 0, 32:64], ['Ssum'], ['Lst'])
            cp('dve', Lst[:, t, 1, :], Ssum[:, 2, 32:64], ['Ssum'], ['Lst'])
        for t in range(15, -1, -1):
            chain(t + 1, t, slice(32, 64), Lst[:, t, 0, :], Lst[:, t, 1, :], ['Lst'])

    if stop is None:
        for l in range(2):
            layer_prep(l)
            wconvert(l)
            sample_prepass(l)
            for i in range(NPS):
                tile_main(l, 'p', i)
            if l == 0:
                for t in range(16):
                    tile_main(l, 's', t)
            else:
                for t in range(4):
                    tile_main(l, 'o', t)
    else:
        layer_prep(0)
        wconvert(0)
        if stop == 'ptile':
            tile_main(0, 'p', 0)
        if stop == 'pre':
            sample_prepass(0)
        if stop == 'own':
            sample_prepass(0)
            tile_main(0, 'o', 0)
            tile_main(0, 'o', 3)
        if stop == 'stile':
            sample_prepass(0)
            tile_main(0, 's', 0)
            if 'one' not in _os.environ.get('SKIP', ''):
                tile_main(0, 's', 1)
        loc = dict(locals())
        for nm in dumps:
            if nm in ('s5w', 'gsc', 'zp', 'zs'):
                src_ap = loc[nm]
                key = nm if nm in ('s5w', 'gsc') else ('dst_p_0' if nm == 'zp' else 'dst_s_0')
                o = nc.dram_tensor("dbg_" + nm, list(src_ap.shape), src_ap.dtype, kind="ExternalOutput").ap()
                sc.dma('sp', o, src_ap, r=[key], w=['dbg_' + nm])
            else:
                t_ = loc[nm]
                o = nc.dram_tensor("dbg_" + nm, list(t_.shape), t_.dtype, kind="ExternalOutput").ap()
                sc.dma('sp', o, t_[:], r=[nm, 'PW', 'C', 'Bb', 'A2', 'A2x', 'V', 'Vs'], w=['dbg_' + nm])
    counts = sc.emit(es)
    es.close()
    return nc, counts


_CACHE = {}


def kernel(x_prompt, x_sample, cache_k, cache_v, state_ssm, c, c_ctx,
           w_ada, b_ada, w_in, ssm_lam_re, ssm_lam_im, ssm_log_step,
           ssm_b_re, ssm_b_im, ssm_c_re, ssm_c_im, ssm_d, w_glu, b_glu,
           attn_sink, sgu_ln_g, sgu_ln_b, w_spatial, b_spatial,
           w_proj_a, w_proj_b, w_proj_c, w_out, ln_g, ln_b):
    f = lambda a: np.ascontiguousarray(np.asarray(a, dtype=np.float32))
    if 'nc' not in _CACHE:
        _CACHE['nc'] = build()[0]
    nc = _CACHE['nc']
    consts = _host_consts()
    shared = dict(w_ada=f(w_ada), b_ada=f(b_ada), w_in=f(w_in), lam_re=f(ssm_lam_re), lam_im=f(ssm_lam_im),
                  log_step=f(ssm_log_step), b_re=f(ssm_b_re), b_im=f(ssm_b_im), c_re=f(ssm_c_re), c_im=f(ssm_c_im),
                  ssm_d=f(ssm_d), w_glu=f(w_glu), b_glu=f(b_glu), sink=f(attn_sink), sgu_g=f(sgu_ln_g), sgu_b=f(sgu_ln_b),
                  w_s=f(w_spatial), b_s=f(np.asarray(b_spatial).reshape(2, 512)),
                  w_pa=f(w_proj_a), w_pb=f(w_proj_b), w_pc=f(w_proj_c), w_out=f(w_out), ln_g=f(ln_g), ln_b=f(ln_b))
    shared.update(consts)
    x_prompt = np.asarray(x_prompt); x_sample = np.asarray(x_sample)
    cache_k = np.asarray(cache_k); cache_v = np.asarray(cache_v); state_ssm = np.asarray(state_ssm)
    c = np.asarray(c); c_ctx = np.asarray(c_ctx)
    in_maps = []
    for core in range(8):
        b = core // 4
        m = dict(shared)
        m['xp'] = f(x_prompt[core * NPS:(core + 1) * NPS].reshape(NPS * LP, D))
        m['xs'] = f(x_sample[b])
        m['ck'] = f(cache_k[b].reshape(2, 256, 256))
        m['cv'] = f(cache_v[b].reshape(2, 256, 256))
        m['st0'] = f(state_ssm[b])
        m['cvec'] = f(np.stack([c_ctx, c[b]], axis=0))
        m.update(_core_consts(core, consts))
        in_maps.append(m)
    res = run_bass_kernel_spmd(nc, in_maps, core_ids=list(range(8)))
    R = res.results
    y_prompt = np.concatenate([R[i]['yp'].reshape(NPS, LP, D) for i in range(8)], axis=0).astype(np.float32)
    y_sample = np.stack([np.concatenate([R[b_ * 4 + j_]['ys_own'] for j_ in range(4)], axis=0) for b_ in range(2)], axis=0).astype(np.float32)
    nk_ = np.concatenate([R[i]['nk'].reshape(NPS, 2, LP, 2, 128) for i in range(8)], axis=0).astype(np.float32)
    nv_ = np.concatenate([R[i]['nv'].reshape(NPS, 2, LP, 2, 128) for i in range(8)], axis=0).astype(np.float32)
    ns_ = np.concatenate([R[i]['nst'] for i in range(8)], axis=0).astype(np.float32)
    return (y_prompt, y_sample, nk_, nv_, ns_)
```

```python
import contextlib
import math
import numpy as np
import concourse.bass as bass
import concourse.mybir as mybir
from concourse.bass_utils import run_bass_kernel_spmd

F32 = mybir.dt.float32
BF = mybir.dt.bfloat16
AF = mybir.ActivationFunctionType
ALU = mybir.AluOpType

D = 2048
KT = 16
NT = 256
DIN = 11264
LP = 256
LS = 4096
NPS = 4
DEPTH = 2
ALPHA = (2 * DEPTH) ** 0.25
EPS = 1e-5
GC1 = 1.5957691216057308
GC2 = 0.044715
SAME_ENG_SYNC = True
SAME_ENG_DIST = 8


class Sched:
    NS = 8

    def __init__(self, nc):
        self.nc = nc
        self.ops = []

    def op(self, eng, fn, r=(), w=()):
        w = tuple(w) + tuple(k for k in r if k.startswith('ps') and k not in w)
        self.ops.append((eng, fn, tuple(r), tuple(w), False))

    def dma(self, q, out, in_, r=(), w=(), slow=False, ind=None):
        self.ops.append((q, (out, in_, slow, ind), tuple(r), tuple(w), True))

    def emit(self, es):
        nc = self.nc
        ops = self.ops
        n = len(ops)
        last_w = {}
        readers = {}
        deps = [None] * n
        for i, (eng, fn, r, w, isd) in enumerate(ops):
            d = set()
            for k in r:
                if k in last_w:
                    d.add(last_w[k])
            for k in w:
                if k in last_w:
                    d.add(last_w[k])
                for j in readers.get(k, ()):
                    d.add(j)
            d.discard(i)
            deps[i] = d
            for k in w:
                last_w[k] = i
                readers[k] = []
            for k in r:
                if k not in w:
                    readers.setdefault(k, []).append(i)
        need = [False] * n
        lidx = [0] * n
        lc = {}
        for i in range(n):
            lidx[i] = lc.get(ops[i][0], 0)
            lc[ops[i][0]] = lidx[i] + 1

        def same_eng_skip(i, j):
            if ops[i][0] == 'pe' or not SAME_ENG_SYNC:
                return True
            return (lidx[i] - lidx[j]) > SAME_ENG_DIST

        for i in range(n):
            ei = ops[i][0]
            for j in deps[i]:
                ej, _, _, _, dj = ops[j]
                if dj or ej != ei or not same_eng_skip(i, j):
                    need[j] = True
        engs = ['pe', 'act', 'dve', 'pool', 'sp']
        esem = {e: es.enter_context(nc.semaphore('es_' + e)) for e in engs}
        dsem = {q: [es.enter_context(nc.semaphore('ds_%s%d' % (q, k))) for k in range(self.NS)]
                for q in ('sp', 'pool', 'act')}
        cnt = {e: 0 for e in engs}
        dcnt = {q: 0 for q in dsem}
        sig = [None] * n
        streams = {e: [] for e in engs}
        waited = {e: {} for e in engs}

        def addwait(e, lst, sem, val):
            key = id(sem)
            if waited[e].get(key, 0) >= val:
                return
            waited[e][key] = val
            lst.append(('w', sem, val))

        for i, (eng, fn, r, w, isd) in enumerate(ops):
            lst = streams[eng]
            wmax = {}
            for j in deps[i]:
                if sig[j] is None:
                    continue
                ej, dj = ops[j][0], ops[j][4]
                if (not dj) and ej == eng and same_eng_skip(i, j):
                    continue
                key = id(sig[j][0])
                if key not in wmax or wmax[key][1] < sig[j][1]:
                    wmax[key] = sig[j]
            for key in sorted(wmax, key=lambda k_: wmax[k_][1]):
                addwait(eng, lst, wmax[key][0], wmax[key][1])
            if isd:
                k = dcnt[eng]
                dcnt[eng] += 1
                sem = dsem[eng][k % self.NS]
                rnd = k // self.NS
                if rnd > 0:
                    addwait(eng, lst, sem, 16 * rnd)
                sig[i] = (sem, 16 * (rnd + 1))
                lst.append(('d', fn, sem))
            else:
                if need[i]:
                    cnt[eng] += 1
                    sig[i] = (esem[eng], cnt[eng])
                    lst.append(('o', fn, esem[eng]))
                else:
                    lst.append(('o', fn, None))
        for q in dsem:
            for k in range(self.NS):
                tot = (dcnt[q] - k + self.NS - 1) // self.NS if dcnt[q] > k else 0
                if tot > 0:
                    streams[q].append(('w', dsem[q][k], 16 * tot))

        def run(engine, lst):
            for it in lst:
                if it[0] == 'w':
                    engine.wait_ge(it[1], it[2])
                elif it[0] == 'd':
                    out, in_, slow, ind = it[1]
                    if ind is not None:
                        engine.indirect_dma_start(out=out, out_offset=None, in_=in_,
                                                  in_offset=bass.IndirectOffsetOnAxis(ap=ind, axis=0)).then_inc(it[2], 16)
                    elif slow:
                        engine.dma_start(out=out, in_=in_, allow_slow_non_contiguous=True).then_inc(it[2], 16)
                    else:
                        engine.dma_start(out=out, in_=in_).then_inc(it[2], 16)
                else:
                    ins = it[1](engine)
                    if it[2] is not None:
                        ins.then_inc(it[2], 1)

        block = es.enter_context(nc.Block())

        @block.tensor
        def _(e):
            run(e, streams['pe'])

        @block.scalar
        def _(e):
            run(e, streams['act'])

        @block.vector
        def _(e):
            run(e, streams['dve'])

        @block.gpsimd
        def _(e):
            run(e, streams['pool'])

        @block.sync
        def _(e):
            run(e, streams['sp'])
        return {e: (len(streams[e]), cnt[e]) for e in engs}


def _core_consts(core, consts):
    j = core % 4
    m = {}
    p = np.arange(128)[:, None]
    cidx = np.arange(10)[None, :]
    m['ridx'] = np.clip(1024 * j + 128 * (cidx - 1) + p, 0, LS - 1).astype(np.int32)
    q = np.clip(1024 * j - 128 + np.arange(1280), 0, LS - 1)
    m['ropec_o'] = np.ascontiguousarray(consts['ropec'][:, q])
    m['ropes_o'] = np.ascontiguousarray(consts['ropes'][:, q])
    oh = np.zeros((128, 4), np.float32); oh[:, j] = 1.0
    m['oh4'] = oh
    vm = np.ones((128, 2), np.float32)
    if j == 0:
        vm[:, 0] = 0.0
    if j == 3:
        vm[:, 1] = 0.0
    m['vmask'] = vm
    return m


def _host_consts():
    c = {}
    c['ident'] = np.eye(128, dtype=np.float32)
    R = np.zeros((128, 128), np.float32)
    for d in range(128):
        if d % 64 < 32:
            R[d, d + 32] = -1.0
        else:
            R[d, d - 32] = 1.0
    c['rotT'] = np.ascontiguousarray(R.T)
    pos = np.arange(LS)
    row = pos // 64
    col = pos % 64
    inv = 10000.0 ** (-np.arange(0, 64, 2, dtype=np.float32) / 64.0)
    ang = np.zeros((128, LS), np.float32)
    for d in range(128):
        p = row if d < 64 else col
        ang[d] = p.astype(np.float32) * inv[d % 32]
    c['ropec'] = np.cos(ang).astype(np.float32)
    c['ropes'] = np.sin(ang).astype(np.float32)
    kk = np.arange(128)[:, None]
    qq = np.arange(128)[None, :]
    c['mlo'] = (kk >= qq).astype(np.float32)
    c['mhi'] = (kk <= qq).astype(np.float32)
    tp = (np.arange(128) // 16)[:, None]
    tt = (np.arange(128) // 16)[None, :]
    c['cmf'] = (tt >= tp).astype(np.float32)
    c['cmb'] = (tp >= tt).astype(np.float32)
    return c


def build(stop=None, dumps=()):
    nc = bass.Bass("TRN2", target_bir_lowering=False)
    es = contextlib.ExitStack()
    sc = Sched(nc)
    PI = math.pi

    def din(name, shape, dt=F32):
        return nc.dram_tensor(name, list(shape), dt, kind="ExternalInput").ap()

    def dout(name, shape):
        return nc.dram_tensor(name, list(shape), F32, kind="ExternalOutput").ap()

    def dscr(name, shape, dt=F32):
        return nc.dram_tensor(name, list(shape), dt, kind="Internal").ap()

    xp = din("xp", [NPS * LP, D]); xs = din("xs", [LS, D])
    ck = din("ck", [2, 256, 256]); cv = din("cv", [2, 256, 256])
    st0 = din("st0", [2, 2, 2, 32, 64]); cvec = din("cvec", [2, D])
    w_ada = din("w_ada", [2, D, 3 * D]); b_ada = din("b_ada", [2, 3 * D]); w_in = din("w_in", [2, D, DIN])
    lam_re = din("lam_re", [2, 2, 32, 64]); lam_im = din("lam_im", [2, 2, 32, 64]); log_step = din("log_step", [2, 2, 32])
    b_re = din("b_re", [2, 2, 32, 64, 16]); b_im = din("b_im", [2, 2, 32, 64, 16])
    c_re = din("c_re", [2, 2, 32, 16, 64]); c_im = din("c_im", [2, 2, 32, 16, 64])
    ssm_d = din("ssm_d", [2, 512]); w_glu = din("w_glu", [2, 512, 512]); b_glu = din("b_glu", [2, 512])
    sink = din("sink", [2, 8]); sgu_g = din("sgu_g", [2, 512]); sgu_b = din("sgu_b", [2, 512])
    w_s = din("w_s", [2, 4, 128, 128]); b_s = din("b_s", [2, 512])
    w_pa = din("w_pa", [2, 512, D]); w_pb = din("w_pb", [2, 1024, D]); w_pc = din("w_pc", [2, 512, D])
    w_out = din("w_out", [2, D, D]); ln_g = din("ln_g", [2, D]); ln_b = din("ln_b", [2, D])
    c_ident = din("ident", [128, 128]); c_rotT = din("rotT", [128, 128])
    c_ropec = din("ropec", [128, LS]); c_ropes = din("ropes", [128, LS])
    c_mlo = din("mlo", [128, 128]); c_mhi = din("mhi", [128, 128])
    c_cmf = din("cmf", [128, 128]); c_cmb = din("cmb", [128, 128])
    ridx_d = din("ridx", [128, 10], mybir.dt.int32); c_ropec_o = din("ropec_o", [128, 1280]); c_ropes_o = din("ropes_o", [128, 1280])
    oh4_d = din("oh4", [128, 4]); vmask_d = din("vmask", [128, 2])
    yp = dout("yp", [NPS * LP, D]); ys = dout("ys_own", [1024, D])
    nk = dout("nk", [NPS, 2, LP, 256]); nv = dout("nv", [NPS, 2, LP, 256]); nst = dout("nst", [NPS, 2, 2, 2, 32, 64])
    zp = dscr("zp", [NPS * LP, D]); zs = dscr("zs", [LS, D]); gsc = dscr("gsc", [2, D])
    s5w = dscr("s5w", [64, 3, 128, 128], BF)
    wsc = dscr("wsc", [2, 76, 128, 16, 256], BF)

    def sb(name, shape, dt=F32):
        return es.enter_context(nc.sbuf_tensor("s_" + name, list(shape), dt))

    ps = [es.enter_context(nc.psum_tensor("ps%d" % i, [128, 512], F32)) for i in range(8)]
    psn = [0]

    def bank():
        i = psn[0] % 8
        psn[0] += 1
        return ps[i], 'ps%d' % i

    xin = [sb("xin0", [128, D]), sb("xin1", [128, D])]
    rr = sb("rr", [128, D])
    stats = sb("stats", [128, 4, 6]); mv = sb("mv", [128, 2]); rstd = sb("rstd", [128, 1])
    hT = sb("hT", [128, KT, 512], BF)
    wb = [sb("wb%d" % i, [128, KT, 256], BF) for i in range(3)]
    zaT = sb("zaT", [128, 4, NT], BF); qT = sb("qT", [128, 8, NT], BF); yaT = zaT; ybT = qT; kT = sb("kT", [128, 2, 512], BF)
    zbT = sb("zbT", [128, 8, NT], BF); uT = sb("uT", [128, 4, NT], BF); zcT = sb("zcT", [128, 4, NT], BF); ycT = zcT
    Xp = sb("Xp", [32, 32, 8, 16], BF); Xpp = sb("Xpp", [128, 32, 32], BF)
    vtok = sb("vtok", [128, 4, 256], BF); vsg = rr[:, 0:1024].rearrange("p (s c) -> p s c", c=512)
    vsln = sb("vsln", [128, 2, 512], BF)
    V = sb("V", [128, 64, 32]); Vs = sb("Vs", [128, 64, 32]); Sb = sb("Sb", [128, 64, 32], BF)
    tA = sb("tA", [128, 64]); tB = sb("tB", [128, 64])
    yg = sb("yg", [32, 8, 32, 16], BF); ygf = sb("ygf", [32, 4, 8, 16]); ygf2 = sb("ygf2", [32, 4, 8, 16])
    ygT = sb("ygT", [128, 4, NT], BF)
    s5wb = [sb("s5wb%d" % i, [128, 8, 3, 128], BF) for i in range(2)]
    merged = sb("merged", [128, KT, NT], BF)
    tfall = sb("tfall", [128, 4, 512])
    tf = [tfall[:, i, :] for i in range(4)]
    pTs = [sb("pTs%d" % i, [128, 512], BF) for i in range(5)]
    Fg = V[:, 0:32, :].rearrange("p (g a) n -> p g (a n)", a=4)
    Gg = V[:, 32:64, :].rearrange("p (g a) n -> p g (a n)", a=4)
    Eg = Vs[:, 0:32, :].rearrange("p (g a) n -> p g (a n)", a=4)
    w3 = s5wb[0]
    wsn = tf[0][:, :].rearrange("p (g q) -> p g q", q=128)
    ckn = tf[1][:, :].rearrange("p (b c) -> p b c", c=256)
    rc = sb("rc", [128, 512]); rs = sb("rs", [128, 512]); qraw = pTs[4]
    ident = sb("ident", [128, 128]); identb = sb("identb", [128, 128], BF); rotT = sb("rotT", [128, 128], BF)
    onesb = sb("onesb", [128, 128], BF); mlo = sb("mlo", [128, 128], BF); mhi = sb("mhi", [128, 128], BF)
    cmf = sb("cmf", [128, 128]); cmb = sb("cmb", [128, 128])
    scT = sb("scT", [128, 2, KT], BF); cvT = sb("cvT", [128, 2, KT])
    modT = sb("modT", [128, 48, 2]); badaT = sb("badaT", [128, 48]); sc1 = sb("sc1", [128, KT, 2])
    sgb = sb("sgb", [128, 512]); sbb = sb("sbb", [128, 512]); bsb = sb("bsb", [128, 512])
    wsT = sb("wsT", [128, 4, 128], BF)
    esink = sb("esink", [128, 8]); ckT = sb("ckT", [128, 2, 256], BF)
    cvb = sb("cvb", [128, 2, 256], BF)
    wglu = sb("wglu", [128, 4, 512], BF); bglu = sb("bglu", [128, 4]); dsk = sb("dsk", [32, 512])
    lr = sb("lr", [128, 32]); li = sb("li", [128, 32]); dtt = sb("dtt", [128, 32])
    p1 = sb("p1", [128, 32]); p2 = sb("p2", [128, 32]); p3 = sb("p3", [128, 32]); p4 = sb("p4", [128, 32])
    cosv = sb("cosv", [128, 32]); sinv = sb("sinv", [128, 32])
    fre = sb("fre", [128, 32]); fim = sb("fim", [128, 32])
    PWr = xin[1][:, 0:544].rearrange("p (k g) -> p k g", g=32); PWi = xin[1][:, 544:1088].rearrange("p (k g) -> p k g", g=32)
    nPWi = xin[1][:, 1088:1632].rearrange("p (k g) -> p k g", g=32)
    Bre = sb("Bre", [128, 8, 16]); Bim = sb("Bim", [128, 8, 16]); Bbr = sb("Bbr", [128, 8, 16]); Bbi = sb("Bbi", [128, 8, 16])
    bt1 = sb("bt1", [128, 8, 16]); bt2 = sb("bt2", [128, 8, 16])
    cnat = sb("cnat", [128, 2, 64]); Cre = sb("Cre", [128, 8, 16]); nCim = sb("nCim", [128, 8, 16])
    Ar = sb("Ar", [128, 64]); Aix = sb("Aix", [128, 64]); nAix = sb("nAix", [128, 64]); sgn = sb("sgn", [128, 1])
    A2r = sb("A2r", [128, 64]); A2i = sb("A2i", [128, 64]); A2ix = sb("A2ix", [128, 64]); nA2ix = sb("nA2ix", [128, 64])
    sinit = sb("sinit", [128, 2, 32]); sinitw = sb("sinitw", [128, 2, 32])
    Pst = sb("Pst", [128, 17, 2, 64])
    Lst = sb("Lst", [128, 16, 2, 32])
    stg = sb("stg", [32, 128])
    Ssum = sb("Ssum", [128, 4, 64])
    ridx = sb("ridx", [128, 10], mybir.dt.int32); oh4 = sb("oh4", [128, 4]); vmask = sb("vmask", [128, 2])
    Psel = sb("Psel", [128, 2, 2, 32])

    def tt(eng, out, a, b, op, r, w):
        sc.op(eng, lambda e: e.tensor_tensor(out=out, in0=a, in1=b, op=op), r, w)

    def ts(eng, out, a, s1, op0, r, w, s2=None, op1=None):
        if op1 is None:
            sc.op(eng, lambda e: e.tensor_scalar(out=out, in0=a, scalar1=s1, scalar2=None, op0=op0), r, w)
        else:
            sc.op(eng, lambda e: e.tensor_scalar(out=out, in0=a, scalar1=s1, scalar2=s2, op0=op0, op1=op1), r, w)

    def act(out, in_, func, r, w, bias=None, scale=None):
        kw = {}
        if bias is not None:
            kw['bias'] = bias
        if scale is not None:
            kw['scale'] = scale
        sc.op('act', lambda e: e.activation(out=out, in_=in_, func=func, **kw), r, w)

    def mm(out, lhsT, rhs, start, stop, r, w):
        sc.op('pe', lambda e: e.matmul(out, lhsT, rhs, start=start, stop=stop), r, w)

    def tr(out, in_, idn, r, w):
        sc.op('pe', lambda e: e.transpose(out, in_, idn), r, w)

    def cp(eng, out, in_, r, w):
        if eng == 'act':
            sc.op(eng, lambda e: e.copy(out=out, in_=in_), r, w)
        else:
            sc.op(eng, lambda e: e.tensor_copy(out=out, in_=in_), r, w)

    def recip(out, in_, r, w):
        sc.op('dve', lambda e: e.reciprocal(out=out, in_=in_), r, w)

    def mset(eng, ap, val, w):
        sc.op(eng, lambda e: e.memset(ap, val), (), w)

    def bc(ap, shape):
        return ap.broadcast_to(list(shape))

    sc.dma('sp', ident[:], c_ident, w=['ident'])
    cp('dve', identb[:], ident[:], ['ident'], ['identb'])
    sc.dma('sp', tf[0][:, 0:128], c_rotT, w=['tf0'])
    cp('dve', rotT[:], tf[0][:, 0:128], ['tf0'], ['rotT'])
    sc.dma('sp', tf[1][:, 0:128], c_mlo, w=['tf1'])
    cp('dve', mlo[:], tf[1][:, 0:128], ['tf1'], ['mlo'])
    sc.dma('sp', tf[2][:, 0:128], c_mhi, w=['tf2'])
    cp('dve', mhi[:], tf[2][:, 0:128], ['tf2'], ['mhi'])
    sc.dma('sp', cmf[:], c_cmf, w=['cmf'])
    sc.dma('sp', cmb[:], c_cmb, w=['cmb'])
    sc.dma('sp', ridx[:], ridx_d, w=['ridx'])
    sc.dma('sp', oh4[:], oh4_d, w=['oh4'])
    sc.dma('sp', vmask[:], vmask_d, w=['vmask'])
    mset('dve', onesb[:], 1.0, ['onesb'])
    mset('dve', sgn[0:64, :], -1.0, ['sgn'])
    mset('dve', sgn[64:128, :], 1.0, ['sgn'])
    sc.dma('sp', cvT[:], cvec.rearrange("c (k p) -> p c k", p=128), w=['cvT'], slow=True)
    act(scT[:], cvT[:], AF.Silu, ['cvT'], ['scT'])

    wslot = [0]

    WNAMES = {'in': (w_in, 16, 0), 'pp': (None, 16, 44), 'out': (w_out, 16, 68)}
    WPACK = (('pa', w_pa, 4, 0), ('pb', w_pb, 8, 4), ('pc', w_pc, 4, 12))

    def wload_cast(src):
        i = wslot[0] % 3
        wslot[0] += 1
        K = src.shape[0] // 128
        sc.dma('pool', wb[i][:, 0:K, :], src.rearrange("(k p) c -> p k c", p=128), w=['wb%d' % i])
        return wb[i], 'wb%d' % i

    def wconvert(l):
        for nm, (wt_, K, g0) in WNAMES.items():
            if wt_ is None:
                continue
            ng = wt_.shape[2] // 256
            for gi in range(ng):
                t_, k_ = wload_cast(wt_[l][:, gi * 256:(gi + 1) * 256])
                sc.dma('sp', wsc[l, g0 + gi, :, 0:K, :], t_[:, 0:K, :], r=[k_], w=['wsc%d_%d' % (l, g0 + gi)])
        for fg in range(8):
            for nm, wt_, K, koff in WPACK:
                t_, k_ = wload_cast(wt_[l][:, fg * 256:(fg + 1) * 256])
                sc.dma('sp', wsc[l, 44 + fg, :, koff:koff + K, :], t_[:, 0:K, :], r=[k_], w=['wsc%d_%d' % (l, 44 + fg)])

    def wload(nm, l, gi, slot=None):
        wt_, K, g0 = WNAMES[nm]
        if slot is None:
            i = wslot[0] % 3
            wslot[0] += 1
        else:
            i = slot
        sc.dma('sp', wb[i][:, 0:K, :], wsc[l, g0 + gi, :, 0:K, :], r=['wsc%d_%d' % (l, g0 + gi)], w=['wb%d' % i])
        return wb[i], 'wb%d' % i

    def gelu_evac(out, pin, pk, shape, tix, rextra=(), wk=()):
        n = 1
        for s_ in shape[1:]:
            n *= s_
        t1 = tf[tix][0:shape[0], 0:n]
        t2 = tf[tix + 1][0:shape[0], 0:n]
        k1, k2 = 'tf%d' % tix, 'tf%d' % (tix + 1)
        pin2 = pin
        act(t1, pin2, AF.Square, [pk], [k1])
        ts('dve', t1, t1, GC2, ALU.mult, [k1], [k1], s2=1.0, op1=ALU.add)
        tt('dve', t1, t1, pin2, ALU.mult, [k1, pk], [k1])
        act(t2, t1, AF.Sigmoid, [k1], [k2], scale=GC1)
        tt('dve', out, t2, pin2, ALU.mult, [k2, pk] + list(rextra), list(wk))

    def wrap(out, x, kx, ko):
        mset('dve', p4[:], 0.0, ['p4'])
        for m in range(1, 9):
            ts('dve', p3[:], x, (2 * m - 1) * PI, ALU.is_gt, [kx], ['p3'])
            tt('dve', p4[:], p4[:], p3[:], ALU.add, ['p3', 'p4'], ['p4'])
        ts('dve', p4[:], p4[:], -2.0 * PI, ALU.mult, ['p4'], ['p4'])
        tt('dve', out, x, p4[:], ALU.add, [kx, 'p4'], [ko])

    def cmul(ore, oim, are, aim, bre, bim, keys_r, ko):
        tt('dve', p1[:], are, bre, ALU.mult, keys_r, ['p1'])
        tt('dve', p2[:], aim, bim, ALU.mult, keys_r, ['p2'])
        tt('dve', p3[:], are, bim, ALU.mult, keys_r, ['p3'])
        tt('dve', p4[:], aim, bre, ALU.mult, keys_r, ['p4'])
        tt('dve', ore, p1[:], p2[:], ALU.subtract, ['p1', 'p2'], ko)
        tt('dve', oim, p3[:], p4[:], ALU.add, ['p3', 'p4'], ko)

    def mix(out, g0, ng, Wre_, Wim_, Xre, Xim, kr, ko):
        for h in (0, 1):
            P = slice(h * 64, h * 64 + 64)
            wr = bc(Wre_[P, g0:g0 + ng].unsqueeze(2), [64, ng, 16])
            wi = bc(Wim_[P, g0:g0 + ng].unsqueeze(2), [64, ng, 16])
            xa_ = Xre[P, 0:ng, :] if h == 0 else Xim[P, 0:ng, :]
            xb_ = Xim[P, 0:ng, :] if h == 0 else Xre[P, 0:ng, :]
            tt('dve', bt1[P, 0:ng, :], xa_, wr, ALU.mult, kr, ['bt1'])
            tt('dve', bt2[P, 0:ng, :], xb_, wi, ALU.mult, kr, ['bt2'])
            tt('dve', out[P], bt1[P, 0:ng, :], bt2[P, 0:ng, :], ALU.subtract if h == 0 else ALU.add,
               ['bt1', 'bt2'], ko)

    EF = [[t + 1 for t in range(8)], [8 - t for t in range(8)]]

    def layer_prep(l):
        sc.dma('act', badaT[:], b_ada[l].rearrange("(c p) -> p c", p=128), w=['badaT'], slow=True)
        pm, pmk = bank()
        for gi in range(24):
            wt, wk = wload_cast(w_ada[l][:, gi * 256:(gi + 1) * 256])
            for j in range(2):
                ch = gi * 2 + j
                for k in range(KT):
                    mm(pm[:, ch * 2:ch * 2 + 2], wt[:, k, j * 128:(j + 1) * 128], scT[:, :, k],
                       k == 0, k == KT - 1, [wk, 'scT'], [pmk])
        tt('dve', modT[:], pm[:, 0:96].rearrange("p (c t) -> p c t", t=2), bc(badaT[:].unsqueeze(2), [128, 48, 2]),
           ALU.add, [pmk, 'badaT'], ['modT'])
        ts('dve', sc1[:], modT[:, 16:32, :], 1.0, ALU.add, ['modT'], ['sc1'])
        for c_ in range(2):
            sc.dma('act', gsc[c_].rearrange("(k p) -> p k", p=128), modT[:, 32:48, c_], r=['modT'], w=['gsc'], slow=True)
        sc.dma('act', sgb[:], sgu_g[l].partition_broadcast(128), w=['sgb'])
        sc.dma('act', sbb[:], sgu_b[l].partition_broadcast(128), w=['sbb'])
        sc.dma('act', bsb[:], b_s[l].partition_broadcast(128), w=['bsb'])
        sc.dma('act', wsn[:], w_s[l].rearrange("g p q -> p g q"), w=['tf0'])
        pw, pwk = bank()
        for g in range(4):
            tr(pw[:, g * 128:(g + 1) * 128], wsn[:, g, :], ident[:], ['tf0', 'ident'], [pwk])
        cp('dve', wsT[:], pw[:, 0:512].rearrange("p (g q) -> p g q", q=128), [pwk], ['wsT'])
        sc.dma('act', esink[:], sink[l].partition_broadcast(128), w=['esink'])
        act(esink[:], esink[:], AF.Exp, ['esink'], ['esink'])
        sc.dma('act', ckn[:], ck[l].rearrange("(b p) c -> p b c", p=128), w=['tf1'])
        pc_, pck = bank()
        for kvh in range(2):
            for b_ in range(2):
                tr(pc_[:, (kvh * 2 + b_) * 128:(kvh * 2 + b_ + 1) * 128], ckn[:, b_, kvh * 128:(kvh + 1) * 128],
                   ident[:], ['tf1', 'ident'], [pck])
        cp('dve', ckT[:], pc_[:, 0:512].rearrange("p (h t) -> p h t", t=256), [pck], ['ckT'])
        sc.dma('pool', cvb[:], cv[l].rearrange("(b p) c -> p b c", p=128), w=['cvb'])
        sc.dma('pool', wglu[:], w_glu[l].rearrange("(j p) c -> p j c", p=128), w=['wglu'])
        sc.dma('act', bglu[:], b_glu[l].rearrange("(j p) -> p j", p=128), w=['bglu'], slow=True)
        sc.dma('act', dsk[:], ssm_d[l].partition_broadcast(32), w=['dsk'])
        for d in range(2):
            for h in range(2):
                P = slice(h * 64, h * 64 + 64)
                sc.dma('act', lr[P, :], lam_re[l, d].rearrange("g p -> p g"), w=['lr'], slow=True)
                sc.dma('act', li[P, :], lam_im[l, d].rearrange("g p -> p g"), w=['li'], slow=True)
                for ri in range(2):
                    sc.dma('act', sinit[ri * 64:(ri + 1) * 64, d, :] if h == 0 else sinitw[(1 - ri) * 64:(2 - ri) * 64, d, :],
                           st0[l, d, ri].rearrange("g p -> p g"), w=['sinit' if h == 0 else 'sinitw'], slow=True)
            sc.dma('act', dtt[:], log_step[l, d].partition_broadcast(128), w=['dtt'])
            act(dtt[:], dtt[:], AF.Exp, ['dtt'], ['dtt'])
            tt('dve', p1[:], li[:], dtt[:], ALU.mult, ['li', 'dtt'], ['p1'])
            wrap(p2[:], p1[:], 'p1', 'p2')
            act(sinv[:], p2[:], AF.Sin, ['p2'], ['sinv'])
            if stop == 'wrap':
                return
            ts('dve', p1[:], p1[:], PI / 2, ALU.add, ['p1'], ['p1'])
            wrap(p2[:], p1[:], 'p1', 'p2')
            act(cosv[:], p2[:], AF.Sin, ['p2'], ['cosv'])
            tt('dve', p1[:], lr[:], dtt[:], ALU.mult, ['lr', 'dtt'], ['p1'])
            act(p2[:], p1[:], AF.Exp, ['p1'], ['p2'])
            act(p3[:], p1[:], AF.Exp, ['p1'], ['p3'], scale=-1.0)
            mset('dve', PWr[:, 8, :], 1.0, ['PW', 'nPW', 'xin1'])
            mset('dve', PWi[:, 8, :], 0.0, ['PW'])
            tt('dve', PWr[:, 9, :], p2[:], cosv[:], ALU.mult, ['p2', 'cosv'], ['PW'])
            tt('dve', PWi[:, 9, :], p2[:], sinv[:], ALU.mult, ['p2', 'sinv'], ['PW'])
            tt('dve', PWr[:, 7, :], p3[:], cosv[:], ALU.mult, ['p3', 'cosv'], ['PW'])
            tt('dve', PWi[:, 7, :], p3[:], sinv[:], ALU.mult, ['p3', 'sinv'], ['PW'])
            ts('dve', PWi[:, 7, :], PWi[:, 7, :], -1.0, ALU.mult, ['PW'], ['PW'])
            for k in range(2, 9):
                cmul(PWr[:, 8 + k, :], PWi[:, 8 + k, :], PWr[:, 7 + k, :], PWi[:, 7 + k, :], PWr[:, 9, :], PWi[:, 9, :], ['PW'], ['PW'])
                cmul(PWr[:, 8 - k, :], PWi[:, 8 - k, :], PWr[:, 9 - k, :], PWi[:, 9 - k, :], PWr[:, 7, :], PWi[:, 7, :], ['PW'], ['PW'])
            ts('dve', nPWi[:], PWi[:], -1.0, ALU.mult, ['PW'], ['nPW'])
            tt('dve', p1[:], lr[:], lr[:], ALU.mult, ['lr'], ['p1'])
            tt('dve', p2[:], li[:], li[:], ALU.mult, ['li'], ['p2'])
            tt('dve', p1[:], p1[:], p2[:], ALU.add, ['p1', 'p2'], ['p1'])
            recip(p1[:], p1[:], ['p1'], ['p1'])
            ts('dve', p2[:], PWr[:, 9, :], -1.0, ALU.add, ['PW'], ['p2'])
            tt('dve', p3[:], p2[:], lr[:], ALU.mult, ['p2', 'lr'], ['p3'])
            tt('dve', p4[:], PWi[:, 9, :], li[:], ALU.mult, ['PW', 'li'], ['p4'])
            tt('dve', p3[:], p3[:], p4[:], ALU.add, ['p3', 'p4'], ['p3'])
            tt('dve', fre[:], p3[:], p1[:], ALU.mult, ['p3', 'p1'], ['fre'])
            tt('dve', p3[:], PWi[:, 9, :], lr[:], ALU.mult, ['PW', 'lr'], ['p3'])
            tt('dve', p4[:], p2[:], li[:], ALU.mult, ['p2', 'li'], ['p4'])
            tt('dve', p3[:], p3[:], p4[:], ALU.subtract, ['p3', 'p4'], ['p3'])
            tt('dve', fim[:], p3[:], p1[:], ALU.mult, ['p3', 'p1'], ['fim'])
            cp('dve', Ar[:, d * 32:(d + 1) * 32], PWr[:, 16, :], ['PW'], ['Ar'])
            ts('dve', Aix[:, d * 32:(d + 1) * 32], PWi[:, 16, :], sgn[:, 0:1], ALU.mult, ['PW', 'sgn'], ['Aix'])
            ts('dve', nAix[:, d * 32:(d + 1) * 32], Aix[:, d * 32:(d + 1) * 32], -1.0, ALU.mult, ['Aix'], ['nAix'])
            cp('dve', A2r[:, d * 32:(d + 1) * 32], PWr[:, 16, :], ['PW'], ['A2'])
            cp('dve', A2i[:, d * 32:(d + 1) * 32], PWi[:, 16, :], ['PW'], ['A2'])
            cm = cmf if d == 0 else cmb
            cmk = 'cmf' if d == 0 else 'cmb'
            for gb in range(4):
                g0 = gb * 8
                for h in range(2):
                    P = slice(h * 64, h * 64 + 64)
                    sc.dma('act', Bre[P], b_re[l, d, g0:g0 + 8].rearrange("g p c -> p g c"), w=['Bre'])
                    sc.dma('act', Bim[P], b_im[l, d, g0:g0 + 8].rearrange("g p c -> p g c"), w=['Bim'])
                for (src_c, dst_c, neg) in ((c_re, Cre, False), (c_im, nCim, True)):
                    for h in range(2):
                        sc.dma('act', cnat[:, h, :], src_c[l, d, g0:g0 + 8].rearrange("g c p -> (g c) p"), w=['cnat'])
                    pb_, pbk = bank()
                    tr(pb_[:, 0:128], cnat[:].rearrange("q h p -> q (h p)"), ident[:], ['cnat', 'ident'], [pbk])
                    if neg:
                        ts('dve', dst_c[:], pb_[:, 0:128].rearrange("p (g c) -> p g c", c=16), -1.0, ALU.mult, [pbk], ['C'])
                    else:
                        cp('dve', dst_c[:], pb_[:, 0:128].rearrange("p (g c) -> p g c", c=16), [pbk], ['C'])
                fr_b = bc(fre[:, g0:g0 + 8].unsqueeze(2), [128, 8, 16]); fi_b = bc(fim[:, g0:g0 + 8].unsqueeze(2), [128, 8, 16])
                tt('dve', bt1[:], Bre[:], fr_b, ALU.mult, ['Bre', 'fre'], ['bt1'])
                tt('dve', bt2[:], Bim[:], fi_b, ALU.mult, ['Bim', 'fim'], ['bt2'])
                tt('dve', Bbr[:], bt1[:], bt2[:], ALU.subtract, ['bt1', 'bt2'], ['Bb'])
                tt('dve', bt1[:], Bim[:], fr_b, ALU.mult, ['Bim', 'fre'], ['bt1'])
                tt('dve', bt2[:], Bre[:], fi_b, ALU.mult, ['Bre', 'fim'], ['bt2'])
                tt('dve', Bbi[:], bt1[:], bt2[:], ALU.add, ['bt1', 'bt2'], ['Bb'])
                for t in range(8):
                    e = EF[d][t]
                    mix(Fg[:, :, t * 16:(t + 1) * 16], g0, 8, PWr[:, 8 + e, :], nPWi[:, 8 + e, :], Cre, nCim, ['PW', 'nPW', 'C'], ['V'])
                    mix(Gg[:, :, t * 16:(t + 1) * 16], g0, 8, PWr[:, 8 - e, :], PWi[:, 8 - e, :], Bbr, Bbi, ['PW', 'Bb'], ['V'])
                    mix(Eg[:, :, t * 16:(t + 1) * 16], g0, 8, PWr[:, 16 - e, :], PWi[:, 16 - e, :], Bbr, Bbi, ['PW', 'Bb'], ['Vs'])
                cp('dve', w3[:, :, 0, :], Fg[:], ['V'], ['s5wb0'])
                for gl in range(8):
                    pb_, pbk = bank()
                    tr(pb_[:, 0:128], Eg[:, gl, :], ident[:], ['Vs', 'ident'], [pbk])
                    mm(pb_[:, 128:256], Gg[:, gl, :], Fg[:, gl, :], True, True, ['V', 'V'], [pbk])
                    cp('dve', w3[:, gl, 1, :], pb_[:, 0:128], [pbk], ['s5wb0'])
                    tt('dve', w3[:, gl, 2, :], pb_[:, 128:256], cm[:], ALU.mult, [pbk, cmk], ['s5wb0'])
                sc.dma('act', s5w[d * 32 + g0:d * 32 + g0 + 8].rearrange("g t k m -> k g t m"), w3[:], r=['s5wb0'], w=['s5w'])
        TRv = rr[:].rearrange("p (g n) -> p g n", n=32)
        TIv = tfall[:].rearrange("p a c -> p (a c)").rearrange("p (g n) -> p g n", n=32)
        tkeys = ['rr', 'tf0', 'tf1', 'tf2', 'tf3']
        mset('dve', TRv[:, 0:32, 31:32], 1.0, tkeys)
        mset('dve', TIv[:, 0:32, 31:32], 0.0, tkeys)
        mset('dve', TRv[:, 32:64, 0:1], 1.0, tkeys)
        mset('dve', TIv[:, 32:64, 0:1], 0.0, tkeys)
        for it_ in range(5):
            m = 1 << it_
            for d in range(2):
                G = slice(d * 32, d * 32 + 32)
                if d == 0:
                    srcs, dsts = slice(32 - m, 32), slice(32 - 2 * m, 32 - m)
                else:
                    srcs, dsts = slice(0, m), slice(m, 2 * m)
                amr = bc(A2r[:, G].unsqueeze(2), [128, 32, m])
                ami = bc(A2i[:, G].unsqueeze(2), [128, 32, m])
                t1_ = V[:, 0:32, 0:m]
                t2_ = V[:, 32:64, 0:m]
                tt('dve', t1_, TRv[:, G, srcs], amr, ALU.mult, tkeys + ['A2'], ['V'])
                tt('dve', t2_, TIv[:, G, srcs], ami, ALU.mult, tkeys + ['A2'], ['V'])
                tt('dve', TRv[:, G, dsts], t1_, t2_, ALU.subtract, ['V'], tkeys)
                tt('dve', t1_, TRv[:, G, srcs], ami, ALU.mult, tkeys + ['A2'], ['V'])
                tt('dve', t2_, TIv[:, G, srcs], amr, ALU.mult, tkeys + ['A2'], ['V'])
                tt('dve', TIv[:, G, dsts], t1_, t2_, ALU.add, ['V'], tkeys)
            tt('dve', tA[:], A2r[:], A2r[:], ALU.mult, ['A2'], ['tA'])
            tt('dve', tB[:], A2i[:], A2i[:], ALU.mult, ['A2'], ['tB'])
            tt('dve', tA[:], tA[:], tB[:], ALU.subtract, ['tA', 'tB'], ['tA'])
            tt('dve', tB[:], A2r[:], A2i[:], ALU.mult, ['A2'], ['tB'])
            cp('dve', A2r[:], tA[:], ['tA'], ['A2'])
            ts('dve', A2i[:], tB[:], 2.0, ALU.mult, ['tB'], ['A2'])
        ts('dve', TIv[:], TIv[:], sgn[:, 0:1], ALU.mult, tkeys + ['sgn'], tkeys)
        ts('dve', A2ix[:], A2i[:], sgn[:, 0:1], ALU.mult, ['A2', 'sgn'], ['A2x'])
        ts('dve', nA2ix[:], A2ix[:], -1.0, ALU.mult, ['A2x'], ['A2x'])

    xslot = [0]

    def ln_ht(src, rows, col0, cond, rkeys=()):
        for si, r0 in enumerate(rows):
            b = xslot[0] % 2
            xslot[0] += 1
            xk = 'xin%d' % b
            xt = xin[b]
            xw = [xk] if b == 0 else [xk, 'PW', 'nPW']
            if isinstance(r0, tuple):
                sc.dma('pool', xt[:], src, r=['ridx'] + list(rkeys), w=xw, ind=ridx[:, r0[1]:r0[1] + 1])
            else:
                sc.dma('sp', xt[:], src[r0:r0 + 128, :], r=list(rkeys), w=xw)
            for q in range(4):
                sc.op('dve', lambda e, q=q, xt=xt: e.bn_stats(out=stats[:, q, :], in_=xt[:, q * 512:(q + 1) * 512]), [xk], ['stats'])
            sc.op('dve', lambda e: e.bn_aggr(out=mv[:], in_=stats[:].rearrange("p a b -> p (a b)")), ['stats'], ['mv'])
            act(rstd[:], mv[:, 1:2], AF.Sqrt, ['mv'], ['rstd'], bias=EPS)
            recip(rstd[:], rstd[:], ['rstd'], ['rstd'])
            ts('dve', xt[:], xt[:], mv[:, 0:1], ALU.subtract, [xk, 'mv', 'rstd'], [xk], s2=rstd[:, 0:1], op1=ALU.mult)
            c0 = col0 + si * 128
            for kq in range(4):
                pb_, pbk = bank()
                for kk_ in range(4):
                    k = kq * 4 + kk_
                    tr(pb_[:, kk_ * 128:(kk_ + 1) * 128], xt[:, k * 128:(k + 1) * 128], ident[:], [xk, 'ident'], [pbk])
                for kk_ in range(4):
                    k = kq * 4 + kk_
                    if k % 2 == 0:
                        act(hT[:, k, c0:c0 + 128], pb_[:, kk_ * 128:(kk_ + 1) * 128], AF.Identity, [pbk, 'sc1', 'modT'], ['hT'],
                            bias=modT[:, k, cond:cond + 1], scale=sc1[:, k, cond:cond + 1])
                    else:
                        ts('dve', hT[:, k, c0:c0 + 128], pb_[:, kk_ * 128:(kk_ + 1) * 128], sc1[:, k, cond:cond + 1], ALU.mult,
                           [pbk, 'sc1', 'modT'], ['hT'], s2=modT[:, k, cond:cond + 1], op1=ALU.add)

    def proj_xa(l, cc):
        for gi in range(2):
            wt, wk = wload('in', l, gi)
            for t in range(8):
                pb_, pbk = bank()
                for k in range(KT):
                    mm(pb_[0:32, 0:256], hT[:, k, cc + t:cc + 256:8], wt[:, k, :], k == 0, k == KT - 1, [wk, 'hT'], [pbk])
                cp('dve' if t % 2 == 0 else 'act', Xp[:, gi * 16:(gi + 1) * 16, t, :],
                   pb_[0:32, 0:256].rearrange("p (g c) -> p g c", c=16), [pbk], ['Xp'])

    def s5_states(t_init, zero_init, reng='dve', sel=False, rec=True):
        for gq in range(4):
            pb_, pbk = bank()
            for gl in range(8):
                g = gq * 8 + gl
                mm(pb_[:, gl * 32:(gl + 1) * 32], Xp[:, g, :, :].rearrange("p t c -> p (t c)"), identb[0:32, 0:32], True, True,
                   ['Xp', 'identb'], [pbk])
            cp('act', Xpp[:, gq * 8:(gq + 1) * 8, :], pb_[:, 0:256].rearrange("p (g n) -> p g n", n=32), [pbk], ['Xpp'])
        for d in range(2):
            for gq in range(4):
                i = (d * 4 + gq) % 2
                wk = 's5wb%d' % i
                sc.dma('sp', s5wb[i][:, :, 1, :], s5w[d * 32 + gq * 8:d * 32 + gq * 8 + 8, 1].rearrange("g k m -> k g m"),
                       r=['s5w'], w=[wk])
                pv, pvk = bank()
                pw_, pwk = bank()
                for gl in range(8):
                    g = gq * 8 + gl
                    mm(pv[:, gl * 32:(gl + 1) * 32], s5wb[i][:, gl, 1, :], Xpp[:, g, :], True, True, [wk, 'Xpp'], [pvk])
                    mm(pw_[0:64, gl * 32:(gl + 1) * 32], s5wb[i][:, gl, 1, 64:128], Xpp[:, g, :], True, True, [wk, 'Xpp'], [pwk])
                    mm(pw_[64:128, gl * 32:(gl + 1) * 32], s5wb[i][:, gl, 1, 0:64], Xpp[:, g, :], True, True, [wk, 'Xpp'], [pwk])
                gd0 = d * 32 + gq * 8
                cp('dve', V[:, gd0:gd0 + 8, :], pv[:, 0:256].rearrange("p (g n) -> p g n", n=32), [pvk], ['V'])
                cp('act', Vs[:, gd0:gd0 + 8, :], pw_[:, 0:256].rearrange("p (g n) -> p g n", n=32), [pwk], ['Vs'])
        for d in (range(2) if rec else ()):
            G = slice(d * 32, d * 32 + 32)
            order = list(range(32)) if d == 0 else list(range(31, -1, -1))
            for step, n_ in enumerate(order):
                if step == 0:
                    if zero_init:
                        continue
                    if sel:
                        pS = Psel[:, d, 0, :]
                        pW = Psel[:, d, 1, :]
                        kp = ['Psel']
                    else:
                        pS = Pst[:, t_init + (0 if d == 0 else 1), 0, G]
                        pW = Pst[:, t_init + (0 if d == 0 else 1), 1, G]
                        kp = ['Pst']
                else:
                    pn = order[step - 1]
                    pS = V[:, G, pn]
                    pW = Vs[:, G, pn]
                    kp = ['V', 'Vs']
                tt(reng, tA[:, 0:32], pS, Ar[:, G], ALU.mult, kp + ['Ar'], ['tA'])
                tt(reng, tB[:, 0:32], pW, Aix[:, G], ALU.mult, kp + ['Aix'], ['tB'])
                tt(reng, tA[:, 0:32], tA[:, 0:32], tB[:, 0:32], ALU.add, ['tA', 'tB'], ['tA'])
                tt(reng, tA[:, 32:64], pW, Ar[:, G], ALU.mult, kp + ['Ar'], ['tA2'])
                tt(reng, tB[:, 32:64], pS, nAix[:, G], ALU.mult, kp + ['nAix'], ['tB2'])
                tt(reng, tA[:, 32:64], tA[:, 32:64], tB[:, 32:64], ALU.add, ['tA2', 'tB2'], ['tA2'])
                tt(reng, V[:, G, n_], V[:, G, n_], tA[:, 0:32], ALU.add, ['V', 'tA'], ['V'])
                tt(reng, Vs[:, G, n_], Vs[:, G, n_], tA[:, 32:64], ALU.add, ['Vs', 'tA2'], ['Vs'])

    def s5_main(l, t_init, zero_init, seq, sel=False):
        if zero_init:
            mset('dve', Sb[:, 0:32, 0:1], 0.0, ['Sb'])
            mset('dve', Sb[:, 32:64, 31:32], 0.0, ['Sb'])
        elif sel:
            cp('dve', Sb[:, 0:32, 0:1], Psel[:, 0, 0, :].unsqueeze(2), ['Psel'], ['Sb'])
            cp('dve', Sb[:, 32:64, 31:32], Psel[:, 1, 0, :].unsqueeze(2), ['Psel'], ['Sb'])
        else:
            cp('dve', Sb[:, 0:32, 0:1], Pst[:, t_init, 0, 0:32].unsqueeze(2), ['Pst'], ['Sb'])
            cp('dve', Sb[:, 32:64, 31:32], Pst[:, t_init + 1, 0, 32:64].unsqueeze(2), ['Pst'], ['Sb'])
        cp('dve', Sb[:, 0:32, 1:32], V[:, 0:32, 0:31], ['V'], ['Sb'])
        cp('act', Sb[:, 32:64, 0:31], V[:, 32:64, 1:32], ['V'], ['Sb'])
        if seq is not None:
            for d in range(2):
                pb_, pbk = bank()
                src_ = V[:, 0:32, 31] if d == 0 else V[:, 32:64, 0]
                cp('dve', tf[0][:, 0:32], src_, ['V'], ['tf0'])
                tr(pb_[0:32, 0:128], tf[0][:, 0:32], ident[:], ['tf0', 'ident'], [pbk])
                cp('dve', stg[:], pb_[0:32, 0:128], [pbk], ['stg'])
                sc.dma('pool', nst[seq, l, d].rearrange("r g p -> g r p"), stg[:].rearrange("g (r p) -> g r p", r=2),
                       r=['stg'], w=['nst'])
        for gq in range(4):
            p0, p0k = bank()
            p1_, p1k = bank()
            for d in range(2):
                for t_ in (0, 2):
                    sc.dma('sp', s5wb[d][:, :, t_, :], s5w[d * 32 + gq * 8:d * 32 + gq * 8 + 8, t_].rearrange("g k m -> k g m"),
                           r=['s5w'], w=['s5wb%d' % d])
            for gl in range(8):
                g = gq * 8 + gl
                pp, ppk = (p0, p0k) if gl < 4 else (p1_, p1k)
                o = pp[0:32, (gl % 4) * 128:(gl % 4 + 1) * 128]
                for d in range(2):
                    wk = 's5wb%d' % d
                    mm(o, Xpp[:, g, :], s5wb[d][:, gl, 2, :], d == 0, False, [wk, 'Xpp'], [ppk])
                    mm(o, Sb[:, d * 32 + g, :], s5wb[d][:, gl, 0, :], False, d == 1, [wk, 'Sb'], [ppk])
            for hf, (pp, ppk) in enumerate(((p0, p0k), (p1_, p1k))):
                g0 = gq * 8 + hf * 4
                yv = ygf[:]
                dsl = bc(dsk[:, g0 * 16:(g0 + 4) * 16].rearrange("p (g c) -> p g c", c=16).unsqueeze(2), [32, 4, 8, 16])
                tt('dve', yv, Xp[:, g0:g0 + 4, :, :], dsl, ALU.mult, ['Xp', 'dsk'], ['ygf'])
                tt('dve', yv, yv, pp[0:32, 0:512].rearrange("p (g t c) -> p g t c", t=8, c=16), ALU.add, ['ygf', ppk], ['ygf'])
                yf = ygf[:].rearrange("p g t c -> p (g t c)")
                y2 = ygf2[:].rearrange("p g t c -> p (g t c)")
                act(y2, yf, AF.Square, ['ygf'], ['ygf2'])
                ts('dve', y2, y2, GC2, ALU.mult, ['ygf2'], ['ygf2'], s2=1.0, op1=ALU.add)
                tt('dve', y2, y2, yf, ALU.mult, ['ygf2', 'ygf'], ['ygf2'])
                act(y2, y2, AF.Sigmoid, ['ygf2'], ['ygf2'], scale=GC1)
                tt('dve', yg[:, :, g0:g0 + 4, :].rearrange("p t g c -> p g t c"), ygf2[:], ygf[:], ALU.mult, ['ygf2', 'ygf'], ['yg'])
        for j in range(4):
            pb_, pbk = bank()
            for t in range(8):
                mm(pb_[:, t * 32:(t + 1) * 32], yg[:, t, j * 8:(j + 1) * 8, :].rearrange("p g c -> p (g c)"), identb[0:32, 0:32], True, True, ['yg', 'identb'], [pbk])
            cp('dve' if j % 2 == 0 else 'act', ygT[:, j, :].rearrange("p (n t) -> p t n", t=8),
               pb_[:, 0:256].rearrange("p (t n) -> p t n", n=32), [pbk], ['ygT'])
        for jo in range(4):
            pb_, pbk = bank()
            for j in range(4):
                mm(pb_[:, 0:NT], wglu[:, j, jo * 128:(jo + 1) * 128], ygT[:, j, :], j == 0, j == 3, ['wglu', 'ygT'], [pbk])
            act(tf[0][:, 0:NT], pb_[:, 0:NT], AF.Sigmoid, [pbk, 'bglu'], ['tf0'], bias=bglu[:, jo:jo + 1])
            tt('dve', tf[0][:, 0:NT], tf[0][:, 0:NT], ygT[:, jo, :], ALU.mult, ['tf0', 'ygT'], ['tf0'])
            tt('dve', yaT[:, jo, :], tf[0][:, 0:NT], zaT[:, jo, :], ALU.mult, ['tf0', 'zaT'], ['zaT'])

    def proj_fm(l, gi, nchunk, cols, evac):
        wt, wk = wload('in', l, gi)
        for j in range(nchunk):
            pb_, pbk = bank()
            n = cols.stop - cols.start
            for k in range(KT):
                mm(pb_[:, 0:n], wt[:, k, j * 128:(j + 1) * 128], hT[:, k, cols], k == 0, k == KT - 1, [wk, 'hT'], [pbk])
            evac(gi, j, pb_[:, 0:n], pbk)

    def rope(out, pin, pk, n, okey):
        cp('act', qraw[:, 0:n], pin, [pk], ['pTs4'])
        pr, prk = bank()
        mm(pr[:, 0:n], rotT[:], qraw[:, 0:n], True, True, ['rotT', 'pTs4'], [prk])
        tt('dve', tf[2][:, 0:n], pin, rc[:, 0:n], ALU.mult, [pk, 'rc'], ['tf2'])
        tt('dve', tf[3][:, 0:n], pr[:, 0:n], rs[:, 0:n], ALU.mult, [prk, 'rs'], ['tf3'])
        tt('dve', out, tf[2][:, 0:n], tf[3][:, 0:n], ALU.add, ['tf2', 'tf3'], [okey])

    def attn_finish(pvb, pvk, pdb, pdk, h0, nh, hw, c0):
        n_ = nh * hw
        cp('act', tf[1][:, 0:n_], pvb[:, 0:n_], [pvk], ['tf1'])
        for hi in range(nh):
            cs = slice(hi * hw, (hi + 1) * hw)
            ts('dve', tf[0][:, cs], pdb[:, cs], esink[:, h0 + hi:h0 + hi + 1], ALU.add, [pdk, 'esink'], ['tf0'])
        recip(tf[0][:, 0:n_], tf[0][:, 0:n_], ['tf0'], ['tf0'])
        tt('dve', tf[0][:, 0:n_], tf[0][:, 0:n_], tf[1][:, 0:n_], ALU.mult, ['tf0', 'tf1'], ['tf0'])
        tt('dve', ybT[:, h0:h0 + nh, c0:c0 + hw], tf[0][:, 0:n_].rearrange("p (h q) -> p h q", q=hw), zbT[:, h0:h0 + nh, c0:c0 + hw],
           ALU.mult, ['tf0', 'zbT'], ['qT'])

    import os as _os

    def tile_main(l, kind, idx):
        cond = 0 if kind == 'p' else 1
        own = kind == 'o'
        src = (xp if l == 0 else zp) if kind == 'p' else (xs if l == 0 else zs)
        dst = (zp if l == 0 else yp) if kind == 'p' else (zs if l == 0 else ys)
        rkeys = [] if l == 0 else ['dst_p_0' if kind == 'p' else 'dst_s_0']
        dkey = 'dst_%s_%d' % ('p' if kind == 'p' else 's', l)
        t0 = idx * NT
        if kind == 'p':
            rows = [t0, t0 + 128]
            cc = 0
            E = 256
        elif own:
            rows = [('ind', 2 * idx + b_) for b_ in range(4)]
            cc = 128
            E = 512
            sc.dma('pool', rc[:, 0:E], c_ropec_o[:, 256 * idx:256 * idx + 512], w=['rc'])
            sc.dma('pool', rs[:, 0:E], c_ropes_o[:, 256 * idx:256 * idx + 512], w=['rs'])
            for d in range(2):
                for w_ in range(2):
                    G = slice(d * 32, d * 32 + 32)
                    ts('dve', Psel[:, d, w_, :], Pst[:, idx + d, w_, G], oh4[:, 0:1], ALU.mult, ['Pst', 'oh4'], ['Psel'])
                    for j_ in range(1, 4):
                        sc.op('dve', lambda e, d=d, w_=w_, j_=j_, G=G: e.scalar_tensor_tensor(
                            out=Psel[:, d, w_, :], in0=Pst[:, 4 * j_ + idx + d, w_, G], scalar=oh4[:, j_:j_ + 1], in1=Psel[:, d, w_, :],
                            op0=ALU.mult, op1=ALU.add), ['Pst', 'oh4', 'Psel'], ['Psel'])
        else:
            lo = t0 - 128 if idx > 0 else t0
            hi = t0 + NT + 128 if idx < 15 else t0 + NT
            rows = list(range(lo, hi, 128))
            cc = t0 - lo
            E = hi - lo
            sc.dma('pool', rc[:, 0:E], c_ropec[:, lo:hi], w=['rc'])
            sc.dma('pool', rs[:, 0:E], c_ropes[:, lo:hi], w=['rs'])
        ln_ht(src, rows, 0, cond, rkeys)
        ctr = slice(cc, cc + NT)
        proj_xa(l, cc)
        scale = 128 ** -0.5

        def ev(gi, j, pin, pk):
            if gi in (2, 3):
                act(zaT[:, (gi - 2) * 2 + j, :], pin, AF.Silu, [pk], ['zaT'])
            elif 4 <= gi <= 7:
                hh = (gi - 4) * 2 + j
                if kind == 'p' or 'rope' in _os.environ.get('SKIP', ''):
                    cp('act', qT[:, hh, :], pin, [pk], ['qT'])
                else:
                    rope_q(hh, pin, pk)
            elif gi == 8:
                if kind == 'p' or 'rope' in _os.environ.get('SKIP', ''):
                    cp('act', kT[:, j, 0:E], pin, [pk], ['kT'])
                else:
                    rope(kT[:, j, 0:E], pin, pk, E, 'kT')
            elif 10 <= gi <= 13:
                act(zbT[:, (gi - 10) * 2 + j, :], pin, AF.Silu, [pk], ['zbT'])
            elif gi in (14, 15):
                gelu_evac(uT[:, (gi - 14) * 2 + j, :], pin, pk, [128, NT], 0, wk=['uT'])
            elif gi in (18, 19):
                act(zcT[:, (gi - 18) * 2 + j, :], pin, AF.Silu, [pk], ['zcT'])

        def rope_q(hh, pin, pk):
            cp('act', qraw[:, 0:NT], pin, [pk], ['pTs4'])
            pr, prk = bank()
            mm(pr[:, 0:NT], rotT[:], qraw[:, 0:NT], True, True, ['rotT', 'pTs4'], [prk])
            tt('dve', tf[2][:, 0:NT], pin, rc[:, ctr], ALU.mult, [pk, 'rc'], ['tf2'])
            tt('dve', tf[3][:, 0:NT], pr[:, 0:NT], rs[:, ctr], ALU.mult, [prk, 'rs'], ['tf3'])
            tt('dve', qT[:, hh, :], tf[2][:, 0:NT], tf[3][:, 0:NT], ALU.add, ['tf2', 'tf3'], ['qT'])

        for gi in (2, 3):
            proj_fm(l, gi, 2, ctr, ev)
        s5_states(idx if kind == 's' else 0, kind == 'p', reng=_os.environ.get('RENG', 'pool'), sel=own)
        for gi in (4, 5, 6, 7):
            proj_fm(l, gi, 2, ctr, ev)
        proj_fm(l, 8, 2, slice(0, E), ev)
        for gi in (8, 9):
            if gi == 8 and kind != 'p':
                continue
            wt, wk = wload('in', l, gi)
            for s_ in range(E // 128):
                pb_, pbk = bank()
                for k in range(KT):
                    mm(pb_[:, 0:256], hT[:, k, s_ * 128:(s_ + 1) * 128], wt[:, k, :], k == 0, k == KT - 1, [wk, 'hT'], [pbk])
                if gi == 9:
                    cp('act', vtok[:, s_, :], pb_[:, 0:256], [pbk], ['vtok'])
                if kind == 'p':
                    o_ = nk if gi == 8 else nv
                    cp('dve', tf[2 + s_][:, 0:256], pb_[:, 0:256], [pbk], ['tf%d' % (2 + s_)])
                    sc.dma('pool', o_[idx, l, s_ * 128:(s_ + 1) * 128, :], tf[2 + s_][:, 0:256], r=['tf%d' % (2 + s_)], w=['nkv'])
        for gi in (10, 11, 12, 13, 14, 15):
            proj_fm(l, gi, 2, ctr, ev)
        for gi in (16, 17):
            wt, wk = wload('in', l, gi)
            for s_ in range(2):
                pb_, pbk = bank()
                for k in range(KT):
                    mm(pb_[:, 0:256], hT[:, k, cc + s_ * 128:cc + (s_ + 1) * 128], wt[:, k, :], k == 0, k == KT - 1, [wk, 'hT'], [pbk])
                gelu_evac(vsg[:, s_, (gi - 16) * 256:(gi - 15) * 256], pb_[:, 0:256], pbk, [128, 256], 0, wk=['rr'])
        for s_ in range(2):
            sc.op('dve', lambda e, s_=s_: e.bn_stats(out=stats[:, 0, :], in_=vsg[:, s_, :]), ['rr'], ['stats'])
            sc.op('dve', lambda e: e.bn_aggr(out=mv[:], in_=stats[:, 0, :]), ['stats'], ['mv'])
            act(rstd[:], mv[:, 1:2], AF.Sqrt, ['mv'], ['rstd'], bias=EPS)
            recip(rstd[:], rstd[:], ['rstd'], ['rstd'])
            ts('dve', vsg[:, s_, :], vsg[:, s_, :], mv[:, 0:1], ALU.subtract, ['rr', 'mv', 'rstd'], ['rr'], s2=rstd[:, 0:1], op1=ALU.mult)
            tt('dve', vsg[:, s_, :], vsg[:, s_, :], sgb[:], ALU.mult, ['rr', 'sgb'], ['rr'])
            tt('dve', vsln[:, s_, :], vsg[:, s_, :], sbb[:], ALU.add, ['rr', 'sbb'], ['vsln'])
        for gi in (18, 19):
            proj_fm(l, gi, 2, ctr, ev)
        for g in range(4):
            for s_ in range(2):
                pb_, pbk = bank()
                mm(pb_[:, 0:128], vsln[:, s_, g * 128:(g + 1) * 128], wsT[:, g, :], True, True, ['vsln', 'wsT'], [pbk])
                tt('dve', tf[1][:, 0:128], pb_[:, 0:128], bsb[:, g * 128:(g + 1) * 128], ALU.add, [pbk, 'bsb'], ['tf1'])
                tt('dve', tf[1][:, 0:128], tf[1][:, 0:128], uT[:, g, s_ * 128:(s_ + 1) * 128], ALU.mult, ['tf1', 'uT'], ['tf1'])
                tt('dve', ycT[:, g, s_ * 128:(s_ + 1) * 128], tf[1][:, 0:128], zcT[:, g, s_ * 128:(s_ + 1) * 128], ALU.mult,
                   ['tf1', 'zcT'], ['zcT'])
        if kind == 'p':
            for kvh in range(2):
                for kb in range(2):
                    pa, pak = bank()
                    pb2, pb2k = bank()
                    for h in range(4):
                        pp, ppk = (pa, pak) if h < 2 else (pb2, pb2k)
                        mm(pp[:, (h % 2) * 256:(h % 2 + 1) * 256], kT[:, kvh, kb * 128:(kb + 1) * 128], qT[:, kvh * 4 + h, :], True, True,
                           ['kT', 'qT'], [ppk])
                    act(pTs[kb * 2][:], pa[:, 0:512], AF.Exp, [pak], ['pTs%d' % (kb * 2)], scale=scale)
                    act(pTs[kb * 2 + 1][:], pb2[:, 0:512], AF.Exp, [pb2k], ['pTs%d' % (kb * 2 + 1)], scale=scale)
                for half in range(2):
                    pv, pvk = bank()
                    pd, pdk = bank()
                    for kb in range(2):
                        mm(pv[:, 0:512], vtok[:, kb, kvh * 128:(kvh + 1) * 128], pTs[kb * 2 + half][:], kb == 0, kb == 1,
                           ['vtok', 'pTs%d' % (kb * 2 + half)], [pvk])
                        mm(pd[:, 0:512], onesb[:], pTs[kb * 2 + half][:], kb == 0, kb == 1, ['onesb', 'pTs%d' % (kb * 2 + half)], [pdk])
                    attn_finish(pv, pvk, pd, pdk, kvh * 4 + half * 2, 2, 256, 0)
        elif 'attn' not in _os.environ.get('SKIP', ''):
            for kvh in range(2):
                for qb in range(2):
                    ia = 2 * idx + qb
                    blocks = []
                    if own:
                        eq = cc + qb * 128
                        blocks.append(('l', eq - 128, mlo, 'mlo', 0 if (idx == 0 and qb == 0) else None))
                        blocks.append(('l', eq, None, None, None))
                        blocks.append(('l', eq + 128, mhi, 'mhi', 1 if (idx == 3 and qb == 1) else None))
                    else:
                        if ia - 1 >= 0:
                            blocks.append(('l', (ia - 1) * 128 - (t0 - cc), mlo, 'mlo', None))
                        blocks.append(('l', ia * 128 - (t0 - cc), None, None, None))
                        if ia + 1 <= 31:
                            blocks.append(('l', (ia + 1) * 128 - (t0 - cc), mhi, 'mhi', None))
                    blocks.append(('c', 0, None, None, None))
                    blocks.append(('c', 1, None, None, None))
                    qv = qT[:, kvh * 4:(kvh + 1) * 4, qb * 128:(qb + 1) * 128]
                    for bi, (bt_, a_, msk, mk, vc) in enumerate(blocks):
                        pa, pak = bank()
                        if bt_ == 'l':
                            e0 = a_
                            kk_ = kT[:, kvh, e0:e0 + 128]
                            kkey = 'kT'
                        else:
                            kk_ = ckT[:, kvh, a_ * 128:(a_ + 1) * 128]
                            kkey = 'ckT'
                        for h_ in range(4):
                            mm(pa[:, h_ * 128:(h_ + 1) * 128], kk_, qT[:, kvh * 4 + h_, qb * 128:(qb + 1) * 128], True, True,
                               [kkey, 'qT'], [pak])
                        act(pTs[bi][:], pa[:, 0:512], AF.Exp, [pak], ['pTs%d' % bi], scale=scale)
                        if msk is not None and vc is not None:
                            sc.op('dve', lambda e, bi=bi, msk=msk, vc=vc: e.scalar_tensor_tensor(
                                out=pTs[bi][:].rearrange("p (h q) -> p h q", q=128), in0=pTs[bi][:].rearrange("p (h q) -> p h q", q=128),
                                scalar=vmask[:, vc:vc + 1], in1=bc(msk[:].unsqueeze(1), [128, 4, 128]), op0=ALU.mult, op1=ALU.mult),
                                ['pTs%d' % bi, mk, 'vmask'], ['pTs%d' % bi])
                        elif msk is not None:
                            tt('dve', pTs[bi][:].rearrange("p (h q) -> p h q", q=128), pTs[bi][:].rearrange("p (h q) -> p h q", q=128),
                               bc(msk[:].unsqueeze(1), [128, 4, 128]), ALU.mult, ['pTs%d' % bi, mk], ['pTs%d' % bi])
                    pv, pvk = bank()
                    pd, pdk = bank()
                    nb = len(blocks)
                    for bi, (bt_, a_, msk, mk, vc) in enumerate(blocks):
                        if bt_ == 'l':
                            e0 = a_
                            vv = vtok[:, e0 // 128, kvh * 128:(kvh + 1) * 128]
                            vkey = 'vtok'
                        else:
                            vv = cvb[:, a_, kvh * 128:(kvh + 1) * 128]
                            vkey = 'cvb'
                        mm(pv[:, 0:512], vv, pTs[bi][:], bi == 0, bi == nb - 1, [vkey, 'pTs%d' % bi], [pvk])
                        mm(pd[:, 0:512], onesb[:], pTs[bi][:], bi == 0, bi == nb - 1, ['onesb', 'pTs%d' % bi], [pdk])
                    attn_finish(pv, pvk, pd, pdk, kvh * 4, 4, 128, qb * 128)
        s5_main(l, idx if kind == 's' else 0, kind == 'p', idx if kind == 'p' else None, sel=own)
        for fg in range(8):
            sl = (0, 1, 2, 1) if fg % 2 == 0 else (2, 0, 1, 0)
            wq, wqk = wload('pp', l, fg, slot=sl[0])
            for br, (yT_, yk, Kp, koff) in enumerate(((yaT, 'zaT', 4, 0), (ybT, 'qT', 8, 4), (ycT, 'zcT', 4, 12))):
                c0 = 5120 + br * 2048 + fg * 256
                wa, wak = wload('in', l, c0 // 256, slot=sl[1 + br])
                for j in range(2):
                    pg, pgk = bank()
                    pq, pqk = bank()
                    for k in range(KT):
                        mm(pg[:, 0:NT], wa[:, k, j * 128:(j + 1) * 128], hT[:, k, ctr], k == 0, k == KT - 1, [wak, 'hT'], [pgk])
                    for k in range(Kp):
                        mm(pq[:, 0:NT], wq[:, koff + k, j * 128:(j + 1) * 128], yT_[:, k, :], k == 0, k == Kp - 1, [wqk, yk], [pqk])
                    act(tf[2][:, 0:NT], pg[:, 0:NT], AF.Sigmoid, [pgk], ['tf2'])
                    f = fg * 2 + j
                    if br == 0:
                        tt('dve', merged[:, f, :], tf[2][:, 0:NT], pq[:, 0:NT], ALU.mult, ['tf2', pqk], ['merged'])
                    else:
                        tt('dve', tf[2][:, 0:NT], tf[2][:, 0:NT], pq[:, 0:NT], ALU.mult, ['tf2', pqk], ['tf2'])
                        tt('dve', merged[:, f, :], merged[:, f, :], tf[2][:, 0:NT], ALU.add, ['merged', 'tf2'], ['merged'])
        gate_b = V[:].rearrange("p g n -> p (g n)")
        lng_b = Vs[:].rearrange("p g n -> p (g n)")
        lnb_b = tfall[:].rearrange("p a c -> p (a c)")
        tfk = ['tf0', 'tf1', 'tf2', 'tf3']
        sc.dma('pool', gate_b, gsc[cond].partition_broadcast(128), r=['gsc'], w=['V'])
        sc.dma('pool', lng_b, ln_g[l].partition_broadcast(128), w=['Vs'])
        sc.dma('pool', lnb_b, ln_b[l].partition_broadcast(128), w=tfk)
        for s_ in range(2):
            for fb in range(8):
                wt, wk = wload('out', l, fb)
                pb_, pbk = bank()
                for k in range(KT):
                    mm(pb_[:, 0:256], merged[:, k, s_ * 128:(s_ + 1) * 128], wt[:, k, :], k == 0, k == KT - 1, [wk, 'merged'], [pbk])
                tt('dve', rr[:, fb * 256:(fb + 1) * 256], pb_[:, 0:256], gate_b[:, fb * 256:(fb + 1) * 256], ALU.mult, [pbk, 'V'], ['rr'])
            xk = 'xin0'
            rk = 'rr'
            if own:
                sc.dma('pool', xin[0][:], src, r=['ridx'] + rkeys, w=[xk], ind=ridx[:, 2 * idx + 1 + s_:2 * idx + 2 + s_])
            else:
                sc.dma('sp', xin[0][:], src[t0 + s_ * 128:t0 + (s_ + 1) * 128, :], r=rkeys, w=[xk])
            sc.op('dve', lambda e: e.scalar_tensor_tensor(out=rr[:], in0=xin[0][:], scalar=ALPHA, in1=rr[:],
                                                          op0=ALU.mult, op1=ALU.add), [xk, rk], [rk])
            for q in range(4):
                sc.op('dve', lambda e, q=q: e.bn_stats(out=stats[:, q, :], in_=rr[:, q * 512:(q + 1) * 512]), [rk], ['stats'])
            sc.op('dve', lambda e: e.bn_aggr(out=mv[:], in_=stats[:].rearrange("p a b -> p (a b)")), ['stats'], ['mv'])
            act(rstd[:], mv[:, 1:2], AF.Sqrt, ['mv'], ['rstd'], bias=EPS)
            recip(rstd[:], rstd[:], ['rstd'], ['rstd'])
            ts('dve', rr[:], rr[:], mv[:, 0:1], ALU.subtract, [rk, 'mv', 'rstd'], [rk], s2=rstd[:, 0:1], op1=ALU.mult)
            tt('dve', rr[:], rr[:], lng_b, ALU.mult, [rk, 'Vs'], [rk])
            tt('dve', rr[:], rr[:], lnb_b, ALU.add, [rk] + tfk, [rk])
            sc.dma('pool', dst[t0 + s_ * 128:t0 + (s_ + 1) * 128, :], rr[:], r=[rk], w=[dkey])

    def chain(pin_, pout, G, Ls, Lw, keys):
        for w_ in range(2):
            cx = A2ix if w_ == 0 else nA2ix
            tt('dve', tA[:, 0:32], Pst[:, pin_, w_, G], A2r[:, G], ALU.mult, ['Pst', 'A2'], ['tA'])
            tt('dve', tB[:, 0:32], Pst[:, pin_, 1 - w_, G], cx[:, G], ALU.mult, ['Pst', 'A2x'], ['tB'])
            tt('dve', tA[:, 0:32], tA[:, 0:32], tB[:, 0:32], ALU.add, ['tA', 'tB'], ['tA'])
            tt('dve', Pst[:, pout, w_, G], tA[:, 0:32], Ls if w_ == 0 else Lw, ALU.add, ['tA'] + keys, ['Pst'])

    def sample_prepass(l):
        src = xs if l == 0 else zs
        cp('dve', Pst[:, 0, 0, 0:32], sinit[:, 0, :], ['sinit'], ['Pst'])
        cp('dve', Pst[:, 0, 1, 0:32], sinitw[:, 0, :], ['sinitw'], ['Pst'])
        cp('dve', Pst[:, 16, 0, 32:64], sinit[:, 1, :], ['sinit'], ['Pst'])
        cp('dve', Pst[:, 16, 1, 32:64], sinitw[:, 1, :], ['sinitw'], ['Pst'])
        import os as _os
        for t in range(int(_os.environ.get('NPRE', '16'))):
            ln_ht(src, [t * NT, t * NT + 128], 0, 1, [] if l == 0 else ['dst_s_0'])
            proj_xa(l, 0)
            s5_states(0, True, rec=False)
            TRv = rr[:].rearrange("p (g n) -> p g n", n=32)
            TIv = tfall[:].rearrange("p a c -> p (a c)").rearrange("p (g n) -> p g n", n=32)
            tkeys = ['rr', 'tf0', 'tf1', 'tf2', 'tf3']
            tmp = xin[0][:].rearrange("p (g n) -> p g n", n=32)
            AX = mybir.AxisListType.X
            for q_, (ta_, va_, vk_) in enumerate(((TRv, V, 'V'), (TIv, Vs, 'Vs'), (TRv, Vs, 'Vs'), (TIv, V, 'V'))):
                tt('dve', tmp, ta_, va_[:], ALU.mult, tkeys + [vk_], ['xin0'])
                sc.op('dve', lambda e, q_=q_: e.tensor_reduce(out=Ssum[:, q_, :], in_=tmp, axis=AX, op=ALU.add), ['xin0'], ['Ssum'])
            tt('dve', Ssum[:, 0, :], Ssum[:, 0, :], Ssum[:, 1, :], ALU.add, ['Ssum'], ['Ssum'])
            tt('dve', Ssum[:, 2, :], Ssum[:, 2, :], Ssum[:, 3, :], ALU.subtract, ['Ssum'], ['Ssum'])
            chain(t, t + 1, slice(0, 32), Ssum[:, 0, 0:32], Ssum[:, 2, 0:32], ['Ssum'])
            cp('dve', Lst[:, t, 0, :], Ssum[:, 0, 32:64], ['Ssum'], ['Lst'])
            cp('dve', Lst[:, t, 1, :], Ssum[:, 2, 32:64], ['Ssum'], ['Lst'])
        for t in range(15, -1, -1):
            chain(t + 1, t, slice(32, 64), Lst[:, t, 0, :], Lst[:, t, 1, :], ['Lst'])

    if stop is None:
        for l in range(2):
            layer_prep(l)
            wconvert(l)
            sample_prepass(l)
            for i in range(NPS):
                tile_main(l, 'p', i)
            if l == 0:
                for t in range(16):
                    tile_main(l, 's', t)
            else:
                for t in range(4):
                    tile_main(l, 'o', t)
    else:
        layer_prep(0)
        wconvert(0)
        if stop == 'ptile':
            tile_main(0, 'p', 0)
        if stop == 'pre':
            sample_prepass(0)
        if stop == 'own':
            sample_prepass(0)
            tile_main(0, 'o', 0)
            tile_main(0, 'o', 3)
        if stop == 'stile':
            sample_prepass(0)
            tile_main(0, 's', 0)
            if 'one' not in _os.environ.get('SKIP', ''):
                tile_main(0, 's', 1)
        loc = dict(locals())
        for nm in dumps:
            if nm in ('s5w', 'gsc', 'zp', 'zs'):
                src_ap = loc[nm]
                key = nm if nm in ('s5w', 'gsc') else ('dst_p_0' if nm == 'zp' else 'dst_s_0')
                o = nc.dram_tensor("dbg_" + nm, list(src_ap.shape), src_ap.dtype, kind="ExternalOutput").ap()
                sc.dma('sp', o, src_ap, r=[key], w=['dbg_' + nm])
            else:
                t_ = loc[nm]
                o = nc.dram_tensor("dbg_" + nm, list(t_.shape), t_.dtype, kind="ExternalOutput").ap()
                sc.dma('sp', o, t_[:], r=[nm, 'PW', 'C', 'Bb', 'A2', 'A2x', 'V', 'Vs'], w=['dbg_' + nm])
    counts = sc.emit(es)
    es.close()
    return nc, counts


_CACHE = {}


def kernel(x_prompt, x_sample, cache_k, cache_v, state_ssm, c, c_ctx,
           w_ada, b_ada, w_in, ssm_lam_re, ssm_lam_im, ssm_log_step,
           ssm_b_re, ssm_b_im, ssm_c_re, ssm_c_im, ssm_d, w_glu, b_glu,
           attn_sink, sgu_ln_g, sgu_ln_b, w_spatial, b_spatial,
           w_proj_a, w_proj_b, w_proj_c, w_out, ln_g, ln_b):
    f = lambda a: np.ascontiguousarray(np.asarray(a, dtype=np.float32))
    if 'nc' not in _CACHE:
        _CACHE['nc'] = build()[0]
    nc = _CACHE['nc']
    consts = _host_consts()
    shared = dict(w_ada=f(w_ada), b_ada=f(b_ada), w_in=f(w_in), lam_re=f(ssm_lam_re), lam_im=f(ssm_lam_im),
                  log_step=f(ssm_log_step), b_re=f(ssm_b_re), b_im=f(ssm_b_im), c_re=f(ssm_c_re), c_im=f(ssm_c_im),
                  ssm_d=f(ssm_d), w_glu=f(w_glu), b_glu=f(b_glu), sink=f(attn_sink), sgu_g=f(sgu_ln_g), sgu_b=f(sgu_ln_b),
                  w_s=f(w_spatial), b_s=f(np.asarray(b_spatial).reshape(2, 512)),
                  w_pa=f(w_proj_a), w_pb=f(w_proj_b), w_pc=f(w_proj_c), w_out=f(w_out), ln_g=f(ln_g), ln_b=f(ln_b))
    shared.update(consts)
    x_prompt = np.asarray(x_prompt); x_sample = np.asarray(x_sample)
    cache_k = np.asarray(cache_k); cache_v = np.asarray(cache_v); state_ssm = np.asarray(state_ssm)
    c = np.asarray(c); c_ctx = np.asarray(c_ctx)
    in_maps = []
    for core in range(8):
        b = core // 4
        m = dict(shared)
        m['xp'] = f(x_prompt[core * NPS:(core + 1) * NPS].reshape(NPS * LP, D))
        m['xs'] = f(x_sample[b])
        m['ck'] = f(cache_k[b].reshape(2, 256, 256))
        m['cv'] = f(cache_v[b].reshape(2, 256, 256))
        m['st0'] = f(state_ssm[b])
        m['cvec'] = f(np.stack([c_ctx, c[b]], axis=0))
        m.update(_core_consts(core, consts))
        in_maps.append(m)
    res = run_bass_kernel_spmd(nc, in_maps, core_ids=list(range(8)))
    R = res.results
    y_prompt = np.concatenate([R[i]['yp'].reshape(NPS, LP, D) for i in range(8)], axis=0).astype(np.float32)
    y_sample = np.stack([np.concatenate([R[b_ * 4 + j_]['ys_own'] for j_ in range(4)], axis=0) for b_ in range(2)], axis=0).astype(np.float32)
    nk_ = np.concatenate([R[i]['nk'].reshape(NPS, 2, LP, 2, 128) for i in range(8)], axis=0).astype(np.float32)
    nv_ = np.concatenate([R[i]['nv'].reshape(NPS, 2, LP, 2, 128) for i in range(8)], axis=0).astype(np.float32)
    ns_ = np.concatenate([R[i]['nst'] for i in range(8)], axis=0).astype(np.float32)
    return (y_prompt, y_sample, nk_, nv_, ns_)
```

```python
import contextlib
import math
import numpy as np
import concourse.bass as bass
import concourse.mybir as mybir
from concourse.bass_utils import run_bass_kernel_spmd

F32 = mybir.dt.float32
BF = mybir.dt.bfloat16
AF = mybir.ActivationFunctionType
ALU = mybir.AluOpType

D = 2048
KT = 16
NT = 256
DIN = 11264
LP = 256
LS = 4096
NPS = 4
DEPTH = 2
ALPHA = (2 * DEPTH) ** 0.25
EPS = 1e-5
GC1 = 1.5957691216057308
GC2 = 0.044715
SAME_ENG_SYNC = True
SAME_ENG_DIST = 8


class Sched:
    NS = 8

    def __init__(self, nc):
        self.nc = nc
        self.ops = []

    def op(self, eng, fn, r=(), w=()):
        w = tuple(w) + tuple(k for k in r if k.startswith('ps') and k not in w)
        self.ops.append((eng, fn, tuple(r), tuple(w), False))

    def dma(self, q, out, in_, r=(), w=(), slow=False, ind=None):
        self.ops.append((q, (out, in_, slow, ind), tuple(r), tuple(w), True))

    def emit(self, es):
        nc = self.nc
        ops = self.ops
        n = len(ops)
        last_w = {}
        readers = {}
        deps = [None] * n
        for i, (eng, fn, r, w, isd) in enumerate(ops):
            d = set()
            for k in r:
                if k in last_w:
                    d.add(last_w[k])
            for k in w:
                if k in last_w:
                    d.add(last_w[k])
                for j in readers.get(k, ()):
                    d.add(j)
            d.discard(i)
            deps[i] = d
            for k in w:
                last_w[k] = i
                readers[k] = []
            for k in r:
                if k not in w:
                    readers.setdefault(k, []).append(i)
        need = [False] * n
        lidx = [0] * n
        lc = {}
        for i in range(n):
            lidx[i] = lc.get(ops[i][0], 0)
            lc[ops[i][0]] = lidx[i] + 1

        def same_eng_skip(i, j):
            if ops[i][0] == 'pe' or not SAME_ENG_SYNC:
                return True
            return (lidx[i] - lidx[j]) > SAME_ENG_DIST

        for i in range(n):
            ei = ops[i][0]
            for j in deps[i]:
                ej, _, _, _, dj = ops[j]
                if dj or ej != ei or not same_eng_skip(i, j):
                    need[j] = True
        engs = ['pe', 'act', 'dve', 'pool', 'sp']
        esem = {e: es.enter_context(nc.semaphore('es_' + e)) for e in engs}
        dsem = {q: [es.enter_context(nc.semaphore('ds_%s%d' % (q, k))) for k in range(self.NS)]
                for q in ('sp', 'pool', 'act')}
        cnt = {e: 0 for e in engs}
        dcnt = {q: 0 for q in dsem}
        sig = [None] * n
        streams = {e: [] for e in engs}
        waited = {e: {} for e in engs}

        def addwait(e, lst, sem, val):
            key = id(sem)
            if waited[e].get(key, 0) >= val:
                return
            waited[e][key] = val
            lst.append(('w', sem, val))

        for i, (eng, fn, r, w, isd) in enumerate(ops):
            lst = streams[eng]
            wmax = {}
            for j in deps[i]:
                if sig[j] is None:
                    continue
                ej, dj = ops[j][0], ops[j][4]
                if (not dj) and ej == eng and same_eng_skip(i, j):
                    continue
                key = id(sig[j][0])
                if key not in wmax or wmax[key][1] < sig[j][1]:
                    wmax[key] = sig[j]
            for key in sorted(wmax, key=lambda k_: wmax[k_][1]):
                addwait(eng, lst, wmax[key][0], wmax[key][1])
            if isd:
                k = dcnt[eng]
                dcnt[eng] += 1
                sem = dsem[eng][k % self.NS]
                rnd = k // self.NS
                if rnd > 0:
                    addwait(eng, lst, sem, 16 * rnd)
                sig[i] = (sem, 16 * (rnd + 1))
                lst.append(('d', fn, sem))
            else:
                if need[i]:
                    cnt[eng] += 1
                    sig[i] = (esem[eng], cnt[eng])
                    lst.append(('o', fn, esem[eng]))
                else:
                    lst.append(('o', fn, None))
        for q in dsem:
            for k in range(self.NS):
                tot = (dcnt[q] - k + self.NS - 1) // self.NS if dcnt[q] > k else 0
                if tot > 0:
                    streams[q].append(('w', dsem[q][k], 16 * tot))

        def run(engine, lst):
            for it in lst:
                if it[0] == 'w':
                    engine.wait_ge(it[1], it[2])
                elif it[0] == 'd':
                    out, in_, slow, ind = it[1]
                    if ind is not None:
                        engine.indirect_dma_start(out=out, out_offset=None, in_=in_,
                                                  in_offset=bass.IndirectOffsetOnAxis(ap=ind, axis=0)).then_inc(it[2], 16)
                    elif slow:
                        engine.dma_start(out=out, in_=in_, allow_slow_non_contiguous=True).then_inc(it[2], 16)
                    else:
                        engine.dma_start(out=out, in_=in_).then_inc(it[2], 16)
                else:
                    ins = it[1](engine)
                    if it[2] is not None:
                        ins.then_inc(it[2], 1)

        block = es.enter_context(nc.Block())

        @block.tensor
        def _(e):
            run(e, streams['pe'])

        @block.scalar
        def _(e):
            run(e, streams['act'])

        @block.vector
        def _(e):
            run(e, streams['dve'])

        @block.gpsimd
        def _(e):
            run(e, streams['pool'])

        @block.sync
        def _(e):
            run(e, streams['sp'])
        return {e: (len(streams[e]), cnt[e]) for e in engs}


def _core_consts(core, consts):
    j = core % 4
    m = {}
    p = np.arange(128)[:, None]
    cidx = np.arange(10)[None, :]
    m['ridx'] = np.clip(1024 * j + 128 * (cidx - 1) + p, 0, LS - 1).astype(np.int32)
    q = np.clip(1024 * j - 128 + np.arange(1280), 0, LS - 1)
    m['ropec_o'] = np.ascontiguousarray(consts['ropec'][:, q])
    m['ropes_o'] = np.ascontiguousarray(consts['ropes'][:, q])
    oh = np.zeros((128, 4), np.float32); oh[:, j] = 1.0
    m['oh4'] = oh
    vm = np.ones((128, 2), np.float32)
    if j == 0:
        vm[:, 0] = 0.0
    if j == 3:
        vm[:, 1] = 0.0
    m['vmask'] = vm
    return m


def _host_consts():
    c = {}
    c['ident'] = np.eye(128, dtype=np.float32)
    R = np.zeros((128, 128), np.float32)
    for d in range(128):
        if d % 64 < 32:
            R[d, d + 32] = -1.0
        else:
            R[d, d - 32] = 1.0
    c['rotT'] = np.ascontiguousarray(R.T)
    pos = np.arange(LS)
    row = pos // 64
    col = pos % 64
    inv = 10000.0 ** (-np.arange(0, 64, 2, dtype=np.float32) / 64.0)
    ang = np.zeros((128, LS), np.float32)
    for d in range(128):
        p = row if d < 64 else col
        ang[d] = p.astype(np.float32) * inv[d % 32]
    c['ropec'] = np.cos(ang).astype(np.float32)
    c['ropes'] = np.sin(ang).astype(np.float32)
    kk = np.arange(128)[:, None]
    qq = np.arange(128)[None, :]
    c['mlo'] = (kk >= qq).astype(np.float32)
    c['mhi'] = (kk <= qq).astype(np.float32)
    tp = (np.arange(128) // 16)[:, None]
    tt = (np.arange(128) // 16)[None, :]
    c['cmf'] = (tt >= tp).astype(np.float32)
    c['cmb'] = (tp >= tt).astype(np.float32)
    return c


def build(stop=None, dumps=()):
    nc = bass.Bass("TRN2", target_bir_lowering=False)
    es = contextlib.ExitStack()
    sc = Sched(nc)
    PI = math.pi

    def din(name, shape, dt=F32):
        return nc.dram_tensor(name, list(shape), dt, kind="ExternalInput").ap()

    def dout(name, shape):
        return nc.dram_tensor(name, list(shape), F32, kind="ExternalOutput").ap()

    def dscr(name, shape, dt=F32):
        return nc.dram_tensor(name, list(shape), dt, kind="Internal").ap()

    xp = din("xp", [NPS * LP, D]); xs = din("xs", [LS, D])
    ck = din("ck", [2, 256, 256]); cv = din("cv", [2, 256, 256])
    st0 = din("st0", [2, 2, 2, 32, 64]); cvec = din("cvec", [2, D])
    w_ada = din("w_ada", [2, D, 3 * D]); b_ada = din("b_ada", [2, 3 * D]); w_in = din("w_in", [2, D, DIN])
    lam_re = din("lam_re", [2, 2, 32, 64]); lam_im = din("lam_im", [2, 2, 32, 64]); log_step = din("log_step", [2, 2, 32])
    b_re = din("b_re", [2, 2, 32, 64, 16]); b_im = din("b_im", [2, 2, 32, 64, 16])
    c_re = din("c_re", [2, 2, 32, 16, 64]); c_im = din("c_im", [2, 2, 32, 16, 64])
    ssm_d = din("ssm_d", [2, 512]); w_glu = din("w_glu", [2, 512, 512]); b_glu = din("b_glu", [2, 512])
    sink = din("sink", [2, 8]); sgu_g = din("sgu_g", [2, 512]); sgu_b = din("sgu_b", [2, 512])
    w_s = din("w_s", [2, 4, 128, 128]); b_s = din("b_s", [2, 512])
    w_pa = din("w_pa", [2, 512, D]); w_pb = din("w_pb", [2, 1024, D]); w_pc = din("w_pc", [2, 512, D])
    w_out = din("w_out", [2, D, D]); ln_g = din("ln_g", [2, D]); ln_b = din("ln_b", [2, D])
    c_ident = din("ident", [128, 128]); c_rotT = din("rotT", [128, 128])
    c_ropec = din("ropec", [128, LS]); c_ropes = din("ropes", [128, LS])
    c_mlo = din("mlo", [128, 128]); c_mhi = din("mhi", [128, 128])
    c_cmf = din("cmf", [128, 128]); c_cmb = din("cmb", [128, 128])
    ridx_d = din("ridx", [128, 10], mybir.dt.int32); c_ropec_o = din("ropec_o", [128, 1280]); c_ropes_o = din("ropes_o", [128, 1280])
    oh4_d = din("oh4", [128, 4]); vmask_d = din("vmask", [128, 2])
    yp = dout("yp", [NPS * LP, D]); ys = dout("ys_own", [1024, D])
    nk = dout("nk", [NPS, 2, LP, 256]); nv = dout("nv", [NPS, 2, LP, 256]); nst = dout("nst", [NPS, 2, 2, 2, 32, 64])
    zp = dscr("zp", [NPS * LP, D]); zs = dscr("zs", [LS, D]); gsc = dscr("gsc", [2, D])
    s5w = dscr("s5w", [64, 3, 128, 128], BF)
    wsc = dscr("wsc", [2, 76, 128, 16, 256], BF)
    wsc2 = dscr("wsc2", [2, 48, 128, 24, 128], BF)

    def sb(name, shape, dt=F32):
        return es.enter_context(nc.sbuf_tensor("s_" + name, list(shape), dt))

    ps = [es.enter_context(nc.psum_tensor("ps%d" % i, [128, 512], F32)) for i in range(8)]
    psn = [0]

    def bank():
        i = psn[0] % 8
        psn[0] += 1
        return ps[i], 'ps%d' % i

    xin = [sb("xin0", [128, D]), sb("xin1", [128, D])]
    rr = sb("rr", [128, D])
    stats = sb("stats", [128, 4, 6]); mv = sb("mv", [128, 2]); rstd = sb("rstd", [128, 1])
    hT = sb("hT", [128, KT, 512], BF)
    wb = [sb("wb%d" % i, [128, KT, 256], BF) for i in range(3)]
    zaT = sb("zaT", [128, 4, NT], BF); qT = sb("qT", [128, 8, NT], BF); yaT = zaT; ybT = qT; kT = sb("kT", [128, 2, 512], BF)
    zbT = sb("zbT", [128, 8, NT], BF); uT = sb("uT", [128, 4, NT], BF); zcT = sb("zcT", [128, 4, NT], BF); ycT = zcT
    Xp = sb("Xp", [32, 32, 8, 16], BF); Xpp = sb("Xpp", [128, 32, 32], BF)
    vtok = sb("vtok", [128, 4, 256], BF); vsg = rr[:, 0:1024].rearrange("p (s c) -> p s c", c=512)
    vsln = sb("vsln", [128, 2, 512], BF)
    V = sb("V", [128, 64, 32]); Vs = sb("Vs", [128, 64, 32]); Sb = sb("Sb", [128, 64, 32], BF)
    tA = sb("tA", [128, 64]); tB = sb("tB", [128, 64])
    yg = sb("yg", [32, 8, 32, 16], BF); ygf = sb("ygf", [32, 4, 8, 16]); ygf2 = sb("ygf2", [32, 4, 8, 16])
    ygT = sb("ygT", [128, 4, NT], BF)
    s5wb = [sb("s5wb%d" % i, [128, 8, 3, 128], BF) for i in range(2)]
    merged = sb("merged", [128, KT, NT], BF)
    tfall = sb("tfall", [128, 4, 512])
    tf = [tfall[:, i, :] for i in range(4)]
    pTs = [sb("pTs%d" % i, [128, 512], BF) for i in range(5)]
    Fg = V[:, 0:32, :].rearrange("p (g a) n -> p g (a n)", a=4)
    Gg = V[:, 32:64, :].rearrange("p (g a) n -> p g (a n)", a=4)
    Eg = Vs[:, 0:32, :].rearrange("p (g a) n -> p g (a n)", a=4)
    w3 = s5wb[0]
    wsn = tf[0][:, :].rearrange("p (g q) -> p g q", q=128)
    ckn = tf[1][:, :].rearrange("p (b c) -> p b c", c=256)
    rc = sb("rc", [128, 512]); rs = sb("rs", [128, 512]); qraw = pTs[4]
    ident = sb("ident", [128, 128]); identb = sb("identb", [128, 128], BF); rotT = sb("rotT", [128, 128], BF)
    onesb = sb("onesb", [128, 128], BF); mlo = sb("mlo", [128, 128], BF); mhi = sb("mhi", [128, 128], BF)
    cmf = sb("cmf", [128, 128]); cmb = sb("cmb", [128, 128])
    scT = sb("scT", [128, 2, KT], BF); cvT = sb("cvT", [128, 2, KT])
    modT = sb("modT", [128, 48, 2]); badaT = sb("badaT", [128, 48]); sc1 = sb("sc1", [128, KT, 2])
    sgb = sb("sgb", [128, 512]); sbb = sb("sbb", [128, 512]); bsb = sb("bsb", [128, 512])
    wsT = sb("wsT", [128, 4, 128], BF)
    esink = sb("esink", [128, 8]); ckT = sb("ckT", [128, 2, 256], BF)
    cvb = sb("cvb", [128, 2, 256], BF)
    wglu = sb("wglu", [128, 4, 512], BF); bglu = sb("bglu", [128, 4]); dsk = sb("dsk", [32, 512])
    lr = sb("lr", [128, 32]); li = sb("li", [128, 32]); dtt = sb("dtt", [128, 32])
    p1 = sb("p1", [128, 32]); p2 = sb("p2", [128, 32]); p3 = sb("p3", [128, 32]); p4 = sb("p4", [128, 32])
    cosv = sb("cosv", [128, 32]); sinv = sb("sinv", [128, 32])
    fre = sb("fre", [128, 32]); fim = sb("fim", [128, 32])
    PWr = xin[1][:, 0:544].rearrange("p (k g) -> p k g", g=32); PWi = xin[1][:, 544:1088].rearrange("p (k g) -> p k g", g=32)
    nPWi = xin[1][:, 1088:1632].rearrange("p (k g) -> p k g", g=32)
    Bre = sb("Bre", [128, 8, 16]); Bim = sb("Bim", [128, 8, 16]); Bbr = sb("Bbr", [128, 8, 16]); Bbi = sb("Bbi", [128, 8, 16])
    bt1 = sb("bt1", [128, 8, 16]); bt2 = sb("bt2", [128, 8, 16])
    cnat = sb("cnat", [128, 2, 64]); Cre = sb("Cre", [128, 8, 16]); nCim = sb("nCim", [128, 8, 16])
    Ar = sb("Ar", [128, 64]); Aix = sb("Aix", [128, 64]); nAix = sb("nAix", [128, 64]); sgn = sb("sgn", [128, 1])
    A2r = sb("A2r", [128, 64]); A2i = sb("A2i", [128, 64]); A2ix = sb("A2ix", [128, 64]); nA2ix = sb("nA2ix", [128, 64])
    sinit = sb("sinit", [128, 2, 32]); sinitw = sb("sinitw", [128, 2, 32])
    Pst = sb("Pst", [128, 17, 2, 64])
    Lst = sb("Lst", [128, 16, 2, 32])
    stg = sb("stg", [32, 128])
    Ssum = sb("Ssum", [128, 4, 64])
    ridx = sb("ridx", [128, 10], mybir.dt.int32); oh4 = sb("oh4", [128, 4]); vmask = sb("vmask", [128, 2])
    Psel = sb("Psel", [128, 2, 2, 32])

    def tt(eng, out, a, b, op, r, w):
        sc.op(eng, lambda e: e.tensor_tensor(out=out, in0=a, in1=b, op=op), r, w)

    def ts(eng, out, a, s1, op0, r, w, s2=None, op1=None):
        if op1 is None:
            sc.op(eng, lambda e: e.tensor_scalar(out=out, in0=a, scalar1=s1, scalar2=None, op0=op0), r, w)
        else:
            sc.op(eng, lambda e: e.tensor_scalar(out=out, in0=a, scalar1=s1, scalar2=s2, op0=op0, op1=op1), r, w)

    def act(out, in_, func, r, w, bias=None, scale=None):
        kw = {}
        if bias is not None:
            kw['bias'] = bias
        if scale is not None:
            kw['scale'] = scale
        sc.op('act', lambda e: e.activation(out=out, in_=in_, func=func, **kw), r, w)

    def mm(out, lhsT, rhs, start, stop, r, w):
        sc.op('pe', lambda e: e.matmul(out, lhsT, rhs, start=start, stop=stop), r, w)

    def tr(out, in_, idn, r, w):
        sc.op('pe', lambda e: e.transpose(out, in_, idn), r, w)

    def cp(eng, out, in_, r, w):
        if eng == 'act':
            sc.op(eng, lambda e: e.copy(out=out, in_=in_), r, w)
        else:
            sc.op(eng, lambda e: e.tensor_copy(out=out, in_=in_), r, w)

    def recip(out, in_, r, w):
        sc.op('dve', lambda e: e.reciprocal(out=out, in_=in_), r, w)

    def mset(eng, ap, val, w):
        sc.op(eng, lambda e: e.memset(ap, val), (), w)

    def bc(ap, shape):
        return ap.broadcast_to(list(shape))

    sc.dma('sp', ident[:], c_ident, w=['ident'])
    cp('dve', identb[:], ident[:], ['ident'], ['identb'])
    sc.dma('sp', tf[0][:, 0:128], c_rotT, w=['tf0'])
    cp('dve', rotT[:], tf[0][:, 0:128], ['tf0'], ['rotT'])
    sc.dma('sp', tf[1][:, 0:128], c_mlo, w=['tf1'])
    cp('dve', mlo[:], tf[1][:, 0:128], ['tf1'], ['mlo'])
    sc.dma('sp', tf[2][:, 0:128], c_mhi, w=['tf2'])
    cp('dve', mhi[:], tf[2][:, 0:128], ['tf2'], ['mhi'])
    sc.dma('sp', cmf[:], c_cmf, w=['cmf'])
    sc.dma('sp', cmb[:], c_cmb, w=['cmb'])
    sc.dma('sp', ridx[:], ridx_d, w=['ridx'])
    sc.dma('sp', oh4[:], oh4_d, w=['oh4'])
    sc.dma('sp', vmask[:], vmask_d, w=['vmask'])
    mset('dve', onesb[:], 1.0, ['onesb'])
    mset('dve', sgn[0:64, :], -1.0, ['sgn'])
    mset('dve', sgn[64:128, :], 1.0, ['sgn'])
    sc.dma('sp', cvT[:], cvec.rearrange("c (k p) -> p c k", p=128), w=['cvT'], slow=True)
    act(scT[:], cvT[:], AF.Silu, ['cvT'], ['scT'])

    wslot = [0]

    WNAMES = {'in': (w_in, 16, 0), 'pp': (None, 16, 44), 'out': (w_out, 16, 68)}
    WPACK = (('pa', w_pa, 4, 0), ('pb', w_pb, 8, 4), ('pc', w_pc, 4, 12))

    def wload_cast(src):
        i = wslot[0] % 3
        wslot[0] += 1
        K = src.shape[0] // 128
        sc.dma('pool', wb[i][:, 0:K, :], src.rearrange("(k p) c -> p k c", p=128), w=['wb%d' % i])
        return wb[i], 'wb%d' % i

    def wconvert(l):
        for nm, (wt_, K, g0) in WNAMES.items():
            if wt_ is None:
                continue
            ng = wt_.shape[2] // 256
            if nm == 'in':
                ng = 20
            for gi in range(ng):
                t_, k_ = wload_cast(wt_[l][:, gi * 256:(gi + 1) * 256])
                sc.dma('sp', wsc[l, g0 + gi, :, 0:K, :], t_[:, 0:K, :], r=[k_], w=['wsc%d_%d' % (l, g0 + gi)])
        for fg in range(8):
            for br, (nm, wt_, K, koff) in enumerate(WPACK):
                c0 = 5120 + br * 2048 + fg * 256
                ta_, ka_ = wload_cast(w_in[l][:, c0:c0 + 256])
                tp_, kp_ = wload_cast(wt_[l][:, fg * 256:(fg + 1) * 256])
                for j in range(2):
                    g2 = (fg * 3 + br) * 2 + j
                    sc.dma('sp', wsc2[l, g2, :, 0:16, :], ta_[:, 0:16, j * 128:(j + 1) * 128], r=[ka_], w=['wsc2_%d_%d' % (l, g2)])
                    sc.dma('sp', wsc2[l, g2, :, 16:16 + K, :], tp_[:, 0:K, j * 128:(j + 1) * 128], r=[kp_], w=['wsc2_%d_%d' % (l, g2)])

    def wload2(l, g2, K):
        i = wslot[0] % 3
        wslot[0] += 1
        v = wb[i][:].rearrange("p k c -> p (k c)")[:, 0:24 * 128].rearrange("p (k c) -> p k c", c=128)
        sc.dma('sp', v[:, 0:16 + K, :], wsc2[l, g2, :, 0:16 + K, :], r=['wsc2_%d_%d' % (l, g2)], w=['wb%d' % i])
        return v, 'wb%d' % i

    def wload(nm, l, gi, slot=None):
        wt_, K, g0 = WNAMES[nm]
        if slot is None:
            i = wslot[0] % 3
            wslot[0] += 1
        else:
            i = slot
        sc.dma('sp', wb[i][:, 0:K, :], wsc[l, g0 + gi, :, 0:K, :], r=['wsc%d_%d' % (l, g0 + gi)], w=['wb%d' % i])
        return wb[i], 'wb%d' % i

    def gelu_evac(out, pin, pk, shape, tix, rextra=(), wk=()):
        n = 1
        for s_ in shape[1:]:
            n *= s_
        t1 = tf[tix][0:shape[0], 0:n]
        t2 = tf[tix + 1][0:shape[0], 0:n]
        k1, k2 = 'tf%d' % tix, 'tf%d' % (tix + 1)
        pin2 = pin
        act(t1, pin2, AF.Square, [pk], [k1])
        ts('dve', t1, t1, GC2, ALU.mult, [k1], [k1], s2=1.0, op1=ALU.add)
        tt('dve', t1, t1, pin2, ALU.mult, [k1, pk], [k1])
        act(t2, t1, AF.Sigmoid, [k1], [k2], scale=GC1)
        tt('dve', out, t2, pin2, ALU.mult, [k2, pk] + list(rextra), list(wk))

    def wrap(out, x, kx, ko):
        mset('dve', p4[:], 0.0, ['p4'])
        for m in range(1, 9):
            ts('dve', p3[:], x, (2 * m - 1) * PI, ALU.is_gt, [kx], ['p3'])
            tt('dve', p4[:], p4[:], p3[:], ALU.add, ['p3', 'p4'], ['p4'])
        ts('dve', p4[:], p4[:], -2.0 * PI, ALU.mult, ['p4'], ['p4'])
        tt('dve', out, x, p4[:], ALU.add, [kx, 'p4'], [ko])

    def cmul(ore, oim, are, aim, bre, bim, keys_r, ko):
        tt('dve', p1[:], are, bre, ALU.mult, keys_r, ['p1'])
        tt('dve', p2[:], aim, bim, ALU.mult, keys_r, ['p2'])
        tt('dve', p3[:], are, bim, ALU.mult, keys_r, ['p3'])
        tt('dve', p4[:], aim, bre, ALU.mult, keys_r, ['p4'])
        tt('dve', ore, p1[:], p2[:], ALU.subtract, ['p1', 'p2'], ko)
        tt('dve', oim, p3[:], p4[:], ALU.add, ['p3', 'p4'], ko)

    def mix(out, g0, ng, Wre_, Wim_, Xre, Xim, kr, ko):
        for h in (0, 1):
            P = slice(h * 64, h * 64 + 64)
            wr = bc(Wre_[P, g0:g0 + ng].unsqueeze(2), [64, ng, 16])
            wi = bc(Wim_[P, g0:g0 + ng].unsqueeze(2), [64, ng, 16])
            xa_ = Xre[P, 0:ng, :] if h == 0 else Xim[P, 0:ng, :]
            xb_ = Xim[P, 0:ng, :] if h == 0 else Xre[P, 0:ng, :]
            tt('dve', bt1[P, 0:ng, :], xa_, wr, ALU.mult, kr, ['bt1'])
            tt('dve', bt2[P, 0:ng, :], xb_, wi, ALU.mult, kr, ['bt2'])
            tt('dve', out[P], bt1[P, 0:ng, :], bt2[P, 0:ng, :], ALU.subtract if h == 0 else ALU.add,
               ['bt1', 'bt2'], ko)

    EF = [[t + 1 for t in range(8)], [8 - t for t in range(8)]]

    def layer_prep(l):
        sc.dma('act', badaT[:], b_ada[l].rearrange("(c p) -> p c", p=128), w=['badaT'], slow=True)
        pm, pmk = bank()
        for gi in range(24):
            wt, wk = wload_cast(w_ada[l][:, gi * 256:(gi + 1) * 256])
            for j in range(2):
                ch = gi * 2 + j
                for k in range(KT):
                    mm(pm[:, ch * 2:ch * 2 + 2], wt[:, k, j * 128:(j + 1) * 128], scT[:, :, k],
                       k == 0, k == KT - 1, [wk, 'scT'], [pmk])
        tt('dve', modT[:], pm[:, 0:96].rearrange("p (c t) -> p c t", t=2), bc(badaT[:].unsqueeze(2), [128, 48, 2]),
           ALU.add, [pmk, 'badaT'], ['modT'])
        ts('dve', sc1[:], modT[:, 16:32, :], 1.0, ALU.add, ['modT'], ['sc1'])
        for c_ in range(2):
            sc.dma('act', gsc[c_].rearrange("(k p) -> p k", p=128), modT[:, 32:48, c_], r=['modT'], w=['gsc'], slow=True)
        sc.dma('act', sgb[:], sgu_g[l].partition_broadcast(128), w=['sgb'])
        sc.dma('act', sbb[:], sgu_b[l].partition_broadcast(128), w=['sbb'])
        sc.dma('act', bsb[:], b_s[l].partition_broadcast(128), w=['bsb'])
        sc.dma('act', wsn[:], w_s[l].rearrange("g p q -> p g q"), w=['tf0'])
        pw, pwk = bank()
        for g in range(4):
            tr(pw[:, g * 128:(g + 1) * 128], wsn[:, g, :], ident[:], ['tf0', 'ident'], [pwk])
        cp('dve', wsT[:], pw[:, 0:512].rearrange("p (g q) -> p g q", q=128), [pwk], ['wsT'])
        sc.dma('act', esink[:], sink[l].partition_broadcast(128), w=['esink'])
        act(esink[:], esink[:], AF.Exp, ['esink'], ['esink'])
        sc.dma('act', ckn[:], ck[l].rearrange("(b p) c -> p b c", p=128), w=['tf1'])
        pc_, pck = bank()
        for kvh in range(2):
            for b_ in range(2):
                tr(pc_[:, (kvh * 2 + b_) * 128:(kvh * 2 + b_ + 1) * 128], ckn[:, b_, kvh * 128:(kvh + 1) * 128],
                   ident[:], ['tf1', 'ident'], [pck])
        cp('dve', ckT[:], pc_[:, 0:512].rearrange("p (h t) -> p h t", t=256), [pck], ['ckT'])
        sc.dma('pool', cvb[:], cv[l].rearrange("(b p) c -> p b c", p=128), w=['cvb'])
        sc.dma('pool', wglu[:], w_glu[l].rearrange("(j p) c -> p j c", p=128), w=['wglu'])
        sc.dma('act', bglu[:], b_glu[l].rearrange("(j p) -> p j", p=128), w=['bglu'], slow=True)
        sc.dma('act', dsk[:], ssm_d[l].partition_broadcast(32), w=['dsk'])
        for d in range(2):
            for h in range(2):
                P = slice(h * 64, h * 64 + 64)
                sc.dma('act', lr[P, :], lam_re[l, d].rearrange("g p -> p g"), w=['lr'], slow=True)
                sc.dma('act', li[P, :], lam_im[l, d].rearrange("g p -> p g"), w=['li'], slow=True)
                for ri in range(2):
                    sc.dma('act', sinit[ri * 64:(ri + 1) * 64, d, :] if h == 0 else sinitw[(1 - ri) * 64:(2 - ri) * 64, d, :],
                           st0[l, d, ri].rearrange("g p -> p g"), w=['sinit' if h == 0 else 'sinitw'], slow=True)
            sc.dma('act', dtt[:], log_step[l, d].partition_broadcast(128), w=['dtt'])
            act(dtt[:], dtt[:], AF.Exp, ['dtt'], ['dtt'])
            tt('dve', p1[:], li[:], dtt[:], ALU.mult, ['li', 'dtt'], ['p1'])
            wrap(p2[:], p1[:], 'p1', 'p2')
            act(sinv[:], p2[:], AF.Sin, ['p2'], ['sinv'])
            if stop == 'wrap':
                return
            ts('dve', p1[:], p1[:], PI / 2, ALU.add, ['p1'], ['p1'])
            wrap(p2[:], p1[:], 'p1', 'p2')
            act(cosv[:], p2[:], AF.Sin, ['p2'], ['cosv'])
            tt('dve', p1[:], lr[:], dtt[:], ALU.mult, ['lr', 'dtt'], ['p1'])
            act(p2[:], p1[:], AF.Exp, ['p1'], ['p2'])
            act(p3[:], p1[:], AF.Exp, ['p1'], ['p3'], scale=-1.0)
            mset('dve', PWr[:, 8, :], 1.0, ['PW', 'nPW', 'xin1'])
            mset('dve', PWi[:, 8, :], 0.0, ['PW'])
            tt('dve', PWr[:, 9, :], p2[:], cosv[:], ALU.mult, ['p2', 'cosv'], ['PW'])
            tt('dve', PWi[:, 9, :], p2[:], sinv[:], ALU.mult, ['p2', 'sinv'], ['PW'])
            tt('dve', PWr[:, 7, :], p3[:], cosv[:], ALU.mult, ['p3', 'cosv'], ['PW'])
            tt('dve', PWi[:, 7, :], p3[:], sinv[:], ALU.mult, ['p3', 'sinv'], ['PW'])
            ts('dve', PWi[:, 7, :], PWi[:, 7, :], -1.0, ALU.mult, ['PW'], ['PW'])
            for k in range(2, 9):
                cmul(PWr[:, 8 + k, :], PWi[:, 8 + k, :], PWr[:, 7 + k, :], PWi[:, 7 + k, :], PWr[:, 9, :], PWi[:, 9, :], ['PW'], ['PW'])
                cmul(PWr[:, 8 - k, :], PWi[:, 8 - k, :], PWr[:, 9 - k, :], PWi[:, 9 - k, :], PWr[:, 7, :], PWi[:, 7, :], ['PW'], ['PW'])
            ts('dve', nPWi[:], PWi[:], -1.0, ALU.mult, ['PW'], ['nPW'])
            tt('dve', p1[:], lr[:], lr[:], ALU.mult, ['lr'], ['p1'])
            tt('dve', p2[:], li[:], li[:], ALU.mult, ['li'], ['p2'])
            tt('dve', p1[:], p1[:], p2[:], ALU.add, ['p1', 'p2'], ['p1'])
            recip(p1[:], p1[:], ['p1'], ['p1'])
            ts('dve', p2[:], PWr[:, 9, :], -1.0, ALU.add, ['PW'], ['p2'])
            tt('dve', p3[:], p2[:], lr[:], ALU.mult, ['p2', 'lr'], ['p3'])
            tt('dve', p4[:], PWi[:, 9, :], li[:], ALU.mult, ['PW', 'li'], ['p4'])
            tt('dve', p3[:], p3[:], p4[:], ALU.add, ['p3', 'p4'], ['p3'])
            tt('dve', fre[:], p3[:], p1[:], ALU.mult, ['p3', 'p1'], ['fre'])
            tt('dve', p3[:], PWi[:, 9, :], lr[:], ALU.mult, ['PW', 'lr'], ['p3'])
            tt('dve', p4[:], p2[:], li[:], ALU.mult, ['p2', 'li'], ['p4'])
            tt('dve', p3[:], p3[:], p4[:], ALU.subtract, ['p3', 'p4'], ['p3'])
            tt('dve', fim[:], p3[:], p1[:], ALU.mult, ['p3', 'p1'], ['fim'])
            cp('dve', Ar[:, d * 32:(d + 1) * 32], PWr[:, 16, :], ['PW'], ['Ar'])
            ts('dve', Aix[:, d * 32:(d + 1) * 32], PWi[:, 16, :], sgn[:, 0:1], ALU.mult, ['PW', 'sgn'], ['Aix'])
            ts('dve', nAix[:, d * 32:(d + 1) * 32], Aix[:, d * 32:(d + 1) * 32], -1.0, ALU.mult, ['Aix'], ['nAix'])
            cp('dve', A2r[:, d * 32:(d + 1) * 32], PWr[:, 16, :], ['PW'], ['A2'])
            cp('dve', A2i[:, d * 32:(d + 1) * 32], PWi[:, 16, :], ['PW'], ['A2'])
            cm = cmf if d == 0 else cmb
            cmk = 'cmf' if d == 0 else 'cmb'
            for gb in range(4):
                g0 = gb * 8
                for h in range(2):
                    P = slice(h * 64, h * 64 + 64)
                    sc.dma('act', Bre[P], b_re[l, d, g0:g0 + 8].rearrange("g p c -> p g c"), w=['Bre'])
                    sc.dma('act', Bim[P], b_im[l, d, g0:g0 + 8].rearrange("g p c -> p g c"), w=['Bim'])
                for (src_c, dst_c, neg) in ((c_re, Cre, False), (c_im, nCim, True)):
                    for h in range(2):
                        sc.dma('act', cnat[:, h, :], src_c[l, d, g0:g0 + 8].rearrange("g c p -> (g c) p"), w=['cnat'])
                    pb_, pbk = bank()
                    tr(pb_[:, 0:128], cnat[:].rearrange("q h p -> q (h p)"), ident[:], ['cnat', 'ident'], [pbk])
                    if neg:
                        ts('dve', dst_c[:], pb_[:, 0:128].rearrange("p (g c) -> p g c", c=16), -1.0, ALU.mult, [pbk], ['C'])
                    else:
                        cp('dve', dst_c[:], pb_[:, 0:128].rearrange("p (g c) -> p g c", c=16), [pbk], ['C'])
                fr_b = bc(fre[:, g0:g0 + 8].unsqueeze(2), [128, 8, 16]); fi_b = bc(fim[:, g0:g0 + 8].unsqueeze(2), [128, 8, 16])
                tt('dve', bt1[:], Bre[:], fr_b, ALU.mult, ['Bre', 'fre'], ['bt1'])
                tt('dve', bt2[:], Bim[:], fi_b, ALU.mult, ['Bim', 'fim'], ['bt2'])
                tt('dve', Bbr[:], bt1[:], bt2[:], ALU.subtract, ['bt1', 'bt2'], ['Bb'])
                tt('dve', bt1[:], Bim[:], fr_b, ALU.mult, ['Bim', 'fre'], ['bt1'])
                tt('dve', bt2[:], Bre[:], fi_b, ALU.mult, ['Bre', 'fim'], ['bt2'])
                tt('dve', Bbi[:], bt1[:], bt2[:], ALU.add, ['bt1', 'bt2'], ['Bb'])
                for t in range(8):
                    e = EF[d][t]
                    mix(Fg[:, :, t * 16:(t + 1) * 16], g0, 8, PWr[:, 8 + e, :], nPWi[:, 8 + e, :], Cre, nCim, ['PW', 'nPW', 'C'], ['V'])
                    mix(Gg[:, :, t * 16:(t + 1) * 16], g0, 8, PWr[:, 8 - e, :], PWi[:, 8 - e, :], Bbr, Bbi, ['PW', 'Bb'], ['V'])
                    mix(Eg[:, :, t * 16:(t + 1) * 16], g0, 8, PWr[:, 16 - e, :], PWi[:, 16 - e, :], Bbr, Bbi, ['PW', 'Bb'], ['Vs'])
                cp('dve', w3[:, :, 0, :], Fg[:], ['V'], ['s5wb0'])
                for gl in range(8):
                    pb_, pbk = bank()
                    tr(pb_[:, 0:128], Eg[:, gl, :], ident[:], ['Vs', 'ident'], [pbk])
                    mm(pb_[:, 128:256], Gg[:, gl, :], Fg[:, gl, :], True, True, ['V', 'V'], [pbk])
                    cp('dve', w3[:, gl, 1, :], pb_[:, 0:128], [pbk], ['s5wb0'])
                    tt('dve', w3[:, gl, 2, :], pb_[:, 128:256], cm[:], ALU.mult, [pbk, cmk], ['s5wb0'])
                sc.dma('act', s5w[d * 32 + g0:d * 32 + g0 + 8].rearrange("g t k m -> k g t m"), w3[:], r=['s5wb0'], w=['s5w'])
        TRv = rr[:].rearrange("p (g n) -> p g n", n=32)
        TIv = tfall[:].rearrange("p a c -> p (a c)").rearrange("p (g n) -> p g n", n=32)
        tkeys = ['rr', 'tf0', 'tf1', 'tf2', 'tf3']
        mset('dve', TRv[:, 0:32, 31:32], 1.0, tkeys)
        mset('dve', TIv[:, 0:32, 31:32], 0.0, tkeys)
        mset('dve', TRv[:, 32:64, 0:1], 1.0, tkeys)
        mset('dve', TIv[:, 32:64, 0:1], 0.0, tkeys)
        for it_ in range(5):
            m = 1 << it_
            for d in range(2):
                G = slice(d * 32, d * 32 + 32)
                if d == 0:
                    srcs, dsts = slice(32 - m, 32), slice(32 - 2 * m, 32 - m)
                else:
                    srcs, dsts = slice(0, m), slice(m, 2 * m)
                amr = bc(A2r[:, G].unsqueeze(2), [128, 32, m])
                ami = bc(A2i[:, G].unsqueeze(2), [128, 32, m])
                t1_ = V[:, 0:32, 0:m]
                t2_ = V[:, 32:64, 0:m]
                tt('dve', t1_, TRv[:, G, srcs], amr, ALU.mult, tkeys + ['A2'], ['V'])
                tt('dve', t2_, TIv[:, G, srcs], ami, ALU.mult, tkeys + ['A2'], ['V'])
                tt('dve', TRv[:, G, dsts], t1_, t2_, ALU.subtract, ['V'], tkeys)
                tt('dve', t1_, TRv[:, G, srcs], ami, ALU.mult, tkeys + ['A2'], ['V'])
                tt('dve', t2_, TIv[:, G, srcs], amr, ALU.mult, tkeys + ['A2'], ['V'])
                tt('dve', TIv[:, G, dsts], t1_, t2_, ALU.add, ['V'], tkeys)
            tt('dve', tA[:], A2r[:], A2r[:], ALU.mult, ['A2'], ['tA'])
            tt('dve', tB[:], A2i[:], A2i[:], ALU.mult, ['A2'], ['tB'])
            tt('dve', tA[:], tA[:], tB[:], ALU.subtract, ['tA', 'tB'], ['tA'])
            tt('dve', tB[:], A2r[:], A2i[:], ALU.mult, ['A2'], ['tB'])
            cp('dve', A2r[:], tA[:], ['tA'], ['A2'])
            ts('dve', A2i[:], tB[:], 2.0, ALU.mult, ['tB'], ['A2'])
        ts('dve', TIv[:], TIv[:], sgn[:, 0:1], ALU.mult, tkeys + ['sgn'], tkeys)
        ts('dve', A2ix[:], A2i[:], sgn[:, 0:1], ALU.mult, ['A2', 'sgn'], ['A2x'])
        ts('dve', nA2ix[:], A2ix[:], -1.0, ALU.mult, ['A2x'], ['A2x'])

    xslot = [0]

    def ln_ht(src, rows, col0, cond, rkeys=()):
        for si, r0 in enumerate(rows):
            b = xslot[0] % 2
            xslot[0] += 1
            xk = 'xin%d' % b
            xt = xin[b]
            xw = [xk] if b == 0 else [xk, 'PW', 'nPW']
            if isinstance(r0, tuple):
                sc.dma('pool', xt[:], src, r=['ridx'] + list(rkeys), w=xw, ind=ridx[:, r0[1]:r0[1] + 1])
            else:
                sc.dma('sp', xt[:], src[r0:r0 + 128, :], r=list(rkeys), w=xw)
            for q in range(4):
                sc.op('dve', lambda e, q=q, xt=xt: e.bn_stats(out=stats[:, q, :], in_=xt[:, q * 512:(q + 1) * 512]), [xk], ['stats'])
            sc.op('dve', lambda e: e.bn_aggr(out=mv[:], in_=stats[:].rearrange("p a b -> p (a b)")), ['stats'], ['mv'])
            act(rstd[:], mv[:, 1:2], AF.Sqrt, ['mv'], ['rstd'], bias=EPS)
            recip(rstd[:], rstd[:], ['rstd'], ['rstd'])
            ts('dve', xt[:], xt[:], mv[:, 0:1], ALU.subtract, [xk, 'mv', 'rstd'], [xk], s2=rstd[:, 0:1], op1=ALU.mult)
            c0 = col0 + si * 128
            for kq in range(4):
                pb_, pbk = bank()
                for kk_ in range(4):
                    k = kq * 4 + kk_
                    tr(pb_[:, kk_ * 128:(kk_ + 1) * 128], xt[:, k * 128:(k + 1) * 128], ident[:], [xk, 'ident'], [pbk])
                for kk_ in range(4):
                    k = kq * 4 + kk_
                    if k % 2 == 0:
                        act(hT[:, k, c0:c0 + 128], pb_[:, kk_ * 128:(kk_ + 1) * 128], AF.Identity, [pbk, 'sc1', 'modT'], ['hT'],
                            bias=modT[:, k, cond:cond + 1], scale=sc1[:, k, cond:cond + 1])
                    else:
                        ts('dve', hT[:, k, c0:c0 + 128], pb_[:, kk_ * 128:(kk_ + 1) * 128], sc1[:, k, cond:cond + 1], ALU.mult,
                           [pbk, 'sc1', 'modT'], ['hT'], s2=modT[:, k, cond:cond + 1], op1=ALU.add)

    def proj_xa(l, cc):
        for gi in range(2):
            wt, wk = wload('in', l, gi)
            for t in range(8):
                pb_, pbk = bank()
                for k in range(KT):
                    mm(pb_[0:32, 0:256], hT[:, k, cc + t:cc + 256:8], wt[:, k, :], k == 0, k == KT - 1, [wk, 'hT'], [pbk])
                cp('dve' if t % 2 == 0 else 'act', Xp[:, gi * 16:(gi + 1) * 16, t, :],
                   pb_[0:32, 0:256].rearrange("p (g c) -> p g c", c=16), [pbk], ['Xp'])

    def s5_states(t_init, zero_init, reng='dve', sel=False, rec=True):
        for gq in range(4):
            pb_, pbk = bank()
            for gl in range(8):
                g = gq * 8 + gl
                mm(pb_[:, gl * 32:(gl + 1) * 32], Xp[:, g, :, :].rearrange("p t c -> p (t c)"), identb[0:32, 0:32], True, True,
                   ['Xp', 'identb'], [pbk])
            cp('act', Xpp[:, gq * 8:(gq + 1) * 8, :], pb_[:, 0:256].rearrange("p (g n) -> p g n", n=32), [pbk], ['Xpp'])
        for d in range(2):
            for gq in range(4):
                i = (d * 4 + gq) % 2
                wk = 's5wb%d' % i
                sc.dma('sp', s5wb[i][:, :, 1, :], s5w[d * 32 + gq * 8:d * 32 + gq * 8 + 8, 1].rearrange("g k m -> k g m"),
                       r=['s5w'], w=[wk])
                pv, pvk = bank()
                pw_, pwk = bank()
                for gl in range(8):
                    g = gq * 8 + gl
                    mm(pv[:, gl * 32:(gl + 1) * 32], s5wb[i][:, gl, 1, :], Xpp[:, g, :], True, True, [wk, 'Xpp'], [pvk])
                    mm(pw_[0:64, gl * 32:(gl + 1) * 32], s5wb[i][:, gl, 1, 64:128], Xpp[:, g, :], True, True, [wk, 'Xpp'], [pwk])
                    mm(pw_[64:128, gl * 32:(gl + 1) * 32], s5wb[i][:, gl, 1, 0:64], Xpp[:, g, :], True, True, [wk, 'Xpp'], [pwk])
                gd0 = d * 32 + gq * 8
                cp('dve', V[:, gd0:gd0 + 8, :], pv[:, 0:256].rearrange("p (g n) -> p g n", n=32), [pvk], ['V'])
                cp('act', Vs[:, gd0:gd0 + 8, :], pw_[:, 0:256].rearrange("p (g n) -> p g n", n=32), [pwk], ['Vs'])
        for d in (range(2) if rec else ()):
            G = slice(d * 32, d * 32 + 32)
            order = list(range(32)) if d == 0 else list(range(31, -1, -1))
            for step, n_ in enumerate(order):
                if step == 0:
                    if zero_init:
                        continue
                    if sel:
                        pS = Psel[:, d, 0, :]
                        pW = Psel[:, d, 1, :]
                        kp = ['Psel']
                    else:
                        pS = Pst[:, t_init + (0 if d == 0 else 1), 0, G]
                        pW = Pst[:, t_init + (0 if d == 0 else 1), 1, G]
                        kp = ['Pst']
                else:
                    pn = order[step - 1]
                    pS = V[:, G, pn]
                    pW = Vs[:, G, pn]
                    kp = ['V', 'Vs']
                tt(reng, tA[:, 0:32], pS, Ar[:, G], ALU.mult, kp + ['Ar'], ['tA'])
                tt(reng, tB[:, 0:32], pW, Aix[:, G], ALU.mult, kp + ['Aix'], ['tB'])
                tt(reng, tA[:, 0:32], tA[:, 0:32], tB[:, 0:32], ALU.add, ['tA', 'tB'], ['tA'])
                tt(reng, tA[:, 32:64], pW, Ar[:, G], ALU.mult, kp + ['Ar'], ['tA2'])
                tt(reng, tB[:, 32:64], pS, nAix[:, G], ALU.mult, kp + ['nAix'], ['tB2'])
                tt(reng, tA[:, 32:64], tA[:, 32:64], tB[:, 32:64], ALU.add, ['tA2', 'tB2'], ['tA2'])
                tt(reng, V[:, G, n_], V[:, G, n_], tA[:, 0:32], ALU.add, ['V', 'tA'], ['V'])
                tt(reng, Vs[:, G, n_], Vs[:, G, n_], tA[:, 32:64], ALU.add, ['Vs', 'tA2'], ['Vs'])

    def s5_main(l, t_init, zero_init, seq, sel=False):
        if zero_init:
            mset('dve', Sb[:, 0:32, 0:1], 0.0, ['Sb'])
            mset('dve', Sb[:, 32:64, 31:32], 0.0, ['Sb'])
        elif sel:
            cp('dve', Sb[:, 0:32, 0:1], Psel[:, 0, 0, :].unsqueeze(2), ['Psel'], ['Sb'])
            cp('dve', Sb[:, 32:64, 31:32], Psel[:, 1, 0, :].unsqueeze(2), ['Psel'], ['Sb'])
        else:
            cp('dve', Sb[:, 0:32, 0:1], Pst[:, t_init, 0, 0:32].unsqueeze(2), ['Pst'], ['Sb'])
            cp('dve', Sb[:, 32:64, 31:32], Pst[:, t_init + 1, 0, 32:64].unsqueeze(2), ['Pst'], ['Sb'])
        cp('dve', Sb[:, 0:32, 1:32], V[:, 0:32, 0:31], ['V'], ['Sb'])
        cp('act', Sb[:, 32:64, 0:31], V[:, 32:64, 1:32], ['V'], ['Sb'])
        if seq is not None:
            for d in range(2):
                pb_, pbk = bank()
                src_ = V[:, 0:32, 31] if d == 0 else V[:, 32:64, 0]
                cp('dve', tf[0][:, 0:32], src_, ['V'], ['tf0'])
                tr(pb_[0:32, 0:128], tf[0][:, 0:32], ident[:], ['tf0', 'ident'], [pbk])
                cp('dve', stg[:], pb_[0:32, 0:128], [pbk], ['stg'])
                sc.dma('pool', nst[seq, l, d].rearrange("r g p -> g r p"), stg[:].rearrange("g (r p) -> g r p", r=2),
                       r=['stg'], w=['nst'])
        for gq in range(4):
            p0, p0k = bank()
            p1_, p1k = bank()
            for d in range(2):
                for t_ in (0, 2):
                    sc.dma('sp', s5wb[d][:, :, t_, :], s5w[d * 32 + gq * 8:d * 32 + gq * 8 + 8, t_].rearrange("g k m -> k g m"),
                           r=['s5w'], w=['s5wb%d' % d])
            for gl in range(8):
                g = gq * 8 + gl
                pp, ppk = (p0, p0k) if gl < 4 else (p1_, p1k)
                o = pp[0:32, (gl % 4) * 128:(gl % 4 + 1) * 128]
                for d in range(2):
                    wk = 's5wb%d' % d
                    mm(o, Xpp[:, g, :], s5wb[d][:, gl, 2, :], d == 0, False, [wk, 'Xpp'], [ppk])
                    mm(o, Sb[:, d * 32 + g, :], s5wb[d][:, gl, 0, :], False, d == 1, [wk, 'Sb'], [ppk])
            for hf, (pp, ppk) in enumerate(((p0, p0k), (p1_, p1k))):
                g0 = gq * 8 + hf * 4
                yv = ygf[:]
                dsl = bc(dsk[:, g0 * 16:(g0 + 4) * 16].rearrange("p (g c) -> p g c", c=16).unsqueeze(2), [32, 4, 8, 16])
                tt('dve', yv, Xp[:, g0:g0 + 4, :, :], dsl, ALU.mult, ['Xp', 'dsk'], ['ygf'])
                tt('dve', yv, yv, pp[0:32, 0:512].rearrange("p (g t c) -> p g t c", t=8, c=16), ALU.add, ['ygf', ppk], ['ygf'])
                yf = ygf[:].rearrange("p g t c -> p (g t c)")
                y2 = ygf2[:].rearrange("p g t c -> p (g t c)")
                act(y2, yf, AF.Square, ['ygf'], ['ygf2'])
                ts('dve', y2, y2, GC2, ALU.mult, ['ygf2'], ['ygf2'], s2=1.0, op1=ALU.add)
                tt('dve', y2, y2, yf, ALU.mult, ['ygf2', 'ygf'], ['ygf2'])
                act(y2, y2, AF.Sigmoid, ['ygf2'], ['ygf2'], scale=GC1)
                tt('dve', yg[:, :, g0:g0 + 4, :].rearrange("p t g c -> p g t c"), ygf2[:], ygf[:], ALU.mult, ['ygf2', 'ygf'], ['yg'])
        for j in range(4):
            pb_, pbk = bank()
            for t in range(8):
                mm(pb_[:, t * 32:(t + 1) * 32], yg[:, t, j * 8:(j + 1) * 8, :].rearrange("p g c -> p (g c)"), identb[0:32, 0:32], True, True, ['yg', 'identb'], [pbk])
            cp('dve' if j % 2 == 0 else 'act', ygT[:, j, :].rearrange("p (n t) -> p t n", t=8),
               pb_[:, 0:256].rearrange("p (t n) -> p t n", n=32), [pbk], ['ygT'])
        for jo in range(4):
            pb_, pbk = bank()
            for j in range(4):
                mm(pb_[:, 0:NT], wglu[:, j, jo * 128:(jo + 1) * 128], ygT[:, j, :], j == 0, j == 3, ['wglu', 'ygT'], [pbk])
            act(tf[0][:, 0:NT], pb_[:, 0:NT], AF.Sigmoid, [pbk, 'bglu'], ['tf0'], bias=bglu[:, jo:jo + 1])
            tt('dve', tf[0][:, 0:NT], tf[0][:, 0:NT], ygT[:, jo, :], ALU.mult, ['tf0', 'ygT'], ['tf0'])
            tt('dve', yaT[:, jo, :], tf[0][:, 0:NT], zaT[:, jo, :], ALU.mult, ['tf0', 'zaT'], ['zaT'])

    def proj_fm(l, gi, nchunk, cols, evac):
        wt, wk = wload('in', l, gi)
        for j in range(nchunk):
            pb_, pbk = bank()
            n = cols.stop - cols.start
            for k in range(KT):
                mm(pb_[:, 0:n], wt[:, k, j * 128:(j + 1) * 128], hT[:, k, cols], k == 0, k == KT - 1, [wk, 'hT'], [pbk])
            evac(gi, j, pb_[:, 0:n], pbk)

    def rope(out, pin, pk, n, okey):
        cp('act', qraw[:, 0:n], pin, [pk], ['pTs4'])
        pr, prk = bank()
        mm(pr[:, 0:n], rotT[:], qraw[:, 0:n], True, True, ['rotT', 'pTs4'], [prk])
        tt('dve', tf[2][:, 0:n], pin, rc[:, 0:n], ALU.mult, [pk, 'rc'], ['tf2'])
        tt('dve', tf[3][:, 0:n], pr[:, 0:n], rs[:, 0:n], ALU.mult, [prk, 'rs'], ['tf3'])
        tt('dve', out, tf[2][:, 0:n], tf[3][:, 0:n], ALU.add, ['tf2', 'tf3'], [okey])

    def attn_finish(pvb, pvk, pdb, pdk, heads, hw):
        n_ = len(heads) * hw
        cp('act', tf[1][:, 0:n_], pvb[:, 0:n_], [pvk], ['tf1'])
        cp('act', tf[2][:, 0:n_], pdb[:, 0:n_], [pdk], ['tf2'])
        for hi, h in enumerate(heads):
            cs = slice(hi * hw, (hi + 1) * hw)
            ts('dve', tf[0][:, 0:hw], tf[2][:, cs], esink[:, h:h + 1], ALU.add, ['tf2', 'esink'], ['tf0'])
            recip(tf[0][:, 0:hw], tf[0][:, 0:hw], ['tf0'], ['tf0'])
            tt('dve', tf[0][:, 0:hw], tf[0][:, 0:hw], tf[1][:, cs], ALU.mult, ['tf0', 'tf1'], ['tf0'])
            yield h, tf[0][:, 0:hw]

    import os as _os

    def tile_main(l, kind, idx):
        cond = 0 if kind == 'p' else 1
        own = kind == 'o'
        src = (xp if l == 0 else zp) if kind == 'p' else (xs if l == 0 else zs)
        dst = (zp if l == 0 else yp) if kind == 'p' else (zs if l == 0 else ys)
        rkeys = [] if l == 0 else ['dst_p_0' if kind == 'p' else 'dst_s_0']
        dkey = 'dst_%s_%d' % ('p' if kind == 'p' else 's', l)
        t0 = idx * NT
        if kind == 'p':
            rows = [t0, t0 + 128]
            cc = 0
            E = 256
        elif own:
            rows = [('ind', 2 * idx + b_) for b_ in range(4)]
            cc = 128
            E = 512
            sc.dma('pool', rc[:, 0:E], c_ropec_o[:, 256 * idx:256 * idx + 512], w=['rc'])
            sc.dma('pool', rs[:, 0:E], c_ropes_o[:, 256 * idx:256 * idx + 512], w=['rs'])
            for d in range(2):
                for w_ in range(2):
                    G = slice(d * 32, d * 32 + 32)
                    ts('dve', Psel[:, d, w_, :], Pst[:, idx + d, w_, G], oh4[:, 0:1], ALU.mult, ['Pst', 'oh4'], ['Psel'])
                    for j_ in range(1, 4):
                        sc.op('dve', lambda e, d=d, w_=w_, j_=j_, G=G: e.scalar_tensor_tensor(
                            out=Psel[:, d, w_, :], in0=Pst[:, 4 * j_ + idx + d, w_, G], scalar=oh4[:, j_:j_ + 1], in1=Psel[:, d, w_, :],
                            op0=ALU.mult, op1=ALU.add), ['Pst', 'oh4', 'Psel'], ['Psel'])
        else:
            lo = t0 - 128 if idx > 0 else t0
            hi = t0 + NT + 128 if idx < 15 else t0 + NT
            rows = list(range(lo, hi, 128))
            cc = t0 - lo
            E = hi - lo
            sc.dma('pool', rc[:, 0:E], c_ropec[:, lo:hi], w=['rc'])
            sc.dma('pool', rs[:, 0:E], c_ropes[:, lo:hi], w=['rs'])
        ln_ht(src, rows, 0, cond, rkeys)
        ctr = slice(cc, cc + NT)
        proj_xa(l, cc)
        scale = 128 ** -0.5

        def ev(gi, j, pin, pk):
            if gi in (2, 3):
                act(zaT[:, (gi - 2) * 2 + j, :], pin, AF.Silu, [pk], ['zaT'])
            elif 4 <= gi <= 7:
                hh = (gi - 4) * 2 + j
                if kind == 'p' or 'rope' in '':
                    cp('act', qT[:, hh, :], pin, [pk], ['qT'])
                else:
                    rope_q(hh, pin, pk)
            elif gi == 8:
                if kind == 'p' or 'rope' in '':
                    cp('act', kT[:, j, 0:E], pin, [pk], ['kT'])
                else:
                    rope(kT[:, j, 0:E], pin, pk, E, 'kT')
            elif 10 <= gi <= 13:
                act(zbT[:, (gi - 10) * 2 + j, :], pin, AF.Silu, [pk], ['zbT'])
            elif gi in (14, 15):
                gelu_evac(uT[:, (gi - 14) * 2 + j, :], pin, pk, [128, NT], 0, wk=['uT'])
            elif gi in (18, 19):
                act(zcT[:, (gi - 18) * 2 + j, :], pin, AF.Silu, [pk], ['zcT'])

        def rope_q(hh, pin, pk):
            cp('act', qraw[:, 0:NT], pin, [pk], ['pTs4'])
            pr, prk = bank()
            mm(pr[:, 0:NT], rotT[:], qraw[:, 0:NT], True, True, ['rotT', 'pTs4'], [prk])
            tt('dve', tf[2][:, 0:NT], pin, rc[:, ctr], ALU.mult, [pk, 'rc'], ['tf2'])
            tt('dve', tf[3][:, 0:NT], pr[:, 0:NT], rs[:, ctr], ALU.mult, [prk, 'rs'], ['tf3'])
            tt('dve', qT[:, hh, :], tf[2][:, 0:NT], tf[3][:, 0:NT], ALU.add, ['tf2', 'tf3'], ['qT'])

        for gi in (2, 3):
            proj_fm(l, gi, 2, ctr, ev)
        s5_states(idx if kind == 's' else 0, kind == 'p', reng='pool', sel=own)
        for gi in (4, 5, 6, 7):
            proj_fm(l, gi, 2, ctr, ev)
        proj_fm(l, 8, 2, slice(0, E), ev)
        for gi in (8, 9):
            if gi == 8 and kind != 'p':
                continue
            wt, wk = wload('in', l, gi)
            for s_ in range(E // 128):
                pb_, pbk = bank()
                for k in range(KT):
                    mm(pb_[:, 0:256], hT[:, k, s_ * 128:(s_ + 1) * 128], wt[:, k, :], k == 0, k == KT - 1, [wk, 'hT'], [pbk])
                if gi == 9:
                    cp('act', vtok[:, s_, :], pb_[:, 0:256], [pbk], ['vtok'])
                if kind == 'p':
                    o_ = nk if gi == 8 else nv
                    cp('dve', tf[2 + s_][:, 0:256], pb_[:, 0:256], [pbk], ['tf%d' % (2 + s_)])
                    sc.dma('pool', o_[idx, l, s_ * 128:(s_ + 1) * 128, :], tf[2 + s_][:, 0:256], r=['tf%d' % (2 + s_)], w=['nkv'])
        for gi in (10, 11, 12, 13, 14, 15):
            proj_fm(l, gi, 2, ctr, ev)
        for gi in (16, 17):
            wt, wk = wload('in', l, gi)
            for s_ in range(2):
                pb_, pbk = bank()
                for k in range(KT):
                    mm(pb_[:, 0:256], hT[:, k, cc + s_ * 128:cc + (s_ + 1) * 128], wt[:, k, :], k == 0, k == KT - 1, [wk, 'hT'], [pbk])
                gelu_evac(vsg[:, s_, (gi - 16) * 256:(gi - 15) * 256], pb_[:, 0:256], pbk, [128, 256], 0, wk=['rr'])
        for s_ in range(2):
            sc.op('dve', lambda e, s_=s_: e.bn_stats(out=stats[:, 0, :], in_=vsg[:, s_, :]), ['rr'], ['stats'])
            sc.op('dve', lambda e: e.bn_aggr(out=mv[:], in_=stats[:, 0, :]), ['stats'], ['mv'])
            act(rstd[:], mv[:, 1:2], AF.Sqrt, ['mv'], ['rstd'], bias=EPS)
            recip(rstd[:], rstd[:], ['rstd'], ['rstd'])
            ts('dve', vsg[:, s_, :], vsg[:, s_, :], mv[:, 0:1], ALU.subtract, ['rr', 'mv', 'rstd'], ['rr'], s2=rstd[:, 0:1], op1=ALU.mult)
            tt('dve', vsg[:, s_, :], vsg[:, s_, :], sgb[:], ALU.mult, ['rr', 'sgb'], ['rr'])
            tt('dve', vsln[:, s_, :], vsg[:, s_, :], sbb[:], ALU.add, ['rr', 'sbb'], ['vsln'])
        for gi in (18, 19):
            proj_fm(l, gi, 2, ctr, ev)
        for g in range(4):
            for s_ in range(2):
                pb_, pbk = bank()
                mm(pb_[:, 0:128], vsln[:, s_, g * 128:(g + 1) * 128], wsT[:, g, :], True, True, ['vsln', 'wsT'], [pbk])
                tt('dve', tf[1][:, 0:128], pb_[:, 0:128], bsb[:, g * 128:(g + 1) * 128], ALU.add, [pbk, 'bsb'], ['tf1'])
                tt('dve', tf[1][:, 0:128], tf[1][:, 0:128], uT[:, g, s_ * 128:(s_ + 1) * 128], ALU.mult, ['tf1', 'uT'], ['tf1'])
                tt('dve', ycT[:, g, s_ * 128:(s_ + 1) * 128], tf[1][:, 0:128], zcT[:, g, s_ * 128:(s_ + 1) * 128], ALU.mult,
                   ['tf1', 'zcT'], ['zcT'])
        if kind == 'p':
            for kvh in range(2):
                for kb in range(2):
                    pa, pak = bank()
                    pb2, pb2k = bank()
                    for h in range(4):
                        pp, ppk = (pa, pak) if h < 2 else (pb2, pb2k)
                        mm(pp[:, (h % 2) * 256:(h % 2 + 1) * 256], kT[:, kvh, kb * 128:(kb + 1) * 128], qT[:, kvh * 4 + h, :], True, True,
                           ['kT', 'qT'], [ppk])
                    act(pTs[kb * 2][:], pa[:, 0:512], AF.Exp, [pak], ['pTs%d' % (kb * 2)], scale=scale)
                    act(pTs[kb * 2 + 1][:], pb2[:, 0:512], AF.Exp, [pb2k], ['pTs%d' % (kb * 2 + 1)], scale=scale)
                for half in range(2):
                    pv, pvk = bank()
                    pd, pdk = bank()
                    for kb in range(2):
                        mm(pv[:, 0:512], vtok[:, kb, kvh * 128:(kvh + 1) * 128], pTs[kb * 2 + half][:], kb == 0, kb == 1,
                           ['vtok', 'pTs%d' % (kb * 2 + half)], [pvk])
                        mm(pd[:, 0:512], onesb[:], pTs[kb * 2 + half][:], kb == 0, kb == 1, ['onesb', 'pTs%d' % (kb * 2 + half)], [pdk])
                    for h, o in attn_finish(pv, pvk, pd, pdk, [kvh * 4 + half * 2, kvh * 4 + half * 2 + 1], 256):
                        tt('dve', ybT[:, h, :], o, zbT[:, h, :], ALU.mult, ['tf0', 'zbT'], ['qT'])
        elif 'attn' not in '':
            for kvh in range(2):
                for qb in range(2):
                    ia = 2 * idx + qb
                    blocks = []
                    if own:
                        eq = cc + qb * 128
                        blocks.append(('l', eq - 128, mlo, 'mlo', 0 if (idx == 0 and qb == 0) else None))
                        blocks.append(('l', eq, None, None, None))
                        blocks.append(('l', eq + 128, mhi, 'mhi', 1 if (idx == 3 and qb == 1) else None))
                    else:
                        if ia - 1 >= 0:
                            blocks.append(('l', (ia - 1) * 128 - (t0 - cc), mlo, 'mlo', None))
                        blocks.append(('l', ia * 128 - (t0 - cc), None, None, None))
                        if ia + 1 <= 31:
                            blocks.append(('l', (ia + 1) * 128 - (t0 - cc), mhi, 'mhi', None))
                    blocks.append(('c', 0, None, None, None))
                    blocks.append(('c', 1, None, None, None))
                    qv = qT[:, kvh * 4:(kvh + 1) * 4, qb * 128:(qb + 1) * 128]
                    for bi, (bt_, a_, msk, mk, vc) in enumerate(blocks):
                        pa, pak = bank()
                        if bt_ == 'l':
                            e0 = a_
                            kk_ = kT[:, kvh, e0:e0 + 128]
                            kkey = 'kT'
                        else:
                            kk_ = ckT[:, kvh, a_ * 128:(a_ + 1) * 128]
                            kkey = 'ckT'
                        for h_ in range(4):
                            mm(pa[:, h_ * 128:(h_ + 1) * 128], kk_, qT[:, kvh * 4 + h_, qb * 128:(qb + 1) * 128], True, True,
                               [kkey, 'qT'], [pak])
                        act(pTs[bi][:], pa[:, 0:512], AF.Exp, [pak], ['pTs%d' % bi], scale=scale)
                        if msk is not None and vc is not None:
                            sc.op('dve', lambda e, bi=bi, msk=msk, vc=vc: e.scalar_tensor_tensor(
                                out=pTs[bi][:].rearrange("p (h q) -> p h q", q=128), in0=pTs[bi][:].rearrange("p (h q) -> p h q", q=128),
                                scalar=vmask[:, vc:vc + 1], in1=bc(msk[:].unsqueeze(1), [128, 4, 128]), op0=ALU.mult, op1=ALU.mult),
                                ['pTs%d' % bi, mk, 'vmask'], ['pTs%d' % bi])
                        elif msk is not None:
                            tt('dve', pTs[bi][:].rearrange("p (h q) -> p h q", q=128), pTs[bi][:].rearrange("p (h q) -> p h q", q=128),
                               bc(msk[:].unsqueeze(1), [128, 4, 128]), ALU.mult, ['pTs%d' % bi, mk], ['pTs%d' % bi])
                    pv, pvk = bank()
                    pd, pdk = bank()
                    nb = len(blocks)
                    for bi, (bt_, a_, msk, mk, vc) in enumerate(blocks):
                        if bt_ == 'l':
                            e0 = a_
                            vv = vtok[:, e0 // 128, kvh * 128:(kvh + 1) * 128]
                            vkey = 'vtok'
                        else:
                            vv = cvb[:, a_, kvh * 128:(kvh + 1) * 128]
                            vkey = 'cvb'
                        mm(pv[:, 0:512], vv, pTs[bi][:], bi == 0, bi == nb - 1, [vkey, 'pTs%d' % bi], [pvk])
                        mm(pd[:, 0:512], onesb[:], pTs[bi][:], bi == 0, bi == nb - 1, ['onesb', 'pTs%d' % bi], [pdk])
                    for h, o in attn_finish(pv, pvk, pd, pdk, [kvh * 4 + i for i in range(4)], 128):
                        tt('dve', ybT[:, h, qb * 128:(qb + 1) * 128], o, zbT[:, h, qb * 128:(qb + 1) * 128], ALU.mult, ['tf0', 'zbT'], ['qT'])
        s5_main(l, idx if kind == 's' else 0, kind == 'p', idx if kind == 'p' else None, sel=own)
        for fg in range(8):
            for br, (yT_, yk, Kp) in enumerate(((yaT, 'zaT', 4), (ybT, 'qT', 8), (ycT, 'zcT', 4))):
                for j in range(2):
                    wt2, wk2 = wload2(l, (fg * 3 + br) * 2 + j, Kp)
                    pg, pgk = bank()
                    pq, pqk = bank()
                    for k in range(KT):
                        mm(pg[:, 0:NT], wt2[:, k, :], hT[:, k, ctr], k == 0, k == KT - 1, [wk2, 'hT'], [pgk])
                    for k in range(Kp):
                        mm(pq[:, 0:NT], wt2[:, 16 + k, :], yT_[:, k, :], k == 0, k == Kp - 1, [wk2, yk], [pqk])
                    act(tf[2][:, 0:NT], pg[:, 0:NT], AF.Sigmoid, [pgk], ['tf2'])
                    f = fg * 2 + j
                    if br == 0:
                        tt('dve', merged[:, f, :], tf[2][:, 0:NT], pq[:, 0:NT], ALU.mult, ['tf2', pqk], ['merged'])
                    else:
                        tt('dve', tf[2][:, 0:NT], tf[2][:, 0:NT], pq[:, 0:NT], ALU.mult, ['tf2', pqk], ['tf2'])
                        tt('dve', merged[:, f, :], merged[:, f, :], tf[2][:, 0:NT], ALU.add, ['merged', 'tf2'], ['merged'])
        gate_b = V[:].rearrange("p g n -> p (g n)")
        lng_b = Vs[:].rearrange("p g n -> p (g n)")
        lnb_b = tfall[:].rearrange("p a c -> p (a c)")
        tfk = ['tf0', 'tf1', 'tf2', 'tf3']
        sc.dma('pool', gate_b, gsc[cond].partition_broadcast(128), r=['gsc'], w=['V'])
        sc.dma('pool', lng_b, ln_g[l].partition_broadcast(128), w=['Vs'])
        sc.dma('pool', lnb_b, ln_b[l].partition_broadcast(128), w=tfk)
        for s_ in range(2):
            for fb in range(8):
                wt, wk = wload('out', l, fb)
                pb_, pbk = bank()
                for k in range(KT):
                    mm(pb_[:, 0:256], merged[:, k, s_ * 128:(s_ + 1) * 128], wt[:, k, :], k == 0, k == KT - 1, [wk, 'merged'], [pbk])
                tt('dve', rr[:, fb * 256:(fb + 1) * 256], pb_[:, 0:256], gate_b[:, fb * 256:(fb + 1) * 256], ALU.mult, [pbk, 'V'], ['rr'])
            xk = 'xin0'
            rk = 'rr'
            if own:
                sc.dma('pool', xin[0][:], src, r=['ridx'] + rkeys, w=[xk], ind=ridx[:, 2 * idx + 1 + s_:2 * idx + 2 + s_])
            else:
                sc.dma('sp', xin[0][:], src[t0 + s_ * 128:t0 + (s_ + 1) * 128, :], r=rkeys, w=[xk])
            sc.op('dve', lambda e: e.scalar_tensor_tensor(out=rr[:], in0=xin[0][:], scalar=ALPHA, in1=rr[:],
                                                          op0=ALU.mult, op1=ALU.add), [xk, rk], [rk])
            for q in range(4):
                sc.op('dve', lambda e, q=q: e.bn_stats(out=stats[:, q, :], in_=rr[:, q * 512:(q + 1) * 512]), [rk], ['stats'])
            sc.op('dve', lambda e: e.bn_aggr(out=mv[:], in_=stats[:].rearrange("p a b -> p (a b)")), ['stats'], ['mv'])
            act(rstd[:], mv[:, 1:2], AF.Sqrt, ['mv'], ['rstd'], bias=EPS)
            recip(rstd[:], rstd[:], ['rstd'], ['rstd'])
            ts('dve', rr[:], rr[:], mv[:, 0:1], ALU.subtract, [rk, 'mv', 'rstd'], [rk], s2=rstd[:, 0:1], op1=ALU.mult)
            tt('dve', rr[:], rr[:], lng_b, ALU.mult, [rk, 'Vs'], [rk])
            tt('dve', rr[:], rr[:], lnb_b, ALU.add, [rk] + tfk, [rk])
            sc.dma('pool', dst[t0 + s_ * 128:t0 + (s_ + 1) * 128, :], rr[:], r=[rk], w=[dkey])

    def chain(pin_, pout, G, Ls, Lw, keys):
        for w_ in range(2):
            cx = A2ix if w_ == 0 else nA2ix
            tt('dve', tA[:, 0:32], Pst[:, pin_, w_, G], A2r[:, G], ALU.mult, ['Pst', 'A2'], ['tA'])
            tt('dve', tB[:, 0:32], Pst[:, pin_, 1 - w_, G], cx[:, G], ALU.mult, ['Pst', 'A2x'], ['tB'])
            tt('dve', tA[:, 0:32], tA[:, 0:32], tB[:, 0:32], ALU.add, ['tA', 'tB'], ['tA'])
            tt('dve', Pst[:, pout, w_, G], tA[:, 0:32], Ls if w_ == 0 else Lw, ALU.add, ['tA'] + keys, ['Pst'])

    def sample_prepass(l):
        src = xs if l == 0 else zs
        cp('dve', Pst[:, 0, 0, 0:32], sinit[:, 0, :], ['sinit'], ['Pst'])
        cp('dve', Pst[:, 0, 1, 0:32], sinitw[:, 0, :], ['sinitw'], ['Pst'])
        cp('dve', Pst[:, 16, 0, 32:64], sinit[:, 1, :], ['sinit'], ['Pst'])
        cp('dve', Pst[:, 16, 1, 32:64], sinitw[:, 1, :], ['sinitw'], ['Pst'])
        import os as _os
        for t in range(16):
            ln_ht(src, [t * NT, t * NT + 128], 0, 1, [] if l == 0 else ['dst_s_0'])
            proj_xa(l, 0)
            s5_states(0, True, rec=False)
            TRv = rr[:].rearrange("p (g n) -> p g n", n=32)
            TIv = tfall[:].rearrange("p a c -> p (a c)").rearrange("p (g n) -> p g n", n=32)
            tkeys = ['rr', 'tf0', 'tf1', 'tf2', 'tf3']
            tmp = xin[0][:].rearrange("p (g n) -> p g n", n=32)
            AX = mybir.AxisListType.X
            for q_, (ta_, va_, vk_) in enumerate(((TRv, V, 'V'), (TIv, Vs, 'Vs'), (TRv, Vs, 'Vs'), (TIv, V, 'V'))):
                tt('dve', tmp, ta_, va_[:], ALU.mult, tkeys + [vk_], ['xin0'])
                sc.op('dve', lambda e, q_=q_: e.tensor_reduce(out=Ssum[:, q_, :], in_=tmp, axis=AX, op=ALU.add), ['xin0'], ['Ssum'])
            tt('dve', Ssum[:, 0, :], Ssum[:, 0, :], Ssum[:, 1, :], ALU.add, ['Ssum'], ['Ssum'])
            tt('dve', Ssum[:, 2, :], Ssum[:, 2, :], Ssum[:, 3, :], ALU.subtract, ['Ssum'], ['Ssum'])
            chain(t, t + 1, slice(0, 32), Ssum[:, 0, 0:32], Ssum[:, 2, 0:32], ['Ssum'])
            cp('dve', Lst[:, t, 0, :], Ssum[:, 0, 32:64], ['Ssum'], ['Lst'])
            cp('dve', Lst[:, t, 1, :], Ssum[:, 2, 32:64], ['Ssum'], ['Lst'])
        for t in range(15, -1, -1):
            chain(t + 1, t, slice(32, 64), Lst[:, t, 0, :], Lst[:, t, 1, :], ['Lst'])

    if stop is None:
        for l in range(2):
            layer_prep(l)
            wconvert(l)
            sample_prepass(l)
            for i in range(NPS):
                tile_main(l, 'p', i)
            if l == 0:
                for t in range(16):
                    tile_main(l, 's', t)
            else:
                for t in range(4):
                    tile_main(l, 'o', t)
    else:
        layer_prep(0)
        wconvert(0)
        if stop == 'ptile':
            tile_main(0, 'p', 0)
        if stop == 'pre':
            sample_prepass(0)
        if stop == 'own':
            sample_prepass(0)
            tile_main(0, 'o', 0)
            tile_main(0, 'o', 3)
        if stop == 'stile':
            sample_prepass(0)
            tile_main(0, 's', 0)
            if 'one' not in '':
                tile_main(0, 's', 1)
        loc = dict(locals())
        for nm in dumps:
            if nm in ('s5w', 'gsc', 'zp', 'zs'):
                src_ap = loc[nm]
                key = nm if nm in ('s5w', 'gsc') else ('dst_p_0' if nm == 'zp' else 'dst_s_0')
                o = nc.dram_tensor("dbg_" + nm, list(src_ap.shape), src_ap.dtype, kind="ExternalOutput").ap()
                sc.dma('sp', o, src_ap, r=[key], w=['dbg_' + nm])
            else:
                t_ = loc[nm]
                o = nc.dram_tensor("dbg_" + nm, list(t_.shape), t_.dtype, kind="ExternalOutput").ap()
                sc.dma('sp', o, t_[:], r=[nm, 'PW', 'C', 'Bb', 'A2', 'A2x', 'V', 'Vs'], w=['dbg_' + nm])
    counts = sc.emit(es)
    es.close()
    return nc, counts


_CACHE = {}


def kernel(x_prompt, x_sample, cache_k, cache_v, state_ssm, c, c_ctx,
           w_ada, b_ada, w_in, ssm_lam_re, ssm_lam_im, ssm_log_step,
           ssm_b_re, ssm_b_im, ssm_c_re, ssm_c_im, ssm_d, w_glu, b_glu,
           attn_sink, sgu_ln_g, sgu_ln_b, w_spatial, b_spatial,
           w_proj_a, w_proj_b, w_proj_c, w_out, ln_g, ln_b):
    f = lambda a: np.ascontiguousarray(np.asarray(a, dtype=np.float32))
    if 'nc' not in _CACHE:
        _CACHE['nc'] = build()[0]
    nc = _CACHE['nc']
    consts = _host_consts()
    shared = dict(w_ada=f(w_ada), b_ada=f(b_ada), w_in=f(w_in), lam_re=f(ssm_lam_re), lam_im=f(ssm_lam_im),
                  log_step=f(ssm_log_step), b_re=f(ssm_b_re), b_im=f(ssm_b_im), c_re=f(ssm_c_re), c_im=f(ssm_c_im),
                  ssm_d=f(ssm_d), w_glu=f(w_glu), b_glu=f(b_glu), sink=f(attn_sink), sgu_g=f(sgu_ln_g), sgu_b=f(sgu_ln_b),
                  w_s=f(w_spatial), b_s=f(np.asarray(b_spatial).reshape(2, 512)),
                  w_pa=f(w_proj_a), w_pb=f(w_proj_b), w_pc=f(w_proj_c), w_out=f(w_out), ln_g=f(ln_g), ln_b=f(ln_b))
    shared.update(consts)
    x_prompt = np.asarray(x_prompt); x_sample = np.asarray(x_sample)
    cache_k = np.asarray(cache_k); cache_v = np.asarray(cache_v); state_ssm = np.asarray(state_ssm)
    c = np.asarray(c); c_ctx = np.asarray(c_ctx)
    in_maps = []
    for core in range(8):
        b = core // 4
        m = dict(shared)
        m['xp'] = f(x_prompt[core * NPS:(core + 1) * NPS].reshape(NPS * LP, D))
        m['xs'] = f(x_sample[b])
        m['ck'] = f(cache_k[b].reshape(2, 256, 256))
        m['cv'] = f(cache_v[b].reshape(2, 256, 256))
        m['st0'] = f(state_ssm[b])
        m['cvec'] = f(np.stack([c_ctx, c[b]], axis=0))
        m.update(_core_consts(core, consts))
        in_maps.append(m)
    res = run_bass_kernel_spmd(nc, in_maps, core_ids=list(range(8)))
    R = res.results
    y_prompt = np.concatenate([R[i]['yp'].reshape(NPS, LP, D) for i in range(8)], axis=0).astype(np.float32)
    y_sample = np.stack([np.concatenate([R[b_ * 4 + j_]['ys_own'] for j_ in range(4)], axis=0) for b_ in range(2)], axis=0).astype(np.float32)
    nk_ = np.concatenate([R[i]['nk'].reshape(NPS, 2, LP, 2, 128) for i in range(8)], axis=0).astype(np.float32)
    nv_ = np.concatenate([R[i]['nv'].reshape(NPS, 2, LP, 2, 128) for i in range(8)], axis=0).astype(np.float32)
    ns_ = np.concatenate([R[i]['nst'] for i in range(8)], axis=0).astype(np.float32)
    return (y_prompt, y_sample, nk_, nv_, ns_)
```

```python
import contextlib
import math
import numpy as np
import concourse.bass as bass
import concourse.mybir as mybir
from concourse.bass_utils import run_bass_kernel_spmd

F32 = mybir.dt.float32
BF = mybir.dt.bfloat16
AF = mybir.ActivationFunctionType
ALU = mybir.AluOpType

D = 2048
KT = 16
NT = 256
DIN = 11264
LP = 256
LS = 4096
NPS = 4
DEPTH = 2
ALPHA = (2 * DEPTH) ** 0.25
EPS = 1e-5
GC1 = 1.5957691216057308
GC2 = 0.044715
SAME_ENG_SYNC = True
SAME_ENG_DIST = 8


class Sched:
    NS = 8

    def __init__(self, nc):
        self.nc = nc
        self.ops = []

    def op(self, eng, fn, r=(), w=()):
        w = tuple(w) + tuple(k for k in r if k.startswith('ps') and k not in w)
        self.ops.append((eng, fn, tuple(r), tuple(w), False))

    def dma(self, q, out, in_, r=(), w=(), slow=False, ind=None):
        self.ops.append((q, (out, in_, slow, ind), tuple(r), tuple(w), True))

    def emit(self, es):
        nc = self.nc
        ops = self.ops
        n = len(ops)
        last_w = {}
        readers = {}
        deps = [None] * n
        for i, (eng, fn, r, w, isd) in enumerate(ops):
            d = set()
            for k in r:
                if k in last_w:
                    d.add(last_w[k])
            for k in w:
                if k in last_w:
                    d.add(last_w[k])
                for j in readers.get(k, ()):
                    d.add(j)
            d.discard(i)
            deps[i] = d
            for k in w:
                last_w[k] = i
                readers[k] = []
            for k in r:
                if k not in w:
                    readers.setdefault(k, []).append(i)
        need = [False] * n
        lidx = [0] * n
        lc = {}
        for i in range(n):
            lidx[i] = lc.get(ops[i][0], 0)
            lc[ops[i][0]] = lidx[i] + 1

        def same_eng_skip(i, j):
            if ops[i][0] == 'pe' or not SAME_ENG_SYNC:
                return True
            return (lidx[i] - lidx[j]) > SAME_ENG_DIST

        for i in range(n):
            ei = ops[i][0]
            for j in deps[i]:
                ej, _, _, _, dj = ops[j]
                if dj or ej != ei or not same_eng_skip(i, j):
                    need[j] = True
        engs = ['pe', 'act', 'dve', 'pool', 'sp']
        esem = {e: es.enter_context(nc.semaphore('es_' + e)) for e in engs}
        dsem = {q: [es.enter_context(nc.semaphore('ds_%s%d' % (q, k))) for k in range(self.NS)]
                for q in ('sp', 'pool', 'act')}
        cnt = {e: 0 for e in engs}
        dcnt = {q: 0 for q in dsem}
        sig = [None] * n
        streams = {e: [] for e in engs}
        waited = {e: {} for e in engs}

        def addwait(e, lst, sem, val):
            key = id(sem)
            if waited[e].get(key, 0) >= val:
                return
            waited[e][key] = val
            lst.append(('w', sem, val))

        for i, (eng, fn, r, w, isd) in enumerate(ops):
            lst = streams[eng]
            wmax = {}
            for j in deps[i]:
                if sig[j] is None:
                    continue
                ej, dj = ops[j][0], ops[j][4]
                if (not dj) and ej == eng and same_eng_skip(i, j):
                    continue
                key = id(sig[j][0])
                if key not in wmax or wmax[key][1] < sig[j][1]:
                    wmax[key] = sig[j]
            for key in sorted(wmax, key=lambda k_: wmax[k_][1]):
                addwait(eng, lst, wmax[key][0], wmax[key][1])
            if isd:
                k = dcnt[eng]
                dcnt[eng] += 1
                sem = dsem[eng][k % self.NS]
                rnd = k // self.NS
                if rnd > 0:
                    addwait(eng, lst, sem, 16 * rnd)
                sig[i] = (sem, 16 * (rnd + 1))
                lst.append(('d', fn, sem))
            else:
                if need[i]:
                    cnt[eng] += 1
                    sig[i] = (esem[eng], cnt[eng])
                    lst.append(('o', fn, esem[eng]))
                else:
                    lst.append(('o', fn, None))
        for q in dsem:
            for k in range(self.NS):
                tot = (dcnt[q] - k + self.NS - 1) // self.NS if dcnt[q] > k else 0
                if tot > 0:
                    streams[q].append(('w', dsem[q][k], 16 * tot))

        def run(engine, lst):
            for it in lst:
                if it[0] == 'w':
                    engine.wait_ge(it[1], it[2])
                elif it[0] == 'd':
                    out, in_, slow, ind = it[1]
                    if ind is not None:
                        engine.indirect_dma_start(out=out, out_offset=None, in_=in_,
                                                  in_offset=bass.IndirectOffsetOnAxis(ap=ind, axis=0)).then_inc(it[2], 16)
                    elif slow:
                        engine.dma_start(out=out, in_=in_, allow_slow_non_contiguous=True).then_inc(it[2], 16)
                    else:
                        engine.dma_start(out=out, in_=in_).then_inc(it[2], 16)
                else:
                    ins = it[1](engine)
                    if it[2] is not None:
                        ins.then_inc(it[2], 1)

        block = es.enter_context(nc.Block())

        @block.tensor
        def _(e):
            run(e, streams['pe'])

        @block.scalar
        def _(e):
            run(e, streams['act'])

        @block.vector
        def _(e):
            run(e, streams['dve'])

        @block.gpsimd
        def _(e):
            run(e, streams['pool'])

        @block.sync
        def _(e):
            run(e, streams['sp'])
        return {e: (len(streams[e]), cnt[e]) for e in engs}


def _core_consts(core, consts):
    j = core % 4
    m = {}
    p = np.arange(128)[:, None]
    cidx = np.arange(10)[None, :]
    m['ridx'] = np.clip(1024 * j + 128 * (cidx - 1) + p, 0, LS - 1).astype(np.int32)
    q = np.clip(1024 * j - 128 + np.arange(1280), 0, LS - 1)
    m['ropec_o'] = np.ascontiguousarray(consts['ropec'][:, q])
    m['ropes_o'] = np.ascontiguousarray(consts['ropes'][:, q])
    oh = np.zeros((128, 4), np.float32); oh[:, j] = 1.0
    m['oh4'] = oh
    vm = np.ones((128, 2), np.float32)
    if j == 0:
        vm[:, 0] = 0.0
    if j == 3:
        vm[:, 1] = 0.0
    m['vmask'] = vm
    return m


def _host_consts():
    c = {}
    c['ident'] = np.eye(128, dtype=np.float32)
    R = np.zeros((128, 128), np.float32)
    for d in range(128):
        if d % 64 < 32:
            R[d, d + 32] = -1.0
        else:
            R[d, d - 32] = 1.0
    c['rotT'] = np.ascontiguousarray(R.T)
    pos = np.arange(LS)
    row = pos // 64
    col = pos % 64
    inv = 10000.0 ** (-np.arange(0, 64, 2, dtype=np.float32) / 64.0)
    ang = np.zeros((128, LS), np.float32)
    for d in range(128):
        p = row if d < 64 else col
        ang[d] = p.astype(np.float32) * inv[d % 32]
    c['ropec'] = np.cos(ang).astype(np.float32)
    c['ropes'] = np.sin(ang).astype(np.float32)
    kk = np.arange(128)[:, None]
    qq = np.arange(128)[None, :]
    c['mlo'] = (kk >= qq).astype(np.float32)
    c['mhi'] = (kk <= qq).astype(np.float32)
    tp = (np.arange(128) // 16)[:, None]
    tt = (np.arange(128) // 16)[None, :]
    c['cmf'] = (tt >= tp).astype(np.float32)
    c['cmb'] = (tp >= tt).astype(np.float32)
    return c


def build(stop=None, dumps=()):
    nc = bass.Bass("TRN2", target_bir_lowering=False)
    es = contextlib.ExitStack()
    sc = Sched(nc)
    PI = math.pi

    def din(name, shape, dt=F32):
        return nc.dram_tensor(name, list(shape), dt, kind="ExternalInput").ap()

    def dout(name, shape):
        return nc.dram_tensor(name, list(shape), F32, kind="ExternalOutput").ap()

    def dscr(name, shape, dt=F32):
        return nc.dram_tensor(name, list(shape), dt, kind="Internal").ap()

    xp = din("xp", [NPS * LP, D]); xs = din("xs", [LS, D])
    ck = din("ck", [2, 256, 256]); cv = din("cv", [2, 256, 256])
    st0 = din("st0", [2, 2, 2, 32, 64]); cvec = din("cvec", [2, D])
    w_ada = din("w_ada", [2, D, 3 * D]); b_ada = din("b_ada", [2, 3 * D]); w_in = din("w_in", [2, D, DIN])
    lam_re = din("lam_re", [2, 2, 32, 64]); lam_im = din("lam_im", [2, 2, 32, 64]); log_step = din("log_step", [2, 2, 32])
    b_re = din("b_re", [2, 2, 32, 64, 16]); b_im = din("b_im", [2, 2, 32, 64, 16])
    c_re = din("c_re", [2, 2, 32, 16, 64]); c_im = din("c_im", [2, 2, 32, 16, 64])
    ssm_d = din("ssm_d", [2, 512]); w_glu = din("w_glu", [2, 512, 512]); b_glu = din("b_glu", [2, 512])
    sink = din("sink", [2, 8]); sgu_g = din("sgu_g", [2, 512]); sgu_b = din("sgu_b", [2, 512])
    w_s = din("w_s", [2, 4, 128, 128]); b_s = din("b_s", [2, 512])
    w_pa = din("w_pa", [2, 512, D]); w_pb = din("w_pb", [2, 1024, D]); w_pc = din("w_pc", [2, 512, D])
    w_out = din("w_out", [2, D, D]); ln_g = din("ln_g", [2, D]); ln_b = din("ln_b", [2, D])
    c_ident = din("ident", [128, 128]); c_rotT = din("rotT", [128, 128])
    c_ropec = din("ropec", [128, LS]); c_ropes = din("ropes", [128, LS])
    c_mlo = din("mlo", [128, 128]); c_mhi = din("mhi", [128, 128])
    c_cmf = din("cmf", [128, 128]); c_cmb = din("cmb", [128, 128])
    ridx_d = din("ridx", [128, 10], mybir.dt.int32); c_ropec_o = din("ropec_o", [128, 1280]); c_ropes_o = din("ropes_o", [128, 1280])
    oh4_d = din("oh4", [128, 4]); vmask_d = din("vmask", [128, 2])
    yp = dout("yp", [NPS * LP, D]); ys = dout("ys_own", [1024, D])
    nk = dout("nk", [NPS, 2, LP, 256]); nv = dout("nv", [NPS, 2, LP, 256]); nst = dout("nst", [NPS, 2, 2, 2, 32, 64])
    zp = dscr("zp", [NPS * LP, D]); zs = dscr("zs", [LS, D]); gsc = dscr("gsc", [2, D])
    s5w = dscr("s5w", [64, 3, 128, 128], BF)
    wsc = dscr("wsc", [2, 76, 128, 16, 256], BF)
    wsc2 = dscr("wsc2", [2, 48, 128, 24, 128], BF)

    def sb(name, shape, dt=F32):
        return es.enter_context(nc.sbuf_tensor("s_" + name, list(shape), dt))

    ps = [es.enter_context(nc.psum_tensor("ps%d" % i, [128, 512], F32)) for i in range(8)]
    psn = [0]

    def bank():
        i = psn[0] % 8
        psn[0] += 1
        return ps[i], 'ps%d' % i

    xin = [sb("xin0", [128, D]), sb("xin1", [128, D])]
    rr = sb("rr", [128, D])
    stats = sb("stats", [128, 4, 6]); mv = sb("mv", [128, 2]); rstd = sb("rstd", [128, 1])
    hT = sb("hT", [128, KT, 512], BF)
    wb = [sb("wb%d" % i, [128, KT, 256], BF) for i in range(3)]
    zaT = sb("zaT", [128, 4, NT], BF); qT = sb("qT", [128, 8, NT], BF); yaT = zaT; ybT = qT; kT = sb("kT", [128, 2, 512], BF)
    zbT = sb("zbT", [128, 8, NT], BF); uT = sb("uT", [128, 4, NT], BF); zcT = sb("zcT", [128, 4, NT], BF); ycT = zcT
    Xp = sb("Xp", [32, 32, 8, 16], BF); Xpp = sb("Xpp", [128, 32, 32], BF)
    vtok = sb("vtok", [128, 4, 256], BF); vsg = rr[:, 0:1024].rearrange("p (s c) -> p s c", c=512)
    vsln = sb("vsln", [128, 2, 512], BF)
    V = sb("V", [128, 64, 32]); Vs = sb("Vs", [128, 64, 32]); Sb = sb("Sb", [128, 64, 32], BF)
    tA = sb("tA", [128, 64]); tB = sb("tB", [128, 64])
    yg = sb("yg", [32, 8, 32, 16], BF); ygf = sb("ygf", [32, 4, 8, 16]); ygf2 = sb("ygf2", [32, 4, 8, 16])
    ygT = sb("ygT", [128, 4, NT], BF)
    s5wb = [sb("s5wb%d" % i, [128, 8, 3, 128], BF) for i in range(2)]
    merged = sb("merged", [128, KT, NT], BF)
    tfall = sb("tfall", [128, 4, 512])
    tf = [tfall[:, i, :] for i in range(4)]
    pTs = [sb("pTs%d" % i, [128, 512], BF) for i in range(5)]
    Fg = V[:, 0:32, :].rearrange("p (g a) n -> p g (a n)", a=4)
    Gg = V[:, 32:64, :].rearrange("p (g a) n -> p g (a n)", a=4)
    Eg = Vs[:, 0:32, :].rearrange("p (g a) n -> p g (a n)", a=4)
    w3 = s5wb[0]
    wsn = tf[0][:, :].rearrange("p (g q) -> p g q", q=128)
    ckn = tf[1][:, :].rearrange("p (b c) -> p b c", c=256)
    rc = sb("rc", [128, 512]); rs = sb("rs", [128, 512]); qraw = pTs[4]
    ident = sb("ident", [128, 128]); identb = sb("identb", [128, 128], BF); rotT = sb("rotT", [128, 128], BF)
    onesb = sb("onesb", [128, 128], BF); mlo = sb("mlo", [128, 128], BF); mhi = sb("mhi", [128, 128], BF)
    cmf = sb("cmf", [128, 128]); cmb = sb("cmb", [128, 128])
    scT = sb("scT", [128, 2, KT], BF); cvT = sb("cvT", [128, 2, KT])
    modT = sb("modT", [128, 48, 2]); badaT = sb("badaT", [128, 48]); sc1 = sb("sc1", [128, KT, 2])
    sgb = sb("sgb", [128, 512]); sbb = sb("sbb", [128, 512]); bsb = sb("bsb", [128, 512])
    wsT = sb("wsT", [128, 4, 128], BF)
    esink = sb("esink", [128, 8]); ckT = sb("ckT", [128, 2, 256], BF)
    cvb = sb("cvb", [128, 2, 256], BF)
    wglu = sb("wglu", [128, 4, 512], BF); bglu = sb("bglu", [128, 4]); dsk = sb("dsk", [32, 512])
    lr = sb("lr", [128, 32]); li = sb("li", [128, 32]); dtt = sb("dtt", [128, 32])
    p1 = sb("p1", [128, 32]); p2 = sb("p2", [128, 32]); p3 = sb("p3", [128, 32]); p4 = sb("p4", [128, 32])
    cosv = sb("cosv", [128, 32]); sinv = sb("sinv", [128, 32])
    fre = sb("fre", [128, 32]); fim = sb("fim", [128, 32])
    PWr = xin[1][:, 0:544].rearrange("p (k g) -> p k g", g=32); PWi = xin[1][:, 544:1088].rearrange("p (k g) -> p k g", g=32)
    nPWi = xin[1][:, 1088:1632].rearrange("p (k g) -> p k g", g=32)
    Bre = sb("Bre", [128, 8, 16]); Bim = sb("Bim", [128, 8, 16]); Bbr = sb("Bbr", [128, 8, 16]); Bbi = sb("Bbi", [128, 8, 16])
    bt1 = sb("bt1", [128, 8, 16]); bt2 = sb("bt2", [128, 8, 16])
    cnat = sb("cnat", [128, 2, 64]); Cre = sb("Cre", [128, 8, 16]); nCim = sb("nCim", [128, 8, 16])
    Ar = sb("Ar", [128, 64]); Aix = sb("Aix", [128, 64]); nAix = sb("nAix", [128, 64]); sgn = sb("sgn", [128, 1])
    A2r = sb("A2r", [128, 64]); A2i = sb("A2i", [128, 64]); A2ix = sb("A2ix", [128, 64]); nA2ix = sb("nA2ix", [128, 64])
    sinit = sb("sinit", [128, 2, 32]); sinitw = sb("sinitw", [128, 2, 32])
    Pst = sb("Pst", [128, 17, 2, 64])
    Lst = sb("Lst", [128, 16, 2, 32])
    stg = sb("stg", [32, 128])
    Ssum = sb("Ssum", [128, 4, 64])
    ridx = sb("ridx", [128, 10], mybir.dt.int32); oh4 = sb("oh4", [128, 4]); vmask = sb("vmask", [128, 2])
    Psel = sb("Psel", [128, 2, 2, 32])

    def tt(eng, out, a, b, op, r, w):
        sc.op(eng, lambda e: e.tensor_tensor(out=out, in0=a, in1=b, op=op), r, w)

    def ts(eng, out, a, s1, op0, r, w, s2=None, op1=None):
        if op1 is None:
            sc.op(eng, lambda e: e.tensor_scalar(out=out, in0=a, scalar1=s1, scalar2=None, op0=op0), r, w)
        else:
            sc.op(eng, lambda e: e.tensor_scalar(out=out, in0=a, scalar1=s1, scalar2=s2, op0=op0, op1=op1), r, w)

    def act(out, in_, func, r, w, bias=None, scale=None):
        kw = {}
        if bias is not None:
            kw['bias'] = bias
        if scale is not None:
            kw['scale'] = scale
        sc.op('act', lambda e: e.activation(out=out, in_=in_, func=func, **kw), r, w)

    def mm(out, lhsT, rhs, start, stop, r, w):
        sc.op('pe', lambda e: e.matmul(out, lhsT, rhs, start=start, stop=stop), r, w)

    def tr(out, in_, idn, r, w):
        sc.op('pe', lambda e: e.transpose(out, in_, idn), r, w)

    def cp(eng, out, in_, r, w):
        if eng == 'act':
            sc.op(eng, lambda e: e.copy(out=out, in_=in_), r, w)
        else:
            sc.op(eng, lambda e: e.tensor_copy(out=out, in_=in_), r, w)

    def recip(out, in_, r, w):
        sc.op('dve', lambda e: e.reciprocal(out=out, in_=in_), r, w)

    def mset(eng, ap, val, w):
        sc.op(eng, lambda e: e.memset(ap, val), (), w)

    def bc(ap, shape):
        return ap.broadcast_to(list(shape))

    sc.dma('sp', ident[:], c_ident, w=['ident'])
    cp('dve', identb[:], ident[:], ['ident'], ['identb'])
    sc.dma('sp', tf[0][:, 0:128], c_rotT, w=['tf0'])
    cp('dve', rotT[:], tf[0][:, 0:128], ['tf0'], ['rotT'])
    sc.dma('sp', tf[1][:, 0:128], c_mlo, w=['tf1'])
    cp('dve', mlo[:], tf[1][:, 0:128], ['tf1'], ['mlo'])
    sc.dma('sp', tf[2][:, 0:128], c_mhi, w=['tf2'])
    cp('dve', mhi[:], tf[2][:, 0:128], ['tf2'], ['mhi'])
    sc.dma('sp', cmf[:], c_cmf, w=['cmf'])
    sc.dma('sp', cmb[:], c_cmb, w=['cmb'])
    sc.dma('sp', ridx[:], ridx_d, w=['ridx'])
    sc.dma('sp', oh4[:], oh4_d, w=['oh4'])
    sc.dma('sp', vmask[:], vmask_d, w=['vmask'])
    mset('dve', onesb[:], 1.0, ['onesb'])
    mset('dve', sgn[0:64, :], -1.0, ['sgn'])
    mset('dve', sgn[64:128, :], 1.0, ['sgn'])
    sc.dma('sp', cvT[:], cvec.rearrange("c (k p) -> p c k", p=128), w=['cvT'], slow=True)
    act(scT[:], cvT[:], AF.Silu, ['cvT'], ['scT'])

    wslot = [0]

    WNAMES = {'in': (w_in, 16, 0), 'pp': (None, 16, 44), 'out': (w_out, 16, 68)}
    WPACK = (('pa', w_pa, 4, 0), ('pb', w_pb, 8, 4), ('pc', w_pc, 4, 12))

    def wload_cast(src):
        i = wslot[0] % 3
        wslot[0] += 1
        K = src.shape[0] // 128
        sc.dma('pool', wb[i][:, 0:K, :], src.rearrange("(k p) c -> p k c", p=128), w=['wb%d' % i])
        return wb[i], 'wb%d' % i

    def wconvert(l):
        for nm, (wt_, K, g0) in WNAMES.items():
            if wt_ is None:
                continue
            ng = wt_.shape[2] // 256
            if nm == 'in':
                ng = 20
            for gi in range(ng):
                t_, k_ = wload_cast(wt_[l][:, gi * 256:(gi + 1) * 256])
                sc.dma('sp', wsc[l, g0 + gi, :, 0:K, :], t_[:, 0:K, :], r=[k_], w=['wsc%d_%d' % (l, g0 + gi)])
        for fg in range(8):
            for br, (nm, wt_, K, koff) in enumerate(WPACK):
                c0 = 5120 + br * 2048 + fg * 256
                ta_, ka_ = wload_cast(w_in[l][:, c0:c0 + 256])
                tp_, kp_ = wload_cast(wt_[l][:, fg * 256:(fg + 1) * 256])
                for j in range(2):
                    g2 = (fg * 3 + br) * 2 + j
                    sc.dma('sp', wsc2[l, g2, :, 0:16, :], ta_[:, 0:16, j * 128:(j + 1) * 128], r=[ka_], w=['wsc2_%d_%d' % (l, g2)])
                    sc.dma('sp', wsc2[l, g2, :, 16:16 + K, :], tp_[:, 0:K, j * 128:(j + 1) * 128], r=[kp_], w=['wsc2_%d_%d' % (l, g2)])

    def wload2(l, g2, K):
        i = wslot[0] % 3
        wslot[0] += 1
        v = wb[i][:].rearrange("p k c -> p (k c)")[:, 0:24 * 128].rearrange("p (k c) -> p k c", c=128)
        sc.dma('sp', v[:, 0:16 + K, :], wsc2[l, g2, :, 0:16 + K, :], r=['wsc2_%d_%d' % (l, g2)], w=['wb%d' % i])
        return v, 'wb%d' % i

    def wload(nm, l, gi, slot=None):
        wt_, K, g0 = WNAMES[nm]
        if slot is None:
            i = wslot[0] % 3
            wslot[0] += 1
        else:
            i = slot
        sc.dma('sp', wb[i][:, 0:K, :], wsc[l, g0 + gi, :, 0:K, :], r=['wsc%d_%d' % (l, g0 + gi)], w=['wb%d' % i])
        return wb[i], 'wb%d' % i

    def gelu_evac(out, pin, pk, shape, tix, rextra=(), wk=()):
        n = 1
        for s_ in shape[1:]:
            n *= s_
        t1 = tf[tix][0:shape[0], 0:n]
        t2 = tf[tix + 1][0:shape[0], 0:n]
        k1, k2 = 'tf%d' % tix, 'tf%d' % (tix + 1)
        pin2 = pin
        act(t1, pin2, AF.Square, [pk], [k1])
        ts('dve', t1, t1, GC2, ALU.mult, [k1], [k1], s2=1.0, op1=ALU.add)
        tt('dve', t1, t1, pin2, ALU.mult, [k1, pk], [k1])
        act(t2, t1, AF.Sigmoid, [k1], [k2], scale=GC1)
        tt('dve', out, t2, pin2, ALU.mult, [k2, pk] + list(rextra), list(wk))

    def wrap(out, x, kx, ko):
        mset('dve', p4[:], 0.0, ['p4'])
        for m in range(1, 9):
            ts('dve', p3[:], x, (2 * m - 1) * PI, ALU.is_gt, [kx], ['p3'])
            tt('dve', p4[:], p4[:], p3[:], ALU.add, ['p3', 'p4'], ['p4'])
        ts('dve', p4[:], p4[:], -2.0 * PI, ALU.mult, ['p4'], ['p4'])
        tt('dve', out, x, p4[:], ALU.add, [kx, 'p4'], [ko])

    def cmul(ore, oim, are, aim, bre, bim, keys_r, ko):
        tt('dve', p1[:], are, bre, ALU.mult, keys_r, ['p1'])
        tt('dve', p2[:], aim, bim, ALU.mult, keys_r, ['p2'])
        tt('dve', p3[:], are, bim, ALU.mult, keys_r, ['p3'])
        tt('dve', p4[:], aim, bre, ALU.mult, keys_r, ['p4'])
        tt('dve', ore, p1[:], p2[:], ALU.subtract, ['p1', 'p2'], ko)
        tt('dve', oim, p3[:], p4[:], ALU.add, ['p3', 'p4'], ko)

    def mix(out, g0, ng, Wre_, Wim_, Xre, Xim, kr, ko):
        for h in (0, 1):
            P = slice(h * 64, h * 64 + 64)
            wr = bc(Wre_[P, g0:g0 + ng].unsqueeze(2), [64, ng, 16])
            wi = bc(Wim_[P, g0:g0 + ng].unsqueeze(2), [64, ng, 16])
            xa_ = Xre[P, 0:ng, :] if h == 0 else Xim[P, 0:ng, :]
            xb_ = Xim[P, 0:ng, :] if h == 0 else Xre[P, 0:ng, :]
            tt('dve', bt1[P, 0:ng, :], xa_, wr, ALU.mult, kr, ['bt1'])
            tt('dve', bt2[P, 0:ng, :], xb_, wi, ALU.mult, kr, ['bt2'])
            tt('dve', out[P], bt1[P, 0:ng, :], bt2[P, 0:ng, :], ALU.subtract if h == 0 else ALU.add,
               ['bt1', 'bt2'], ko)

    EF = [[t + 1 for t in range(8)], [8 - t for t in range(8)]]

    def layer_prep(l):
        sc.dma('act', badaT[:], b_ada[l].rearrange("(c p) -> p c", p=128), w=['badaT'], slow=True)
        pm, pmk = bank()
        for gi in range(24):
            wt, wk = wload_cast(w_ada[l][:, gi * 256:(gi + 1) * 256])
            for j in range(2):
                ch = gi * 2 + j
                for k in range(KT):
                    mm(pm[:, ch * 2:ch * 2 + 2], wt[:, k, j * 128:(j + 1) * 128], scT[:, :, k],
                       k == 0, k == KT - 1, [wk, 'scT'], [pmk])
        tt('dve', modT[:], pm[:, 0:96].rearrange("p (c t) -> p c t", t=2), bc(badaT[:].unsqueeze(2), [128, 48, 2]),
           ALU.add, [pmk, 'badaT'], ['modT'])
        ts('dve', sc1[:], modT[:, 16:32, :], 1.0, ALU.add, ['modT'], ['sc1'])
        for c_ in range(2):
            sc.dma('act', gsc[c_].rearrange("(k p) -> p k", p=128), modT[:, 32:48, c_], r=['modT'], w=['gsc'], slow=True)
        sc.dma('act', sgb[:], sgu_g[l].partition_broadcast(128), w=['sgb'])
        sc.dma('act', sbb[:], sgu_b[l].partition_broadcast(128), w=['sbb'])
        sc.dma('act', bsb[:], b_s[l].partition_broadcast(128), w=['bsb'])
        sc.dma('act', wsn[:], w_s[l].rearrange("g p q -> p g q"), w=['tf0'])
        pw, pwk = bank()
        for g in range(4):
            tr(pw[:, g * 128:(g + 1) * 128], wsn[:, g, :], ident[:], ['tf0', 'ident'], [pwk])
        cp('dve', wsT[:], pw[:, 0:512].rearrange("p (g q) -> p g q", q=128), [pwk], ['wsT'])
        sc.dma('act', esink[:], sink[l].partition_broadcast(128), w=['esink'])
        act(esink[:], esink[:], AF.Exp, ['esink'], ['esink'])
        sc.dma('act', ckn[:], ck[l].rearrange("(b p) c -> p b c", p=128), w=['tf1'])
        pc_, pck = bank()
        for kvh in range(2):
            for b_ in range(2):
                tr(pc_[:, (kvh * 2 + b_) * 128:(kvh * 2 + b_ + 1) * 128], ckn[:, b_, kvh * 128:(kvh + 1) * 128],
                   ident[:], ['tf1', 'ident'], [pck])
        cp('dve', ckT[:], pc_[:, 0:512].rearrange("p (h t) -> p h t", t=256), [pck], ['ckT'])
        sc.dma('pool', cvb[:], cv[l].rearrange("(b p) c -> p b c", p=128), w=['cvb'])
        sc.dma('pool', wglu[:], w_glu[l].rearrange("(j p) c -> p j c", p=128), w=['wglu'])
        sc.dma('act', bglu[:], b_glu[l].rearrange("(j p) -> p j", p=128), w=['bglu'], slow=True)
        sc.dma('act', dsk[:], ssm_d[l].partition_broadcast(32), w=['dsk'])
        for d in range(2):
            for h in range(2):
                P = slice(h * 64, h * 64 + 64)
                sc.dma('act', lr[P, :], lam_re[l, d].rearrange("g p -> p g"), w=['lr'], slow=True)
                sc.dma('act', li[P, :], lam_im[l, d].rearrange("g p -> p g"), w=['li'], slow=True)
                for ri in range(2):
                    sc.dma('act', sinit[ri * 64:(ri + 1) * 64, d, :] if h == 0 else sinitw[(1 - ri) * 64:(2 - ri) * 64, d, :],
                           st0[l, d, ri].rearrange("g p -> p g"), w=['sinit' if h == 0 else 'sinitw'], slow=True)
            sc.dma('act', dtt[:], log_step[l, d].partition_broadcast(128), w=['dtt'])
            act(dtt[:], dtt[:], AF.Exp, ['dtt'], ['dtt'])
            tt('dve', p1[:], li[:], dtt[:], ALU.mult, ['li', 'dtt'], ['p1'])
            wrap(p2[:], p1[:], 'p1', 'p2')
            act(sinv[:], p2[:], AF.Sin, ['p2'], ['sinv'])
            if stop == 'wrap':
                return
            ts('dve', p1[:], p1[:], PI / 2, ALU.add, ['p1'], ['p1'])
            wrap(p2[:], p1[:], 'p1', 'p2')
            act(cosv[:], p2[:], AF.Sin, ['p2'], ['cosv'])
            tt('dve', p1[:], lr[:], dtt[:], ALU.mult, ['lr', 'dtt'], ['p1'])
            act(p2[:], p1[:], AF.Exp, ['p1'], ['p2'])
            act(p3[:], p1[:], AF.Exp, ['p1'], ['p3'], scale=-1.0)
            mset('dve', PWr[:, 8, :], 1.0, ['PW', 'nPW', 'xin1'])
            mset('dve', PWi[:, 8, :], 0.0, ['PW'])
            tt('dve', PWr[:, 9, :], p2[:], cosv[:], ALU.mult, ['p2', 'cosv'], ['PW'])
            tt('dve', PWi[:, 9, :], p2[:], sinv[:], ALU.mult, ['p2', 'sinv'], ['PW'])
            tt('dve', PWr[:, 7, :], p3[:], cosv[:], ALU.mult, ['p3', 'cosv'], ['PW'])
            tt('dve', PWi[:, 7, :], p3[:], sinv[:], ALU.mult, ['p3', 'sinv'], ['PW'])
            ts('dve', PWi[:, 7, :], PWi[:, 7, :], -1.0, ALU.mult, ['PW'], ['PW'])
            for k in range(2, 9):
                cmul(PWr[:, 8 + k, :], PWi[:, 8 + k, :], PWr[:, 7 + k, :], PWi[:, 7 + k, :], PWr[:, 9, :], PWi[:, 9, :], ['PW'], ['PW'])
                cmul(PWr[:, 8 - k, :], PWi[:, 8 - k, :], PWr[:, 9 - k, :], PWi[:, 9 - k, :], PWr[:, 7, :], PWi[:, 7, :], ['PW'], ['PW'])
            ts('dve', nPWi[:], PWi[:], -1.0, ALU.mult, ['PW'], ['nPW'])
            tt('dve', p1[:], lr[:], lr[:], ALU.mult, ['lr'], ['p1'])
            tt('dve', p2[:], li[:], li[:], ALU.mult, ['li'], ['p2'])
            tt('dve', p1[:], p1[:], p2[:], ALU.add, ['p1', 'p2'], ['p1'])
            recip(p1[:], p1[:], ['p1'], ['p1'])
            ts('dve', p2[:], PWr[:, 9, :], -1.0, ALU.add, ['PW'], ['p2'])
            tt('dve', p3[:], p2[:], lr[:], ALU.mult, ['p2', 'lr'], ['p3'])
            tt('dve', p4[:], PWi[:, 9, :], li[:], ALU.mult, ['PW', 'li'], ['p4'])
            tt('dve', p3[:], p3[:], p4[:], ALU.add, ['p3', 'p4'], ['p3'])
            tt('dve', fre[:], p3[:], p1[:], ALU.mult, ['p3', 'p1'], ['fre'])
            tt('dve', p3[:], PWi[:, 9, :], lr[:], ALU.mult, ['PW', 'lr'], ['p3'])
            tt('dve', p4[:], p2[:], li[:], ALU.mult, ['p2', 'li'], ['p4'])
            tt('dve', p3[:], p3[:], p4[:], ALU.subtract, ['p3', 'p4'], ['p3'])
            tt('dve', fim[:], p3[:], p1[:], ALU.mult, ['p3', 'p1'], ['fim'])
            cp('dve', Ar[:, d * 32:(d + 1) * 32], PWr[:, 16, :], ['PW'], ['Ar'])
            ts('dve', Aix[:, d * 32:(d + 1) * 32], PWi[:, 16, :], sgn[:, 0:1], ALU.mult, ['PW', 'sgn'], ['Aix'])
            ts('dve', nAix[:, d * 32:(d + 1) * 32], Aix[:, d * 32:(d + 1) * 32], -1.0, ALU.mult, ['Aix'], ['nAix'])
            cp('dve', A2r[:, d * 32:(d + 1) * 32], PWr[:, 16, :], ['PW'], ['A2'])
            cp('dve', A2i[:, d * 32:(d + 1) * 32], PWi[:, 16, :], ['PW'], ['A2'])
            cm = cmf if d == 0 else cmb
            cmk = 'cmf' if d == 0 else 'cmb'
            for gb in range(4):
                g0 = gb * 8
                for h in range(2):
                    P = slice(h * 64, h * 64 + 64)
                    sc.dma('act', Bre[P], b_re[l, d, g0:g0 + 8].rearrange("g p c -> p g c"), w=['Bre'])
                    sc.dma('act', Bim[P], b_im[l, d, g0:g0 + 8].rearrange("g p c -> p g c"), w=['Bim'])
                for (src_c, dst_c, neg) in ((c_re, Cre, False), (c_im, nCim, True)):
                    for h in range(2):
                        sc.dma('act', cnat[:, h, :], src_c[l, d, g0:g0 + 8].rearrange("g c p -> (g c) p"), w=['cnat'])
                    pb_, pbk = bank()
                    tr(pb_[:, 0:128], cnat[:].rearrange("q h p -> q (h p)"), ident[:], ['cnat', 'ident'], [pbk])
                    if neg:
                        ts('dve', dst_c[:], pb_[:, 0:128].rearrange("p (g c) -> p g c", c=16), -1.0, ALU.mult, [pbk], ['C'])
                    else:
                        cp('dve', dst_c[:], pb_[:, 0:128].rearrange("p (g c) -> p g c", c=16), [pbk], ['C'])
                fr_b = bc(fre[:, g0:g0 + 8].unsqueeze(2), [128, 8, 16]); fi_b = bc(fim[:, g0:g0 + 8].unsqueeze(2), [128, 8, 16])
                tt('dve', bt1[:], Bre[:], fr_b, ALU.mult, ['Bre', 'fre'], ['bt1'])
                tt('dve', bt2[:], Bim[:], fi_b, ALU.mult, ['Bim', 'fim'], ['bt2'])
                tt('dve', Bbr[:], bt1[:], bt2[:], ALU.subtract, ['bt1', 'bt2'], ['Bb'])
                tt('dve', bt1[:], Bim[:], fr_b, ALU.mult, ['Bim', 'fre'], ['bt1'])
                tt('dve', bt2[:], Bre[:], fi_b, ALU.mult, ['Bre', 'fim'], ['bt2'])
                tt('dve', Bbi[:], bt1[:], bt2[:], ALU.add, ['bt1', 'bt2'], ['Bb'])
                for t in range(8):
                    e = EF[d][t]
                    mix(Fg[:, :, t * 16:(t + 1) * 16], g0, 8, PWr[:, 8 + e, :], nPWi[:, 8 + e, :], Cre, nCim, ['PW', 'nPW', 'C'], ['V'])
                    mix(Gg[:, :, t * 16:(t + 1) * 16], g0, 8, PWr[:, 8 - e, :], PWi[:, 8 - e, :], Bbr, Bbi, ['PW', 'Bb'], ['V'])
                    mix(Eg[:, :, t * 16:(t + 1) * 16], g0, 8, PWr[:, 16 - e, :], PWi[:, 16 - e, :], Bbr, Bbi, ['PW', 'Bb'], ['Vs'])
                cp('dve', w3[:, :, 0, :], Fg[:], ['V'], ['s5wb0'])
                for gl in range(8):
                    pb_, pbk = bank()
                    tr(pb_[:, 0:128], Eg[:, gl, :], ident[:], ['Vs', 'ident'], [pbk])
                    mm(pb_[:, 128:256], Gg[:, gl, :], Fg[:, gl, :], True, True, ['V', 'V'], [pbk])
                    cp('dve', w3[:, gl, 1, :], pb_[:, 0:128], [pbk], ['s5wb0'])
                    tt('dve', w3[:, gl, 2, :], pb_[:, 128:256], cm[:], ALU.mult, [pbk, cmk], ['s5wb0'])
                sc.dma('act', s5w[d * 32 + g0:d * 32 + g0 + 8].rearrange("g t k m -> k g t m"), w3[:], r=['s5wb0'], w=['s5w'])
        TRv = rr[:].rearrange("p (g n) -> p g n", n=32)
        TIv = tfall[:].rearrange("p a c -> p (a c)").rearrange("p (g n) -> p g n", n=32)
        tkeys = ['rr', 'tf0', 'tf1', 'tf2', 'tf3']
        mset('dve', TRv[:, 0:32, 31:32], 1.0, tkeys)
        mset('dve', TIv[:, 0:32, 31:32], 0.0, tkeys)
        mset('dve', TRv[:, 32:64, 0:1], 1.0, tkeys)
        mset('dve', TIv[:, 32:64, 0:1], 0.0, tkeys)
        for it_ in range(5):
            m = 1 << it_
            for d in range(2):
                G = slice(d * 32, d * 32 + 32)
                if d == 0:
                    srcs, dsts = slice(32 - m, 32), slice(32 - 2 * m, 32 - m)
                else:
                    srcs, dsts = slice(0, m), slice(m, 2 * m)
                amr = bc(A2r[:, G].unsqueeze(2), [128, 32, m])
                ami = bc(A2i[:, G].unsqueeze(2), [128, 32, m])
                t1_ = V[:, 0:32, 0:m]
                t2_ = V[:, 32:64, 0:m]
                tt('dve', t1_, TRv[:, G, srcs], amr, ALU.mult, tkeys + ['A2'], ['V'])
                tt('dve', t2_, TIv[:, G, srcs], ami, ALU.mult, tkeys + ['A2'], ['V'])
                tt('dve', TRv[:, G, dsts], t1_, t2_, ALU.subtract, ['V'], tkeys)
                tt('dve', t1_, TRv[:, G, srcs], ami, ALU.mult, tkeys + ['A2'], ['V'])
                tt('dve', t2_, TIv[:, G, srcs], amr, ALU.mult, tkeys + ['A2'], ['V'])
                tt('dve', TIv[:, G, dsts], t1_, t2_, ALU.add, ['V'], tkeys)
            tt('dve', tA[:], A2r[:], A2r[:], ALU.mult, ['A2'], ['tA'])
            tt('dve', tB[:], A2i[:], A2i[:], ALU.mult, ['A2'], ['tB'])
            tt('dve', tA[:], tA[:], tB[:], ALU.subtract, ['tA', 'tB'], ['tA'])
            tt('dve', tB[:], A2r[:], A2i[:], ALU.mult, ['A2'], ['tB'])
            cp('dve', A2r[:], tA[:], ['tA'], ['A2'])
            ts('dve', A2i[:], tB[:], 2.0, ALU.mult, ['tB'], ['A2'])
        ts('dve', TIv[:], TIv[:], sgn[:, 0:1], ALU.mult, tkeys + ['sgn'], tkeys)
        ts('dve', A2ix[:], A2i[:], sgn[:, 0:1], ALU.mult, ['A2', 'sgn'], ['A2x'])
        ts('dve', nA2ix[:], A2ix[:], -1.0, ALU.mult, ['A2x'], ['A2x'])

    xslot = [0]

    def ln_ht(src, rows, col0, cond, rkeys=()):
        for si, r0 in enumerate(rows):
            b = xslot[0] % 2
            xslot[0] += 1
            xk = 'xin%d' % b
            xt = xin[b]
            xw = [xk] if b == 0 else [xk, 'PW', 'nPW']
            if isinstance(r0, tuple):
                sc.dma('pool', xt[:], src, r=['ridx'] + list(rkeys), w=xw, ind=ridx[:, r0[1]:r0[1] + 1])
            else:
                sc.dma('sp', xt[:], src[r0:r0 + 128, :], r=list(rkeys), w=xw)
            for q in range(4):
                sc.op('dve', lambda e, q=q, xt=xt: e.bn_stats(out=stats[:, q, :], in_=xt[:, q * 512:(q + 1) * 512]), [xk], ['stats'])
            sc.op('dve', lambda e: e.bn_aggr(out=mv[:], in_=stats[:].rearrange("p a b -> p (a b)")), ['stats'], ['mv'])
            act(rstd[:], mv[:, 1:2], AF.Sqrt, ['mv'], ['rstd'], bias=EPS)
            recip(rstd[:], rstd[:], ['rstd'], ['rstd'])
            ts('dve', xt[:], xt[:], mv[:, 0:1], ALU.subtract, [xk, 'mv', 'rstd'], [xk], s2=rstd[:, 0:1], op1=ALU.mult)
            c0 = col0 + si * 128
            for kq in range(4):
                pb_, pbk = bank()
                for kk_ in range(4):
                    k = kq * 4 + kk_
                    tr(pb_[:, kk_ * 128:(kk_ + 1) * 128], xt[:, k * 128:(k + 1) * 128], ident[:], [xk, 'ident'], [pbk])
                for kk_ in range(4):
                    k = kq * 4 + kk_
                    if k % 2 == 0:
                        act(hT[:, k, c0:c0 + 128], pb_[:, kk_ * 128:(kk_ + 1) * 128], AF.Identity, [pbk, 'sc1', 'modT'], ['hT'],
                            bias=modT[:, k, cond:cond + 1], scale=sc1[:, k, cond:cond + 1])
                    else:
                        ts('dve', hT[:, k, c0:c0 + 128], pb_[:, kk_ * 128:(kk_ + 1) * 128], sc1[:, k, cond:cond + 1], ALU.mult,
                           [pbk, 'sc1', 'modT'], ['hT'], s2=modT[:, k, cond:cond + 1], op1=ALU.add)

    def proj_xa(l, cc):
        for gi in range(2):
            wt, wk = wload('in', l, gi)
            for t in range(8):
                pb_, pbk = bank()
                for k in range(KT):
                    mm(pb_[0:32, 0:256], hT[:, k, cc + t:cc + 256:8], wt[:, k, :], k == 0, k == KT - 1, [wk, 'hT'], [pbk])
                cp('dve' if t % 2 == 0 else 'act', Xp[:, gi * 16:(gi + 1) * 16, t, :],
                   pb_[0:32, 0:256].rearrange("p (g c) -> p g c", c=16), [pbk], ['Xp'])

    def s5_states(t_init, zero_init, reng='dve', sel=False, rec=True):
        for gq in range(4):
            pb_, pbk = bank()
            for gl in range(8):
                g = gq * 8 + gl
                mm(pb_[:, gl * 32:(gl + 1) * 32], Xp[:, g, :, :].rearrange("p t c -> p (t c)"), identb[0:32, 0:32], True, True,
                   ['Xp', 'identb'], [pbk])
            cp('act', Xpp[:, gq * 8:(gq + 1) * 8, :], pb_[:, 0:256].rearrange("p (g n) -> p g n", n=32), [pbk], ['Xpp'])
        for d in range(2):
            for gq in range(4):
                i = (d * 4 + gq) % 2
                wk = 's5wb%d' % i
                sc.dma('sp', s5wb[i][:, :, 1, :], s5w[d * 32 + gq * 8:d * 32 + gq * 8 + 8, 1].rearrange("g k m -> k g m"),
                       r=['s5w'], w=[wk])
                pv, pvk = bank()
                pw_, pwk = bank()
                for gl in range(8):
                    g = gq * 8 + gl
                    mm(pv[:, gl * 32:(gl + 1) * 32], s5wb[i][:, gl, 1, :], Xpp[:, g, :], True, True, [wk, 'Xpp'], [pvk])
                    mm(pw_[0:64, gl * 32:(gl + 1) * 32], s5wb[i][:, gl, 1, 64:128], Xpp[:, g, :], True, True, [wk, 'Xpp'], [pwk])
                    mm(pw_[64:128, gl * 32:(gl + 1) * 32], s5wb[i][:, gl, 1, 0:64], Xpp[:, g, :], True, True, [wk, 'Xpp'], [pwk])
                gd0 = d * 32 + gq * 8
                cp('dve', V[:, gd0:gd0 + 8, :], pv[:, 0:256].rearrange("p (g n) -> p g n", n=32), [pvk], ['V'])
                cp('act', Vs[:, gd0:gd0 + 8, :], pw_[:, 0:256].rearrange("p (g n) -> p g n", n=32), [pwk], ['Vs'])
        for d in (range(2) if rec else ()):
            G = slice(d * 32, d * 32 + 32)
            order = list(range(32)) if d == 0 else list(range(31, -1, -1))
            for step, n_ in enumerate(order):
                if step == 0:
                    if zero_init:
                        continue
                    if sel:
                        pS = Psel[:, d, 0, :]
                        pW = Psel[:, d, 1, :]
                        kp = ['Psel']
                    else:
                        pS = Pst[:, t_init + (0 if d == 0 else 1), 0, G]
                        pW = Pst[:, t_init + (0 if d == 0 else 1), 1, G]
                        kp = ['Pst']
                else:
                    pn = order[step - 1]
                    pS = V[:, G, pn]
                    pW = Vs[:, G, pn]
                    kp = ['V', 'Vs']
                tt(reng, tA[:, 0:32], pS, Ar[:, G], ALU.mult, kp + ['Ar'], ['tA'])
                tt(reng, tB[:, 0:32], pW, Aix[:, G], ALU.mult, kp + ['Aix'], ['tB'])
                tt(reng, tA[:, 0:32], tA[:, 0:32], tB[:, 0:32], ALU.add, ['tA', 'tB'], ['tA'])
                tt(reng, tA[:, 32:64], pW, Ar[:, G], ALU.mult, kp + ['Ar'], ['tA2'])
                tt(reng, tB[:, 32:64], pS, nAix[:, G], ALU.mult, kp + ['nAix'], ['tB2'])
                tt(reng, tA[:, 32:64], tA[:, 32:64], tB[:, 32:64], ALU.add, ['tA2', 'tB2'], ['tA2'])
                tt(reng, V[:, G, n_], V[:, G, n_], tA[:, 0:32], ALU.add, ['V', 'tA'], ['V'])
                tt(reng, Vs[:, G, n_], Vs[:, G, n_], tA[:, 32:64], ALU.add, ['Vs', 'tA2'], ['Vs'])

    def s5_main(l, t_init, zero_init, seq, sel=False):
        if zero_init:
            mset('dve', Sb[:, 0:32, 0:1], 0.0, ['Sb'])
            mset('dve', Sb[:, 32:64, 31:32], 0.0, ['Sb'])
        elif sel:
            cp('dve', Sb[:, 0:32, 0:1], Psel[:, 0, 0, :].unsqueeze(2), ['Psel'], ['Sb'])
            cp('dve', Sb[:, 32:64, 31:32], Psel[:, 1, 0, :].unsqueeze(2), ['Psel'], ['Sb'])
        else:
            cp('dve', Sb[:, 0:32, 0:1], Pst[:, t_init, 0, 0:32].unsqueeze(2), ['Pst'], ['Sb'])
            cp('dve', Sb[:, 32:64, 31:32], Pst[:, t_init + 1, 0, 32:64].unsqueeze(2), ['Pst'], ['Sb'])
        cp('dve', Sb[:, 0:32, 1:32], V[:, 0:32, 0:31], ['V'], ['Sb'])
        cp('act', Sb[:, 32:64, 0:31], V[:, 32:64, 1:32], ['V'], ['Sb'])
        if seq is not None:
            for d in range(2):
                pb_, pbk = bank()
                src_ = V[:, 0:32, 31] if d == 0 else V[:, 32:64, 0]
                cp('dve', tf[0][:, 0:32], src_, ['V'], ['tf0'])
                tr(pb_[0:32, 0:128], tf[0][:, 0:32], ident[:], ['tf0', 'ident'], [pbk])
                cp('dve', stg[:], pb_[0:32, 0:128], [pbk], ['stg'])
                sc.dma('pool', nst[seq, l, d].rearrange("r g p -> g r p"), stg[:].rearrange("g (r p) -> g r p", r=2),
                       r=['stg'], w=['nst'])
        for gq in range(4):
            p0, p0k = bank()
            p1_, p1k = bank()
            for d in range(2):
                for t_ in (0, 2):
                    sc.dma('sp', s5wb[d][:, :, t_, :], s5w[d * 32 + gq * 8:d * 32 + gq * 8 + 8, t_].rearrange("g k m -> k g m"),
                           r=['s5w'], w=['s5wb%d' % d])
            for gl in range(8):
                g = gq * 8 + gl
                pp, ppk = (p0, p0k) if gl < 4 else (p1_, p1k)
                o = pp[0:32, (gl % 4) * 128:(gl % 4 + 1) * 128]
                for d in range(2):
                    wk = 's5wb%d' % d
                    mm(o, Xpp[:, g, :], s5wb[d][:, gl, 2, :], d == 0, False, [wk, 'Xpp'], [ppk])
                    mm(o, Sb[:, d * 32 + g, :], s5wb[d][:, gl, 0, :], False, d == 1, [wk, 'Sb'], [ppk])
            for hf, (pp, ppk) in enumerate(((p0, p0k), (p1_, p1k))):
                g0 = gq * 8 + hf * 4
                yv = ygf[:]
                dsl = bc(dsk[:, g0 * 16:(g0 + 4) * 16].rearrange("p (g c) -> p g c", c=16).unsqueeze(2), [32, 4, 8, 16])
                tt('dve', yv, Xp[:, g0:g0 + 4, :, :], dsl, ALU.mult, ['Xp', 'dsk'], ['ygf'])
                tt('dve', yv, yv, pp[0:32, 0:512].rearrange("p (g t c) -> p g t c", t=8, c=16), ALU.add, ['ygf', ppk], ['ygf'])
                yf = ygf[:].rearrange("p g t c -> p (g t c)")
                y2 = ygf2[:].rearrange("p g t c -> p (g t c)")
                act(y2, yf, AF.Square, ['ygf'], ['ygf2'])
                ts('dve', y2, y2, GC2, ALU.mult, ['ygf2'], ['ygf2'], s2=1.0, op1=ALU.add)
                tt('dve', y2, y2, yf, ALU.mult, ['ygf2', 'ygf'], ['ygf2'])
                act(y2, y2, AF.Sigmoid, ['ygf2'], ['ygf2'], scale=GC1)
                tt('dve', yg[:, :, g0:g0 + 4, :].rearrange("p t g c -> p g t c"), ygf2[:], ygf[:], ALU.mult, ['ygf2', 'ygf'], ['yg'])
        for j in range(4):
            pb_, pbk = bank()
            for t in range(8):
                mm(pb_[:, t * 32:(t + 1) * 32], yg[:, t, j * 8:(j + 1) * 8, :].rearrange("p g c -> p (g c)"), identb[0:32, 0:32], True, True, ['yg', 'identb'], [pbk])
            cp('dve' if j % 2 == 0 else 'act', ygT[:, j, :].rearrange("p (n t) -> p t n", t=8),
               pb_[:, 0:256].rearrange("p (t n) -> p t n", n=32), [pbk], ['ygT'])
        for jo in range(4):
            pb_, pbk = bank()
            for j in range(4):
                mm(pb_[:, 0:NT], wglu[:, j, jo * 128:(jo + 1) * 128], ygT[:, j, :], j == 0, j == 3, ['wglu', 'ygT'], [pbk])
            act(tf[0][:, 0:NT], pb_[:, 0:NT], AF.Sigmoid, [pbk, 'bglu'], ['tf0'], bias=bglu[:, jo:jo + 1])
            tt('dve', tf[0][:, 0:NT], tf[0][:, 0:NT], ygT[:, jo, :], ALU.mult, ['tf0', 'ygT'], ['tf0'])
            tt('dve', yaT[:, jo, :], tf[0][:, 0:NT], zaT[:, jo, :], ALU.mult, ['tf0', 'zaT'], ['zaT'])

    def proj_fm(l, gi, nchunk, cols, evac):
        wt, wk = wload('in', l, gi)
        for j in range(nchunk):
            pb_, pbk = bank()
            n = cols.stop - cols.start
            for k in range(KT):
                mm(pb_[:, 0:n], wt[:, k, j * 128:(j + 1) * 128], hT[:, k, cols], k == 0, k == KT - 1, [wk, 'hT'], [pbk])
            evac(gi, j, pb_[:, 0:n], pbk)

    def rope(out, pin, pk, n, okey):
        cp('act', qraw[:, 0:n], pin, [pk], ['pTs4'])
        pr, prk = bank()
        mm(pr[:, 0:n], rotT[:], qraw[:, 0:n], True, True, ['rotT', 'pTs4'], [prk])
        tt('dve', tf[2][:, 0:n], pin, rc[:, 0:n], ALU.mult, [pk, 'rc'], ['tf2'])
        tt('dve', tf[3][:, 0:n], pr[:, 0:n], rs[:, 0:n], ALU.mult, [prk, 'rs'], ['tf3'])
        tt('dve', out, tf[2][:, 0:n], tf[3][:, 0:n], ALU.add, ['tf2', 'tf3'], [okey])

    def attn_finish(pvb, pvk, pdb, pdk, heads, hw):
        n_ = len(heads) * hw
        cp('act', tf[1][:, 0:n_], pvb[:, 0:n_], [pvk], ['tf1'])
        cp('act', tf[2][:, 0:n_], pdb[:, 0:n_], [pdk], ['tf2'])
        for hi, h in enumerate(heads):
            cs = slice(hi * hw, (hi + 1) * hw)
            ts('dve', tf[0][:, 0:hw], tf[2][:, cs], esink[:, h:h + 1], ALU.add, ['tf2', 'esink'], ['tf0'])
            recip(tf[0][:, 0:hw], tf[0][:, 0:hw], ['tf0'], ['tf0'])
            tt('dve', tf[0][:, 0:hw], tf[0][:, 0:hw], tf[1][:, cs], ALU.mult, ['tf0', 'tf1'], ['tf0'])
            yield h, tf[0][:, 0:hw]

    import os as _os

    def tile_main(l, kind, idx):
        cond = 0 if kind == 'p' else 1
        own = kind == 'o'
        src = (xp if l == 0 else zp) if kind == 'p' else (xs if l == 0 else zs)
        dst = (zp if l == 0 else yp) if kind == 'p' else (zs if l == 0 else ys)
        rkeys = [] if l == 0 else ['dst_p_0' if kind == 'p' else 'dst_s_0']
        dkey = 'dst_%s_%d' % ('p' if kind == 'p' else 's', l)
        t0 = idx * NT
        if kind == 'p':
            rows = [t0, t0 + 128]
            cc = 0
            E = 256
        elif own:
            rows = [('ind', 2 * idx + b_) for b_ in range(4)]
            cc = 128
            E = 512
            sc.dma('pool', rc[:, 0:E], c_ropec_o[:, 256 * idx:256 * idx + 512], w=['rc'])
            sc.dma('pool', rs[:, 0:E], c_ropes_o[:, 256 * idx:256 * idx + 512], w=['rs'])
            for d in range(2):
                for w_ in range(2):
                    G = slice(d * 32, d * 32 + 32)
                    ts('dve', Psel[:, d, w_, :], Pst[:, idx + d, w_, G], oh4[:, 0:1], ALU.mult, ['Pst', 'oh4'], ['Psel'])
                    for j_ in range(1, 4):
                        sc.op('dve', lambda e, d=d, w_=w_, j_=j_, G=G: e.scalar_tensor_tensor(
                            out=Psel[:, d, w_, :], in0=Pst[:, 4 * j_ + idx + d, w_, G], scalar=oh4[:, j_:j_ + 1], in1=Psel[:, d, w_, :],
                            op0=ALU.mult, op1=ALU.add), ['Pst', 'oh4', 'Psel'], ['Psel'])
        else:
            lo = t0 - 128 if idx > 0 else t0
            hi = t0 + NT + 128 if idx < 15 else t0 + NT
            rows = list(range(lo, hi, 128))
            cc = t0 - lo
            E = hi - lo
            sc.dma('pool', rc[:, 0:E], c_ropec[:, lo:hi], w=['rc'])
            sc.dma('pool', rs[:, 0:E], c_ropes[:, lo:hi], w=['rs'])
        ln_ht(src, rows, 0, cond, rkeys)
        ctr = slice(cc, cc + NT)
        proj_xa(l, cc)
        scale = 128 ** -0.5

        def ev(gi, j, pin, pk):
            if gi in (2, 3):
                act(zaT[:, (gi - 2) * 2 + j, :], pin, AF.Silu, [pk], ['zaT'])
            elif 4 <= gi <= 7:
                hh = (gi - 4) * 2 + j
                if kind == 'p' or 'rope' in '':
                    cp('act', qT[:, hh, :], pin, [pk], ['qT'])
                else:
                    rope_q(hh, pin, pk)
            elif gi == 8:
                if kind == 'p' or 'rope' in '':
                    cp('act', kT[:, j, 0:E], pin, [pk], ['kT'])
                else:
                    rope(kT[:, j, 0:E], pin, pk, E, 'kT')
            elif 10 <= gi <= 13:
                act(zbT[:, (gi - 10) * 2 + j, :], pin, AF.Silu, [pk], ['zbT'])
            elif gi in (14, 15):
                gelu_evac(uT[:, (gi - 14) * 2 + j, :], pin, pk, [128, NT], 0, wk=['uT'])
            elif gi in (18, 19):
                act(zcT[:, (gi - 18) * 2 + j, :], pin, AF.Silu, [pk], ['zcT'])

        def rope_q(hh, pin, pk):
            cp('act', qraw[:, 0:NT], pin, [pk], ['pTs4'])
            pr, prk = bank()
            mm(pr[:, 0:NT], rotT[:], qraw[:, 0:NT], True, True, ['rotT', 'pTs4'], [prk])
            tt('dve', tf[2][:, 0:NT], pin, rc[:, ctr], ALU.mult, [pk, 'rc'], ['tf2'])
            tt('dve', tf[3][:, 0:NT], pr[:, 0:NT], rs[:, ctr], ALU.mult, [prk, 'rs'], ['tf3'])
            tt('dve', qT[:, hh, :], tf[2][:, 0:NT], tf[3][:, 0:NT], ALU.add, ['tf2', 'tf3'], ['qT'])

        for gi in (2, 3):
            proj_fm(l, gi, 2, ctr, ev)
        s5_states(idx if kind == 's' else 0, kind == 'p', reng='pool', sel=own)
        for gi in (4, 5, 6, 7):
            proj_fm(l, gi, 2, ctr, ev)
        proj_fm(l, 8, 2, slice(0, E), ev)
        for gi in (8, 9):
            if gi == 8 and kind != 'p':
                continue
            wt, wk = wload('in', l, gi)
            for s_ in range(E // 128):
                pb_, pbk = bank()
                for k in range(KT):
                    mm(pb_[:, 0:256], hT[:, k, s_ * 128:(s_ + 1) * 128], wt[:, k, :], k == 0, k == KT - 1, [wk, 'hT'], [pbk])
                if gi == 9:
                    cp('act', vtok[:, s_, :], pb_[:, 0:256], [pbk], ['vtok'])
                if kind == 'p':
                    o_ = nk if gi == 8 else nv
                    cp('dve', tf[2 + s_][:, 0:256], pb_[:, 0:256], [pbk], ['tf%d' % (2 + s_)])
                    sc.dma('pool', o_[idx, l, s_ * 128:(s_ + 1) * 128, :], tf[2 + s_][:, 0:256], r=['tf%d' % (2 + s_)], w=['nkv'])
        for gi in (10, 11, 12, 13, 14, 15):
            proj_fm(l, gi, 2, ctr, ev)
        for gi in (16, 17):
            wt, wk = wload('in', l, gi)
            for s_ in range(2):
                pb_, pbk = bank()
                for k in range(KT):
                    mm(pb_[:, 0:256], hT[:, k, cc + s_ * 128:cc + (s_ + 1) * 128], wt[:, k, :], k == 0, k == KT - 1, [wk, 'hT'], [pbk])
                gelu_evac(vsg[:, s_, (gi - 16) * 256:(gi - 15) * 256], pb_[:, 0:256], pbk, [128, 256], 0, wk=['rr'])
        for s_ in range(2):
            sc.op('dve', lambda e, s_=s_: e.bn_stats(out=stats[:, 0, :], in_=vsg[:, s_, :]), ['rr'], ['stats'])
            sc.op('dve', lambda e: e.bn_aggr(out=mv[:], in_=stats[:, 0, :]), ['stats'], ['mv'])
            act(rstd[:], mv[:, 1:2], AF.Sqrt, ['mv'], ['rstd'], bias=EPS)
            recip(rstd[:], rstd[:], ['rstd'], ['rstd'])
            ts('dve', vsg[:, s_, :], vsg[:, s_, :], mv[:, 0:1], ALU.subtract, ['rr', 'mv', 'rstd'], ['rr'], s2=rstd[:, 0:1], op1=ALU.mult)
            tt('dve', vsg[:, s_, :], vsg[:, s_, :], sgb[:], ALU.mult, ['rr', 'sgb'], ['rr'])
            tt('dve', vsln[:, s_, :], vsg[:, s_, :], sbb[:], ALU.add, ['rr', 'sbb'], ['vsln'])
        for gi in (18, 19):
            proj_fm(l, gi, 2, ctr, ev)
        for g in range(4):
            for s_ in range(2):
                pb_, pbk = bank()
                mm(pb_[:, 0:128], vsln[:, s_, g * 128:(g + 1) * 128], wsT[:, g, :], True, True, ['vsln', 'wsT'], [pbk])
                tt('dve', tf[1][:, 0:128], pb_[:, 0:128], bsb[:, g * 128:(g + 1) * 128], ALU.add, [pbk, 'bsb'], ['tf1'])
                tt('dve', tf[1][:, 0:128], tf[1][:, 0:128], uT[:, g, s_ * 128:(s_ + 1) * 128], ALU.mult, ['tf1', 'uT'], ['tf1'])
                tt('dve', ycT[:, g, s_ * 128:(s_ + 1) * 128], tf[1][:, 0:128], zcT[:, g, s_ * 128:(s_ + 1) * 128], ALU.mult,
                   ['tf1', 'zcT'], ['zcT'])
        if kind == 'p':
            for kvh in range(2):
                for kb in range(2):
                    pa, pak = bank()
                    pb2, pb2k = bank()
                    for h in range(4):
                        pp, ppk = (pa, pak) if h < 2 else (pb2, pb2k)
                        mm(pp[:, (h % 2) * 256:(h % 2 + 1) * 256], kT[:, kvh, kb * 128:(kb + 1) * 128], qT[:, kvh * 4 + h, :], True, True,
                           ['kT', 'qT'], [ppk])
                    act(pTs[kb * 2][:], pa[:, 0:512], AF.Exp, [pak], ['pTs%d' % (kb * 2)], scale=scale)
                    act(pTs[kb * 2 + 1][:], pb2[:, 0:512], AF.Exp, [pb2k], ['pTs%d' % (kb * 2 + 1)], scale=scale)
                for half in range(2):
                    pv, pvk = bank()
                    pd, pdk = bank()
                    for kb in range(2):
                        mm(pv[:, 0:512], vtok[:, kb, kvh * 128:(kvh + 1) * 128], pTs[kb * 2 + half][:], kb == 0, kb == 1,
                           ['vtok', 'pTs%d' % (kb * 2 + half)], [pvk])
                        mm(pd[:, 0:512], onesb[:], pTs[kb * 2 + half][:], kb == 0, kb == 1, ['onesb', 'pTs%d' % (kb * 2 + half)], [pdk])
                    for h, o in attn_finish(pv, pvk, pd, pdk, [kvh * 4 + half * 2, kvh * 4 + half * 2 + 1], 256):
                        tt('dve', ybT[:, h, :], o, zbT[:, h, :], ALU.mult, ['tf0', 'zbT'], ['qT'])
        elif 'attn' not in '':
            for kvh in range(2):
                for qb in range(2):
                    ia = 2 * idx + qb
                    blocks = []
                    if own:
                        eq = cc + qb * 128
                        blocks.append(('l', eq - 128, mlo, 'mlo', 0 if (idx == 0 and qb == 0) else None))
                        blocks.append(('l', eq, None, None, None))
                        blocks.append(('l', eq + 128, mhi, 'mhi', 1 if (idx == 3 and qb == 1) else None))
                    else:
                        if ia - 1 >= 0:
                            blocks.append(('l', (ia - 1) * 128 - (t0 - cc), mlo, 'mlo', None))
                        blocks.append(('l', ia * 128 - (t0 - cc), None, None, None))
                        if ia + 1 <= 31:
                            blocks.append(('l', (ia + 1) * 128 - (t0 - cc), mhi, 'mhi', None))
                    blocks.append(('c', 0, None, None, None))
                    blocks.append(('c', 1, None, None, None))
                    qv = qT[:, kvh * 4:(kvh + 1) * 4, qb * 128:(qb + 1) * 128]
                    for bi, (bt_, a_, msk, mk, vc) in enumerate(blocks):
                        pa, pak = bank()
                        if bt_ == 'l':
                            e0 = a_
                            kk_ = kT[:, kvh, e0:e0 + 128]
                            kkey = 'kT'
                        else:
                            kk_ = ckT[:, kvh, a_ * 128:(a_ + 1) * 128]
                            kkey = 'ckT'
                        for h_ in range(4):
                            mm(pa[:, h_ * 128:(h_ + 1) * 128], kk_, qT[:, kvh * 4 + h_, qb * 128:(qb + 1) * 128], True, True,
                               [kkey, 'qT'], [pak])
                        act(pTs[bi][:], pa[:, 0:512], AF.Exp, [pak], ['pTs%d' % bi], scale=scale)
                        if msk is not None and vc is not None:
                            sc.op('dve', lambda e, bi=bi, msk=msk, vc=vc: e.scalar_tensor_tensor(
                                out=pTs[bi][:].rearrange("p (h q) -> p h q", q=128), in0=pTs[bi][:].rearrange("p (h q) -> p h q", q=128),
                                scalar=vmask[:, vc:vc + 1], in1=bc(msk[:].unsqueeze(1), [128, 4, 128]), op0=ALU.mult, op1=ALU.mult),
                                ['pTs%d' % bi, mk, 'vmask'], ['pTs%d' % bi])
                        elif msk is not None:
                            tt('dve', pTs[bi][:].rearrange("p (h q) -> p h q", q=128), pTs[bi][:].rearrange("p (h q) -> p h q", q=128),
                               bc(msk[:].unsqueeze(1), [128, 4, 128]), ALU.mult, ['pTs%d' % bi, mk], ['pTs%d' % bi])
                    pv, pvk = bank()
                    pd, pdk = bank()
                    nb = len(blocks)
                    for bi, (bt_, a_, msk, mk, vc) in enumerate(blocks):
                        if bt_ == 'l':
                            e0 = a_
                            vv = vtok[:, e0 // 128, kvh * 128:(kvh + 1) * 128]
                            vkey = 'vtok'
                        else:
                            vv = cvb[:, a_, kvh * 128:(kvh + 1) * 128]
                            vkey = 'cvb'
                        mm(pv[:, 0:512], vv, pTs[bi][:], bi == 0, bi == nb - 1, [vkey, 'pTs%d' % bi], [pvk])
                        mm(pd[:, 0:512], onesb[:], pTs[bi][:], bi == 0, bi == nb - 1, ['onesb', 'pTs%d' % bi], [pdk])
                    for h, o in attn_finish(pv, pvk, pd, pdk, [kvh * 4 + i for i in range(4)], 128):
                        tt('dve', ybT[:, h, qb * 128:(qb + 1) * 128], o, zbT[:, h, qb * 128:(qb + 1) * 128], ALU.mult, ['tf0', 'zbT'], ['qT'])
        s5_main(l, idx if kind == 's' else 0, kind == 'p', idx if kind == 'p' else None, sel=own)
        for fg in range(8):
            for br, (yT_, yk, Kp) in enumerate(((yaT, 'zaT', 4), (ybT, 'qT', 8), (ycT, 'zcT', 4))):
                for j in range(2):
                    wt2, wk2 = wload2(l, (fg * 3 + br) * 2 + j, Kp)
                    pg, pgk = bank()
                    pq, pqk = bank()
                    for k in range(KT):
                        mm(pg[:, 0:NT], wt2[:, k, :], hT[:, k, ctr], k == 0, k == KT - 1, [wk2, 'hT'], [pgk])
                    for k in range(Kp):
                        mm(pq[:, 0:NT], wt2[:, 16 + k, :], yT_[:, k, :], k == 0, k == Kp - 1, [wk2, yk], [pqk])
                    act(tf[2][:, 0:NT], pg[:, 0:NT], AF.Sigmoid, [pgk], ['tf2'])
                    f = fg * 2 + j
                    if br == 0:
                        tt('dve', merged[:, f, :], tf[2][:, 0:NT], pq[:, 0:NT], ALU.mult, ['tf2', pqk], ['merged'])
                    else:
                        tt('dve', tf[2][:, 0:NT], tf[2][:, 0:NT], pq[:, 0:NT], ALU.mult, ['tf2', pqk], ['tf2'])
                        tt('dve', merged[:, f, :], merged[:, f, :], tf[2][:, 0:NT], ALU.add, ['merged', 'tf2'], ['merged'])
        gate_b = V[:].rearrange("p g n -> p (g n)")
        lng_b = Vs[:].rearrange("p g n -> p (g n)")
        lnb_b = tfall[:].rearrange("p a c -> p (a c)")
        tfk = ['tf0', 'tf1', 'tf2', 'tf3']
        sc.dma('pool', gate_b, gsc[cond].partition_broadcast(128), r=['gsc'], w=['V'])
        sc.dma('pool', lng_b, ln_g[l].partition_broadcast(128), w=['Vs'])
        sc.dma('pool', lnb_b, ln_b[l].partition_broadcast(128), w=tfk)
        for s_ in range(2):
            for fb in range(8):
                wt, wk = wload('out', l, fb)
                pb_, pbk = bank()
                for k in range(KT):
                    mm(pb_[:, 0:256], merged[:, k, s_ * 128:(s_ + 1) * 128], wt[:, k, :], k == 0, k == KT - 1, [wk, 'merged'], [pbk])
                tt('dve', rr[:, fb * 256:(fb + 1) * 256], pb_[:, 0:256], gate_b[:, fb * 256:(fb + 1) * 256], ALU.mult, [pbk, 'V'], ['rr'])
            xk = 'xin0'
            rk = 'rr'
            if own:
                sc.dma('pool', xin[0][:], src, r=['ridx'] + rkeys, w=[xk], ind=ridx[:, 2 * idx + 1 + s_:2 * idx + 2 + s_])
            else:
                sc.dma('sp', xin[0][:], src[t0 + s_ * 128:t0 + (s_ + 1) * 128, :], r=rkeys, w=[xk])
            sc.op('dve', lambda e: e.scalar_tensor_tensor(out=rr[:], in0=xin[0][:], scalar=ALPHA, in1=rr[:],
                                                          op0=ALU.mult, op1=ALU.add), [xk, rk], [rk])
            for q in range(4):
                sc.op('dve', lambda e, q=q: e.bn_stats(out=stats[:, q, :], in_=rr[:, q * 512:(q + 1) * 512]), [rk], ['stats'])
            sc.op('dve', lambda e: e.bn_aggr(out=mv[:], in_=stats[:].rearrange("p a b -> p (a b)")), ['stats'], ['mv'])
            act(rstd[:], mv[:, 1:2], AF.Sqrt, ['mv'], ['rstd'], bias=EPS)
            recip(rstd[:], rstd[:], ['rstd'], ['rstd'])
            ts('dve', rr[:], rr[:], mv[:, 0:1], ALU.subtract, [rk, 'mv', 'rstd'], [rk], s2=rstd[:, 0:1], op1=ALU.mult)
            tt('dve', rr[:], rr[:], lng_b, ALU.mult, [rk, 'Vs'], [rk])
            tt('dve', rr[:], rr[:], lnb_b, ALU.add, [rk] + tfk, [rk])
            sc.dma('pool', dst[t0 + s_ * 128:t0 + (s_ + 1) * 128, :], rr[:], r=[rk], w=[dkey])

    def chain(pin_, pout, G, Ls, Lw, keys):
        for w_ in range(2):
            cx = A2ix if w_ == 0 else nA2ix
            tt('dve', tA[:, 0:32], Pst[:, pin_, w_, G], A2r[:, G], ALU.mult, ['Pst', 'A2'], ['tA'])
            tt('dve', tB[:, 0:32], Pst[:, pin_, 1 - w_, G], cx[:, G], ALU.mult, ['Pst', 'A2x'], ['tB'])
            tt('dve', tA[:, 0:32], tA[:, 0:32], tB[:, 0:32], ALU.add, ['tA', 'tB'], ['tA'])
            tt('dve', Pst[:, pout, w_, G], tA[:, 0:32], Ls if w_ == 0 else Lw, ALU.add, ['tA'] + keys, ['Pst'])

    def sample_prepass(l):
        src = xs if l == 0 else zs
        cp('dve', Pst[:, 0, 0, 0:32], sinit[:, 0, :], ['sinit'], ['Pst'])
        cp('dve', Pst[:, 0, 1, 0:32], sinitw[:, 0, :], ['sinitw'], ['Pst'])
        cp('dve', Pst[:, 16, 0, 32:64], sinit[:, 1, :], ['sinit'], ['Pst'])
        cp('dve', Pst[:, 16, 1, 32:64], sinitw[:, 1, :], ['sinitw'], ['Pst'])
        TRv = rr[:].rearrange("p (g n) -> p g n", n=32)
        TIv = tfall[:].rearrange("p a c -> p (a c)").rearrange("p (g n) -> p g n", n=32)
        tkeys = ['rr', 'tf0', 'tf1', 'tf2', 'tf3']
        tmp = xin[0][:].rearrange("p (g n) -> p g n", n=32)
        AX = mybir.AxisListType.X

        def sums(t):
            for q_, (ta_, va_, vk_) in enumerate(((TRv, V, 'V'), (TIv, Vs, 'Vs'), (TRv, Vs, 'Vs'), (TIv, V, 'V'))):
                tt('dve', tmp, ta_, va_[:], ALU.mult, tkeys + [vk_], ['xin0'])
                sc.op('dve', lambda e, q_=q_: e.tensor_reduce(out=Ssum[:, q_, :], in_=tmp, axis=AX, op=ALU.add), ['xin0'], ['Ssum'])
            tt('dve', Ssum[:, 0, :], Ssum[:, 0, :], Ssum[:, 1, :], ALU.add, ['Ssum'], ['Ssum'])
            tt('dve', Ssum[:, 2, :], Ssum[:, 2, :], Ssum[:, 3, :], ALU.subtract, ['Ssum'], ['Ssum'])
            chain(t, t + 1, slice(0, 32), Ssum[:, 0, 0:32], Ssum[:, 2, 0:32], ['Ssum'])
            cp('dve', Lst[:, t, 0, :], Ssum[:, 0, 32:64], ['Ssum'], ['Lst'])
            cp('dve', Lst[:, t, 1, :], Ssum[:, 2, 32:64], ['Ssum'], ['Lst'])

        for t in range(16):
            ln_ht(src, [t * NT, t * NT + 128], 0, 1, [] if l == 0 else ['dst_s_0'])
            proj_xa(l, 0)
            if t > 0:
                sums(t - 1)
            s5_states(0, True, rec=False)
        sums(15)
        for t in range(15, -1, -1):
            chain(t + 1, t, slice(32, 64), Lst[:, t, 0, :], Lst[:, t, 1, :], ['Lst'])

    if stop is None:
        for l in range(2):
            layer_prep(l)
            wconvert(l)
            sample_prepass(l)
            for i in range(NPS):
                tile_main(l, 'p', i)
            if l == 0:
                for t in range(16):
                    tile_main(l, 's', t)
            else:
                for t in range(4):
                    tile_main(l, 'o', t)
    else:
        layer_prep(0)
        wconvert(0)
        if stop == 'ptile':
            tile_main(0, 'p', 0)
        if stop == 'pre':
            sample_prepass(0)
        if stop == 'own':
            sample_prepass(0)
            tile_main(0, 'o', 0)
            tile_main(0, 'o', 3)
        if stop == 'stile':
            sample_prepass(0)
            tile_main(0, 's', 0)
            if 'one' not in '':
                tile_main(0, 's', 1)
        loc = dict(locals())
        for nm in dumps:
            if nm in ('s5w', 'gsc', 'zp', 'zs'):
                src_ap = loc[nm]
                key = nm if nm in ('s5w', 'gsc') else ('dst_p_0' if nm == 'zp' else 'dst_s_0')
                o = nc.dram_tensor("dbg_" + nm, list(src_ap.shape), src_ap.dtype, kind="ExternalOutput").ap()
                sc.dma('sp', o, src_ap, r=[key], w=['dbg_' + nm])
            else:
                t_ = loc[nm]
                o = nc.dram_tensor("dbg_" + nm, list(t_.shape), t_.dtype, kind="ExternalOutput").ap()
                sc.dma('sp', o, t_[:], r=[nm, 'PW', 'C', 'Bb', 'A2', 'A2x', 'V', 'Vs'], w=['dbg_' + nm])
    counts = sc.emit(es)
    es.close()
    return nc, counts


_CACHE = {}


def kernel(x_prompt, x_sample, cache_k, cache_v, state_ssm, c, c_ctx,
           w_ada, b_ada, w_in, ssm_lam_re, ssm_lam_im, ssm_log_step,
           ssm_b_re, ssm_b_im, ssm_c_re, ssm_c_im, ssm_d, w_glu, b_glu,
           attn_sink, sgu_ln_g, sgu_ln_b, w_spatial, b_spatial,
           w_proj_a, w_proj_b, w_proj_c, w_out, ln_g, ln_b):
    f = lambda a: np.ascontiguousarray(np.asarray(a, dtype=np.float32))
    if 'nc' not in _CACHE:
        _CACHE['nc'] = build()[0]
    nc = _CACHE['nc']
    consts = _host_consts()
    shared = dict(w_ada=f(w_ada), b_ada=f(b_ada), w_in=f(w_in), lam_re=f(ssm_lam_re), lam_im=f(ssm_lam_im),
                  log_step=f(ssm_log_step), b_re=f(ssm_b_re), b_im=f(ssm_b_im), c_re=f(ssm_c_re), c_im=f(ssm_c_im),
                  ssm_d=f(ssm_d), w_glu=f(w_glu), b_glu=f(b_glu), sink=f(attn_sink), sgu_g=f(sgu_ln_g), sgu_b=f(sgu_ln_b),
                  w_s=f(w_spatial), b_s=f(np.asarray(b_spatial).reshape(2, 512)),
                  w_pa=f(w_proj_a), w_pb=f(w_proj_b), w_pc=f(w_proj_c), w_out=f(w_out), ln_g=f(ln_g), ln_b=f(ln_b))
    shared.update(consts)
    x_prompt = np.asarray(x_prompt); x_sample = np.asarray(x_sample)
    cache_k = np.asarray(cache_k); cache_v = np.asarray(cache_v); state_ssm = np.asarray(state_ssm)
    c = np.asarray(c); c_ctx = np.asarray(c_ctx)
    in_maps = []
    for core in range(8):
        b = core // 4
        m = dict(shared)
        m['xp'] = f(x_prompt[core * NPS:(core + 1) * NPS].reshape(NPS * LP, D))
        m['xs'] = f(x_sample[b])
        m['ck'] = f(cache_k[b].reshape(2, 256, 256))
        m['cv'] = f(cache_v[b].reshape(2, 256, 256))
        m['st0'] = f(state_ssm[b])
        m['cvec'] = f(np.stack([c_ctx, c[b]], axis=0))
        m.update(_core_consts(core, consts))
        in_maps.append(m)
    res = run_bass_kernel_spmd(nc, in_maps, core_ids=list(range(8)))
    R = res.results
    y_prompt = np.concatenate([R[i]['yp'].reshape(NPS, LP, D) for i in range(8)], axis=0).astype(np.float32)
    y_sample = np.stack([np.concatenate([R[b_ * 4 + j_]['ys_own'] for j_ in range(4)], axis=0) for b_ in range(2)], axis=0).astype(np.float32)
    nk_ = np.concatenate([R[i]['nk'].reshape(NPS, 2, LP, 2, 128) for i in range(8)], axis=0).astype(np.float32)
    nv_ = np.concatenate([R[i]['nv'].reshape(NPS, 2, LP, 2, 128) for i in range(8)], axis=0).astype(np.float32)
    ns_ = np.concatenate([R[i]['nst'] for i in range(8)], axis=0).astype(np.float32)
    return (y_prompt, y_sample, nk_, nv_, ns_)
```

```python
import contextlib
import math
import numpy as np
import concourse.bass as bass
import concourse.mybir as mybir
from concourse.bass_utils import run_bass_kernel_spmd

F32 = mybir.dt.float32
BF = mybir.dt.bfloat16
AF = mybir.ActivationFunctionType
ALU = mybir.AluOpType

D = 2048
KT = 16
NT = 256
DIN = 11264
LP = 256
LS = 4096
NPS = 4
DEPTH = 2
ALPHA = (2 * DEPTH) ** 0.25
EPS = 1e-5
GC1 = 1.5957691216057308
GC2 = 0.044715
SAME_ENG_SYNC = True
SAME_ENG_DIST = 8


class Sched:
    NS = 8

    def __init__(self, nc):
        self.nc = nc
        self.ops = []

    def op(self, eng, fn, r=(), w=()):
        w = tuple(w) + tuple(k for k in r if k.startswith('ps') and k not in w)
        self.ops.append((eng, fn, tuple(r), tuple(w), False))

    def dma(self, q, out, in_, r=(), w=(), slow=False, ind=None):
        self.ops.append((q, (out, in_, slow, ind), tuple(r), tuple(w), True))

    def emit(self, es):
        nc = self.nc
        ops = self.ops
        n = len(ops)
        last_w = {}
        readers = {}
        deps = [None] * n
        for i, (eng, fn, r, w, isd) in enumerate(ops):
            d = set()
            for k in r:
                if k in last_w:
                    d.add(last_w[k])
            for k in w:
                if k in last_w:
                    d.add(last_w[k])
                for j in readers.get(k, ()):
                    d.add(j)
            d.discard(i)
            deps[i] = d
            for k in w:
                last_w[k] = i
                readers[k] = []
            for k in r:
                if k not in w:
                    readers.setdefault(k, []).append(i)
        need = [False] * n
        lidx = [0] * n
        lc = {}
        for i in range(n):
            lidx[i] = lc.get(ops[i][0], 0)
            lc[ops[i][0]] = lidx[i] + 1

        def same_eng_skip(i, j):
            if ops[i][0] == 'pe' or not SAME_ENG_SYNC:
                return True
            return (lidx[i] - lidx[j]) > SAME_ENG_DIST

        for i in range(n):
            ei = ops[i][0]
            for j in deps[i]:
                ej, _, _, _, dj = ops[j]
                if dj or ej != ei or not same_eng_skip(i, j):
                    need[j] = True
        engs = ['pe', 'act', 'dve', 'pool', 'sp']
        esem = {e: es.enter_context(nc.semaphore('es_' + e)) for e in engs}
        dsem = {q: [es.enter_context(nc.semaphore('ds_%s%d' % (q, k))) for k in range(self.NS)]
                for q in ('sp', 'pool', 'act')}
        cnt = {e: 0 for e in engs}
        dcnt = {q: 0 for q in dsem}
        sig = [None] * n
        streams = {e: [] for e in engs}
        waited = {e: {} for e in engs}

        def addwait(e, lst, sem, val):
            key = id(sem)
            if waited[e].get(key, 0) >= val:
                return
            waited[e][key] = val
            lst.append(('w', sem, val))

        for i, (eng, fn, r, w, isd) in enumerate(ops):
            lst = streams[eng]
            wmax = {}
            for j in deps[i]:
                if sig[j] is None:
                    continue
                ej, dj = ops[j][0], ops[j][4]
                if (not dj) and ej == eng and same_eng_skip(i, j):
                    continue
                key = id(sig[j][0])
                if key not in wmax or wmax[key][1] < sig[j][1]:
                    wmax[key] = sig[j]
            for key in sorted(wmax, key=lambda k_: wmax[k_][1]):
                addwait(eng, lst, wmax[key][0], wmax[key][1])
            if isd:
                k = dcnt[eng]
                dcnt[eng] += 1
                sem = dsem[eng][k % self.NS]
                rnd = k // self.NS
                if rnd > 0:
                    addwait(eng, lst, sem, 16 * rnd)
                sig[i] = (sem, 16 * (rnd + 1))
                lst.append(('d', fn, sem))
            else:
                if need[i]:
                    cnt[eng] += 1
                    sig[i] = (esem[eng], cnt[eng])
                    lst.append(('o', fn, esem[eng]))
                else:
                    lst.append(('o', fn, None))
        for q in dsem:
            for k in range(self.NS):
                tot = (dcnt[q] - k + self.NS - 1) // self.NS if dcnt[q] > k else 0
                if tot > 0:
                    streams[q].append(('w', dsem[q][k], 16 * tot))

        def run(engine, lst):
            for it in lst:
                if it[0] == 'w':
                    engine.wait_ge(it[1], it[2])
                elif it[0] == 'd':
                    out, in_, slow, ind = it[1]
                    if ind is not None:
                        engine.indirect_dma_start(out=out, out_offset=None, in_=in_,
                                                  in_offset=bass.IndirectOffsetOnAxis(ap=ind, axis=0)).then_inc(it[2], 16)
                    elif slow:
                        engine.dma_start(out=out, in_=in_, allow_slow_non_contiguous=True).then_inc(it[2], 16)
                    else:
                        engine.dma_start(out=out, in_=in_).then_inc(it[2], 16)
                else:
                    ins = it[1](engine)
                    if it[2] is not None:
                        ins.then_inc(it[2], 1)

        block = es.enter_context(nc.Block())

        @block.tensor
        def _(e):
            run(e, streams['pe'])

        @block.scalar
        def _(e):
            run(e, streams['act'])

        @block.vector
        def _(e):
            run(e, streams['dve'])

        @block.gpsimd
        def _(e):
            run(e, streams['pool'])

        @block.sync
        def _(e):
            run(e, streams['sp'])
        return {e: (len(streams[e]), cnt[e]) for e in engs}


def _core_consts(core, consts):
    j = core % 4
    m = {}
    p = np.arange(128)[:, None]
    cidx = np.arange(10)[None, :]
    m['ridx'] = np.clip(1024 * j + 128 * (cidx - 1) + p, 0, LS - 1).astype(np.int32)
    q = np.clip(1024 * j - 128 + np.arange(1280), 0, LS - 1)
    m['ropec_o'] = np.ascontiguousarray(consts['ropec'][:, q])
    m['ropes_o'] = np.ascontiguousarray(consts['ropes'][:, q])
    oh = np.zeros((128, 4), np.float32); oh[:, j] = 1.0
    m['oh4'] = oh
    vm = np.ones((128, 2), np.float32)
    if j == 0:
        vm[:, 0] = 0.0
    if j == 3:
        vm[:, 1] = 0.0
    m['vmask'] = vm
    return m


def _host_consts():
    c = {}
    c['ident'] = np.eye(128, dtype=np.float32)
    R = np.zeros((128, 128), np.float32)
    for d in range(128):
        if d % 64 < 32:
            R[d, d + 32] = -1.0
        else:
            R[d, d - 32] = 1.0
    c['rotT'] = np.ascontiguousarray(R.T)
    pos = np.arange(LS)
    row = pos // 64
    col = pos % 64
    inv = 10000.0 ** (-np.arange(0, 64, 2, dtype=np.float32) / 64.0)
    ang = np.zeros((128, LS), np.float32)
    for d in range(128):
        p = row if d < 64 else col
        ang[d] = p.astype(np.float32) * inv[d % 32]
    c['ropec'] = np.cos(ang).astype(np.float32)
    c['ropes'] = np.sin(ang).astype(np.float32)
    kk = np.arange(128)[:, None]
    qq = np.arange(128)[None, :]
    c['mlo'] = (kk >= qq).astype(np.float32)
    c['mhi'] = (kk <= qq).astype(np.float32)
    tp = (np.arange(128) // 16)[:, None]
    tt = (np.arange(128) // 16)[None, :]
    c['cmf'] = (tt >= tp).astype(np.float32)
    c['cmb'] = (tp >= tt).astype(np.float32)
    return c


def build(stop=None, dumps=()):
    nc = bass.Bass("TRN2", target_bir_lowering=False)
    es = contextlib.ExitStack()
    sc = Sched(nc)
    PI = math.pi

    def din(name, shape, dt=F32):
        return nc.dram_tensor(name, list(shape), dt, kind="ExternalInput").ap()

    def dout(name, shape):
        return nc.dram_tensor(name, list(shape), F32, kind="ExternalOutput").ap()

    def dscr(name, shape, dt=F32):
        return nc.dram_tensor(name, list(shape), dt, kind="Internal").ap()

    xp = din("xp", [NPS * LP, D]); xs = din("xs", [LS, D])
    ck = din("ck", [2, 256, 256]); cv = din("cv", [2, 256, 256])
    st0 = din("st0", [2, 2, 2, 32, 64]); cvec = din("cvec", [2, D])
    w_ada = din("w_ada", [2, D, 3 * D]); b_ada = din("b_ada", [2, 3 * D]); w_in = din("w_in", [2, D, DIN])
    lam_re = din("lam_re", [2, 2, 32, 64]); lam_im = din("lam_im", [2, 2, 32, 64]); log_step = din("log_step", [2, 2, 32])
    b_re = din("b_re", [2, 2, 32, 64, 16]); b_im = din("b_im", [2, 2, 32, 64, 16])
    c_re = din("c_re", [2, 2, 32, 16, 64]); c_im = din("c_im", [2, 2, 32, 16, 64])
    ssm_d = din("ssm_d", [2, 512]); w_glu = din("w_glu", [2, 512, 512]); b_glu = din("b_glu", [2, 512])
    sink = din("sink", [2, 8]); sgu_g = din("sgu_g", [2, 512]); sgu_b = din("sgu_b", [2, 512])
    w_s = din("w_s", [2, 4, 128, 128]); b_s = din("b_s", [2, 512])
    w_pa = din("w_pa", [2, 512, D]); w_pb = din("w_pb", [2, 1024, D]); w_pc = din("w_pc", [2, 512, D])
    w_out = din("w_out", [2, D, D]); ln_g = din("ln_g", [2, D]); ln_b = din("ln_b", [2, D])
    c_ident = din("ident", [128, 128]); c_rotT = din("rotT", [128, 128])
    c_ropec = din("ropec", [128, LS]); c_ropes = din("ropes", [128, LS])
    c_mlo = din("mlo", [128, 128]); c_mhi = din("mhi", [128, 128])
    c_cmf = din("cmf", [128, 128]); c_cmb = din("cmb", [128, 128])
    ridx_d = din("ridx", [128, 10], mybir.dt.int32); c_ropec_o = din("ropec_o", [128, 1280]); c_ropes_o = din("ropes_o", [128, 1280])
    oh4_d = din("oh4", [128, 4]); vmask_d = din("vmask", [128, 2])
    yp = dout("yp", [NPS * LP, D]); ys = dout("ys_own", [1024, D])
    nk = dout("nk", [NPS, 2, LP, 256]); nv = dout("nv", [NPS, 2, LP, 256]); nst = dout("nst", [NPS, 2, 2, 2, 32, 64])
    zp = dscr("zp", [NPS * LP, D]); zs = dscr("zs", [LS, D]); gsc = dscr("gsc", [2, D])
    s5w = dscr("s5w", [64, 3, 128, 128], BF)
    wsc = dscr("wsc", [2, 76, 128, 16, 256], BF)
    wsc2 = dscr("wsc2", [2, 48, 128, 24, 128], BF)

    def sb(name, shape, dt=F32):
        return es.enter_context(nc.sbuf_tensor("s_" + name, list(shape), dt))

    ps = [es.enter_context(nc.psum_tensor("ps%d" % i, [128, 512], F32)) for i in range(8)]
    psn = [0]

    def bank():
        i = psn[0] % 8
        psn[0] += 1
        return ps[i], 'ps%d' % i

    xin = [sb("xin0", [128, D]), sb("xin1", [128, D])]
    rr = sb("rr", [128, D])
    stats = sb("stats", [128, 4, 6]); mv = sb("mv", [128, 2]); rstd = sb("rstd", [128, 1])
    hT = sb("hT", [128, KT, 512], BF)
    wb = [sb("wb%d" % i, [128, KT, 256], BF) for i in range(3)]
    zaT = sb("zaT", [128, 4, NT], BF); qT = sb("qT", [128, 8, NT], BF); yaT = zaT; ybT = qT; kT = sb("kT", [128, 2, 512], BF)
    zbT = sb("zbT", [128, 8, NT], BF); uT = sb("uT", [128, 4, NT], BF); zcT = sb("zcT", [128, 4, NT], BF); ycT = zcT
    Xp = sb("Xp", [32, 32, 8, 16], BF); Xpp = sb("Xpp", [128, 32, 32], BF)
    vtok = sb("vtok", [128, 4, 256], BF); vsg = rr[:, 0:1024].rearrange("p (s c) -> p s c", c=512)
    vsln = sb("vsln", [128, 2, 512], BF)
    V = sb("V", [128, 64, 32]); Vs = sb("Vs", [128, 64, 32]); Sb = sb("Sb", [128, 64, 32], BF)
    tA = sb("tA", [128, 64]); tB = sb("tB", [128, 64])
    yg = sb("yg", [32, 8, 32, 16], BF); ygf = sb("ygf", [32, 4, 8, 16]); ygf2 = sb("ygf2", [32, 4, 8, 16])
    ygT = sb("ygT", [128, 4, NT], BF)
    s5wb = [sb("s5wb%d" % i, [128, 8, 3, 128], BF) for i in range(2)]
    merged = sb("merged", [128, KT, NT], BF)
    tfall = sb("tfall", [128, 4, 512])
    tf = [tfall[:, i, :] for i in range(4)]
    pTs = [sb("pTs%d" % i, [128, 512], BF) for i in range(5)]
    Fg = V[:, 0:32, :].rearrange("p (g a) n -> p g (a n)", a=4)
    Gg = V[:, 32:64, :].rearrange("p (g a) n -> p g (a n)", a=4)
    Eg = Vs[:, 0:32, :].rearrange("p (g a) n -> p g (a n)", a=4)
    w3 = s5wb[0]
    wsn = tf[0][:, :].rearrange("p (g q) -> p g q", q=128)
    ckn = tf[1][:, :].rearrange("p (b c) -> p b c", c=256)
    rc = sb("rc", [128, 512]); rs = sb("rs", [128, 512]); qraw = pTs[4]
    ident = sb("ident", [128, 128]); identb = sb("identb", [128, 128], BF); rotT = sb("rotT", [128, 128], BF)
    onesb = sb("onesb", [128, 128], BF); mlo = sb("mlo", [128, 128], BF); mhi = sb("mhi", [128, 128], BF)
    cmf = sb("cmf", [128, 128]); cmb = sb("cmb", [128, 128])
    scT = sb("scT", [128, 2, KT], BF); cvT = sb("cvT", [128, 2, KT])
    modT = sb("modT", [128, 48, 2]); badaT = sb("badaT", [128, 48]); sc1 = sb("sc1", [128, KT, 2])
    sgb = sb("sgb", [128, 512]); sbb = sb("sbb", [128, 512]); bsb = sb("bsb", [128, 512])
    wsT = sb("wsT", [128, 4, 128], BF)
    esink = sb("esink", [128, 8]); ckT = sb("ckT", [128, 2, 256], BF)
    cvb = sb("cvb", [128, 2, 256], BF)
    wglu = sb("wglu", [128, 4, 512], BF); bglu = sb("bglu", [128, 4]); dsk = sb("dsk", [32, 512])
    lr = sb("lr", [128, 32]); li = sb("li", [128, 32]); dtt = sb("dtt", [128, 32])
    p1 = sb("p1", [128, 32]); p2 = sb("p2", [128, 32]); p3 = sb("p3", [128, 32]); p4 = sb("p4", [128, 32])
    cosv = sb("cosv", [128, 32]); sinv = sb("sinv", [128, 32])
    fre = sb("fre", [128, 32]); fim = sb("fim", [128, 32])
    PWr = xin[1][:, 0:544].rearrange("p (k g) -> p k g", g=32); PWi = xin[1][:, 544:1088].rearrange("p (k g) -> p k g", g=32)
    nPWi = xin[1][:, 1088:1632].rearrange("p (k g) -> p k g", g=32)
    Bre = sb("Bre", [128, 8, 16]); Bim = sb("Bim", [128, 8, 16]); Bbr = sb("Bbr", [128, 8, 16]); Bbi = sb("Bbi", [128, 8, 16])
    bt1 = sb("bt1", [128, 8, 16]); bt2 = sb("bt2", [128, 8, 16])
    cnat = sb("cnat", [128, 2, 64]); Cre = sb("Cre", [128, 8, 16]); nCim = sb("nCim", [128, 8, 16])
    Ar = sb("Ar", [128, 64]); Aix = sb("Aix", [128, 64]); nAix = sb("nAix", [128, 64]); sgn = sb("sgn", [128, 1])
    A2r = sb("A2r", [128, 64]); A2i = sb("A2i", [128, 64]); A2ix = sb("A2ix", [128, 64]); nA2ix = sb("nA2ix", [128, 64])
    sinit = sb("sinit", [128, 2, 32]); sinitw = sb("sinitw", [128, 2, 32])
    Pst = sb("Pst", [128, 17, 2, 64])
    Lst = sb("Lst", [128, 16, 2, 32])
    stg = sb("stg", [32, 128])
    Ssum = sb("Ssum", [128, 4, 64])
    ridx = sb("ridx", [128, 10], mybir.dt.int32); oh4 = sb("oh4", [128, 4]); vmask = sb("vmask", [128, 2])
    Psel = sb("Psel", [128, 2, 2, 32])

    def tt(eng, out, a, b, op, r, w):
        sc.op(eng, lambda e: e.tensor_tensor(out=out, in0=a, in1=b, op=op), r, w)

    def ts(eng, out, a, s1, op0, r, w, s2=None, op1=None):
        if op1 is None:
            sc.op(eng, lambda e: e.tensor_scalar(out=out, in0=a, scalar1=s1, scalar2=None, op0=op0), r, w)
        else:
            sc.op(eng, lambda e: e.tensor_scalar(out=out, in0=a, scalar1=s1, scalar2=s2, op0=op0, op1=op1), r, w)

    def act(out, in_, func, r, w, bias=None, scale=None):
        kw = {}
        if bias is not None:
            kw['bias'] = bias
        if scale is not None:
            kw['scale'] = scale
        sc.op('act', lambda e: e.activation(out=out, in_=in_, func=func, **kw), r, w)

    def mm(out, lhsT, rhs, start, stop, r, w):
        sc.op('pe', lambda e: e.matmul(out, lhsT, rhs, start=start, stop=stop), r, w)

    def tr(out, in_, idn, r, w):
        sc.op('pe', lambda e: e.transpose(out, in_, idn), r, w)

    def cp(eng, out, in_, r, w):
        if eng == 'act':
            sc.op(eng, lambda e: e.copy(out=out, in_=in_), r, w)
        else:
            sc.op(eng, lambda e: e.tensor_copy(out=out, in_=in_), r, w)

    def recip(out, in_, r, w):
        sc.op('dve', lambda e: e.reciprocal(out=out, in_=in_), r, w)

    def mset(eng, ap, val, w):
        sc.op(eng, lambda e: e.memset(ap, val), (), w)

    def bc(ap, shape):
        return ap.broadcast_to(list(shape))

    sc.dma('sp', ident[:], c_ident, w=['ident'])
    cp('dve', identb[:], ident[:], ['ident'], ['identb'])
    sc.dma('sp', tf[0][:, 0:128], c_rotT, w=['tf0'])
    cp('dve', rotT[:], tf[0][:, 0:128], ['tf0'], ['rotT'])
    sc.dma('sp', tf[1][:, 0:128], c_mlo, w=['tf1'])
    cp('dve', mlo[:], tf[1][:, 0:128], ['tf1'], ['mlo'])
    sc.dma('sp', tf[2][:, 0:128], c_mhi, w=['tf2'])
    cp('dve', mhi[:], tf[2][:, 0:128], ['tf2'], ['mhi'])
    sc.dma('sp', cmf[:], c_cmf, w=['cmf'])
    sc.dma('sp', cmb[:], c_cmb, w=['cmb'])
    sc.dma('sp', ridx[:], ridx_d, w=['ridx'])
    sc.dma('sp', oh4[:], oh4_d, w=['oh4'])
    sc.dma('sp', vmask[:], vmask_d, w=['vmask'])
    mset('dve', onesb[:], 1.0, ['onesb'])
    mset('dve', sgn[0:64, :], -1.0, ['sgn'])
    mset('dve', sgn[64:128, :], 1.0, ['sgn'])
    sc.dma('sp', cvT[:], cvec.rearrange("c (k p) -> p c k", p=128), w=['cvT'], slow=True)
    act(scT[:], cvT[:], AF.Silu, ['cvT'], ['scT'])

    wslot = [0]

    WNAMES = {'in': (w_in, 16, 0), 'pp': (None, 16, 44), 'out': (w_out, 16, 68)}
    WPACK = (('pa', w_pa, 4, 0), ('pb', w_pb, 8, 4), ('pc', w_pc, 4, 12))

    def wload_cast(src):
        i = wslot[0] % 3
        wslot[0] += 1
        K = src.shape[0] // 128
        sc.dma('pool', wb[i][:, 0:K, :], src.rearrange("(k p) c -> p k c", p=128), w=['wb%d' % i])
        return wb[i], 'wb%d' % i

    def wconvert(l):
        for nm, (wt_, K, g0) in WNAMES.items():
            if wt_ is None:
                continue
            ng = wt_.shape[2] // 256
            if nm == 'in':
                ng = 20
            for gi in range(ng):
                t_, k_ = wload_cast(wt_[l][:, gi * 256:(gi + 1) * 256])
                sc.dma('sp', wsc[l, g0 + gi, :, 0:K, :], t_[:, 0:K, :], r=[k_], w=['wsc%d_%d' % (l, g0 + gi)])
        for fg in range(8):
            for br, (nm, wt_, K, koff) in enumerate(WPACK):
                c0 = 5120 + br * 2048 + fg * 256
                ta_, ka_ = wload_cast(w_in[l][:, c0:c0 + 256])
                tp_, kp_ = wload_cast(wt_[l][:, fg * 256:(fg + 1) * 256])
                for j in range(2):
                    g2 = (fg * 3 + br) * 2 + j
                    sc.dma('sp', wsc2[l, g2, :, 0:16, :], ta_[:, 0:16, j * 128:(j + 1) * 128], r=[ka_], w=['wsc2_%d_%d' % (l, g2)])
                    sc.dma('sp', wsc2[l, g2, :, 16:16 + K, :], tp_[:, 0:K, j * 128:(j + 1) * 128], r=[kp_], w=['wsc2_%d_%d' % (l, g2)])

    def wload2(l, g2, K):
        i = wslot[0] % 3
        wslot[0] += 1
        v = wb[i][:].rearrange("p k c -> p (k c)")[:, 0:24 * 128].rearrange("p (k c) -> p k c", c=128)
        sc.dma('sp', v[:, 0:16 + K, :], wsc2[l, g2, :, 0:16 + K, :], r=['wsc2_%d_%d' % (l, g2)], w=['wb%d' % i])
        return v, 'wb%d' % i

    def wload(nm, l, gi, slot=None):
        wt_, K, g0 = WNAMES[nm]
        if slot is None:
            i = wslot[0] % 3
            wslot[0] += 1
        else:
            i = slot
        sc.dma('sp', wb[i][:, 0:K, :], wsc[l, g0 + gi, :, 0:K, :], r=['wsc%d_%d' % (l, g0 + gi)], w=['wb%d' % i])
        return wb[i], 'wb%d' % i

    def gelu_evac(out, pin, pk, shape, tix, rextra=(), wk=()):
        n = 1
        for s_ in shape[1:]:
            n *= s_
        t1 = tf[tix][0:shape[0], 0:n]
        t2 = tf[tix + 1][0:shape[0], 0:n]
        k1, k2 = 'tf%d' % tix, 'tf%d' % (tix + 1)
        pin2 = pin
        act(t1, pin2, AF.Square, [pk], [k1])
        ts('dve', t1, t1, GC2, ALU.mult, [k1], [k1], s2=1.0, op1=ALU.add)
        tt('dve', t1, t1, pin2, ALU.mult, [k1, pk], [k1])
        act(t2, t1, AF.Sigmoid, [k1], [k2], scale=GC1)
        tt('dve', out, t2, pin2, ALU.mult, [k2, pk] + list(rextra), list(wk))

    def wrap(out, x, kx, ko):
        mset('dve', p4[:], 0.0, ['p4'])
        for m in range(1, 9):
            ts('dve', p3[:], x, (2 * m - 1) * PI, ALU.is_gt, [kx], ['p3'])
            tt('dve', p4[:], p4[:], p3[:], ALU.add, ['p3', 'p4'], ['p4'])
        ts('dve', p4[:], p4[:], -2.0 * PI, ALU.mult, ['p4'], ['p4'])
        tt('dve', out, x, p4[:], ALU.add, [kx, 'p4'], [ko])

    def cmul(ore, oim, are, aim, bre, bim, keys_r, ko):
        tt('dve', p1[:], are, bre, ALU.mult, keys_r, ['p1'])
        tt('dve', p2[:], aim, bim, ALU.mult, keys_r, ['p2'])
        tt('dve', p3[:], are, bim, ALU.mult, keys_r, ['p3'])
        tt('dve', p4[:], aim, bre, ALU.mult, keys_r, ['p4'])
        tt('dve', ore, p1[:], p2[:], ALU.subtract, ['p1', 'p2'], ko)
        tt('dve', oim, p3[:], p4[:], ALU.add, ['p3', 'p4'], ko)

    def mix(out, g0, ng, Wre_, Wim_, Xre, Xim, kr, ko):
        for h in (0, 1):
            P = slice(h * 64, h * 64 + 64)
            wr = bc(Wre_[P, g0:g0 + ng].unsqueeze(2), [64, ng, 16])
            wi = bc(Wim_[P, g0:g0 + ng].unsqueeze(2), [64, ng, 16])
            xa_ = Xre[P, 0:ng, :] if h == 0 else Xim[P, 0:ng, :]
            xb_ = Xim[P, 0:ng, :] if h == 0 else Xre[P, 0:ng, :]
            tt('dve', bt1[P, 0:ng, :], xa_, wr, ALU.mult, kr, ['bt1'])
            tt('dve', bt2[P, 0:ng, :], xb_, wi, ALU.mult, kr, ['bt2'])
            tt('dve', out[P], bt1[P, 0:ng, :], bt2[P, 0:ng, :], ALU.subtract if h == 0 else ALU.add,
               ['bt1', 'bt2'], ko)

    EF = [[t + 1 for t in range(8)], [8 - t for t in range(8)]]

    def layer_prep(l):
        sc.dma('act', badaT[:], b_ada[l].rearrange("(c p) -> p c", p=128), w=['badaT'], slow=True)
        pm, pmk = bank()
        for gi in range(24):
            wt, wk = wload_cast(w_ada[l][:, gi * 256:(gi + 1) * 256])
            for j in range(2):
                ch = gi * 2 + j
                for k in range(KT):
                    mm(pm[:, ch * 2:ch * 2 + 2], wt[:, k, j * 128:(j + 1) * 128], scT[:, :, k],
                       k == 0, k == KT - 1, [wk, 'scT'], [pmk])
        tt('dve', modT[:], pm[:, 0:96].rearrange("p (c t) -> p c t", t=2), bc(badaT[:].unsqueeze(2), [128, 48, 2]),
           ALU.add, [pmk, 'badaT'], ['modT'])
        ts('dve', sc1[:], modT[:, 16:32, :], 1.0, ALU.add, ['modT'], ['sc1'])
        for c_ in range(2):
            sc.dma('act', gsc[c_].rearrange("(k p) -> p k", p=128), modT[:, 32:48, c_], r=['modT'], w=['gsc'], slow=True)
        sc.dma('act', sgb[:], sgu_g[l].partition_broadcast(128), w=['sgb'])
        sc.dma('act', sbb[:], sgu_b[l].partition_broadcast(128), w=['sbb'])
        sc.dma('act', bsb[:], b_s[l].partition_broadcast(128), w=['bsb'])
        sc.dma('act', wsn[:], w_s[l].rearrange("g p q -> p g q"), w=['tf0'])
        pw, pwk = bank()
        for g in range(4):
            tr(pw[:, g * 128:(g + 1) * 128], wsn[:, g, :], ident[:], ['tf0', 'ident'], [pwk])
        cp('dve', wsT[:], pw[:, 0:512].rearrange("p (g q) -> p g q", q=128), [pwk], ['wsT'])
        sc.dma('act', esink[:], sink[l].partition_broadcast(128), w=['esink'])
        act(esink[:], esink[:], AF.Exp, ['esink'], ['esink'])
        sc.dma('act', ckn[:], ck[l].rearrange("(b p) c -> p b c", p=128), w=['tf1'])
        pc_, pck = bank()
        for kvh in range(2):
            for b_ in range(2):
                tr(pc_[:, (kvh * 2 + b_) * 128:(kvh * 2 + b_ + 1) * 128], ckn[:, b_, kvh * 128:(kvh + 1) * 128],
                   ident[:], ['tf1', 'ident'], [pck])
        cp('dve', ckT[:], pc_[:, 0:512].rearrange("p (h t) -> p h t", t=256), [pck], ['ckT'])
        sc.dma('pool', cvb[:], cv[l].rearrange("(b p) c -> p b c", p=128), w=['cvb'])
        sc.dma('pool', wglu[:], w_glu[l].rearrange("(j p) c -> p j c", p=128), w=['wglu'])
        sc.dma('act', bglu[:], b_glu[l].rearrange("(j p) -> p j", p=128), w=['bglu'], slow=True)
        sc.dma('act', dsk[:], ssm_d[l].partition_broadcast(32), w=['dsk'])
        for d in range(2):
            for h in range(2):
                P = slice(h * 64, h * 64 + 64)
                sc.dma('act', lr[P, :], lam_re[l, d].rearrange("g p -> p g"), w=['lr'], slow=True)
                sc.dma('act', li[P, :], lam_im[l, d].rearrange("g p -> p g"), w=['li'], slow=True)
                for ri in range(2):
                    sc.dma('act', sinit[ri * 64:(ri + 1) * 64, d, :] if h == 0 else sinitw[(1 - ri) * 64:(2 - ri) * 64, d, :],
                           st0[l, d, ri].rearrange("g p -> p g"), w=['sinit' if h == 0 else 'sinitw'], slow=True)
            sc.dma('act', dtt[:], log_step[l, d].partition_broadcast(128), w=['dtt'])
            act(dtt[:], dtt[:], AF.Exp, ['dtt'], ['dtt'])
            tt('dve', p1[:], li[:], dtt[:], ALU.mult, ['li', 'dtt'], ['p1'])
            wrap(p2[:], p1[:], 'p1', 'p2')
            act(sinv[:], p2[:], AF.Sin, ['p2'], ['sinv'])
            if stop == 'wrap':
                return
            ts('dve', p1[:], p1[:], PI / 2, ALU.add, ['p1'], ['p1'])
            wrap(p2[:], p1[:], 'p1', 'p2')
            act(cosv[:], p2[:], AF.Sin, ['p2'], ['cosv'])
            tt('dve', p1[:], lr[:], dtt[:], ALU.mult, ['lr', 'dtt'], ['p1'])
            act(p2[:], p1[:], AF.Exp, ['p1'], ['p2'])
            act(p3[:], p1[:], AF.Exp, ['p1'], ['p3'], scale=-1.0)
            mset('dve', PWr[:, 8, :], 1.0, ['PW', 'nPW', 'xin1'])
            mset('dve', PWi[:, 8, :], 0.0, ['PW'])
            tt('dve', PWr[:, 9, :], p2[:], cosv[:], ALU.mult, ['p2', 'cosv'], ['PW'])
            tt('dve', PWi[:, 9, :], p2[:], sinv[:], ALU.mult, ['p2', 'sinv'], ['PW'])
            tt('dve', PWr[:, 7, :], p3[:], cosv[:], ALU.mult, ['p3', 'cosv'], ['PW'])
            tt('dve', PWi[:, 7, :], p3[:], sinv[:], ALU.mult, ['p3', 'sinv'], ['PW'])
            ts('dve', PWi[:, 7, :], PWi[:, 7, :], -1.0, ALU.mult, ['PW'], ['PW'])
            for k in range(2, 9):
                cmul(PWr[:, 8 + k, :], PWi[:, 8 + k, :], PWr[:, 7 + k, :], PWi[:, 7 + k, :], PWr[:, 9, :], PWi[:, 9, :], ['PW'], ['PW'])
                cmul(PWr[:, 8 - k, :], PWi[:, 8 - k, :], PWr[:, 9 - k, :], PWi[:, 9 - k, :], PWr[:, 7, :], PWi[:, 7, :], ['PW'], ['PW'])
            ts('dve', nPWi[:], PWi[:], -1.0, ALU.mult, ['PW'], ['nPW'])
            tt('dve', p1[:], lr[:], lr[:], ALU.mult, ['lr'], ['p1'])
            tt('dve', p2[:], li[:], li[:], ALU.mult, ['li'], ['p2'])
            tt('dve', p1[:], p1[:], p2[:], ALU.add, ['p1', 'p2'], ['p1'])
            recip(p1[:], p1[:], ['p1'], ['p1'])
            ts('dve', p2[:], PWr[:, 9, :], -1.0, ALU.add, ['PW'], ['p2'])
            tt('dve', p3[:], p2[:], lr[:], ALU.mult, ['p2', 'lr'], ['p3'])
            tt('dve', p4[:], PWi[:, 9, :], li[:], ALU.mult, ['PW', 'li'], ['p4'])
            tt('dve', p3[:], p3[:], p4[:], ALU.add, ['p3', 'p4'], ['p3'])
            tt('dve', fre[:], p3[:], p1[:], ALU.mult, ['p3', 'p1'], ['fre'])
            tt('dve', p3[:], PWi[:, 9, :], lr[:], ALU.mult, ['PW', 'lr'], ['p3'])
            tt('dve', p4[:], p2[:], li[:], ALU.mult, ['p2', 'li'], ['p4'])
            tt('dve', p3[:], p3[:], p4[:], ALU.subtract, ['p3', 'p4'], ['p3'])
            tt('dve', fim[:], p3[:], p1[:], ALU.mult, ['p3', 'p1'], ['fim'])
            cp('dve', Ar[:, d * 32:(d + 1) * 32], PWr[:, 16, :], ['PW'], ['Ar'])
            ts('dve', Aix[:, d * 32:(d + 1) * 32], PWi[:, 16, :], sgn[:, 0:1], ALU.mult, ['PW', 'sgn'], ['Aix'])
            ts('dve', nAix[:, d * 32:(d + 1) * 32], Aix[:, d * 32:(d + 1) * 32], -1.0, ALU.mult, ['Aix'], ['nAix'])
            cp('dve', A2r[:, d * 32:(d + 1) * 32], PWr[:, 16, :], ['PW'], ['A2'])
            cp('dve', A2i[:, d * 32:(d + 1) * 32], PWi[:, 16, :], ['PW'], ['A2'])
            cm = cmf if d == 0 else cmb
            cmk = 'cmf' if d == 0 else 'cmb'
            for gb in range(4):
                g0 = gb * 8
                for h in range(2):
                    P = slice(h * 64, h * 64 + 64)
                    sc.dma('act', Bre[P], b_re[l, d, g0:g0 + 8].rearrange("g p c -> p g c"), w=['Bre'])
                    sc.dma('act', Bim[P], b_im[l, d, g0:g0 + 8].rearrange("g p c -> p g c"), w=['Bim'])
                for (src_c, dst_c, neg) in ((c_re, Cre, False), (c_im, nCim, True)):
                    for h in range(2):
                        sc.dma('act', cnat[:, h, :], src_c[l, d, g0:g0 + 8].rearrange("g c p -> (g c) p"), w=['cnat'])
                    pb_, pbk = bank()
                    tr(pb_[:, 0:128], cnat[:].rearrange("q h p -> q (h p)"), ident[:], ['cnat', 'ident'], [pbk])
                    if neg:
                        ts('dve', dst_c[:], pb_[:, 0:128].rearrange("p (g c) -> p g c", c=16), -1.0, ALU.mult, [pbk], ['C'])
                    else:
                        cp('dve', dst_c[:], pb_[:, 0:128].rearrange("p (g c) -> p g c", c=16), [pbk], ['C'])
                fr_b = bc(fre[:, g0:g0 + 8].unsqueeze(2), [128, 8, 16]); fi_b = bc(fim[:, g0:g0 + 8].unsqueeze(2), [128, 8, 16])
                tt('dve', bt1[:], Bre[:], fr_b, ALU.mult, ['Bre', 'fre'], ['bt1'])
                tt('dve', bt2[:], Bim[:], fi_b, ALU.mult, ['Bim', 'fim'], ['bt2'])
                tt('dve', Bbr[:], bt1[:], bt2[:], ALU.subtract, ['bt1', 'bt2'], ['Bb'])
                tt('dve', bt1[:], Bim[:], fr_b, ALU.mult, ['Bim', 'fre'], ['bt1'])
                tt('dve', bt2[:], Bre[:], fi_b, ALU.mult, ['Bre', 'fim'], ['bt2'])
                tt('dve', Bbi[:], bt1[:], bt2[:], ALU.add, ['bt1', 'bt2'], ['Bb'])
                for t in range(8):
                    e = EF[d][t]
                    mix(Fg[:, :, t * 16:(t + 1) * 16], g0, 8, PWr[:, 8 + e, :], nPWi[:, 8 + e, :], Cre, nCim, ['PW', 'nPW', 'C'], ['V'])
                    mix(Gg[:, :, t * 16:(t + 1) * 16], g0, 8, PWr[:, 8 - e, :], PWi[:, 8 - e, :], Bbr, Bbi, ['PW', 'Bb'], ['V'])
                    mix(Eg[:, :, t * 16:(t + 1) * 16], g0, 8, PWr[:, 16 - e, :], PWi[:, 16 - e, :], Bbr, Bbi, ['PW', 'Bb'], ['Vs'])
                cp('dve', w3[:, :, 0, :], Fg[:], ['V'], ['s5wb0'])
                for gl in range(8):
                    pb_, pbk = bank()
                    tr(pb_[:, 0:128], Eg[:, gl, :], ident[:], ['Vs', 'ident'], [pbk])
                    mm(pb_[:, 128:256], Gg[:, gl, :], Fg[:, gl, :], True, True, ['V', 'V'], [pbk])
                    cp('dve', w3[:, gl, 1, :], pb_[:, 0:128], [pbk], ['s5wb0'])
                    tt('dve', w3[:, gl, 2, :], pb_[:, 128:256], cm[:], ALU.mult, [pbk, cmk], ['s5wb0'])
                sc.dma('act', s5w[d * 32 + g0:d * 32 + g0 + 8].rearrange("g t k m -> k g t m"), w3[:], r=['s5wb0'], w=['s5w'])
        TRv = rr[:].rearrange("p (g n) -> p g n", n=32)
        TIv = tfall[:].rearrange("p a c -> p (a c)").rearrange("p (g n) -> p g n", n=32)
        tkeys = ['rr', 'tf0', 'tf1', 'tf2', 'tf3']
        mset('dve', TRv[:, 0:32, 31:32], 1.0, tkeys)
        mset('dve', TIv[:, 0:32, 31:32], 0.0, tkeys)
        mset('dve', TRv[:, 32:64, 0:1], 1.0, tkeys)
        mset('dve', TIv[:, 32:64, 0:1], 0.0, tkeys)
        for it_ in range(5):
            m = 1 << it_
            for d in range(2):
                G = slice(d * 32, d * 32 + 32)
                if d == 0:
                    srcs, dsts = slice(32 - m, 32), slice(32 - 2 * m, 32 - m)
                else:
                    srcs, dsts = slice(0, m), slice(m, 2 * m)
                amr = bc(A2r[:, G].unsqueeze(2), [128, 32, m])
                ami = bc(A2i[:, G].unsqueeze(2), [128, 32, m])
                t1_ = V[:, 0:32, 0:m]
                t2_ = V[:, 32:64, 0:m]
                tt('dve', t1_, TRv[:, G, srcs], amr, ALU.mult, tkeys + ['A2'], ['V'])
                tt('dve', t2_, TIv[:, G, srcs], ami, ALU.mult, tkeys + ['A2'], ['V'])
                tt('dve', TRv[:, G, dsts], t1_, t2_, ALU.subtract, ['V'], tkeys)
                tt('dve', t1_, TRv[:, G, srcs], ami, ALU.mult, tkeys + ['A2'], ['V'])
                tt('dve', t2_, TIv[:, G, srcs], amr, ALU.mult, tkeys + ['A2'], ['V'])
                tt('dve', TIv[:, G, dsts], t1_, t2_, ALU.add, ['V'], tkeys)
            tt('dve', tA[:], A2r[:], A2r[:], ALU.mult, ['A2'], ['tA'])
            tt('dve', tB[:], A2i[:], A2i[:], ALU.mult, ['A2'], ['tB'])
            tt('dve', tA[:], tA[:], tB[:], ALU.subtract, ['tA', 'tB'], ['tA'])
            tt('dve', tB[:], A2r[:], A2i[:], ALU.mult, ['A2'], ['tB'])
            cp('dve', A2r[:], tA[:], ['tA'], ['A2'])
            ts('dve', A2i[:], tB[:], 2.0, ALU.mult, ['tB'], ['A2'])
        ts('dve', TIv[:], TIv[:], sgn[:, 0:1], ALU.mult, tkeys + ['sgn'], tkeys)
        ts('dve', A2ix[:], A2i[:], sgn[:, 0:1], ALU.mult, ['A2', 'sgn'], ['A2x'])
        ts('dve', nA2ix[:], A2ix[:], -1.0, ALU.mult, ['A2x'], ['A2x'])

    xslot = [0]

    def ln_ht(src, rows, col0, cond, rkeys=()):
        for si, r0 in enumerate(rows):
            b = xslot[0] % 2
            xslot[0] += 1
            xk = 'xin%d' % b
            xt = xin[b]
            xw = [xk] if b == 0 else [xk, 'PW', 'nPW']
            if isinstance(r0, tuple):
                sc.dma('pool', xt[:], src, r=['ridx'] + list(rkeys), w=xw, ind=ridx[:, r0[1]:r0[1] + 1])
            else:
                sc.dma('sp', xt[:], src[r0:r0 + 128, :], r=list(rkeys), w=xw)
            for q in range(4):
                sc.op('dve', lambda e, q=q, xt=xt: e.bn_stats(out=stats[:, q, :], in_=xt[:, q * 512:(q + 1) * 512]), [xk], ['stats'])
            sc.op('dve', lambda e: e.bn_aggr(out=mv[:], in_=stats[:].rearrange("p a b -> p (a b)")), ['stats'], ['mv'])
            act(rstd[:], mv[:, 1:2], AF.Sqrt, ['mv'], ['rstd'], bias=EPS)
            recip(rstd[:], rstd[:], ['rstd'], ['rstd'])
            ts('dve', xt[:], xt[:], mv[:, 0:1], ALU.subtract, [xk, 'mv', 'rstd'], [xk], s2=rstd[:, 0:1], op1=ALU.mult)
            c0 = col0 + si * 128
            for kq in range(4):
                pb_, pbk = bank()
                for kk_ in range(4):
                    k = kq * 4 + kk_
                    tr(pb_[:, kk_ * 128:(kk_ + 1) * 128], xt[:, k * 128:(k + 1) * 128], ident[:], [xk, 'ident'], [pbk])
                for kk_ in range(4):
                    k = kq * 4 + kk_
                    if k % 2 == 0:
                        act(hT[:, k, c0:c0 + 128], pb_[:, kk_ * 128:(kk_ + 1) * 128], AF.Identity, [pbk, 'sc1', 'modT'], ['hT'],
                            bias=modT[:, k, cond:cond + 1], scale=sc1[:, k, cond:cond + 1])
                    else:
                        ts('dve', hT[:, k, c0:c0 + 128], pb_[:, kk_ * 128:(kk_ + 1) * 128], sc1[:, k, cond:cond + 1], ALU.mult,
                           [pbk, 'sc1', 'modT'], ['hT'], s2=modT[:, k, cond:cond + 1], op1=ALU.add)

    def proj_xa(l, cc):
        for gi in range(2):
            wt, wk = wload('in', l, gi)
            for t in range(8):
                pb_, pbk = bank()
                for k in range(KT):
                    mm(pb_[0:32, 0:256], hT[:, k, cc + t:cc + 256:8], wt[:, k, :], k == 0, k == KT - 1, [wk, 'hT'], [pbk])
                cp('dve' if t % 2 == 0 else 'act', Xp[:, gi * 16:(gi + 1) * 16, t, :],
                   pb_[0:32, 0:256].rearrange("p (g c) -> p g c", c=16), [pbk], ['Xp'])

    def s5_states(t_init, zero_init, reng='dve', sel=False, rec=True):
        for gq in range(4):
            pb_, pbk = bank()
            for gl in range(8):
                g = gq * 8 + gl
                mm(pb_[:, gl * 32:(gl + 1) * 32], Xp[:, g, :, :].rearrange("p t c -> p (t c)"), identb[0:32, 0:32], True, True,
                   ['Xp', 'identb'], [pbk])
            cp('act', Xpp[:, gq * 8:(gq + 1) * 8, :], pb_[:, 0:256].rearrange("p (g n) -> p g n", n=32), [pbk], ['Xpp'])
        for d in range(2):
            for gq in range(4):
                i = (d * 4 + gq) % 2
                wk = 's5wb%d' % i
                sc.dma('sp', s5wb[i][:, :, 1, :], s5w[d * 32 + gq * 8:d * 32 + gq * 8 + 8, 1].rearrange("g k m -> k g m"),
                       r=['s5w'], w=[wk])
                pv, pvk = bank()
                pw_, pwk = bank()
                for gl in range(8):
                    g = gq * 8 + gl
                    mm(pv[:, gl * 32:(gl + 1) * 32], s5wb[i][:, gl, 1, :], Xpp[:, g, :], True, True, [wk, 'Xpp'], [pvk])
                    mm(pw_[0:64, gl * 32:(gl + 1) * 32], s5wb[i][:, gl, 1, 64:128], Xpp[:, g, :], True, True, [wk, 'Xpp'], [pwk])
                    mm(pw_[64:128, gl * 32:(gl + 1) * 32], s5wb[i][:, gl, 1, 0:64], Xpp[:, g, :], True, True, [wk, 'Xpp'], [pwk])
                gd0 = d * 32 + gq * 8
                cp('dve', V[:, gd0:gd0 + 8, :], pv[:, 0:256].rearrange("p (g n) -> p g n", n=32), [pvk], ['V'])
                cp('act', Vs[:, gd0:gd0 + 8, :], pw_[:, 0:256].rearrange("p (g n) -> p g n", n=32), [pwk], ['Vs'])
        for d in (range(2) if rec else ()):
            G = slice(d * 32, d * 32 + 32)
            order = list(range(32)) if d == 0 else list(range(31, -1, -1))
            for step, n_ in enumerate(order):
                if step == 0:
                    if zero_init:
                        continue
                    if sel:
                        pS = Psel[:, d, 0, :]
                        pW = Psel[:, d, 1, :]
                        kp = ['Psel']
                    else:
                        pS = Pst[:, t_init + (0 if d == 0 else 1), 0, G]
                        pW = Pst[:, t_init + (0 if d == 0 else 1), 1, G]
                        kp = ['Pst']
                else:
                    pn = order[step - 1]
                    pS = V[:, G, pn]
                    pW = Vs[:, G, pn]
                    kp = ['V', 'Vs']
                tt(reng, tA[:, 0:32], pS, Ar[:, G], ALU.mult, kp + ['Ar'], ['tA'])
                tt(reng, tB[:, 0:32], pW, Aix[:, G], ALU.mult, kp + ['Aix'], ['tB'])
                tt(reng, tA[:, 0:32], tA[:, 0:32], tB[:, 0:32], ALU.add, ['tA', 'tB'], ['tA'])
                tt(reng, tA[:, 32:64], pW, Ar[:, G], ALU.mult, kp + ['Ar'], ['tA2'])
                tt(reng, tB[:, 32:64], pS, nAix[:, G], ALU.mult, kp + ['nAix'], ['tB2'])
                tt(reng, tA[:, 32:64], tA[:, 32:64], tB[:, 32:64], ALU.add, ['tA2', 'tB2'], ['tA2'])
                tt(reng, V[:, G, n_], V[:, G, n_], tA[:, 0:32], ALU.add, ['V', 'tA'], ['V'])
                tt(reng, Vs[:, G, n_], Vs[:, G, n_], tA[:, 32:64], ALU.add, ['Vs', 'tA2'], ['Vs'])

    def s5_main(l, t_init, zero_init, seq, sel=False):
        if zero_init:
            mset('dve', Sb[:, 0:32, 0:1], 0.0, ['Sb'])
            mset('dve', Sb[:, 32:64, 31:32], 0.0, ['Sb'])
        elif sel:
            cp('dve', Sb[:, 0:32, 0:1], Psel[:, 0, 0, :].unsqueeze(2), ['Psel'], ['Sb'])
            cp('dve', Sb[:, 32:64, 31:32], Psel[:, 1, 0, :].unsqueeze(2), ['Psel'], ['Sb'])
        else:
            cp('dve', Sb[:, 0:32, 0:1], Pst[:, t_init, 0, 0:32].unsqueeze(2), ['Pst'], ['Sb'])
            cp('dve', Sb[:, 32:64, 31:32], Pst[:, t_init + 1, 0, 32:64].unsqueeze(2), ['Pst'], ['Sb'])
        cp('dve', Sb[:, 0:32, 1:32], V[:, 0:32, 0:31], ['V'], ['Sb'])
        cp('act', Sb[:, 32:64, 0:31], V[:, 32:64, 1:32], ['V'], ['Sb'])
        if seq is not None:
            for d in range(2):
                pb_, pbk = bank()
                src_ = V[:, 0:32, 31] if d == 0 else V[:, 32:64, 0]
                cp('dve', tf[0][:, 0:32], src_, ['V'], ['tf0'])
                tr(pb_[0:32, 0:128], tf[0][:, 0:32], ident[:], ['tf0', 'ident'], [pbk])
                cp('dve', stg[:], pb_[0:32, 0:128], [pbk], ['stg'])
                sc.dma('pool', nst[seq, l, d].rearrange("r g p -> g r p"), stg[:].rearrange("g (r p) -> g r p", r=2),
                       r=['stg'], w=['nst'])
        for gq in range(4):
            p0, p0k = bank()
            p1_, p1k = bank()
            for d in range(2):
                for t_ in (0, 2):
                    sc.dma('sp', s5wb[d][:, :, t_, :], s5w[d * 32 + gq * 8:d * 32 + gq * 8 + 8, t_].rearrange("g k m -> k g m"),
                           r=['s5w'], w=['s5wb%d' % d])
            for gl in range(8):
                g = gq * 8 + gl
                pp, ppk = (p0, p0k) if gl < 4 else (p1_, p1k)
                o = pp[0:32, (gl % 4) * 128:(gl % 4 + 1) * 128]
                for d in range(2):
                    wk = 's5wb%d' % d
                    mm(o, Xpp[:, g, :], s5wb[d][:, gl, 2, :], d == 0, False, [wk, 'Xpp'], [ppk])
                    mm(o, Sb[:, d * 32 + g, :], s5wb[d][:, gl, 0, :], False, d == 1, [wk, 'Sb'], [ppk])
            for hf, (pp, ppk) in enumerate(((p0, p0k), (p1_, p1k))):
                g0 = gq * 8 + hf * 4
                yv = ygf[:]
                dsl = bc(dsk[:, g0 * 16:(g0 + 4) * 16].rearrange("p (g c) -> p g c", c=16).unsqueeze(2), [32, 4, 8, 16])
                tt('dve', yv, Xp[:, g0:g0 + 4, :, :], dsl, ALU.mult, ['Xp', 'dsk'], ['ygf'])
                tt('dve', yv, yv, pp[0:32, 0:512].rearrange("p (g t c) -> p g t c", t=8, c=16), ALU.add, ['ygf', ppk], ['ygf'])
                yf = ygf[:].rearrange("p g t c -> p (g t c)")
                y2 = ygf2[:].rearrange("p g t c -> p (g t c)")
                act(y2, yf, AF.Square, ['ygf'], ['ygf2'])
                ts('dve', y2, y2, GC2, ALU.mult, ['ygf2'], ['ygf2'], s2=1.0, op1=ALU.add)
                tt('dve', y2, y2, yf, ALU.mult, ['ygf2', 'ygf'], ['ygf2'])
                act(y2, y2, AF.Sigmoid, ['ygf2'], ['ygf2'], scale=GC1)
                tt('dve', yg[:, :, g0:g0 + 4, :].rearrange("p t g c -> p g t c"), ygf2[:], ygf[:], ALU.mult, ['ygf2', 'ygf'], ['yg'])
        for j in range(4):
            pb_, pbk = bank()
            for t in range(8):
                mm(pb_[:, t * 32:(t + 1) * 32], yg[:, t, j * 8:(j + 1) * 8, :].rearrange("p g c -> p (g c)"), identb[0:32, 0:32], True, True, ['yg', 'identb'], [pbk])
            cp('dve' if j % 2 == 0 else 'act', ygT[:, j, :].rearrange("p (n t) -> p t n", t=8),
               pb_[:, 0:256].rearrange("p (t n) -> p t n", n=32), [pbk], ['ygT'])
        for jo in range(4):
            pb_, pbk = bank()
            for j in range(4):
                mm(pb_[:, 0:NT], wglu[:, j, jo * 128:(jo + 1) * 128], ygT[:, j, :], j == 0, j == 3, ['wglu', 'ygT'], [pbk])
            act(tf[0][:, 0:NT], pb_[:, 0:NT], AF.Sigmoid, [pbk, 'bglu'], ['tf0'], bias=bglu[:, jo:jo + 1])
            tt('dve', tf[0][:, 0:NT], tf[0][:, 0:NT], ygT[:, jo, :], ALU.mult, ['tf0', 'ygT'], ['tf0'])
            tt('dve', yaT[:, jo, :], tf[0][:, 0:NT], zaT[:, jo, :], ALU.mult, ['tf0', 'zaT'], ['zaT'])

    def proj_fm(l, gi, nchunk, cols, evac):
        wt, wk = wload('in', l, gi)
        for j in range(nchunk):
            pb_, pbk = bank()
            n = cols.stop - cols.start
            for k in range(KT):
                mm(pb_[:, 0:n], wt[:, k, j * 128:(j + 1) * 128], hT[:, k, cols], k == 0, k == KT - 1, [wk, 'hT'], [pbk])
            evac(gi, j, pb_[:, 0:n], pbk)

    def rope(out, pin, pk, n, okey):
        cp('act', qraw[:, 0:n], pin, [pk], ['pTs4'])
        pr, prk = bank()
        mm(pr[:, 0:n], rotT[:], qraw[:, 0:n], True, True, ['rotT', 'pTs4'], [prk])
        tt('dve', tf[2][:, 0:n], pin, rc[:, 0:n], ALU.mult, [pk, 'rc'], ['tf2'])
        tt('dve', tf[3][:, 0:n], pr[:, 0:n], rs[:, 0:n], ALU.mult, [prk, 'rs'], ['tf3'])
        tt('dve', out, tf[2][:, 0:n], tf[3][:, 0:n], ALU.add, ['tf2', 'tf3'], [okey])

    def attn_finish(pvb, pvk, pdb, pdk, heads, hw):
        n_ = len(heads) * hw
        cp('act', tf[1][:, 0:n_], pvb[:, 0:n_], [pvk], ['tf1'])
        cp('act', tf[2][:, 0:n_], pdb[:, 0:n_], [pdk], ['tf2'])
        for hi, h in enumerate(heads):
            cs = slice(hi * hw, (hi + 1) * hw)
            ts('dve', tf[0][:, 0:hw], tf[2][:, cs], esink[:, h:h + 1], ALU.add, ['tf2', 'esink'], ['tf0'])
            recip(tf[0][:, 0:hw], tf[0][:, 0:hw], ['tf0'], ['tf0'])
            tt('dve', tf[0][:, 0:hw], tf[0][:, 0:hw], tf[1][:, cs], ALU.mult, ['tf0', 'tf1'], ['tf0'])
            yield h, tf[0][:, 0:hw]

    import os as _os

    pre_ctx = {}

    def tile_head(l, kind, idx):
        cond = 0 if kind == 'p' else 1
        own = kind == 'o'
        src = (xp if l == 0 else zp) if kind == 'p' else (xs if l == 0 else zs)
        dst = (zp if l == 0 else yp) if kind == 'p' else (zs if l == 0 else ys)
        rkeys = [] if l == 0 else ['dst_p_0' if kind == 'p' else 'dst_s_0']
        dkey = 'dst_%s_%d' % ('p' if kind == 'p' else 's', l)
        t0 = idx * NT
        if kind == 'p':
            rows = [t0, t0 + 128]
            cc = 0
            E = 256
        elif own:
            rows = [('ind', 2 * idx + b_) for b_ in range(4)]
            cc = 128
            E = 512
            sc.dma('pool', rc[:, 0:E], c_ropec_o[:, 256 * idx:256 * idx + 512], w=['rc'])
            sc.dma('pool', rs[:, 0:E], c_ropes_o[:, 256 * idx:256 * idx + 512], w=['rs'])
            for d in range(2):
                for w_ in range(2):
                    G = slice(d * 32, d * 32 + 32)
                    ts('dve', Psel[:, d, w_, :], Pst[:, idx + d, w_, G], oh4[:, 0:1], ALU.mult, ['Pst', 'oh4'], ['Psel'])
                    for j_ in range(1, 4):
                        sc.op('dve', lambda e, d=d, w_=w_, j_=j_, G=G: e.scalar_tensor_tensor(
                            out=Psel[:, d, w_, :], in0=Pst[:, 4 * j_ + idx + d, w_, G], scalar=oh4[:, j_:j_ + 1], in1=Psel[:, d, w_, :],
                            op0=ALU.mult, op1=ALU.add), ['Pst', 'oh4', 'Psel'], ['Psel'])
        else:
            lo = t0 - 128 if idx > 0 else t0
            hi = t0 + NT + 128 if idx < 15 else t0 + NT
            rows = list(range(lo, hi, 128))
            cc = t0 - lo
            E = hi - lo
            sc.dma('pool', rc[:, 0:E], c_ropec[:, lo:hi], w=['rc'])
            sc.dma('pool', rs[:, 0:E], c_ropes[:, lo:hi], w=['rs'])
        ln_ht(src, rows, 0, cond, rkeys)
        return dict(cond=cond, own=own, src=src, dst=dst, rkeys=rkeys, dkey=dkey, t0=t0, cc=cc, E=E)

    def tile_main(l, kind, idx, nxt=None):
        c_ = pre_ctx.pop((l, kind, idx), None)
        if c_ is None:
            c_ = tile_head(l, kind, idx)
        cond, own, src, dst, rkeys, dkey, t0, cc, E = (c_[k_] for k_ in ('cond', 'own', 'src', 'dst', 'rkeys', 'dkey', 't0', 'cc', 'E'))
        ctr = slice(cc, cc + NT)
        proj_xa(l, cc)
        scale = 128 ** -0.5

        def ev(gi, j, pin, pk):
            if gi in (2, 3):
                act(zaT[:, (gi - 2) * 2 + j, :], pin, AF.Silu, [pk], ['zaT'])
            elif 4 <= gi <= 7:
                hh = (gi - 4) * 2 + j
                if kind == 'p' or 'rope' in '':
                    cp('act', qT[:, hh, :], pin, [pk], ['qT'])
                else:
                    rope_q(hh, pin, pk)
            elif gi == 8:
                if kind == 'p' or 'rope' in '':
                    cp('act', kT[:, j, 0:E], pin, [pk], ['kT'])
                else:
                    rope(kT[:, j, 0:E], pin, pk, E, 'kT')
            elif 10 <= gi <= 13:
                act(zbT[:, (gi - 10) * 2 + j, :], pin, AF.Silu, [pk], ['zbT'])
            elif gi in (14, 15):
                gelu_evac(uT[:, (gi - 14) * 2 + j, :], pin, pk, [128, NT], 0, wk=['uT'])
            elif gi in (18, 19):
                act(zcT[:, (gi - 18) * 2 + j, :], pin, AF.Silu, [pk], ['zcT'])

        def rope_q(hh, pin, pk):
            cp('act', qraw[:, 0:NT], pin, [pk], ['pTs4'])
            pr, prk = bank()
            mm(pr[:, 0:NT], rotT[:], qraw[:, 0:NT], True, True, ['rotT', 'pTs4'], [prk])
            tt('dve', tf[2][:, 0:NT], pin, rc[:, ctr], ALU.mult, [pk, 'rc'], ['tf2'])
            tt('dve', tf[3][:, 0:NT], pr[:, 0:NT], rs[:, ctr], ALU.mult, [prk, 'rs'], ['tf3'])
            tt('dve', qT[:, hh, :], tf[2][:, 0:NT], tf[3][:, 0:NT], ALU.add, ['tf2', 'tf3'], ['qT'])

        for gi in (2, 3):
            proj_fm(l, gi, 2, ctr, ev)
        s5_states(idx if kind == 's' else 0, kind == 'p', reng='pool', sel=own)
        for gi in (4, 5, 6, 7):
            proj_fm(l, gi, 2, ctr, ev)
        proj_fm(l, 8, 2, slice(0, E), ev)
        for gi in (8, 9):
            if gi == 8 and kind != 'p':
                continue
            wt, wk = wload('in', l, gi)
            for s_ in range(E // 128):
                pb_, pbk = bank()
                for k in range(KT):
                    mm(pb_[:, 0:256], hT[:, k, s_ * 128:(s_ + 1) * 128], wt[:, k, :], k == 0, k == KT - 1, [wk, 'hT'], [pbk])
                if gi == 9:
                    cp('act', vtok[:, s_, :], pb_[:, 0:256], [pbk], ['vtok'])
                if kind == 'p':
                    o_ = nk if gi == 8 else nv
                    cp('dve', tf[2 + s_][:, 0:256], pb_[:, 0:256], [pbk], ['tf%d' % (2 + s_)])
                    sc.dma('pool', o_[idx, l, s_ * 128:(s_ + 1) * 128, :], tf[2 + s_][:, 0:256], r=['tf%d' % (2 + s_)], w=['nkv'])
        for gi in (10, 11, 12, 13, 14, 15):
            proj_fm(l, gi, 2, ctr, ev)
        for gi in (16, 17):
            wt, wk = wload('in', l, gi)
            for s_ in range(2):
                pb_, pbk = bank()
                for k in range(KT):
                    mm(pb_[:, 0:256], hT[:, k, cc + s_ * 128:cc + (s_ + 1) * 128], wt[:, k, :], k == 0, k == KT - 1, [wk, 'hT'], [pbk])
                gelu_evac(vsg[:, s_, (gi - 16) * 256:(gi - 15) * 256], pb_[:, 0:256], pbk, [128, 256], 0, wk=['rr'])
        for s_ in range(2):
            sc.op('dve', lambda e, s_=s_: e.bn_stats(out=stats[:, 0, :], in_=vsg[:, s_, :]), ['rr'], ['stats'])
            sc.op('dve', lambda e: e.bn_aggr(out=mv[:], in_=stats[:, 0, :]), ['stats'], ['mv'])
            act(rstd[:], mv[:, 1:2], AF.Sqrt, ['mv'], ['rstd'], bias=EPS)
            recip(rstd[:], rstd[:], ['rstd'], ['rstd'])
            ts('dve', vsg[:, s_, :], vsg[:, s_, :], mv[:, 0:1], ALU.subtract, ['rr', 'mv', 'rstd'], ['rr'], s2=rstd[:, 0:1], op1=ALU.mult)
            tt('dve', vsg[:, s_, :], vsg[:, s_, :], sgb[:], ALU.mult, ['rr', 'sgb'], ['rr'])
            tt('dve', vsln[:, s_, :], vsg[:, s_, :], sbb[:], ALU.add, ['rr', 'sbb'], ['vsln'])
        for gi in (18, 19):
            proj_fm(l, gi, 2, ctr, ev)
        for g in range(4):
            for s_ in range(2):
                pb_, pbk = bank()
                mm(pb_[:, 0:128], vsln[:, s_, g * 128:(g + 1) * 128], wsT[:, g, :], True, True, ['vsln', 'wsT'], [pbk])
                tt('dve', tf[1][:, 0:128], pb_[:, 0:128], bsb[:, g * 128:(g + 1) * 128], ALU.add, [pbk, 'bsb'], ['tf1'])
                tt('dve', tf[1][:, 0:128], tf[1][:, 0:128], uT[:, g, s_ * 128:(s_ + 1) * 128], ALU.mult, ['tf1', 'uT'], ['tf1'])
                tt('dve', ycT[:, g, s_ * 128:(s_ + 1) * 128], tf[1][:, 0:128], zcT[:, g, s_ * 128:(s_ + 1) * 128], ALU.mult,
                   ['tf1', 'zcT'], ['zcT'])
        if kind == 'p':
            for kvh in range(2):
                for kb in range(2):
                    pa, pak = bank()
                    pb2, pb2k = bank()
                    for h in range(4):
                        pp, ppk = (pa, pak) if h < 2 else (pb2, pb2k)
                        mm(pp[:, (h % 2) * 256:(h % 2 + 1) * 256], kT[:, kvh, kb * 128:(kb + 1) * 128], qT[:, kvh * 4 + h, :], True, True,
                           ['kT', 'qT'], [ppk])
                    act(pTs[kb * 2][:], pa[:, 0:512], AF.Exp, [pak], ['pTs%d' % (kb * 2)], scale=scale)
                    act(pTs[kb * 2 + 1][:], pb2[:, 0:512], AF.Exp, [pb2k], ['pTs%d' % (kb * 2 + 1)], scale=scale)
                for half in range(2):
                    pv, pvk = bank()
                    pd, pdk = bank()
                    for kb in range(2):
                        mm(pv[:, 0:512], vtok[:, kb, kvh * 128:(kvh + 1) * 128], pTs[kb * 2 + half][:], kb == 0, kb == 1,
                           ['vtok', 'pTs%d' % (kb * 2 + half)], [pvk])
                        mm(pd[:, 0:512], onesb[:], pTs[kb * 2 + half][:], kb == 0, kb == 1, ['onesb', 'pTs%d' % (kb * 2 + half)], [pdk])
                    for h, o in attn_finish(pv, pvk, pd, pdk, [kvh * 4 + half * 2, kvh * 4 + half * 2 + 1], 256):
                        tt('dve', ybT[:, h, :], o, zbT[:, h, :], ALU.mult, ['tf0', 'zbT'], ['qT'])
        elif 'attn' not in '':
            for kvh in range(2):
                for qb in range(2):
                    ia = 2 * idx + qb
                    blocks = []
                    if own:
                        eq = cc + qb * 128
                        blocks.append(('l', eq - 128, mlo, 'mlo', 0 if (idx == 0 and qb == 0) else None))
                        blocks.append(('l', eq, None, None, None))
                        blocks.append(('l', eq + 128, mhi, 'mhi', 1 if (idx == 3 and qb == 1) else None))
                    else:
                        if ia - 1 >= 0:
                            blocks.append(('l', (ia - 1) * 128 - (t0 - cc), mlo, 'mlo', None))
                        blocks.append(('l', ia * 128 - (t0 - cc), None, None, None))
                        if ia + 1 <= 31:
                            blocks.append(('l', (ia + 1) * 128 - (t0 - cc), mhi, 'mhi', None))
                    blocks.append(('c', 0, None, None, None))
                    blocks.append(('c', 1, None, None, None))
                    qv = qT[:, kvh * 4:(kvh + 1) * 4, qb * 128:(qb + 1) * 128]
                    for bi, (bt_, a_, msk, mk, vc) in enumerate(blocks):
                        pa, pak = bank()
                        if bt_ == 'l':
                            e0 = a_
                            kk_ = kT[:, kvh, e0:e0 + 128]
                            kkey = 'kT'
                        else:
                            kk_ = ckT[:, kvh, a_ * 128:(a_ + 1) * 128]
                            kkey = 'ckT'
                        for h_ in range(4):
                            mm(pa[:, h_ * 128:(h_ + 1) * 128], kk_, qT[:, kvh * 4 + h_, qb * 128:(qb + 1) * 128], True, True,
                               [kkey, 'qT'], [pak])
                        act(pTs[bi][:], pa[:, 0:512], AF.Exp, [pak], ['pTs%d' % bi], scale=scale)
                        if msk is not None and vc is not None:
                            sc.op('dve', lambda e, bi=bi, msk=msk, vc=vc: e.scalar_tensor_tensor(
                                out=pTs[bi][:].rearrange("p (h q) -> p h q", q=128), in0=pTs[bi][:].rearrange("p (h q) -> p h q", q=128),
                                scalar=vmask[:, vc:vc + 1], in1=bc(msk[:].unsqueeze(1), [128, 4, 128]), op0=ALU.mult, op1=ALU.mult),
                                ['pTs%d' % bi, mk, 'vmask'], ['pTs%d' % bi])
                        elif msk is not None:
                            tt('dve', pTs[bi][:].rearrange("p (h q) -> p h q", q=128), pTs[bi][:].rearrange("p (h q) -> p h q", q=128),
                               bc(msk[:].unsqueeze(1), [128, 4, 128]), ALU.mult, ['pTs%d' % bi, mk], ['pTs%d' % bi])
                    pv, pvk = bank()
                    pd, pdk = bank()
                    nb = len(blocks)
                    for bi, (bt_, a_, msk, mk, vc) in enumerate(blocks):
                        if bt_ == 'l':
                            e0 = a_
                            vv = vtok[:, e0 // 128, kvh * 128:(kvh + 1) * 128]
                            vkey = 'vtok'
                        else:
                            vv = cvb[:, a_, kvh * 128:(kvh + 1) * 128]
                            vkey = 'cvb'
                        mm(pv[:, 0:512], vv, pTs[bi][:], bi == 0, bi == nb - 1, [vkey, 'pTs%d' % bi], [pvk])
                        mm(pd[:, 0:512], onesb[:], pTs[bi][:], bi == 0, bi == nb - 1, ['onesb', 'pTs%d' % bi], [pdk])
                    for h, o in attn_finish(pv, pvk, pd, pdk, [kvh * 4 + i for i in range(4)], 128):
                        tt('dve', ybT[:, h, qb * 128:(qb + 1) * 128], o, zbT[:, h, qb * 128:(qb + 1) * 128], ALU.mult, ['tf0', 'zbT'], ['qT'])
        s5_main(l, idx if kind == 's' else 0, kind == 'p', idx if kind == 'p' else None, sel=own)
        for fg in range(8):
            for br, (yT_, yk, Kp) in enumerate(((yaT, 'zaT', 4), (ybT, 'qT', 8), (ycT, 'zcT', 4))):
                for j in range(2):
                    wt2, wk2 = wload2(l, (fg * 3 + br) * 2 + j, Kp)
                    pg, pgk = bank()
                    pq, pqk = bank()
                    for k in range(KT):
                        mm(pg[:, 0:NT], wt2[:, k, :], hT[:, k, ctr], k == 0, k == KT - 1, [wk2, 'hT'], [pgk])
                    for k in range(Kp):
                        mm(pq[:, 0:NT], wt2[:, 16 + k, :], yT_[:, k, :], k == 0, k == Kp - 1, [wk2, yk], [pqk])
                    act(tf[2][:, 0:NT], pg[:, 0:NT], AF.Sigmoid, [pgk], ['tf2'])
                    f = fg * 2 + j
                    if br == 0:
                        tt('dve', merged[:, f, :], tf[2][:, 0:NT], pq[:, 0:NT], ALU.mult, ['tf2', pqk], ['merged'])
                    else:
                        tt('dve', tf[2][:, 0:NT], tf[2][:, 0:NT], pq[:, 0:NT], ALU.mult, ['tf2', pqk], ['tf2'])
                        tt('dve', merged[:, f, :], merged[:, f, :], tf[2][:, 0:NT], ALU.add, ['merged', 'tf2'], ['merged'])
        if nxt is not None:
            pre_ctx[nxt] = tile_head(*nxt)
        gate_b = V[:].rearrange("p g n -> p (g n)")
        lng_b = Vs[:].rearrange("p g n -> p (g n)")
        lnb_b = tfall[:].rearrange("p a c -> p (a c)")
        tfk = ['tf0', 'tf1', 'tf2', 'tf3']
        sc.dma('pool', gate_b, gsc[cond].partition_broadcast(128), r=['gsc'], w=['V'])
        sc.dma('pool', lng_b, ln_g[l].partition_broadcast(128), w=['Vs'])
        sc.dma('pool', lnb_b, ln_b[l].partition_broadcast(128), w=tfk)
        for s_ in range(2):
            for fb in range(8):
                wt, wk = wload('out', l, fb)
                pb_, pbk = bank()
                for k in range(KT):
                    mm(pb_[:, 0:256], merged[:, k, s_ * 128:(s_ + 1) * 128], wt[:, k, :], k == 0, k == KT - 1, [wk, 'merged'], [pbk])
                tt('dve', rr[:, fb * 256:(fb + 1) * 256], pb_[:, 0:256], gate_b[:, fb * 256:(fb + 1) * 256], ALU.mult, [pbk, 'V'], ['rr'])
            xk = 'xin0'
            rk = 'rr'
            if own:
                sc.dma('pool', xin[0][:], src, r=['ridx'] + rkeys, w=[xk], ind=ridx[:, 2 * idx + 1 + s_:2 * idx + 2 + s_])
            else:
                sc.dma('sp', xin[0][:], src[t0 + s_ * 128:t0 + (s_ + 1) * 128, :], r=rkeys, w=[xk])
            sc.op('dve', lambda e: e.scalar_tensor_tensor(out=rr[:], in0=xin[0][:], scalar=ALPHA, in1=rr[:],
                                                          op0=ALU.mult, op1=ALU.add), [xk, rk], [rk])
            for q in range(4):
                sc.op('dve', lambda e, q=q: e.bn_stats(out=stats[:, q, :], in_=rr[:, q * 512:(q + 1) * 512]), [rk], ['stats'])
            sc.op('dve', lambda e: e.bn_aggr(out=mv[:], in_=stats[:].rearrange("p a b -> p (a b)")), ['stats'], ['mv'])
            act(rstd[:], mv[:, 1:2], AF.Sqrt, ['mv'], ['rstd'], bias=EPS)
            recip(rstd[:], rstd[:], ['rstd'], ['rstd'])
            ts('dve', rr[:], rr[:], mv[:, 0:1], ALU.subtract, [rk, 'mv', 'rstd'], [rk], s2=rstd[:, 0:1], op1=ALU.mult)
            tt('dve', rr[:], rr[:], lng_b, ALU.mult, [rk, 'Vs'], [rk])
            tt('dve', rr[:], rr[:], lnb_b, ALU.add, [rk] + tfk, [rk])
            sc.dma('pool', dst[t0 + s_ * 128:t0 + (s_ + 1) * 128, :], rr[:], r=[rk], w=[dkey])

    def chain(pin_, pout, G, Ls, Lw, keys):
        for w_ in range(2):
            cx = A2ix if w_ == 0 else nA2ix
            tt('dve', tA[:, 0:32], Pst[:, pin_, w_, G], A2r[:, G], ALU.mult, ['Pst', 'A2'], ['tA'])
            tt('dve', tB[:, 0:32], Pst[:, pin_, 1 - w_, G], cx[:, G], ALU.mult, ['Pst', 'A2x'], ['tB'])
            tt('dve', tA[:, 0:32], tA[:, 0:32], tB[:, 0:32], ALU.add, ['tA', 'tB'], ['tA'])
            tt('dve', Pst[:, pout, w_, G], tA[:, 0:32], Ls if w_ == 0 else Lw, ALU.add, ['tA'] + keys, ['Pst'])

    def sample_prepass(l):
        src = xs if l == 0 else zs
        cp('dve', Pst[:, 0, 0, 0:32], sinit[:, 0, :], ['sinit'], ['Pst'])
        cp('dve', Pst[:, 0, 1, 0:32], sinitw[:, 0, :], ['sinitw'], ['Pst'])
        cp('dve', Pst[:, 16, 0, 32:64], sinit[:, 1, :], ['sinit'], ['Pst'])
        cp('dve', Pst[:, 16, 1, 32:64], sinitw[:, 1, :], ['sinitw'], ['Pst'])
        TRv = rr[:].rearrange("p (g n) -> p g n", n=32)
        TIv = tfall[:].rearrange("p a c -> p (a c)").rearrange("p (g n) -> p g n", n=32)
        tkeys = ['rr', 'tf0', 'tf1', 'tf2', 'tf3']
        tmp = xin[0][:].rearrange("p (g n) -> p g n", n=32)
        AX = mybir.AxisListType.X

        def sums(t):
            for q_, (ta_, va_, vk_) in enumerate(((TRv, V, 'V'), (TIv, Vs, 'Vs'), (TRv, Vs, 'Vs'), (TIv, V, 'V'))):
                tt('dve', tmp, ta_, va_[:], ALU.mult, tkeys + [vk_], ['xin0'])
                sc.op('dve', lambda e, q_=q_: e.tensor_reduce(out=Ssum[:, q_, :], in_=tmp, axis=AX, op=ALU.add), ['xin0'], ['Ssum'])
            tt('dve', Ssum[:, 0, :], Ssum[:, 0, :], Ssum[:, 1, :], ALU.add, ['Ssum'], ['Ssum'])
            tt('dve', Ssum[:, 2, :], Ssum[:, 2, :], Ssum[:, 3, :], ALU.subtract, ['Ssum'], ['Ssum'])
            chain(t, t + 1, slice(0, 32), Ssum[:, 0, 0:32], Ssum[:, 2, 0:32], ['Ssum'])
            cp('dve', Lst[:, t, 0, :], Ssum[:, 0, 32:64], ['Ssum'], ['Lst'])
            cp('dve', Lst[:, t, 1, :], Ssum[:, 2, 32:64], ['Ssum'], ['Lst'])

        for t in range(16):
            ln_ht(src, [t * NT, t * NT + 128], 0, 1, [] if l == 0 else ['dst_s_0'])
            proj_xa(l, 0)
            if t > 0:
                sums(t - 1)
            s5_states(0, True, rec=False)
        sums(15)
        for t in range(15, -1, -1):
            chain(t + 1, t, slice(32, 64), Lst[:, t, 0, :], Lst[:, t, 1, :], ['Lst'])

    if stop is None:
        for l in range(2):
            layer_prep(l)
            wconvert(l)
            sample_prepass(l)
            seq_ = [(l, 'p', i) for i in range(NPS)]
            seq_ += [(l, 's', t) for t in range(16)] if l == 0 else [(l, 'o', t) for t in range(4)]
            for i_, tl_ in enumerate(seq_):
                tile_main(*tl_, nxt=seq_[i_ + 1] if i_ + 1 < len(seq_) else None)
    else:
        layer_prep(0)
        wconvert(0)
        if stop == 'ptile':
            tile_main(0, 'p', 0)
        if stop == 'pre':
            sample_prepass(0)
        if stop == 'own':
            sample_prepass(0)
            tile_main(0, 'o', 0)
            tile_main(0, 'o', 3)
        if stop == 'stile':
            sample_prepass(0)
            tile_main(0, 's', 0, nxt=(0, 's', 1))
            tile_main(0, 's', 1)
        loc = dict(locals())
        for nm in dumps:
            if nm in ('s5w', 'gsc', 'zp', 'zs'):
                src_ap = loc[nm]
                key = nm if nm in ('s5w', 'gsc') else ('dst_p_0' if nm == 'zp' else 'dst_s_0')
                o = nc.dram_tensor("dbg_" + nm, list(src_ap.shape), src_ap.dtype, kind="ExternalOutput").ap()
                sc.dma('sp', o, src_ap, r=[key], w=['dbg_' + nm])
            else:
                t_ = loc[nm]
                o = nc.dram_tensor("dbg_" + nm, list(t_.shape), t_.dtype, kind="ExternalOutput").ap()
                sc.dma('sp', o, t_[:], r=[nm, 'PW', 'C', 'Bb', 'A2', 'A2x', 'V', 'Vs'], w=['dbg_' + nm])
    counts = sc.emit(es)
    es.close()
    return nc, counts


_CACHE = {}


def kernel(x_prompt, x_sample, cache_k, cache_v, state_ssm, c, c_ctx,
           w_ada, b_ada, w_in, ssm_lam_re, ssm_lam_im, ssm_log_step,
           ssm_b_re, ssm_b_im, ssm_c_re, ssm_c_im, ssm_d, w_glu, b_glu,
           attn_sink, sgu_ln_g, sgu_ln_b, w_spatial, b_spatial,
           w_proj_a, w_proj_b, w_proj_c, w_out, ln_g, ln_b):
    f = lambda a: np.ascontiguousarray(np.asarray(a, dtype=np.float32))
    if 'nc' not in _CACHE:
        _CACHE['nc'] = build()[0]
    nc = _CACHE['nc']
    consts = _host_consts()
    shared = dict(w_ada=f(w_ada), b_ada=f(b_ada), w_in=f(w_in), lam_re=f(ssm_lam_re), lam_im=f(ssm_lam_im),
                  log_step=f(ssm_log_step), b_re=f(ssm_b_re), b_im=f(ssm_b_im), c_re=f(ssm_c_re), c_im=f(ssm_c_im),
                  ssm_d=f(ssm_d), w_glu=f(w_glu), b_glu=f(b_glu), sink=f(attn_sink), sgu_g=f(sgu_ln_g), sgu_b=f(sgu_ln_b),
                  w_s=f(w_spatial), b_s=f(np.asarray(b_spatial).reshape(2, 512)),
                  w_pa=f(w_proj_a), w_pb=f(w_proj_b), w_pc=f(w_proj_c), w_out=f(w_out), ln_g=f(ln_g), ln_b=f(ln_b))
    shared.update(consts)
    x_prompt = np.asarray(x_prompt); x_sample = np.asarray(x_sample)
    cache_k = np.asarray(cache_k); cache_v = np.asarray(cache_v); state_ssm = np.asarray(state_ssm)
    c = np.asarray(c); c_ctx = np.asarray(c_ctx)
    in_maps = []
    for core in range(8):
        b = core // 4
        m = dict(shared)
        m['xp'] = f(x_prompt[core * NPS:(core + 1) * NPS].reshape(NPS * LP, D))
        m['xs'] = f(x_sample[b])
        m['ck'] = f(cache_k[b].reshape(2, 256, 256))
        m['cv'] = f(cache_v[b].reshape(2, 256, 256))
        m['st0'] = f(state_ssm[b])
        m['cvec'] = f(np.stack([c_ctx, c[b]], axis=0))
        m.update(_core_consts(core, consts))
        in_maps.append(m)
    res = run_bass_kernel_spmd(nc, in_maps, core_ids=list(range(8)))
    R = res.results
    y_prompt = np.concatenate([R[i]['yp'].reshape(NPS, LP, D) for i in range(8)], axis=0).astype(np.float32)
    y_sample = np.stack([np.concatenate([R[b_ * 4 + j_]['ys_own'] for j_ in range(4)], axis=0) for b_ in range(2)], axis=0).astype(np.float32)
    nk_ = np.concatenate([R[i]['nk'].reshape(NPS, 2, LP, 2, 128) for i in range(8)], axis=0).astype(np.float32)
    nv_ = np.concatenate([R[i]['nv'].reshape(NPS, 2, LP, 2, 128) for i in range(8)], axis=0).astype(np.float32)
    ns_ = np.concatenate([R[i]['nst'] for i in range(8)], axis=0).astype(np.float32)
    return (y_prompt, y_sample, nk_, nv_, ns_)
```

```python
import contextlib
import math
import numpy as np
import concourse.bass as bass
import concourse.mybir as mybir
from concourse.bass_utils import run_bass_kernel_spmd

F32 = mybir.dt.float32
BF = mybir.dt.bfloat16
AF = mybir.ActivationFunctionType
ALU = mybir.AluOpType

D = 2048
KT = 16
NT = 256
DIN = 11264
LP = 256
LS = 4096
NPS = 4
DEPTH = 2
ALPHA = (2 * DEPTH) ** 0.25
EPS = 1e-5
GC1 = 1.5957691216057308
GC2 = 0.044715
SAME_ENG_SYNC = True
SAME_ENG_DIST = 8


class Sched:
    NS = 8

    def __init__(self, nc):
        self.nc = nc
        self.ops = []

    def op(self, eng, fn, r=(), w=()):
        w = tuple(w) + tuple(k for k in r if k.startswith('ps') and k not in w)
        self.ops.append((eng, fn, tuple(r), tuple(w), False))

    def dma(self, q, out, in_, r=(), w=(), slow=False, ind=None):
        self.ops.append((q, (out, in_, slow, ind), tuple(r), tuple(w), True))

    def emit(self, es):
        nc = self.nc
        ops = self.ops
        n = len(ops)
        last_w = {}
        readers = {}
        deps = [None] * n
        for i, (eng, fn, r, w, isd) in enumerate(ops):
            d = set()
            for k in r:
                if k in last_w:
                    d.add(last_w[k])
            for k in w:
                if k in last_w:
                    d.add(last_w[k])
                for j in readers.get(k, ()):
                    d.add(j)
            d.discard(i)
            deps[i] = d
            for k in w:
                last_w[k] = i
                readers[k] = []
            for k in r:
                if k not in w:
                    readers.setdefault(k, []).append(i)
        need = [False] * n
        lidx = [0] * n
        lc = {}
        for i in range(n):
            lidx[i] = lc.get(ops[i][0], 0)
            lc[ops[i][0]] = lidx[i] + 1

        def same_eng_skip(i, j):
            if ops[i][0] == 'pe' or not SAME_ENG_SYNC:
                return True
            return (lidx[i] - lidx[j]) > SAME_ENG_DIST

        for i in range(n):
            ei = ops[i][0]
            for j in deps[i]:
                ej, _, _, _, dj = ops[j]
                if dj or ej != ei or not same_eng_skip(i, j):
                    need[j] = True
        engs = ['pe', 'act', 'dve', 'pool', 'sp']
        esem = {e: es.enter_context(nc.semaphore('es_' + e)) for e in engs}
        dsem = {q: [es.enter_context(nc.semaphore('ds_%s%d' % (q, k))) for k in range(self.NS)]
                for q in ('sp', 'pool', 'act')}
        cnt = {e: 0 for e in engs}
        dcnt = {q: 0 for q in dsem}
        sig = [None] * n
        streams = {e: [] for e in engs}
        waited = {e: {} for e in engs}

        def addwait(e, lst, sem, val):
            key = id(sem)
            if waited[e].get(key, 0) >= val:
                return
            waited[e][key] = val
            lst.append(('w', sem, val))

        for i, (eng, fn, r, w, isd) in enumerate(ops):
            lst = streams[eng]
            wmax = {}
            for j in deps[i]:
                if sig[j] is None:
                    continue
                ej, dj = ops[j][0], ops[j][4]
                if (not dj) and ej == eng and same_eng_skip(i, j):
                    continue
                key = id(sig[j][0])
                if key not in wmax or wmax[key][1] < sig[j][1]:
                    wmax[key] = sig[j]
            for key in sorted(wmax, key=lambda k_: wmax[k_][1]):
                addwait(eng, lst, wmax[key][0], wmax[key][1])
            if isd:
                k = dcnt[eng]
                dcnt[eng] += 1
                sem = dsem[eng][k % self.NS]
                rnd = k // self.NS
                if rnd > 0:
                    addwait(eng, lst, sem, 16 * rnd)
                sig[i] = (sem, 16 * (rnd + 1))
                lst.append(('d', fn, sem))
            else:
                if need[i]:
                    cnt[eng] += 1
                    sig[i] = (esem[eng], cnt[eng])
                    lst.append(('o', fn, esem[eng]))
                else:
                    lst.append(('o', fn, None))
        for q in dsem:
            for k in range(self.NS):
                tot = (dcnt[q] - k + self.NS - 1) // self.NS if dcnt[q] > k else 0
                if tot > 0:
                    streams[q].append(('w', dsem[q][k], 16 * tot))

        def run(engine, lst):
            for it in lst:
                if it[0] == 'w':
                    engine.wait_ge(it[1], it[2])
                elif it[0] == 'd':
                    out, in_, slow, ind = it[1]
                    if ind is not None:
                        engine.indirect_dma_start(out=out, out_offset=None, in_=in_,
                                                  in_offset=bass.IndirectOffsetOnAxis(ap=ind, axis=0)).then_inc(it[2], 16)
                    elif slow:
                        engine.dma_start(out=out, in_=in_, allow_slow_non_contiguous=True).then_inc(it[2], 16)
                    else:
                        engine.dma_start(out=out, in_=in_).then_inc(it[2], 16)
                else:
                    ins = it[1](engine)
                    if it[2] is not None:
                        ins.then_inc(it[2], 1)

        block = es.enter_context(nc.Block())

        @block.tensor
        def _(e):
            run(e, streams['pe'])

        @block.scalar
        def _(e):
            run(e, streams['act'])

        @block.vector
        def _(e):
            run(e, streams['dve'])

        @block.gpsimd
        def _(e):
            run(e, streams['pool'])

        @block.sync
        def _(e):
            run(e, streams['sp'])
        return {e: (len(streams[e]), cnt[e]) for e in engs}


def _core_consts(core, consts):
    j = core % 4
    m = {}
    p = np.arange(128)[:, None]
    cidx = np.arange(10)[None, :]
    m['ridx'] = np.clip(1024 * j + 128 * (cidx - 1) + p, 0, LS - 1).astype(np.int32)
    q = np.clip(1024 * j - 128 + np.arange(1280), 0, LS - 1)
    m['ropec_o'] = np.ascontiguousarray(consts['ropec'][:, q])
    m['ropes_o'] = np.ascontiguousarray(consts['ropes'][:, q])
    oh = np.zeros((128, 4), np.float32); oh[:, j] = 1.0
    m['oh4'] = oh
    vm = np.ones((128, 2), np.float32)
    if j == 0:
        vm[:, 0] = 0.0
    if j == 3:
        vm[:, 1] = 0.0
    m['vmask'] = vm
    return m


def _host_consts():
    c = {}
    c['ident'] = np.eye(128, dtype=np.float32)
    R = np.zeros((128, 128), np.float32)
    for d in range(128):
        if d % 64 < 32:
            R[d, d + 32] = -1.0
        else:
            R[d, d - 32] = 1.0
    c['rotT'] = np.ascontiguousarray(R.T)
    pos = np.arange(LS)
    row = pos // 64
    col = pos % 64
    inv = 10000.0 ** (-np.arange(0, 64, 2, dtype=np.float32) / 64.0)
    ang = np.zeros((128, LS), np.float32)
    for d in range(128):
        p = row if d < 64 else col
        ang[d] = p.astype(np.float32) * inv[d % 32]
    c['ropec'] = np.cos(ang).astype(np.float32)
    c['ropes'] = np.sin(ang).astype(np.float32)
    kk = np.arange(128)[:, None]
    qq = np.arange(128)[None, :]
    c['mlo'] = (kk >= qq).astype(np.float32)
    c['mhi'] = (kk <= qq).astype(np.float32)
    tp = (np.arange(128) // 16)[:, None]
    tt = (np.arange(128) // 16)[None, :]
    c['cmf'] = (tt >= tp).astype(np.float32)
    c['cmb'] = (tp >= tt).astype(np.float32)
    return c


def build(stop=None, dumps=()):
    nc = bass.Bass("TRN2", target_bir_lowering=False)
    es = contextlib.ExitStack()
    sc = Sched(nc)
    PI = math.pi

    def din(name, shape, dt=F32):
        return nc.dram_tensor(name, list(shape), dt, kind="ExternalInput").ap()

    def dout(name, shape):
        return nc.dram_tensor(name, list(shape), F32, kind="ExternalOutput").ap()

    def dscr(name, shape, dt=F32):
        return nc.dram_tensor(name, list(shape), dt, kind="Internal").ap()

    xp = din("xp", [NPS * LP, D]); xs = din("xs", [LS, D])
    ck = din("ck", [2, 256, 256]); cv = din("cv", [2, 256, 256])
    st0 = din("st0", [2, 2, 2, 32, 64]); cvec = din("cvec", [2, D])
    w_ada = din("w_ada", [2, D, 3 * D]); b_ada = din("b_ada", [2, 3 * D]); w_in = din("w_in", [2, D, DIN])
    lam_re = din("lam_re", [2, 2, 32, 64]); lam_im = din("lam_im", [2, 2, 32, 64]); log_step = din("log_step", [2, 2, 32])
    b_re = din("b_re", [2, 2, 32, 64, 16]); b_im = din("b_im", [2, 2, 32, 64, 16])
    c_re = din("c_re", [2, 2, 32, 16, 64]); c_im = din("c_im", [2, 2, 32, 16, 64])
    ssm_d = din("ssm_d", [2, 512]); w_glu = din("w_glu", [2, 512, 512]); b_glu = din("b_glu", [2, 512])
    sink = din("sink", [2, 8]); sgu_g = din("sgu_g", [2, 512]); sgu_b = din("sgu_b", [2, 512])
    w_s = din("w_s", [2, 4, 128, 128]); b_s = din("b_s", [2, 512])
    w_pa = din("w_pa", [2, 512, D]); w_pb = din("w_pb", [2, 1024, D]); w_pc = din("w_pc", [2, 512, D])
    w_out = din("w_out", [2, D, D]); ln_g = din("ln_g", [2, D]); ln_b = din("ln_b", [2, D])
    c_ident = din("ident", [128, 128]); c_rotT = din("rotT", [128, 128])
    c_ropec = din("ropec", [128, LS]); c_ropes = din("ropes", [128, LS])
    c_mlo = din("mlo", [128, 128]); c_mhi = din("mhi", [128, 128])
    c_cmf = din("cmf", [128, 128]); c_cmb = din("cmb", [128, 128])
    ridx_d = din("ridx", [128, 10], mybir.dt.int32); c_ropec_o = din("ropec_o", [128, 1280]); c_ropes_o = din("ropes_o", [128, 1280])
    oh4_d = din("oh4", [128, 4]); vmask_d = din("vmask", [128, 2])
    yp = dout("yp", [NPS * LP, D]); ys = dout("ys_own", [1024, D])
    nk = dout("nk", [NPS, 2, LP, 256]); nv = dout("nv", [NPS, 2, LP, 256]); nst = dout("nst", [NPS, 2, 2, 2, 32, 64])
    zp = dscr("zp", [NPS * LP, D]); zs = dscr("zs", [LS, D]); gsc = dscr("gsc", [2, D])
    s5w = dscr("s5w", [64, 3, 128, 128], BF)
    wsc = dscr("wsc", [2, 76, 128, 16, 256], BF)
    wsc2 = dscr("wsc2", [2, 48, 128, 24, 128], BF)

    def sb(name, shape, dt=F32):
        return es.enter_context(nc.sbuf_tensor("s_" + name, list(shape), dt))

    ps = [es.enter_context(nc.psum_tensor("ps%d" % i, [128, 512], F32)) for i in range(8)]
    psn = [0]

    def bank():
        i = psn[0] % 8
        psn[0] += 1
        return ps[i], 'ps%d' % i

    xin = [sb("xin0", [128, D]), sb("xin1", [128, D])]
    rr = sb("rr", [128, D])
    stats = sb("stats", [128, 4, 6]); mv = sb("mv", [128, 2]); rstd = sb("rstd", [128, 1])
    hT = sb("hT", [128, KT, 512], BF)
    wb = [sb("wb%d" % i, [128, KT, 256], BF) for i in range(3)]
    zaT = sb("zaT", [128, 4, NT], BF); qT = sb("qT", [128, 8, NT], BF); yaT = zaT; ybT = qT; kT = sb("kT", [128, 2, 512], BF)
    zbT = sb("zbT", [128, 8, NT], BF); uT = sb("uT", [128, 4, NT], BF); zcT = sb("zcT", [128, 4, NT], BF); ycT = zcT
    Xp = sb("Xp", [32, 32, 8, 16], BF); Xpp = sb("Xpp", [128, 32, 32], BF)
    vtok = sb("vtok", [128, 4, 256], BF); vsg = rr[:, 0:1024].rearrange("p (s c) -> p s c", c=512)
    vsln = sb("vsln", [128, 2, 512], BF)
    V = sb("V", [128, 64, 32]); Vs = sb("Vs", [128, 64, 32]); Sb = sb("Sb", [128, 64, 32], BF)
    tA = sb("tA", [128, 64]); tB = sb("tB", [128, 64])
    yg = sb("yg", [32, 8, 32, 16], BF); ygf = sb("ygf", [32, 4, 8, 16]); ygf2 = sb("ygf2", [32, 4, 8, 16])
    ygT = sb("ygT", [128, 4, NT], BF)
    s5wb = [sb("s5wb%d" % i, [128, 8, 3, 128], BF) for i in range(2)]
    merged = sb("merged", [128, KT, NT], BF)
    tfall = sb("tfall", [128, 4, 512])
    tf = [tfall[:, i, :] for i in range(4)]
    pTs = [sb("pTs%d" % i, [128, 512], BF) for i in range(5)]
    Fg = V[:, 0:32, :].rearrange("p (g a) n -> p g (a n)", a=4)
    Gg = V[:, 32:64, :].rearrange("p (g a) n -> p g (a n)", a=4)
    Eg = Vs[:, 0:32, :].rearrange("p (g a) n -> p g (a n)", a=4)
    w3 = s5wb[0]
    wsn = tf[0][:, :].rearrange("p (g q) -> p g q", q=128)
    ckn = tf[1][:, :].rearrange("p (b c) -> p b c", c=256)
    rc = sb("rc", [128, 512]); rs = sb("rs", [128, 512]); qraw = pTs[4]
    ident = sb("ident", [128, 128]); identb = sb("identb", [128, 128], BF); rotT = sb("rotT", [128, 128], BF)
    onesb = sb("onesb", [128, 128], BF); mlo = sb("mlo", [128, 128], BF); mhi = sb("mhi", [128, 128], BF)
    cmf = sb("cmf", [128, 128]); cmb = sb("cmb", [128, 128])
    scT = sb("scT", [128, 2, KT], BF); cvT = sb("cvT", [128, 2, KT])
    modT = sb("modT", [128, 48, 2]); badaT = sb("badaT", [128, 48]); sc1 = sb("sc1", [128, KT, 2])
    sgb = sb("sgb", [128, 512]); sbb = sb("sbb", [128, 512]); bsb = sb("bsb", [128, 512])
    wsT = sb("wsT", [128, 4, 128], BF)
    esink = sb("esink", [128, 8]); ckT = sb("ckT", [128, 2, 256], BF)
    cvb = sb("cvb", [128, 2, 256], BF)
    wglu = sb("wglu", [128, 4, 512], BF); bglu = sb("bglu", [128, 4]); dsk = sb("dsk", [32, 512])
    lr = sb("lr", [128, 32]); li = sb("li", [128, 32]); dtt = sb("dtt", [128, 32])
    p1 = sb("p1", [128, 32]); p2 = sb("p2", [128, 32]); p3 = sb("p3", [128, 32]); p4 = sb("p4", [128, 32])
    cosv = sb("cosv", [128, 32]); sinv = sb("sinv", [128, 32])
    fre = sb("fre", [128, 32]); fim = sb("fim", [128, 32])
    PWr = xin[1][:, 0:544].rearrange("p (k g) -> p k g", g=32); PWi = xin[1][:, 544:1088].rearrange("p (k g) -> p k g", g=32)
    nPWi = xin[1][:, 1088:1632].rearrange("p (k g) -> p k g", g=32)
    Bre = sb("Bre", [128, 8, 16]); Bim = sb("Bim", [128, 8, 16]); Bbr = sb("Bbr", [128, 8, 16]); Bbi = sb("Bbi", [128, 8, 16])
    bt1 = sb("bt1", [128, 8, 16]); bt2 = sb("bt2", [128, 8, 16])
    cnat = sb("cnat", [128, 2, 64]); Cre = sb("Cre", [128, 8, 16]); nCim = sb("nCim", [128, 8, 16])
    Ar = sb("Ar", [128, 64]); Aix = sb("Aix", [128, 64]); nAix = sb("nAix", [128, 64]); sgn = sb("sgn", [128, 1])
    A2r = sb("A2r", [128, 64]); A2i = sb("A2i", [128, 64]); A2ix = sb("A2ix", [128, 64]); nA2ix = sb("nA2ix", [128, 64])
    sinit = sb("sinit", [128, 2, 32]); sinitw = sb("sinitw", [128, 2, 32])
    Pst = sb("Pst", [128, 17, 2, 64])
    Lst = sb("Lst", [128, 16, 2, 32])
    stg = sb("stg", [32, 128])
    Ssum = sb("Ssum", [128, 4, 64])
    ridx = sb("ridx", [128, 10], mybir.dt.int32); oh4 = sb("oh4", [128, 4]); vmask = sb("vmask", [128, 2])
    Psel = sb("Psel", [128, 2, 2, 32])

    def tt(eng, out, a, b, op, r, w):
        sc.op(eng, lambda e: e.tensor_tensor(out=out, in0=a, in1=b, op=op), r, w)

    def ts(eng, out, a, s1, op0, r, w, s2=None, op1=None):
        if op1 is None:
            sc.op(eng, lambda e: e.tensor_scalar(out=out, in0=a, scalar1=s1, scalar2=None, op0=op0), r, w)
        else:
            sc.op(eng, lambda e: e.tensor_scalar(out=out, in0=a, scalar1=s1, scalar2=s2, op0=op0, op1=op1), r, w)

    def act(out, in_, func, r, w, bias=None, scale=None):
        kw = {}
        if bias is not None:
            kw['bias'] = bias
        if scale is not None:
            kw['scale'] = scale
        sc.op('act', lambda e: e.activation(out=out, in_=in_, func=func, **kw), r, w)

    def mm(out, lhsT, rhs, start, stop, r, w):
        sc.op('pe', lambda e: e.matmul(out, lhsT, rhs, start=start, stop=stop), r, w)

    def tr(out, in_, idn, r, w):
        sc.op('pe', lambda e: e.transpose(out, in_, idn), r, w)

    def cp(eng, out, in_, r, w):
        if eng == 'act':
            sc.op(eng, lambda e: e.copy(out=out, in_=in_), r, w)
        else:
            sc.op(eng, lambda e: e.tensor_copy(out=out, in_=in_), r, w)

    def recip(out, in_, r, w):
        sc.op('dve', lambda e: e.reciprocal(out=out, in_=in_), r, w)

    def mset(eng, ap, val, w):
        sc.op(eng, lambda e: e.memset(ap, val), (), w)

    def bc(ap, shape):
        return ap.broadcast_to(list(shape))

    sc.dma('sp', ident[:], c_ident, w=['ident'])
    cp('dve', identb[:], ident[:], ['ident'], ['identb'])
    sc.dma('sp', tf[0][:, 0:128], c_rotT, w=['tf0'])
    cp('dve', rotT[:], tf[0][:, 0:128], ['tf0'], ['rotT'])
    sc.dma('sp', tf[1][:, 0:128], c_mlo, w=['tf1'])
    cp('dve', mlo[:], tf[1][:, 0:128], ['tf1'], ['mlo'])
    sc.dma('sp', tf[2][:, 0:128], c_mhi, w=['tf2'])
    cp('dve', mhi[:], tf[2][:, 0:128], ['tf2'], ['mhi'])
    sc.dma('sp', cmf[:], c_cmf, w=['cmf'])
    sc.dma('sp', cmb[:], c_cmb, w=['cmb'])
    sc.dma('sp', ridx[:], ridx_d, w=['ridx'])
    sc.dma('sp', oh4[:], oh4_d, w=['oh4'])
    sc.dma('sp', vmask[:], vmask_d, w=['vmask'])
    mset('dve', onesb[:], 1.0, ['onesb'])
    mset('dve', sgn[0:64, :], -1.0, ['sgn'])
    mset('dve', sgn[64:128, :], 1.0, ['sgn'])
    sc.dma('sp', cvT[:], cvec.rearrange("c (k p) -> p c k", p=128), w=['cvT'], slow=True)
    act(scT[:], cvT[:], AF.Silu, ['cvT'], ['scT'])

    wslot = [0]

    WNAMES = {'in': (w_in, 16, 0), 'pp': (None, 16, 44), 'out': (w_out, 16, 68)}
    WPACK = (('pa', w_pa, 4, 0), ('pb', w_pb, 8, 4), ('pc', w_pc, 4, 12))

    def wload_cast(src):
        i = wslot[0] % 3
        wslot[0] += 1
        K = src.shape[0] // 128
        sc.dma('pool', wb[i][:, 0:K, :], src.rearrange("(k p) c -> p k c", p=128), w=['wb%d' % i])
        return wb[i], 'wb%d' % i

    def wconvert(l):
        for nm, (wt_, K, g0) in WNAMES.items():
            if wt_ is None:
                continue
            ng = wt_.shape[2] // 256
            if nm == 'in':
                ng = 20
            for gi in range(ng):
                t_, k_ = wload_cast(wt_[l][:, gi * 256:(gi + 1) * 256])
                sc.dma('sp', wsc[l, g0 + gi, :, 0:K, :], t_[:, 0:K, :], r=[k_], w=['wsc%d_%d' % (l, g0 + gi)])
        for fg in range(8):
            for br, (nm, wt_, K, koff) in enumerate(WPACK):
                c0 = 5120 + br * 2048 + fg * 256
                ta_, ka_ = wload_cast(w_in[l][:, c0:c0 + 256])
                tp_, kp_ = wload_cast(wt_[l][:, fg * 256:(fg + 1) * 256])
                for j in range(2):
                    g2 = (fg * 3 + br) * 2 + j
                    sc.dma('sp', wsc2[l, g2, :, 0:16, :], ta_[:, 0:16, j * 128:(j + 1) * 128], r=[ka_], w=['wsc2_%d_%d' % (l, g2)])
                    sc.dma('sp', wsc2[l, g2, :, 16:16 + K, :], tp_[:, 0:K, j * 128:(j + 1) * 128], r=[kp_], w=['wsc2_%d_%d' % (l, g2)])

    def wload2(l, g2, K):
        i = wslot[0] % 3
        wslot[0] += 1
        v = wb[i][:].rearrange("p k c -> p (k c)")[:, 0:24 * 128].rearrange("p (k c) -> p k c", c=128)
        sc.dma('sp', v[:, 0:16 + K, :], wsc2[l, g2, :, 0:16 + K, :], r=['wsc2_%d_%d' % (l, g2)], w=['wb%d' % i])
        return v, 'wb%d' % i

    def wload(nm, l, gi, slot=None):
        wt_, K, g0 = WNAMES[nm]
        if slot is None:
            i = wslot[0] % 3
            wslot[0] += 1
        else:
            i = slot
        sc.dma('sp', wb[i][:, 0:K, :], wsc[l, g0 + gi, :, 0:K, :], r=['wsc%d_%d' % (l, g0 + gi)], w=['wb%d' % i])
        return wb[i], 'wb%d' % i

    def gelu_evac(out, pin, pk, shape, tix, rextra=(), wk=()):
        n = 1
        for s_ in shape[1:]:
            n *= s_
        t1 = tf[tix][0:shape[0], 0:n]
        t2 = tf[tix + 1][0:shape[0], 0:n]
        k1, k2 = 'tf%d' % tix, 'tf%d' % (tix + 1)
        pin2 = pin
        act(t1, pin2, AF.Square, [pk], [k1])
        ts('dve', t1, t1, GC2, ALU.mult, [k1], [k1], s2=1.0, op1=ALU.add)
        tt('dve', t1, t1, pin2, ALU.mult, [k1, pk], [k1])
        act(t2, t1, AF.Sigmoid, [k1], [k2], scale=GC1)
        tt('dve', out, t2, pin2, ALU.mult, [k2, pk] + list(rextra), list(wk))

    def wrap(out, x, kx, ko):
        mset('dve', p4[:], 0.0, ['p4'])
        for m in range(1, 9):
            ts('dve', p3[:], x, (2 * m - 1) * PI, ALU.is_gt, [kx], ['p3'])
            tt('dve', p4[:], p4[:], p3[:], ALU.add, ['p3', 'p4'], ['p4'])
        ts('dve', p4[:], p4[:], -2.0 * PI, ALU.mult, ['p4'], ['p4'])
        tt('dve', out, x, p4[:], ALU.add, [kx, 'p4'], [ko])

    def cmul(ore, oim, are, aim, bre, bim, keys_r, ko):
        tt('dve', p1[:], are, bre, ALU.mult, keys_r, ['p1'])
        tt('dve', p2[:], aim, bim, ALU.mult, keys_r, ['p2'])
        tt('dve', p3[:], are, bim, ALU.mult, keys_r, ['p3'])
        tt('dve', p4[:], aim, bre, ALU.mult, keys_r, ['p4'])
        tt('dve', ore, p1[:], p2[:], ALU.subtract, ['p1', 'p2'], ko)
        tt('dve', oim, p3[:], p4[:], ALU.add, ['p3', 'p4'], ko)

    def mix(out, g0, ng, Wre_, Wim_, Xre, Xim, kr, ko):
        for h in (0, 1):
            P = slice(h * 64, h * 64 + 64)
            wr = bc(Wre_[P, g0:g0 + ng].unsqueeze(2), [64, ng, 16])
            wi = bc(Wim_[P, g0:g0 + ng].unsqueeze(2), [64, ng, 16])
            xa_ = Xre[P, 0:ng, :] if h == 0 else Xim[P, 0:ng, :]
            xb_ = Xim[P, 0:ng, :] if h == 0 else Xre[P, 0:ng, :]
            tt('dve', bt1[P, 0:ng, :], xa_, wr, ALU.mult, kr, ['bt1'])
            tt('dve', bt2[P, 0:ng, :], xb_, wi, ALU.mult, kr, ['bt2'])
            tt('dve', out[P], bt1[P, 0:ng, :], bt2[P, 0:ng, :], ALU.subtract if h == 0 else ALU.add,
               ['bt1', 'bt2'], ko)

    EF = [[t + 1 for t in range(8)], [8 - t for t in range(8)]]

    def layer_prep(l):
        sc.dma('act', badaT[:], b_ada[l].rearrange("(c p) -> p c", p=128), w=['badaT'], slow=True)
        pm, pmk = bank()
        for gi in range(24):
            wt, wk = wload_cast(w_ada[l][:, gi * 256:(gi + 1) * 256])
            for j in range(2):
                ch = gi * 2 + j
                for k in range(KT):
                    mm(pm[:, ch * 2:ch * 2 + 2], wt[:, k, j * 128:(j + 1) * 128], scT[:, :, k],
                       k == 0, k == KT - 1, [wk, 'scT'], [pmk])
        tt('dve', modT[:], pm[:, 0:96].rearrange("p (c t) -> p c t", t=2), bc(badaT[:].unsqueeze(2), [128, 48, 2]),
           ALU.add, [pmk, 'badaT'], ['modT'])
        ts('dve', sc1[:], modT[:, 16:32, :], 1.0, ALU.add, ['modT'], ['sc1'])
        for c_ in range(2):
            sc.dma('act', gsc[c_].rearrange("(k p) -> p k", p=128), modT[:, 32:48, c_], r=['modT'], w=['gsc'], slow=True)
        sc.dma('act', sgb[:], sgu_g[l].partition_broadcast(128), w=['sgb'])
        sc.dma('act', sbb[:], sgu_b[l].partition_broadcast(128), w=['sbb'])
        sc.dma('act', bsb[:], b_s[l].partition_broadcast(128), w=['bsb'])
        sc.dma('act', wsn[:], w_s[l].rearrange("g p q -> p g q"), w=['tf0'])
        pw, pwk = bank()
        for g in range(4):
            tr(pw[:, g * 128:(g + 1) * 128], wsn[:, g, :], ident[:], ['tf0', 'ident'], [pwk])
        cp('dve', wsT[:], pw[:, 0:512].rearrange("p (g q) -> p g q", q=128), [pwk], ['wsT'])
        sc.dma('act', esink[:], sink[l].partition_broadcast(128), w=['esink'])
        act(esink[:], esink[:], AF.Exp, ['esink'], ['esink'])
        sc.dma('act', ckn[:], ck[l].rearrange("(b p) c -> p b c", p=128), w=['tf1'])
        pc_, pck = bank()
        for kvh in range(2):
            for b_ in range(2):
                tr(pc_[:, (kvh * 2 + b_) * 128:(kvh * 2 + b_ + 1) * 128], ckn[:, b_, kvh * 128:(kvh + 1) * 128],
                   ident[:], ['tf1', 'ident'], [pck])
        cp('dve', ckT[:], pc_[:, 0:512].rearrange("p (h t) -> p h t", t=256), [pck], ['ckT'])
        sc.dma('pool', cvb[:], cv[l].rearrange("(b p) c -> p b c", p=128), w=['cvb'])
        sc.dma('pool', wglu[:], w_glu[l].rearrange("(j p) c -> p j c", p=128), w=['wglu'])
        sc.dma('act', bglu[:], b_glu[l].rearrange("(j p) -> p j", p=128), w=['bglu'], slow=True)
        sc.dma('act', dsk[:], ssm_d[l].partition_broadcast(32), w=['dsk'])
        for d in range(2):
            for h in range(2):
                P = slice(h * 64, h * 64 + 64)
                sc.dma('act', lr[P, :], lam_re[l, d].rearrange("g p -> p g"), w=['lr'], slow=True)
                sc.dma('act', li[P, :], lam_im[l, d].rearrange("g p -> p g"), w=['li'], slow=True)
                for ri in range(2):
                    sc.dma('act', sinit[ri * 64:(ri + 1) * 64, d, :] if h == 0 else sinitw[(1 - ri) * 64:(2 - ri) * 64, d, :],
                           st0[l, d, ri].rearrange("g p -> p g"), w=['sinit' if h == 0 else 'sinitw'], slow=True)
            sc.dma('act', dtt[:], log_step[l, d].partition_broadcast(128), w=['dtt'])
            act(dtt[:], dtt[:], AF.Exp, ['dtt'], ['dtt'])
            tt('dve', p1[:], li[:], dtt[:], ALU.mult, ['li', 'dtt'], ['p1'])
            wrap(p2[:], p1[:], 'p1', 'p2')
            act(sinv[:], p2[:], AF.Sin, ['p2'], ['sinv'])
            if stop == 'wrap':
                return
            ts('dve', p1[:], p1[:], PI / 2, ALU.add, ['p1'], ['p1'])
            wrap(p2[:], p1[:], 'p1', 'p2')
            act(cosv[:], p2[:], AF.Sin, ['p2'], ['cosv'])
            tt('dve', p1[:], lr[:], dtt[:], ALU.mult, ['lr', 'dtt'], ['p1'])
            act(p2[:], p1[:], AF.Exp, ['p1'], ['p2'])
            act(p3[:], p1[:], AF.Exp, ['p1'], ['p3'], scale=-1.0)
            mset('dve', PWr[:, 8, :], 1.0, ['PW', 'nPW', 'xin1'])
            mset('dve', PWi[:, 8, :], 0.0, ['PW'])
            tt('dve', PWr[:, 9, :], p2[:], cosv[:], ALU.mult, ['p2', 'cosv'], ['PW'])
            tt('dve', PWi[:, 9, :], p2[:], sinv[:], ALU.mult, ['p2', 'sinv'], ['PW'])
            tt('dve', PWr[:, 7, :], p3[:], cosv[:], ALU.mult, ['p3', 'cosv'], ['PW'])
            tt('dve', PWi[:, 7, :], p3[:], sinv[:], ALU.mult, ['p3', 'sinv'], ['PW'])
            ts('dve', PWi[:, 7, :], PWi[:, 7, :], -1.0, ALU.mult, ['PW'], ['PW'])
            for k in range(2, 9):
                cmul(PWr[:, 8 + k, :], PWi[:, 8 + k, :], PWr[:, 7 + k, :], PWi[:, 7 + k, :], PWr[:, 9, :], PWi[:, 9, :], ['PW'], ['PW'])
                cmul(PWr[:, 8 - k, :], PWi[:, 8 - k, :], PWr[:, 9 - k, :], PWi[:, 9 - k, :], PWr[:, 7, :], PWi[:, 7, :], ['PW'], ['PW'])
            ts('dve', nPWi[:], PWi[:], -1.0, ALU.mult, ['PW'], ['nPW'])
            tt('dve', p1[:], lr[:], lr[:], ALU.mult, ['lr'], ['p1'])
            tt('dve', p2[:], li[:], li[:], ALU.mult, ['li'], ['p2'])
            tt('dve', p1[:], p1[:], p2[:], ALU.add, ['p1', 'p2'], ['p1'])
            recip(p1[:], p1[:], ['p1'], ['p1'])
            ts('dve', p2[:], PWr[:, 9, :], -1.0, ALU.add, ['PW'], ['p2'])
            tt('dve', p3[:], p2[:], lr[:], ALU.mult, ['p2', 'lr'], ['p3'])
            tt('dve', p4[:], PWi[:, 9, :], li[:], ALU.mult, ['PW', 'li'], ['p4'])
            tt('dve', p3[:], p3[:], p4[:], ALU.add, ['p3', 'p4'], ['p3'])
            tt('dve', fre[:], p3[:], p1[:], ALU.mult, ['p3', 'p1'], ['fre'])
            tt('dve', p3[:], PWi[:, 9, :], lr[:], ALU.mult, ['PW', 'lr'], ['p3'])
            tt('dve', p4[:], p2[:], li[:], ALU.mult, ['p2', 'li'], ['p4'])
            tt('dve', p3[:], p3[:], p4[:], ALU.subtract, ['p3', 'p4'], ['p3'])
            tt('dve', fim[:], p3[:], p1[:], ALU.mult, ['p3', 'p1'], ['fim'])
            cp('dve', Ar[:, d * 32:(d + 1) * 32], PWr[:, 16, :], ['PW'], ['Ar'])
            ts('dve', Aix[:, d * 32:(d + 1) * 32], PWi[:, 16, :], sgn[:, 0:1], ALU.mult, ['PW', 'sgn'], ['Aix'])
            ts('dve', nAix[:, d * 32:(d + 1) * 32], Aix[:, d * 32:(d + 1) * 32], -1.0, ALU.mult, ['Aix'], ['nAix'])
            cp('dve', A2r[:, d * 32:(d + 1) * 32], PWr[:, 16, :], ['PW'], ['A2'])
            cp('dve', A2i[:, d * 32:(d + 1) * 32], PWi[:, 16, :], ['PW'], ['A2'])
            cm = cmf if d == 0 else cmb
            cmk = 'cmf' if d == 0 else 'cmb'
            for gb in range(4):
                g0 = gb * 8
                for h in range(2):
                    P = slice(h * 64, h * 64 + 64)
                    sc.dma('act', Bre[P], b_re[l, d, g0:g0 + 8].rearrange("g p c -> p g c"), w=['Bre'])
                    sc.dma('act', Bim[P], b_im[l, d, g0:g0 + 8].rearrange("g p c -> p g c"), w=['Bim'])
                for (src_c, dst_c, neg) in ((c_re, Cre, False), (c_im, nCim, True)):
                    for h in range(2):
                        sc.dma('act', cnat[:, h, :], src_c[l, d, g0:g0 + 8].rearrange("g c p -> (g c) p"), w=['cnat'])
                    pb_, pbk = bank()
                    tr(pb_[:, 0:128], cnat[:].rearrange("q h p -> q (h p)"), ident[:], ['cnat', 'ident'], [pbk])
                    if neg:
                        ts('dve', dst_c[:], pb_[:, 0:128].rearrange("p (g c) -> p g c", c=16), -1.0, ALU.mult, [pbk], ['C'])
                    else:
                        cp('dve', dst_c[:], pb_[:, 0:128].rearrange("p (g c) -> p g c", c=16), [pbk], ['C'])
                fr_b = bc(fre[:, g0:g0 + 8].unsqueeze(2), [128, 8, 16]); fi_b = bc(fim[:, g0:g0 + 8].unsqueeze(2), [128, 8, 16])
                tt('dve', bt1[:], Bre[:], fr_b, ALU.mult, ['Bre', 'fre'], ['bt1'])
                tt('dve', bt2[:], Bim[:], fi_b, ALU.mult, ['Bim', 'fim'], ['bt2'])
                tt('dve', Bbr[:], bt1[:], bt2[:], ALU.subtract, ['bt1', 'bt2'], ['Bb'])
                tt('dve', bt1[:], Bim[:], fr_b, ALU.mult, ['Bim', 'fre'], ['bt1'])
                tt('dve', bt2[:], Bre[:], fi_b, ALU.mult, ['Bre', 'fim'], ['bt2'])
                tt('dve', Bbi[:], bt1[:], bt2[:], ALU.add, ['bt1', 'bt2'], ['Bb'])
                for t in range(8):
                    e = EF[d][t]
                    mix(Fg[:, :, t * 16:(t + 1) * 16], g0, 8, PWr[:, 8 + e, :], nPWi[:, 8 + e, :], Cre, nCim, ['PW', 'nPW', 'C'], ['V'])
                    mix(Gg[:, :, t * 16:(t + 1) * 16], g0, 8, PWr[:, 8 - e, :], PWi[:, 8 - e, :], Bbr, Bbi, ['PW', 'Bb'], ['V'])
                    mix(Eg[:, :, t * 16:(t + 1) * 16], g0, 8, PWr[:, 16 - e, :], PWi[:, 16 - e, :], Bbr, Bbi, ['PW', 'Bb'], ['Vs'])
                cp('dve', w3[:, :, 0, :], Fg[:], ['V'], ['s5wb0'])
                for gl in range(8):
                    pb_, pbk = bank()
                    tr(pb_[:, 0:128], Eg[:, gl, :], ident[:], ['Vs', 'ident'], [pbk])
                    mm(pb_[:, 128:256], Gg[:, gl, :], Fg[:, gl, :], True, True, ['V', 'V'], [pbk])
                    cp('dve', w3[:, gl, 1, :], pb_[:, 0:128], [pbk], ['s5wb0'])
                    tt('dve', w3[:, gl, 2, :], pb_[:, 128:256], cm[:], ALU.mult, [pbk, cmk], ['s5wb0'])
                sc.dma('act', s5w[d * 32 + g0:d * 32 + g0 + 8].rearrange("g t k m -> k g t m"), w3[:], r=['s5wb0'], w=['s5w'])
        TRv = rr[:].rearrange("p (g n) -> p g n", n=32)
        TIv = tfall[:].rearrange("p a c -> p (a c)").rearrange("p (g n) -> p g n", n=32)
        tkeys = ['rr', 'tf0', 'tf1', 'tf2', 'tf3']
        mset('dve', TRv[:, 0:32, 31:32], 1.0, tkeys)
        mset('dve', TIv[:, 0:32, 31:32], 0.0, tkeys)
        mset('dve', TRv[:, 32:64, 0:1], 1.0, tkeys)
        mset('dve', TIv[:, 32:64, 0:1], 0.0, tkeys)
        for it_ in range(5):
            m = 1 << it_
            for d in range(2):
                G = slice(d * 32, d * 32 + 32)
                if d == 0:
                    srcs, dsts = slice(32 - m, 32), slice(32 - 2 * m, 32 - m)
                else:
                    srcs, dsts = slice(0, m), slice(m, 2 * m)
                amr = bc(A2r[:, G].unsqueeze(2), [128, 32, m])
                ami = bc(A2i[:, G].unsqueeze(2), [128, 32, m])
                t1_ = V[:, 0:32, 0:m]
                t2_ = V[:, 32:64, 0:m]
                tt('dve', t1_, TRv[:, G, srcs], amr, ALU.mult, tkeys + ['A2'], ['V'])
                tt('dve', t2_, TIv[:, G, srcs], ami, ALU.mult, tkeys + ['A2'], ['V'])
                tt('dve', TRv[:, G, dsts], t1_, t2_, ALU.subtract, ['V'], tkeys)
                tt('dve', t1_, TRv[:, G, srcs], ami, ALU.mult, tkeys + ['A2'], ['V'])
                tt('dve', t2_, TIv[:, G, srcs], amr, ALU.mult, tkeys + ['A2'], ['V'])
                tt('dve', TIv[:, G, dsts], t1_, t2_, ALU.add, ['V'], tkeys)
            tt('dve', tA[:], A2r[:], A2r[:], ALU.mult, ['A2'], ['tA'])
            tt('dve', tB[:], A2i[:], A2i[:], ALU.mult, ['A2'], ['tB'])
            tt('dve', tA[:], tA[:], tB[:], ALU.subtract, ['tA', 'tB'], ['tA'])
            tt('dve', tB[:], A2r[:], A2i[:], ALU.mult, ['A2'], ['tB'])
            cp('dve', A2r[:], tA[:], ['tA'], ['A2'])
            ts('dve', A2i[:], tB[:], 2.0, ALU.mult, ['tB'], ['A2'])
        ts('dve', TIv[:], TIv[:], sgn[:, 0:1], ALU.mult, tkeys + ['sgn'], tkeys)
        ts('dve', A2ix[:], A2i[:], sgn[:, 0:1], ALU.mult, ['A2', 'sgn'], ['A2x'])
        ts('dve', nA2ix[:], A2ix[:], -1.0, ALU.mult, ['A2x'], ['A2x'])

    xslot = [0]

    def ln_ht(src, rows, col0, cond, rkeys=()):
        for si, r0 in enumerate(rows):
            b = xslot[0] % 2
            xslot[0] += 1
            xk = 'xin%d' % b
            xt = xin[b]
            xw = [xk] if b == 0 else [xk, 'PW', 'nPW']
            if isinstance(r0, tuple):
                sc.dma('pool', xt[:], src, r=['ridx'] + list(rkeys), w=xw, ind=ridx[:, r0[1]:r0[1] + 1])
            else:
                sc.dma('sp', xt[:], src[r0:r0 + 128, :], r=list(rkeys), w=xw)
            for q in range(4):
                sc.op('dve', lambda e, q=q, xt=xt: e.bn_stats(out=stats[:, q, :], in_=xt[:, q * 512:(q + 1) * 512]), [xk], ['stats'])
            sc.op('dve', lambda e: e.bn_aggr(out=mv[:], in_=stats[:].rearrange("p a b -> p (a b)")), ['stats'], ['mv'])
            act(rstd[:], mv[:, 1:2], AF.Sqrt, ['mv'], ['rstd'], bias=EPS)
            recip(rstd[:], rstd[:], ['rstd'], ['rstd'])
            ts('dve', xt[:], xt[:], mv[:, 0:1], ALU.subtract, [xk, 'mv', 'rstd'], [xk], s2=rstd[:, 0:1], op1=ALU.mult)
            c0 = col0 + si * 128
            for kq in range(4):
                pb_, pbk = bank()
                for kk_ in range(4):
                    k = kq * 4 + kk_
                    tr(pb_[:, kk_ * 128:(kk_ + 1) * 128], xt[:, k * 128:(k + 1) * 128], ident[:], [xk, 'ident'], [pbk])
                for kk_ in range(4):
                    k = kq * 4 + kk_
                    if k % 4 != 3:
                        act(hT[:, k, c0:c0 + 128], pb_[:, kk_ * 128:(kk_ + 1) * 128], AF.Identity, [pbk, 'sc1', 'modT'], ['hT'],
                            bias=modT[:, k, cond:cond + 1], scale=sc1[:, k, cond:cond + 1])
                    else:
                        ts('dve', hT[:, k, c0:c0 + 128], pb_[:, kk_ * 128:(kk_ + 1) * 128], sc1[:, k, cond:cond + 1], ALU.mult,
                           [pbk, 'sc1', 'modT'], ['hT'], s2=modT[:, k, cond:cond + 1], op1=ALU.add)

    def proj_xa(l, cc):
        for gi in range(2):
            wt, wk = wload('in', l, gi)
            for t in range(8):
                pb_, pbk = bank()
                for k in range(KT):
                    mm(pb_[0:32, 0:256], hT[:, k, cc + t:cc + 256:8], wt[:, k, :], k == 0, k == KT - 1, [wk, 'hT'], [pbk])
                cp('dve' if t % 2 == 0 else 'act', Xp[:, gi * 16:(gi + 1) * 16, t, :],
                   pb_[0:32, 0:256].rearrange("p (g c) -> p g c", c=16), [pbk], ['Xp'])

    def s5_states(t_init, zero_init, reng='dve', sel=False, rec=True):
        for gq in range(4):
            pb_, pbk = bank()
            for gl in range(8):
                g = gq * 8 + gl
                mm(pb_[:, gl * 32:(gl + 1) * 32], Xp[:, g, :, :].rearrange("p t c -> p (t c)"), identb[0:32, 0:32], True, True,
                   ['Xp', 'identb'], [pbk])
            cp('act', Xpp[:, gq * 8:(gq + 1) * 8, :], pb_[:, 0:256].rearrange("p (g n) -> p g n", n=32), [pbk], ['Xpp'])
        for d in range(2):
            for gq in range(4):
                i = (d * 4 + gq) % 2
                wk = 's5wb%d' % i
                sc.dma('sp', s5wb[i][:, :, 1, :], s5w[d * 32 + gq * 8:d * 32 + gq * 8 + 8, 1].rearrange("g k m -> k g m"),
                       r=['s5w'], w=[wk])
                pv, pvk = bank()
                pw_, pwk = bank()
                for gl in range(8):
                    g = gq * 8 + gl
                    mm(pv[:, gl * 32:(gl + 1) * 32], s5wb[i][:, gl, 1, :], Xpp[:, g, :], True, True, [wk, 'Xpp'], [pvk])
                    mm(pw_[0:64, gl * 32:(gl + 1) * 32], s5wb[i][:, gl, 1, 64:128], Xpp[:, g, :], True, True, [wk, 'Xpp'], [pwk])
                    mm(pw_[64:128, gl * 32:(gl + 1) * 32], s5wb[i][:, gl, 1, 0:64], Xpp[:, g, :], True, True, [wk, 'Xpp'], [pwk])
                gd0 = d * 32 + gq * 8
                cp('dve', V[:, gd0:gd0 + 8, :], pv[:, 0:256].rearrange("p (g n) -> p g n", n=32), [pvk], ['V'])
                cp('act', Vs[:, gd0:gd0 + 8, :], pw_[:, 0:256].rearrange("p (g n) -> p g n", n=32), [pwk], ['Vs'])
        for d in (range(2) if rec else ()):
            G = slice(d * 32, d * 32 + 32)
            order = list(range(32)) if d == 0 else list(range(31, -1, -1))
            for step, n_ in enumerate(order):
                if step == 0:
                    if zero_init:
                        continue
                    if sel:
                        pS = Psel[:, d, 0, :]
                        pW = Psel[:, d, 1, :]
                        kp = ['Psel']
                    else:
                        pS = Pst[:, t_init + (0 if d == 0 else 1), 0, G]
                        pW = Pst[:, t_init + (0 if d == 0 else 1), 1, G]
                        kp = ['Pst']
                else:
                    pn = order[step - 1]
                    pS = V[:, G, pn]
                    pW = Vs[:, G, pn]
                    kp = ['V', 'Vs']
                tt(reng, tA[:, 0:32], pS, Ar[:, G], ALU.mult, kp + ['Ar'], ['tA'])
                tt(reng, tB[:, 0:32], pW, Aix[:, G], ALU.mult, kp + ['Aix'], ['tB'])
                tt(reng, tA[:, 0:32], tA[:, 0:32], tB[:, 0:32], ALU.add, ['tA', 'tB'], ['tA'])
                tt(reng, tA[:, 32:64], pW, Ar[:, G], ALU.mult, kp + ['Ar'], ['tA2'])
                tt(reng, tB[:, 32:64], pS, nAix[:, G], ALU.mult, kp + ['nAix'], ['tB2'])
                tt(reng, tA[:, 32:64], tA[:, 32:64], tB[:, 32:64], ALU.add, ['tA2', 'tB2'], ['tA2'])
                tt(reng, V[:, G, n_], V[:, G, n_], tA[:, 0:32], ALU.add, ['V', 'tA'], ['V'])
                tt(reng, Vs[:, G, n_], Vs[:, G, n_], tA[:, 32:64], ALU.add, ['Vs', 'tA2'], ['Vs'])

    def s5_main(l, t_init, zero_init, seq, sel=False):
        if zero_init:
            mset('dve', Sb[:, 0:32, 0:1], 0.0, ['Sb'])
            mset('dve', Sb[:, 32:64, 31:32], 0.0, ['Sb'])
        elif sel:
            cp('dve', Sb[:, 0:32, 0:1], Psel[:, 0, 0, :].unsqueeze(2), ['Psel'], ['Sb'])
            cp('dve', Sb[:, 32:64, 31:32], Psel[:, 1, 0, :].unsqueeze(2), ['Psel'], ['Sb'])
        else:
            cp('dve', Sb[:, 0:32, 0:1], Pst[:, t_init, 0, 0:32].unsqueeze(2), ['Pst'], ['Sb'])
            cp('dve', Sb[:, 32:64, 31:32], Pst[:, t_init + 1, 0, 32:64].unsqueeze(2), ['Pst'], ['Sb'])
        cp('dve', Sb[:, 0:32, 1:32], V[:, 0:32, 0:31], ['V'], ['Sb'])
        cp('act', Sb[:, 32:64, 0:31], V[:, 32:64, 1:32], ['V'], ['Sb'])
        if seq is not None:
            for d in range(2):
                pb_, pbk = bank()
                src_ = V[:, 0:32, 31] if d == 0 else V[:, 32:64, 0]
                cp('dve', tf[0][:, 0:32], src_, ['V'], ['tf0'])
                tr(pb_[0:32, 0:128], tf[0][:, 0:32], ident[:], ['tf0', 'ident'], [pbk])
                cp('dve', stg[:], pb_[0:32, 0:128], [pbk], ['stg'])
                sc.dma('pool', nst[seq, l, d].rearrange("r g p -> g r p"), stg[:].rearrange("g (r p) -> g r p", r=2),
                       r=['stg'], w=['nst'])
        for gq in range(4):
            p0, p0k = bank()
            p1_, p1k = bank()
            for d in range(2):
                for t_ in (0, 2):
                    sc.dma('sp', s5wb[d][:, :, t_, :], s5w[d * 32 + gq * 8:d * 32 + gq * 8 + 8, t_].rearrange("g k m -> k g m"),
                           r=['s5w'], w=['s5wb%d' % d])
            for gl in range(8):
                g = gq * 8 + gl
                pp, ppk = (p0, p0k) if gl < 4 else (p1_, p1k)
                o = pp[0:32, (gl % 4) * 128:(gl % 4 + 1) * 128]
                for d in range(2):
                    wk = 's5wb%d' % d
                    mm(o, Xpp[:, g, :], s5wb[d][:, gl, 2, :], d == 0, False, [wk, 'Xpp'], [ppk])
                    mm(o, Sb[:, d * 32 + g, :], s5wb[d][:, gl, 0, :], False, d == 1, [wk, 'Sb'], [ppk])
            for hf, (pp, ppk) in enumerate(((p0, p0k), (p1_, p1k))):
                g0 = gq * 8 + hf * 4
                yv = ygf[:]
                dsl = bc(dsk[:, g0 * 16:(g0 + 4) * 16].rearrange("p (g c) -> p g c", c=16).unsqueeze(2), [32, 4, 8, 16])
                tt('dve', yv, Xp[:, g0:g0 + 4, :, :], dsl, ALU.mult, ['Xp', 'dsk'], ['ygf'])
                tt('dve', yv, yv, pp[0:32, 0:512].rearrange("p (g t c) -> p g t c", t=8, c=16), ALU.add, ['ygf', ppk], ['ygf'])
                yf = ygf[:].rearrange("p g t c -> p (g t c)")
                y2 = ygf2[:].rearrange("p g t c -> p (g t c)")
                act(y2, yf, AF.Square, ['ygf'], ['ygf2'])
                ts('dve', y2, y2, GC2, ALU.mult, ['ygf2'], ['ygf2'], s2=1.0, op1=ALU.add)
                tt('dve', y2, y2, yf, ALU.mult, ['ygf2', 'ygf'], ['ygf2'])
                act(y2, y2, AF.Sigmoid, ['ygf2'], ['ygf2'], scale=GC1)
                tt('dve', yg[:, :, g0:g0 + 4, :].rearrange("p t g c -> p g t c"), ygf2[:], ygf[:], ALU.mult, ['ygf2', 'ygf'], ['yg'])
        for j in range(4):
            pb_, pbk = bank()
            for t in range(8):
                mm(pb_[:, t * 32:(t + 1) * 32], yg[:, t, j * 8:(j + 1) * 8, :].rearrange("p g c -> p (g c)"), identb[0:32, 0:32], True, True, ['yg', 'identb'], [pbk])
            cp('dve' if j % 2 == 0 else 'act', ygT[:, j, :].rearrange("p (n t) -> p t n", t=8),
               pb_[:, 0:256].rearrange("p (t n) -> p t n", n=32), [pbk], ['ygT'])
        for jo in range(4):
            pb_, pbk = bank()
            for j in range(4):
                mm(pb_[:, 0:NT], wglu[:, j, jo * 128:(jo + 1) * 128], ygT[:, j, :], j == 0, j == 3, ['wglu', 'ygT'], [pbk])
            act(tf[0][:, 0:NT], pb_[:, 0:NT], AF.Sigmoid, [pbk, 'bglu'], ['tf0'], bias=bglu[:, jo:jo + 1])
            tt('dve', tf[0][:, 0:NT], tf[0][:, 0:NT], ygT[:, jo, :], ALU.mult, ['tf0', 'ygT'], ['tf0'])
            tt('dve', yaT[:, jo, :], tf[0][:, 0:NT], zaT[:, jo, :], ALU.mult, ['tf0', 'zaT'], ['zaT'])

    def proj_fm(l, gi, nchunk, cols, evac):
        wt, wk = wload('in', l, gi)
        for j in range(nchunk):
            pb_, pbk = bank()
            n = cols.stop - cols.start
            for k in range(KT):
                mm(pb_[:, 0:n], wt[:, k, j * 128:(j + 1) * 128], hT[:, k, cols], k == 0, k == KT - 1, [wk, 'hT'], [pbk])
            evac(gi, j, pb_[:, 0:n], pbk)

    def rope(out, pin, pk, n, okey):
        cp('act', qraw[:, 0:n], pin, [pk], ['pTs4'])
        pr, prk = bank()
        mm(pr[:, 0:n], rotT[:], qraw[:, 0:n], True, True, ['rotT', 'pTs4'], [prk])
        tt('dve', tf[2][:, 0:n], pin, rc[:, 0:n], ALU.mult, [pk, 'rc'], ['tf2'])
        tt('dve', tf[3][:, 0:n], pr[:, 0:n], rs[:, 0:n], ALU.mult, [prk, 'rs'], ['tf3'])
        tt('dve', out, tf[2][:, 0:n], tf[3][:, 0:n], ALU.add, ['tf2', 'tf3'], [okey])

    def attn_finish(pvb, pvk, pdb, pdk, heads, hw):
        n_ = len(heads) * hw
        cp('act', tf[1][:, 0:n_], pvb[:, 0:n_], [pvk], ['tf1'])
        cp('act', tf[2][:, 0:n_], pdb[:, 0:n_], [pdk], ['tf2'])
        for hi, h in enumerate(heads):
            cs = slice(hi * hw, (hi + 1) * hw)
            ts('dve', tf[0][:, 0:hw], tf[2][:, cs], esink[:, h:h + 1], ALU.add, ['tf2', 'esink'], ['tf0'])
            recip(tf[0][:, 0:hw], tf[0][:, 0:hw], ['tf0'], ['tf0'])
            tt('dve', tf[0][:, 0:hw], tf[0][:, 0:hw], tf[1][:, cs], ALU.mult, ['tf0', 'tf1'], ['tf0'])
            yield h, tf[0][:, 0:hw]

    import os as _os

    pre_ctx = {}

    def tile_head(l, kind, idx):
        cond = 0 if kind == 'p' else 1
        own = kind == 'o'
        src = (xp if l == 0 else zp) if kind == 'p' else (xs if l == 0 else zs)
        dst = (zp if l == 0 else yp) if kind == 'p' else (zs if l == 0 else ys)
        rkeys = [] if l == 0 else ['dst_p_0' if kind == 'p' else 'dst_s_0']
        dkey = 'dst_%s_%d' % ('p' if kind == 'p' else 's', l)
        t0 = idx * NT
        if kind == 'p':
            rows = [t0, t0 + 128]
            cc = 0
            E = 256
        elif own:
            rows = [('ind', 2 * idx + b_) for b_ in range(4)]
            cc = 128
            E = 512
            sc.dma('pool', rc[:, 0:E], c_ropec_o[:, 256 * idx:256 * idx + 512], w=['rc'])
            sc.dma('pool', rs[:, 0:E], c_ropes_o[:, 256 * idx:256 * idx + 512], w=['rs'])
            for d in range(2):
                for w_ in range(2):
                    G = slice(d * 32, d * 32 + 32)
                    ts('dve', Psel[:, d, w_, :], Pst[:, idx + d, w_, G], oh4[:, 0:1], ALU.mult, ['Pst', 'oh4'], ['Psel'])
                    for j_ in range(1, 4):
                        sc.op('dve', lambda e, d=d, w_=w_, j_=j_, G=G: e.scalar_tensor_tensor(
                            out=Psel[:, d, w_, :], in0=Pst[:, 4 * j_ + idx + d, w_, G], scalar=oh4[:, j_:j_ + 1], in1=Psel[:, d, w_, :],
                            op0=ALU.mult, op1=ALU.add), ['Pst', 'oh4', 'Psel'], ['Psel'])
        else:
            lo = t0 - 128 if idx > 0 else t0
            hi = t0 + NT + 128 if idx < 15 else t0 + NT
            rows = list(range(lo, hi, 128))
            cc = t0 - lo
            E = hi - lo
            sc.dma('pool', rc[:, 0:E], c_ropec[:, lo:hi], w=['rc'])
            sc.dma('pool', rs[:, 0:E], c_ropes[:, lo:hi], w=['rs'])
        ln_ht(src, rows, 0, cond, rkeys)
        return dict(cond=cond, own=own, src=src, dst=dst, rkeys=rkeys, dkey=dkey, t0=t0, cc=cc, E=E)

    def tile_main(l, kind, idx, nxt=None):
        c_ = pre_ctx.pop((l, kind, idx), None)
        if c_ is None:
            c_ = tile_head(l, kind, idx)
        cond, own, src, dst, rkeys, dkey, t0, cc, E = (c_[k_] for k_ in ('cond', 'own', 'src', 'dst', 'rkeys', 'dkey', 't0', 'cc', 'E'))
        ctr = slice(cc, cc + NT)
        proj_xa(l, cc)
        scale = 128 ** -0.5

        def ev(gi, j, pin, pk):
            if gi in (2, 3):
                act(zaT[:, (gi - 2) * 2 + j, :], pin, AF.Silu, [pk], ['zaT'])
            elif 4 <= gi <= 7:
                hh = (gi - 4) * 2 + j
                if kind == 'p' or 'rope' in '':
                    cp('act', qT[:, hh, :], pin, [pk], ['qT'])
                else:
                    rope_q(hh, pin, pk)
            elif gi == 8:
                if kind == 'p' or 'rope' in '':
                    cp('act', kT[:, j, 0:E], pin, [pk], ['kT'])
                else:
                    rope(kT[:, j, 0:E], pin, pk, E, 'kT')
            elif 10 <= gi <= 13:
                act(zbT[:, (gi - 10) * 2 + j, :], pin, AF.Silu, [pk], ['zbT'])
            elif gi in (14, 15):
                gelu_evac(uT[:, (gi - 14) * 2 + j, :], pin, pk, [128, NT], 0, wk=['uT'])
            elif gi in (18, 19):
                act(zcT[:, (gi - 18) * 2 + j, :], pin, AF.Silu, [pk], ['zcT'])

        def rope_q(hh, pin, pk):
            cp('act', qraw[:, 0:NT], pin, [pk], ['pTs4'])
            pr, prk = bank()
            mm(pr[:, 0:NT], rotT[:], qraw[:, 0:NT], True, True, ['rotT', 'pTs4'], [prk])
            tt('dve', tf[2][:, 0:NT], pin, rc[:, ctr], ALU.mult, [pk, 'rc'], ['tf2'])
            tt('dve', tf[3][:, 0:NT], pr[:, 0:NT], rs[:, ctr], ALU.mult, [prk, 'rs'], ['tf3'])
            tt('dve', qT[:, hh, :], tf[2][:, 0:NT], tf[3][:, 0:NT], ALU.add, ['tf2', 'tf3'], ['qT'])

        for gi in (2, 3):
            proj_fm(l, gi, 2, ctr, ev)
        s5_states(idx if kind == 's' else 0, kind == 'p', reng='pool', sel=own)
        for gi in (4, 5, 6, 7):
            proj_fm(l, gi, 2, ctr, ev)
        proj_fm(l, 8, 2, slice(0, E), ev)
        for gi in (8, 9):
            if gi == 8 and kind != 'p':
                continue
            wt, wk = wload('in', l, gi)
            for s_ in range(E // 128):
                pb_, pbk = bank()
                for k in range(KT):
                    mm(pb_[:, 0:256], hT[:, k, s_ * 128:(s_ + 1) * 128], wt[:, k, :], k == 0, k == KT - 1, [wk, 'hT'], [pbk])
                if gi == 9:
                    cp('act', vtok[:, s_, :], pb_[:, 0:256], [pbk], ['vtok'])
                if kind == 'p':
                    o_ = nk if gi == 8 else nv
                    cp('dve', tf[2 + s_][:, 0:256], pb_[:, 0:256], [pbk], ['tf%d' % (2 + s_)])
                    sc.dma('pool', o_[idx, l, s_ * 128:(s_ + 1) * 128, :], tf[2 + s_][:, 0:256], r=['tf%d' % (2 + s_)], w=['nkv'])
        for gi in (10, 11, 12, 13, 14, 15):
            proj_fm(l, gi, 2, ctr, ev)
        for gi in (16, 17):
            wt, wk = wload('in', l, gi)
            for s_ in range(2):
                pb_, pbk = bank()
                for k in range(KT):
                    mm(pb_[:, 0:256], hT[:, k, cc + s_ * 128:cc + (s_ + 1) * 128], wt[:, k, :], k == 0, k == KT - 1, [wk, 'hT'], [pbk])
                gelu_evac(vsg[:, s_, (gi - 16) * 256:(gi - 15) * 256], pb_[:, 0:256], pbk, [128, 256], 0, wk=['rr'])
        for s_ in range(2):
            sc.op('dve', lambda e, s_=s_: e.bn_stats(out=stats[:, 0, :], in_=vsg[:, s_, :]), ['rr'], ['stats'])
            sc.op('dve', lambda e: e.bn_aggr(out=mv[:], in_=stats[:, 0, :]), ['stats'], ['mv'])
            act(rstd[:], mv[:, 1:2], AF.Sqrt, ['mv'], ['rstd'], bias=EPS)
            recip(rstd[:], rstd[:], ['rstd'], ['rstd'])
            ts('dve', vsg[:, s_, :], vsg[:, s_, :], mv[:, 0:1], ALU.subtract, ['rr', 'mv', 'rstd'], ['rr'], s2=rstd[:, 0:1], op1=ALU.mult)
            tt('dve', vsg[:, s_, :], vsg[:, s_, :], sgb[:], ALU.mult, ['rr', 'sgb'], ['rr'])
            tt('dve', vsln[:, s_, :], vsg[:, s_, :], sbb[:], ALU.add, ['rr', 'sbb'], ['vsln'])
        for gi in (18, 19):
            proj_fm(l, gi, 2, ctr, ev)
        for g in range(4):
            for s_ in range(2):
                pb_, pbk = bank()
                mm(pb_[:, 0:128], vsln[:, s_, g * 128:(g + 1) * 128], wsT[:, g, :], True, True, ['vsln', 'wsT'], [pbk])
                tt('dve', tf[1][:, 0:128], pb_[:, 0:128], bsb[:, g * 128:(g + 1) * 128], ALU.add, [pbk, 'bsb'], ['tf1'])
                tt('dve', tf[1][:, 0:128], tf[1][:, 0:128], uT[:, g, s_ * 128:(s_ + 1) * 128], ALU.mult, ['tf1', 'uT'], ['tf1'])
                tt('dve', ycT[:, g, s_ * 128:(s_ + 1) * 128], tf[1][:, 0:128], zcT[:, g, s_ * 128:(s_ + 1) * 128], ALU.mult,
                   ['tf1', 'zcT'], ['zcT'])
        if kind == 'p':
            for kvh in range(2):
                for kb in range(2):
                    pa, pak = bank()
                    pb2, pb2k = bank()
                    for h in range(4):
                        pp, ppk = (pa, pak) if h < 2 else (pb2, pb2k)
                        mm(pp[:, (h % 2) * 256:(h % 2 + 1) * 256], kT[:, kvh, kb * 128:(kb + 1) * 128], qT[:, kvh * 4 + h, :], True, True,
                           ['kT', 'qT'], [ppk])
                    act(pTs[kb * 2][:], pa[:, 0:512], AF.Exp, [pak], ['pTs%d' % (kb * 2)], scale=scale)
                    act(pTs[kb * 2 + 1][:], pb2[:, 0:512], AF.Exp, [pb2k], ['pTs%d' % (kb * 2 + 1)], scale=scale)
                for half in range(2):
                    pv, pvk = bank()
                    pd, pdk = bank()
                    for kb in range(2):
                        mm(pv[:, 0:512], vtok[:, kb, kvh * 128:(kvh + 1) * 128], pTs[kb * 2 + half][:], kb == 0, kb == 1,
                           ['vtok', 'pTs%d' % (kb * 2 + half)], [pvk])
                        mm(pd[:, 0:512], onesb[:], pTs[kb * 2 + half][:], kb == 0, kb == 1, ['onesb', 'pTs%d' % (kb * 2 + half)], [pdk])
                    for h, o in attn_finish(pv, pvk, pd, pdk, [kvh * 4 + half * 2, kvh * 4 + half * 2 + 1], 256):
                        tt('dve', ybT[:, h, :], o, zbT[:, h, :], ALU.mult, ['tf0', 'zbT'], ['qT'])
        elif 'attn' not in '':
            for kvh in range(2):
                for qb in range(2):
                    ia = 2 * idx + qb
                    blocks = []
                    if own:
                        eq = cc + qb * 128
                        blocks.append(('l', eq - 128, mlo, 'mlo', 0 if (idx == 0 and qb == 0) else None))
                        blocks.append(('l', eq, None, None, None))
                        blocks.append(('l', eq + 128, mhi, 'mhi', 1 if (idx == 3 and qb == 1) else None))
                    else:
                        if ia - 1 >= 0:
                            blocks.append(('l', (ia - 1) * 128 - (t0 - cc), mlo, 'mlo', None))
                        blocks.append(('l', ia * 128 - (t0 - cc), None, None, None))
                        if ia + 1 <= 31:
                            blocks.append(('l', (ia + 1) * 128 - (t0 - cc), mhi, 'mhi', None))
                    blocks.append(('c', 0, None, None, None))
                    blocks.append(('c', 1, None, None, None))
                    qv = qT[:, kvh * 4:(kvh + 1) * 4, qb * 128:(qb + 1) * 128]
                    for bi, (bt_, a_, msk, mk, vc) in enumerate(blocks):
                        pa, pak = bank()
                        if bt_ == 'l':
                            e0 = a_
                            kk_ = kT[:, kvh, e0:e0 + 128]
                            kkey = 'kT'
                        else:
                            kk_ = ckT[:, kvh, a_ * 128:(a_ + 1) * 128]
                            kkey = 'ckT'
                        for h_ in range(4):
                            mm(pa[:, h_ * 128:(h_ + 1) * 128], kk_, qT[:, kvh * 4 + h_, qb * 128:(qb + 1) * 128], True, True,
                               [kkey, 'qT'], [pak])
                        act(pTs[bi][:], pa[:, 0:512], AF.Exp, [pak], ['pTs%d' % bi], scale=scale)
                        if msk is not None and vc is not None:
                            sc.op('dve', lambda e, bi=bi, msk=msk, vc=vc: e.scalar_tensor_tensor(
                                out=pTs[bi][:].rearrange("p (h q) -> p h q", q=128), in0=pTs[bi][:].rearrange("p (h q) -> p h q", q=128),
                                scalar=vmask[:, vc:vc + 1], in1=bc(msk[:].unsqueeze(1), [128, 4, 128]), op0=ALU.mult, op1=ALU.mult),
                                ['pTs%d' % bi, mk, 'vmask'], ['pTs%d' % bi])
                        elif msk is not None:
                            tt('dve', pTs[bi][:].rearrange("p (h q) -> p h q", q=128), pTs[bi][:].rearrange("p (h q) -> p h q", q=128),
                               bc(msk[:].unsqueeze(1), [128, 4, 128]), ALU.mult, ['pTs%d' % bi, mk], ['pTs%d' % bi])
                    pv, pvk = bank()
                    pd, pdk = bank()
                    nb = len(blocks)
                    for bi, (bt_, a_, msk, mk, vc) in enumerate(blocks):
                        if bt_ == 'l':
                            e0 = a_
                            vv = vtok[:, e0 // 128, kvh * 128:(kvh + 1) * 128]
                            vkey = 'vtok'
                        else:
                            vv = cvb[:, a_, kvh * 128:(kvh + 1) * 128]
                            vkey = 'cvb'
                        mm(pv[:, 0:512], vv, pTs[bi][:], bi == 0, bi == nb - 1, [vkey, 'pTs%d' % bi], [pvk])
                        mm(pd[:, 0:512], onesb[:], pTs[bi][:], bi == 0, bi == nb - 1, ['onesb', 'pTs%d' % bi], [pdk])
                    for h, o in attn_finish(pv, pvk, pd, pdk, [kvh * 4 + i for i in range(4)], 128):
                        tt('dve', ybT[:, h, qb * 128:(qb + 1) * 128], o, zbT[:, h, qb * 128:(qb + 1) * 128], ALU.mult, ['tf0', 'zbT'], ['qT'])
        s5_main(l, idx if kind == 's' else 0, kind == 'p', idx if kind == 'p' else None, sel=own)
        for fg in range(8):
            for br, (yT_, yk, Kp) in enumerate(((yaT, 'zaT', 4), (ybT, 'qT', 8), (ycT, 'zcT', 4))):
                for j in range(2):
                    wt2, wk2 = wload2(l, (fg * 3 + br) * 2 + j, Kp)
                    pg, pgk = bank()
                    pq, pqk = bank()
                    for k in range(KT):
                        mm(pg[:, 0:NT], wt2[:, k, :], hT[:, k, ctr], k == 0, k == KT - 1, [wk2, 'hT'], [pgk])
                    for k in range(Kp):
                        mm(pq[:, 0:NT], wt2[:, 16 + k, :], yT_[:, k, :], k == 0, k == Kp - 1, [wk2, yk], [pqk])
                    act(tf[2][:, 0:NT], pg[:, 0:NT], AF.Sigmoid, [pgk], ['tf2'])
                    f = fg * 2 + j
                    if br == 0:
                        tt('dve', merged[:, f, :], tf[2][:, 0:NT], pq[:, 0:NT], ALU.mult, ['tf2', pqk], ['merged'])
                    else:
                        tt('dve', tf[2][:, 0:NT], tf[2][:, 0:NT], pq[:, 0:NT], ALU.mult, ['tf2', pqk], ['tf2'])
                        tt('dve', merged[:, f, :], merged[:, f, :], tf[2][:, 0:NT], ALU.add, ['merged', 'tf2'], ['merged'])
        if nxt is not None:
            pre_ctx[nxt] = tile_head(*nxt)
        gate_b = V[:].rearrange("p g n -> p (g n)")
        lng_b = Vs[:].rearrange("p g n -> p (g n)")
        lnb_b = tfall[:].rearrange("p a c -> p (a c)")
        tfk = ['tf0', 'tf1', 'tf2', 'tf3']
        sc.dma('pool', gate_b, gsc[cond].partition_broadcast(128), r=['gsc'], w=['V'])
        sc.dma('pool', lng_b, ln_g[l].partition_broadcast(128), w=['Vs'])
        sc.dma('pool', lnb_b, ln_b[l].partition_broadcast(128), w=tfk)
        for s_ in range(2):
            for fb in range(8):
                wt, wk = wload('out', l, fb)
                pb_, pbk = bank()
                for k in range(KT):
                    mm(pb_[:, 0:256], merged[:, k, s_ * 128:(s_ + 1) * 128], wt[:, k, :], k == 0, k == KT - 1, [wk, 'merged'], [pbk])
                tt('dve', rr[:, fb * 256:(fb + 1) * 256], pb_[:, 0:256], gate_b[:, fb * 256:(fb + 1) * 256], ALU.mult, [pbk, 'V'], ['rr'])
            xk = 'xin0'
            rk = 'rr'
            if own:
                sc.dma('pool', xin[0][:], src, r=['ridx'] + rkeys, w=[xk], ind=ridx[:, 2 * idx + 1 + s_:2 * idx + 2 + s_])
            else:
                sc.dma('sp', xin[0][:], src[t0 + s_ * 128:t0 + (s_ + 1) * 128, :], r=rkeys, w=[xk])
            sc.op('dve', lambda e: e.scalar_tensor_tensor(out=rr[:], in0=xin[0][:], scalar=ALPHA, in1=rr[:],
                                                          op0=ALU.mult, op1=ALU.add), [xk, rk], [rk])
            for q in range(4):
                sc.op('dve', lambda e, q=q: e.bn_stats(out=stats[:, q, :], in_=rr[:, q * 512:(q + 1) * 512]), [rk], ['stats'])
            sc.op('dve', lambda e: e.bn_aggr(out=mv[:], in_=stats[:].rearrange("p a b -> p (a b)")), ['stats'], ['mv'])
            act(rstd[:], mv[:, 1:2], AF.Sqrt, ['mv'], ['rstd'], bias=EPS)
            recip(rstd[:], rstd[:], ['rstd'], ['rstd'])
            ts('dve', rr[:], rr[:], mv[:, 0:1], ALU.subtract, [rk, 'mv', 'rstd'], [rk], s2=rstd[:, 0:1], op1=ALU.mult)
            tt('dve', rr[:], rr[:], lng_b, ALU.mult, [rk, 'Vs'], [rk])
            tt('dve', rr[:], rr[:], lnb_b, ALU.add, [rk] + tfk, [rk])
            sc.dma('pool', dst[t0 + s_ * 128:t0 + (s_ + 1) * 128, :], rr[:], r=[rk], w=[dkey])

    def chain(pin_, pout, G, Ls, Lw, keys):
        for w_ in range(2):
            cx = A2ix if w_ == 0 else nA2ix
            tt('dve', tA[:, 0:32], Pst[:, pin_, w_, G], A2r[:, G], ALU.mult, ['Pst', 'A2'], ['tA'])
            tt('dve', tB[:, 0:32], Pst[:, pin_, 1 - w_, G], cx[:, G], ALU.mult, ['Pst', 'A2x'], ['tB'])
            tt('dve', tA[:, 0:32], tA[:, 0:32], tB[:, 0:32], ALU.add, ['tA', 'tB'], ['tA'])
            tt('dve', Pst[:, pout, w_, G], tA[:, 0:32], Ls if w_ == 0 else Lw, ALU.add, ['tA'] + keys, ['Pst'])

    def sample_prepass(l):
        src = xs if l == 0 else zs
        cp('dve', Pst[:, 0, 0, 0:32], sinit[:, 0, :], ['sinit'], ['Pst'])
        cp('dve', Pst[:, 0, 1, 0:32], sinitw[:, 0, :], ['sinitw'], ['Pst'])
        cp('dve', Pst[:, 16, 0, 32:64], sinit[:, 1, :], ['sinit'], ['Pst'])
        cp('dve', Pst[:, 16, 1, 32:64], sinitw[:, 1, :], ['sinitw'], ['Pst'])
        TRv = rr[:].rearrange("p (g n) -> p g n", n=32)
        TIv = tfall[:].rearrange("p a c -> p (a c)").rearrange("p (g n) -> p g n", n=32)
        tkeys = ['rr', 'tf0', 'tf1', 'tf2', 'tf3']
        tmp = xin[0][:].rearrange("p (g n) -> p g n", n=32)
        AX = mybir.AxisListType.X

        def sums(t):
            for q_, (ta_, va_, vk_) in enumerate(((TRv, V, 'V'), (TIv, Vs, 'Vs'), (TRv, Vs, 'Vs'), (TIv, V, 'V'))):
                tt('dve', tmp, ta_, va_[:], ALU.mult, tkeys + [vk_], ['xin0'])
                sc.op('dve', lambda e, q_=q_: e.tensor_reduce(out=Ssum[:, q_, :], in_=tmp, axis=AX, op=ALU.add), ['xin0'], ['Ssum'])
            tt('dve', Ssum[:, 0, :], Ssum[:, 0, :], Ssum[:, 1, :], ALU.add, ['Ssum'], ['Ssum'])
            tt('dve', Ssum[:, 2, :], Ssum[:, 2, :], Ssum[:, 3, :], ALU.subtract, ['Ssum'], ['Ssum'])
            chain(t, t + 1, slice(0, 32), Ssum[:, 0, 0:32], Ssum[:, 2, 0:32], ['Ssum'])
            cp('dve', Lst[:, t, 0, :], Ssum[:, 0, 32:64], ['Ssum'], ['Lst'])
            cp('dve', Lst[:, t, 1, :], Ssum[:, 2, 32:64], ['Ssum'], ['Lst'])

        for t in range(16):
            ln_ht(src, [t * NT, t * NT + 128], 0, 1, [] if l == 0 else ['dst_s_0'])
            proj_xa(l, 0)
            if t > 0:
                sums(t - 1)
            s5_states(0, True, rec=False)
        sums(15)
        for t in range(15, -1, -1):
            chain(t + 1, t, slice(32, 64), Lst[:, t, 0, :], Lst[:, t, 1, :], ['Lst'])

    if stop is None:
        for l in range(2):
            layer_prep(l)
            wconvert(l)
            sample_prepass(l)
            seq_ = [(l, 'p', i) for i in range(NPS)]
            seq_ += [(l, 's', t) for t in range(16)] if l == 0 else [(l, 'o', t) for t in range(4)]
            for i_, tl_ in enumerate(seq_):
                tile_main(*tl_, nxt=seq_[i_ + 1] if i_ + 1 < len(seq_) else None)
    else:
        layer_prep(0)
        wconvert(0)
        if stop == 'ptile':
            tile_main(0, 'p', 0)
        if stop == 'pre':
            sample_prepass(0)
        if stop == 'own':
            sample_prepass(0)
            tile_main(0, 'o', 0)
            tile_main(0, 'o', 3)
        if stop == 'stile':
            sample_prepass(0)
            tile_main(0, 's', 0, nxt=(0, 's', 1))
            tile_main(0, 's', 1)
        loc = dict(locals())
        for nm in dumps:
            if nm in ('s5w', 'gsc', 'zp', 'zs'):
                src_ap = loc[nm]
                key = nm if nm in ('s5w', 'gsc') else ('dst_p_0' if nm == 'zp' else 'dst_s_0')
                o = nc.dram_tensor("dbg_" + nm, list(src_ap.shape), src_ap.dtype, kind="ExternalOutput").ap()
                sc.dma('sp', o, src_ap, r=[key], w=['dbg_' + nm])
            else:
                t_ = loc[nm]
                o = nc.dram_tensor("dbg_" + nm, list(t_.shape), t_.dtype, kind="ExternalOutput").ap()
                sc.dma('sp', o, t_[:], r=[nm, 'PW', 'C', 'Bb', 'A2', 'A2x', 'V', 'Vs'], w=['dbg_' + nm])
    counts = sc.emit(es)
    es.close()
    return nc, counts


_CACHE = {}


def kernel(x_prompt, x_sample, cache_k, cache_v, state_ssm, c, c_ctx,
           w_ada, b_ada, w_in, ssm_lam_re, ssm_lam_im, ssm_log_step,
           ssm_b_re, ssm_b_im, ssm_c_re, ssm_c_im, ssm_d, w_glu, b_glu,
           attn_sink, sgu_ln_g, sgu_ln_b, w_spatial, b_spatial,
           w_proj_a, w_proj_b, w_proj_c, w_out, ln_g, ln_b):
    f = lambda a: np.ascontiguousarray(np.asarray(a, dtype=np.float32))
    if 'nc' not in _CACHE:
        _CACHE['nc'] = build()[0]
    nc = _CACHE['nc']
    consts = _host_consts()
    shared = dict(w_ada=f(w_ada), b_ada=f(b_ada), w_in=f(w_in), lam_re=f(ssm_lam_re), lam_im=f(ssm_lam_im),
                  log_step=f(ssm_log_step), b_re=f(ssm_b_re), b_im=f(ssm_b_im), c_re=f(ssm_c_re), c_im=f(ssm_c_im),
                  ssm_d=f(ssm_d), w_glu=f(w_glu), b_glu=f(b_glu), sink=f(attn_sink), sgu_g=f(sgu_ln_g), sgu_b=f(sgu_ln_b),
                  w_s=f(w_spatial), b_s=f(np.asarray(b_spatial).reshape(2, 512)),
                  w_pa=f(w_proj_a), w_pb=f(w_proj_b), w_pc=f(w_proj_c), w_out=f(w_out), ln_g=f(ln_g), ln_b=f(ln_b))
    shared.update(consts)
    x_prompt = np.asarray(x_prompt); x_sample = np.asarray(x_sample)
    cache_k = np.asarray(cache_k); cache_v = np.asarray(cache_v); state_ssm = np.asarray(state_ssm)
    c = np.asarray(c); c_ctx = np.asarray(c_ctx)
    in_maps = []
    for core in range(8):
        b = core // 4
        m = dict(shared)
        m['xp'] = f(x_prompt[core * NPS:(core + 1) * NPS].reshape(NPS * LP, D))
        m['xs'] = f(x_sample[b])
        m['ck'] = f(cache_k[b].reshape(2, 256, 256))
        m['cv'] = f(cache_v[b].reshape(2, 256, 256))
        m['st0'] = f(state_ssm[b])
        m['cvec'] = f(np.stack([c_ctx, c[b]], axis=0))
        m.update(_core_consts(core, consts))
        in_maps.append(m)
    res = run_bass_kernel_spmd(nc, in_maps, core_ids=list(range(8)))
    R = res.results
    y_prompt = np.concatenate([R[i]['yp'].reshape(NPS, LP, D) for i in range(8)], axis=0).astype(np.float32)
    y_sample = np.stack([np.concatenate([R[b_ * 4 + j_]['ys_own'] for j_ in range(4)], axis=0) for b_ in range(2)], axis=0).astype(np.float32)
    nk_ = np.concatenate([R[i]['nk'].reshape(NPS, 2, LP, 2, 128) for i in range(8)], axis=0).astype(np.float32)
    nv_ = np.concatenate([R[i]['nv'].reshape(NPS, 2, LP, 2, 128) for i in range(8)], axis=0).astype(np.float32)
    ns_ = np.concatenate([R[i]['nst'] for i in range(8)], axis=0).astype(np.float32)
    return (y_prompt, y_sample, nk_, nv_, ns_)
```
